# Optimizing a Trainium2 kernel written in Bass

```python
import math
import jax, jax.numpy as jnp
from jax import lax
import numpy as np

D_MODEL = 1024
BATCH = 4
SEQ = 8192
DEPTH = 1
DEC_BATCH = 8
DEC_SEQ = 32
PAST_LEN = 4096

CHUNK = 64
D_MIX = D_MODEL
D_ATT = D_MIX // 2
D_LRU = D_MIX - D_ATT
N_HEADS_A = 4
D_HEAD_V = D_ATT // N_HEADS_A
D_HEAD_QK = D_HEAD_V // 2
N_BLOCKS_LRU = 8
D_BLOCK_LRU = D_LRU // N_BLOCKS_LRU
CONV_W = 4
LRU_C = 8.0
D_FF = 2816
Q_BLOCK = 128
RMS_EPS = 1e-6
D_IN = 3 * D_ATT + 2 * D_LRU
NEG_INF = -1e30

kernel_name = "hybrid_diffattn_rglru_streaming_step"


def rmsnorm(x, g):
    xf = x.astype(jnp.float32)
    y = xf * lax.rsqrt(jnp.mean(xf * xf, axis=-1, keepdims=True) + RMS_EPS)
    return (y * g.astype(jnp.float32)).astype(x.dtype)


def swiglu(x, w_gate, w_up, w_down):
    return (jax.nn.silu(x @ w_gate) * (x @ w_up)) @ w_down


def alibi_slopes():
    return jnp.asarray(2.0 ** (-8.0 * np.arange(1, N_HEADS_A + 1) / N_HEADS_A), dtype=jnp.float32)


def diff_attn_core(q, k, v, q_pos, k_pos, lam):
    scale = 1.0 / math.sqrt(D_HEAD_QK)
    s = jnp.einsum('bqhcd,bkhcd->bhcqk', q.astype(jnp.float32), k.astype(jnp.float32)) * scale
    dist = jnp.abs(q_pos[:, None] - k_pos[None, :]).astype(jnp.float32)
    bias = -alibi_slopes()[None, :, None, None, None] * dist[None, None, None]
    mask = (q_pos[:, None] // CHUNK) >= (k_pos[None, :] // CHUNK)
    s = jnp.where(mask[None, None, None], s + bias, NEG_INF)
    p = jax.nn.softmax(s, axis=-1)
    p = p[:, :, 0] - lam * p[:, :, 1]
    return jnp.einsum('bhqk,bkhd->bqhd', p, v.astype(jnp.float32))


def diff_attn_prompt(q, k, v, lam):
    B, T = q.shape[0], q.shape[1]
    nb = T // Q_BLOCK
    qb = q.reshape(B, nb, Q_BLOCK, N_HEADS_A, 2, D_HEAD_QK).swapaxes(0, 1)
    pb = jnp.arange(T, dtype=jnp.int32).reshape(nb, Q_BLOCK)
    kpos = jnp.arange(T, dtype=jnp.int32)
    o = lax.map(lambda a: diff_attn_core(a[0], k, v, a[1], kpos, lam), (qb, pb))
    return o.swapaxes(0, 1).reshape(B, T, N_HEADS_A, D_HEAD_V)


def causal_conv(x, buf, w, b):
    T = x.shape[1]
    xp = jnp.concatenate([buf.astype(x.dtype), x], axis=1)
    y = b + sum(xp[:, j:j + T] * w[j] for j in range(CONV_W))
    return y, xp[:, -(CONV_W - 1):]


def block_diag(x, w, b):
    B, T = x.shape[0], x.shape[1]
    xb = x.reshape(B, T, N_BLOCKS_LRU, D_BLOCK_LRU)
    return jnp.einsum('btnc,ncd->btnd', xb, w).reshape(B, T, D_LRU) + b


def rglru(x, h0, w_r, b_r, w_i, b_i, lru_lambda):
    xf = x.astype(jnp.float32)
    r = jax.nn.sigmoid(block_diag(xf, w_r.astype(jnp.float32), b_r.astype(jnp.float32)))
    i = jax.nn.sigmoid(block_diag(xf, w_i.astype(jnp.float32), b_i.astype(jnp.float32)))
    log_a = -LRU_C * r * jax.nn.softplus(-lru_lambda.astype(jnp.float32))
    a = jnp.exp(log_a)
    bx = jnp.sqrt(-jnp.expm1(2.0 * log_a)) * (i * xf)
    bx = bx.at[:, 0].add(a[:, 0] * h0.astype(jnp.float32))

    def combine(c1, c2):
        a1, b1 = c1
        a2, b2 = c2
        return a1 * a2, a2 * b1 + b2

    _, h = lax.associative_scan(combine, (a, bx), axis=1)
    return h.astype(x.dtype), h[:, -1].astype(x.dtype)


def mixer(xn, past_k, past_v, h0, conv_buf, w_in, w_out, lq1, lk1, lq2, lk2, subln_g,
          conv_w, conv_b, w_rg, b_rg, w_ig, b_ig, lru_lambda, lambda_init):
    B, T = xn.shape[0], xn.shape[1]
    proj = xn @ w_in
    q = proj[..., :D_ATT].reshape(B, T, N_HEADS_A, 2, D_HEAD_QK)
    k = proj[..., D_ATT:2 * D_ATT].reshape(B, T, N_HEADS_A, 2, D_HEAD_QK)
    v = proj[..., 2 * D_ATT:3 * D_ATT].reshape(B, T, N_HEADS_A, D_HEAD_V)
    lru_x = proj[..., 3 * D_ATT:3 * D_ATT + D_LRU]
    lru_gate = proj[..., 3 * D_ATT + D_LRU:]
    lam = (jnp.exp(jnp.sum(lq1.astype(jnp.float32) * lk1.astype(jnp.float32)))
           - jnp.exp(jnp.sum(lq2.astype(jnp.float32) * lk2.astype(jnp.float32))) + lambda_init)
    if past_k is None:
        o = diff_attn_prompt(q, k, v, lam)
    else:
        P = past_k.shape[1]
        k_all = jnp.concatenate([past_k.astype(k.dtype), k], axis=1)
        v_all = jnp.concatenate([past_v.astype(v.dtype), v], axis=1)
        q_pos = P + jnp.arange(T, dtype=jnp.int32)
        k_pos = jnp.arange(P + T, dtype=jnp.int32)
        o = diff_attn_core(q, k_all, v_all, q_pos, k_pos, lam)
    o = rmsnorm(o, subln_g) * (1.0 - lambda_init)
    att_out = o.reshape(B, T, D_ATT).astype(xn.dtype)
    xc, new_buf = causal_conv(lru_x, conv_buf, conv_w, conv_b)
    h, h_last = rglru(xc, h0, w_rg, b_rg, w_ig, b_ig, lru_lambda)
    lru_out = h * jax.nn.gelu(lru_gate, approximate=True)
    out = jnp.concatenate([att_out, lru_out], axis=-1) @ w_out
    return out, k, v, h_last, new_buf


def layer(x, past_k, past_v, h0, conv_buf, lp, lambda_init):
    (w_in, w_out, lq1, lk1, lq2, lk2, subln_g, conv_w, conv_b, w_rg, b_rg, w_ig, b_ig, lru_lambda,
     f1g, f1u, f1d, f2g, f2u, f2d, g1a, g1b, gma, gmb, g2a, g2b) = lp
    x = x + 0.5 * rmsnorm(swiglu(rmsnorm(x, g1a), f1g, f1u, f1d), g1b)
    m, k_new, v_new, h_last, new_buf = mixer(rmsnorm(x, gma), past_k, past_v, h0, conv_buf, w_in, w_out,
                                             lq1, lk1, lq2, lk2, subln_g, conv_w, conv_b, w_rg, b_rg,
                                             w_ig, b_ig, lru_lambda, lambda_init)
    x = x + rmsnorm(m, gmb)
    x = x + 0.5 * rmsnorm(swiglu(rmsnorm(x, g2a), f2g, f2u, f2d), g2b)
    return x, k_new, v_new, h_last, new_buf


def setup_inputs(seed: int = 0) -> dict:
    key = jax.random.key(seed)
    ks = iter(jax.random.split(key, 40))
    f32 = jnp.float32

    def nrm(shape, scale):
        return jax.random.normal(next(ks), shape, f32) * scale

    def gain():
        return jnp.ones((DEPTH, D_MODEL), f32) + nrm((DEPTH, D_MODEL), 0.01)

    a_init = jax.random.uniform(next(ks), (DEPTH, D_LRU), f32, 0.9, 0.999)
    s_init = a_init ** (1.0 / LRU_C)
    lru_lambda = jnp.log(s_init) - jnp.log1p(-s_init)
    return {
        "x_prompt": nrm((BATCH, SEQ, D_MODEL), 1.0),
        "x_sample": nrm((DEC_BATCH, DEC_SEQ, D_MODEL), 1.0),
        "cache_k": nrm((DEPTH, DEC_BATCH, PAST_LEN, N_HEADS_A, 2, D_HEAD_QK), 1.0),
        "cache_v": nrm((DEPTH, DEC_BATCH, PAST_LEN, N_HEADS_A, D_HEAD_V), 1.0),
        "state_lru_h": nrm((DEPTH, DEC_BATCH, D_LRU), 0.5),
        "state_conv": nrm((DEPTH, DEC_BATCH, CONV_W - 1, D_LRU), 1.0),
        "w_in": nrm((DEPTH, D_MODEL, D_IN), D_MODEL ** -0.5),
        "w_out": nrm((DEPTH, D_MIX, D_MODEL), D_MIX ** -0.5),
        "lambda_q1": nrm((DEPTH, D_HEAD_QK), 0.1),
        "lambda_k1": nrm((DEPTH, D_HEAD_QK), 0.1),
        "lambda_q2": nrm((DEPTH, D_HEAD_QK), 0.1),
        "lambda_k2": nrm((DEPTH, D_HEAD_QK), 0.1),
        "subln_g": jnp.ones((DEPTH, D_HEAD_V), f32) + nrm((DEPTH, D_HEAD_V), 0.01),
        "conv_w": nrm((DEPTH, CONV_W, D_LRU), CONV_W ** -0.5),
        "conv_b": nrm((DEPTH, D_LRU), 0.01),
        "w_rgate": nrm((DEPTH, N_BLOCKS_LRU, D_BLOCK_LRU, D_BLOCK_LRU), D_BLOCK_LRU ** -0.5),
        "b_rgate": nrm((DEPTH, D_LRU), 0.01),
        "w_igate": nrm((DEPTH, N_BLOCKS_LRU, D_BLOCK_LRU, D_BLOCK_LRU), D_BLOCK_LRU ** -0.5),
        "b_igate": nrm((DEPTH, D_LRU), 0.01),
        "lru_lambda": lru_lambda,
        "ffn1_w_gate": nrm((DEPTH, D_MODEL, D_FF), D_MODEL ** -0.5),
        "ffn1_w_up": nrm((DEPTH, D_MODEL, D_FF), D_MODEL ** -0.5),
        "ffn1_w_down": nrm((DEPTH, D_FF, D_MODEL), D_FF ** -0.5),
        "ffn2_w_gate": nrm((DEPTH, D_MODEL, D_FF), D_MODEL ** -0.5),
        "ffn2_w_up": nrm((DEPTH, D_MODEL, D_FF), D_MODEL ** -0.5),
        "ffn2_w_down": nrm((DEPTH, D_FF, D_MODEL), D_FF ** -0.5),
        "g_ffn1_pre": gain(),
        "g_ffn1_post": gain(),
        "g_mix_pre": gain(),
        "g_mix_post": gain(),
        "g_ffn2_pre": gain(),
        "g_ffn2_post": gain(),
    }


def reference(x_prompt, x_sample, cache_k, cache_v, state_lru_h, state_conv, w_in, w_out,
              lambda_q1, lambda_k1, lambda_q2, lambda_k2, subln_g, conv_w, conv_b,
              w_rgate, b_rgate, w_igate, b_igate, lru_lambda,
              ffn1_w_gate, ffn1_w_up, ffn1_w_down, ffn2_w_gate, ffn2_w_up, ffn2_w_down,
              g_ffn1_pre, g_ffn1_post, g_mix_pre, g_mix_post, g_ffn2_pre, g_ffn2_post):
    xp, xs = x_prompt, x_sample
    kp_l, vp_l, hp_l, cp_l, ks_l, vs_l, hs_l, cs_l = [], [], [], [], [], [], [], []
    for l in range(DEPTH):
        lambda_init = 0.8 - 0.6 * math.exp(-0.3 * l)
        lp = (w_in[l], w_out[l], lambda_q1[l], lambda_k1[l], lambda_q2[l], lambda_k2[l], subln_g[l],
              conv_w[l], conv_b[l], w_rgate[l], b_rgate[l], w_igate[l], b_igate[l], lru_lambda[l],
              ffn1_w_gate[l], ffn1_w_up[l], ffn1_w_down[l], ffn2_w_gate[l], ffn2_w_up[l], ffn2_w_down[l],
              g_ffn1_pre[l], g_ffn1_post[l], g_mix_pre[l], g_mix_post[l], g_ffn2_pre[l], g_ffn2_post[l])
        h0_p = jnp.zeros((xp.shape[0], D_LRU), xp.dtype)
        buf_p = jnp.zeros((xp.shape[0], CONV_W - 1, D_LRU), xp.dtype)
        xp, k_p, v_p, h_p, c_p = layer(xp, None, None, h0_p, buf_p, lp, lambda_init)
        xs, k_s, v_s, h_s, c_s = layer(xs, cache_k[l], cache_v[l], state_lru_h[l], state_conv[l], lp, lambda_init)
        kp_l.append(k_p); vp_l.append(v_p); hp_l.append(h_p); cp_l.append(c_p)
        ks_l.append(k_s); vs_l.append(v_s); hs_l.append(h_s); cs_l.append(c_s)
    k_prompt = jnp.stack(kp_l); v_prompt = jnp.stack(vp_l)
    lru_h_prompt = jnp.stack(hp_l); conv_prompt = jnp.stack(cp_l)
    k_sample = jnp.stack(ks_l); v_sample = jnp.stack(vs_l)
    lru_h_sample = jnp.stack(hs_l); conv_sample = jnp.stack(cs_l)
    return (xp, xs, k_prompt, v_prompt, lru_h_prompt, conv_prompt, k_sample, v_sample, lru_h_sample, conv_sample)
```

```python
import numpy as np
import concourse.bass as bass
import concourse.mybir as mybir
from concourse.bass_utils import run_bass_kernel_spmd
from contextlib import ExitStack

F32 = mybir.dt.float32
BF16 = mybir.dt.bfloat16
AF = mybir.ActivationFunctionType
ALU = mybir.AluOpType
AX = mybir.AxisListType

D = 1024
DFF = 2816
NFF = DFF // 128
KC = D // 128
EPS = 1e-6
NEG = -1e30


class Op:
    __slots__ = ("eng", "fn", "deps", "sem", "inc", "val", "is_dma", "needs_inc")


class Prog:
    ENGS = ("pe", "act", "dve", "pool", "sp")

    def __init__(self, nc, stack):
        self.nc = nc
        self.stack = stack
        self.ops = {e: [] for e in self.ENGS}
        self.last_w = {}
        self.readers = {}
        self.barrier_ops = []
        self.esem = {e: stack.enter_context(nc.semaphore("s_" + e)) for e in ("pe", "act", "dve", "pool")}
        self.dma_sems = {}
        self.dma_last = {}
        self.n_sem = 4

    def dsem(self, name):
        if name not in self.dma_sems:
            self.dma_sems[name] = self.stack.enter_context(self.nc.semaphore("d_" + name))
            self.n_sem += 1
        return name

    PSUM_IDS = ("gu", "dn", "fm", "tm", "rg", "bank", "cv")

    def _is_psum(self, t):
        return t == "ps_tr" or (isinstance(t, tuple) and t[0] in self.PSUM_IDS)

    def _add(self, o, reads, writes):
        xr = [t for t in reads if self._is_psum(t)]
        if xr:
            reads = [t for t in reads if not self._is_psum(t)]
            writes = list(writes) + xr
        deps = set(self.barrier_ops)
        for t in reads:
            w = self.last_w.get(t)
            if w is not None:
                deps.add(w)
        for t in writes:
            w = self.last_w.get(t)
            if w is not None:
                deps.add(w)
            for r in self.readers.get(t, ()):
                deps.add(r)
        if o.eng == "pe" and not o.is_dma:
            deps = {d for d in deps if not (d.eng == "pe" and not d.is_dma)}
        deps = {(self.dma_last[d.sem] if d.is_dma else d) for d in deps}
        o.deps = deps
        for d in deps:
            d.needs_inc = True
        for t in reads:
            self.readers.setdefault(t, []).append(o)
        for t in writes:
            self.last_w[t] = o
            self.readers[t] = []
        self.ops[o.eng].append(o)

    def op(self, eng, fn, reads=(), writes=()):
        o = Op()
        o.eng = eng
        o.fn = fn
        o.is_dma = False
        o.needs_inc = False
        o.sem = None
        o.val = 0
        self._add(o, reads, writes)
        return o

    def dma(self, sem_name, fn, reads=(), writes=(), q="sp"):
        o = Op()
        o.eng = q
        o.fn = fn
        o.is_dma = True
        o.needs_inc = True
        o.sem = self.dsem(sem_name)
        o.val = 0
        self._add(o, reads, writes)
        self.dma_last[sem_name] = o
        return o

    def barrier(self):
        b = []
        for e in self.ENGS:
            for o in reversed(self.ops[e]):
                if not o.is_dma:
                    b.append(o)
                    break
        for o in self.dma_last.values():
            b.append(o)
        for o in b:
            o.needs_inc = True
        self.barrier_ops = b
        self.last_w = {}
        self.readers = {}

    def emit(self):
        nc = self.nc
        dcount = {}
        for e in self.ENGS:
            cnt = 0
            for o in self.ops[e]:
                if o.is_dma:
                    dcount[o.sem] = dcount.get(o.sem, 0) + 16
                    o.val = dcount[o.sem]
                elif o.needs_inc:
                    cnt += 1
                    o.val = cnt
        ops = self.ops
        esem = self.esem
        dsems = self.dma_sems

        def run(ename, eng):
            waited = {}
            for o in ops[ename]:
                need = {}
                for d in o.deps:
                    key = d.sem if d.is_dma else d.eng
                    if d.val > need.get(key, 0):
                        need[key] = d.val
                for key, v in need.items():
                    if waited.get(key, 0) < v:
                        sem = esem[key] if key in esem else dsems[key]
                        eng.wait_ge(sem, v)
                        waited[key] = v
                ins = o.fn(eng)
                if o.is_dma:
                    ins.then_inc(dsems[o.sem], 16)
                elif o.needs_inc:
                    ins.then_inc(esem[ename], 1)
            fin = {}
            for o in ops[ename]:
                if o.is_dma:
                    fin[o.sem] = max(fin.get(o.sem, 0), o.val)
            for s, v in fin.items():
                if waited.get(s, 0) < v:
                    eng.wait_ge(dsems[s], v)

        with nc.Block() as block:
            @block.tensor
            def _(e):
                run("pe", e)

            @block.scalar
            def _(e):
                run("act", e)

            @block.vector
            def _(e):
                run("dve", e)

            @block.gpsimd
            def _(e):
                run("pool", e)

            @block.sync
            def _(e):
                run("sp", e)


class Arena:
    def __init__(self, nc, stack, nbytes):
        self.t = stack.enter_context(nc.sbuf_tensor("arena", [128, nbytes // 4], F32))
        self.cap = nbytes
        self.off = 0
        self.marks = []

    def push(self):
        self.marks.append(self.off)

    def pop(self):
        self.off = self.marks.pop()

    def alloc(self, shape, dt):
        n = 1
        for s in shape:
            n *= s
        esz = 4 if dt == F32 else 2
        nb = (n * esz + 31) // 32 * 32
        assert self.off + nb <= self.cap, ("arena overflow", self.off, nb, self.cap)
        ap = self.t[:, self.off // 4:(self.off + nb) // 4]
        self.off += nb
        self.peak = max(getattr(self, 'peak', 0), self.off)
        if dt != F32:
            ap = ap.bitcast(dt)
        ap = ap[:, 0:n]
        if len(shape) == 2:
            ap = ap.rearrange("p (a b) -> p a b", a=shape[0])
        elif len(shape) == 3:
            ap = ap.rearrange("p (a b c) -> p a b c", a=shape[0], b=shape[1])
        return ap


import os
DBG = set(os.environ.get("KDBG", "").split(","))


def cdiv(a, b):
    return (a + b - 1) // b


class Builder:
    def __init__(self, T_OTH, T_OWN, n_samp=32, past=4096, stages="all"):
        self.T_OTH, self.T_OWN, self.NS, self.PAST = T_OTH, T_OWN, n_samp, past
        self.T = T_OTH + T_OWN
        self.stages = stages
        self.nc = bass.Bass("TRN2", target_bir_lowering=False)
        self.stack = ExitStack()
        self.P = Prog(self.nc, self.stack)
        self.A = Arena(self.nc, self.stack, 211456)
        self.inputs = {}
        self.outputs = {}
        nc = self.nc
        self.psum_all = self.stack.enter_context(nc.psum_tensor("psall", [128, 4096], F32))
        self.psum = [self.psum_all[:, i * 512:(i + 1) * 512] for i in range(8)]

    def din(self, name, shape, dt=F32):
        t = self.nc.dram_tensor(name, list(shape), dt, kind="ExternalInput").ap()
        self.inputs[name] = t
        return t

    def dout(self, name, shape, dt=F32):
        t = self.nc.dram_tensor(name, list(shape), dt, kind="ExternalOutput").ap()
        self.outputs[name] = t
        return t

    def dscr(self, name, shape, dt=F32):
        return self.nc.dram_tensor(name, list(shape), dt, kind="Internal").ap()

    def prep_weight(self, w_dram, K, N, dst, gcol, stage_bufs, tag, eng_cycle=("dve", "pool")):
        P = self.P
        nk = K // 128
        for kc in range(nk):
            sb = stage_bufs[kc % len(stage_bufs)]
            sid = ("wst", kc % len(stage_bufs))
            src = w_dram[kc * 128:(kc + 1) * 128, :]
            P.dma("wst%d" % (kc % len(stage_bufs)),
                  lambda e, sb=sb, src=src, N=N: e.dma_start(out=sb[:, 0:N], in_=src),
                  writes=[sid])
            ec = tuple(os.environ.get("K_PREP", "dve,act").split(","))
            en = ec[kc % len(ec)]
            if en == "act":
                if gcol is None:
                    P.op("act", lambda e, sb=sb, kc=kc, N=N, dst=dst: e.activation(out=dst[:, kc, :], in_=sb[:, 0:N], func=AF.Copy),
                         reads=[sid], writes=[(tag, kc)])
                else:
                    P.op("act", lambda e, sb=sb, kc=kc, N=N, dst=dst, gcol=gcol: e.activation(
                        out=dst[:, kc, :], in_=sb[:, 0:N], func=AF.Copy, scale=gcol[:, kc:kc + 1]),
                         reads=[sid, "consts"], writes=[(tag, kc)])
                continue
            if gcol is None:
                P.op(en, lambda e, sb=sb, kc=kc, N=N, dst=dst: e.tensor_copy(out=dst[:, kc, :], in_=sb[:, 0:N]),
                     reads=[sid], writes=[(tag, kc)])
            else:
                P.op(en, lambda e, sb=sb, kc=kc, N=N, dst=dst, gcol=gcol: e.tensor_scalar(
                    out=dst[:, kc, :], in0=sb[:, 0:N], scalar1=gcol[:, kc:kc + 1], scalar2=None, op0=ALU.mult),
                     reads=[sid, "consts"], writes=[(tag, kc)])

    def rstd_of(self, src_ap, np_, junk, ss, rstd, src_ids, tagid):
        P = self.P
        P.op("act", lambda e: e.activation(out=junk[:np_, :], in_=src_ap, func=AF.Square, accum_out=ss[:np_, :]),
             reads=src_ids, writes=[("junk", tagid), ("ss", tagid)])
        P.op("pool", lambda e: e.tensor_scalar(out=ss[:np_, :], in0=ss[:np_, :], scalar1=1.0 / D, scalar2=EPS,
                                               op0=ALU.mult, op1=ALU.add),
             reads=[("ss", tagid)], writes=[("ss", tagid)])
        P.op("pool", lambda e: e.tensor_tensor(out=rstd[:np_, :], in0=ss[:np_, :], in1=self.c_mhalf[:np_, :], op=ALU.pow),
             reads=[("ss", tagid), "consts"], writes=[("rstd", tagid)])

    def ffn_stage(self, segs, wg_d, wu_d, wd_d, gpre_col, gpost_bc, sname):
        P, A, nc = self.P, self.A, self.nc
        A.push()
        Wg = A.alloc([KC, DFF], BF16)
        Wu = A.alloc([KC, DFF], BF16)
        Wd = A.alloc([NFF, D], BF16)
        gph = A.alloc([D], F32)
        mark_act = A.off
        wst = [A.alloc([DFF], F32) for _ in range(5)]
        P.dma("c0", lambda e: e.dma_start(out=gph[:, :], in_=gpost_bc), writes=["gph"])
        P.op("pool", lambda e: e.tensor_scalar(out=gph[:, :], in0=gph[:, :], scalar1=0.5, scalar2=None,
                                               op0=ALU.mult), reads=["gph"], writes=["gph"])
        self.prep_weight(wg_d, D, DFF, Wg, gpre_col, wst, "Wg")
        self.prep_weight(wu_d, D, DFF, Wu, gpre_col, wst, "Wu")
        self.prep_weight(wd_d, DFF, D, Wd, None, wst, "Wd")
        P.barrier()
        A.off = mark_act
        TT = 256
        NXR = 5
        xr = [A.alloc([D], F32) for _ in range(NXR)]
        xs = [A.alloc([D], BF16) for _ in range(2)]
        xnT = [A.alloc([KC, TT], BF16) for _ in range(2)]
        actT = A.alloc([NFF, TT], BF16)
        stmp = [A.alloc([TT], F32) for _ in range(3)]
        ost = [A.alloc([D], F32) for _ in range(2)]
        junk = A.alloc([D], BF16)
        ssb = [A.alloc([1], F32) for _ in range(4)]
        rsb = [A.alloc([1], F32) for _ in range(4)]
        ssp = [A.alloc([1], F32) for _ in range(4)]
        ssq = [A.alloc([1], F32) for _ in range(4)]
        ps = self.psum
        ps_tr = ps[0][:, :].bitcast(BF16)
        gu = [ps[1], ps[2], ps[3]]
        dn = [(ps[4], ps[5]), (ps[6], ps[7])]

        tiles = []
        for (src, dst, n) in segs:
            t0 = 0
            while t0 < n:
                nt = min(TT, n - t0)
                tiles.append((src, dst, t0, nt))
                t0 += nt
        sub_ctr = [0]

        def load_tile(ti):
            src, dst, t0, nt = tiles[ti]
            subs = []
            for s0 in range(0, nt, 128):
                ns = min(128, nt - s0)
                k = sub_ctr[0] % NXR
                sub_ctr[0] += 1
                P.dma(sname + "x%d" % k, lambda e, k=k, src=src, a=t0 + s0, ns=ns: e.dma_start(
                    out=xr[k][:ns, :], in_=src[a:a + ns, :]), writes=[("xr", k)])
                subs.append((k, s0, ns))
            return subs

        loaded = {0: load_tile(0)}
        gu_ctr = 0
        st_ctr = 0
        o_ctr = 0
        for ti in range(len(tiles)):
            src, dst, t0, nt = tiles[ti]
            subs = loaded.pop(ti)
            if ti + 1 < len(tiles):
                loaded[ti + 1] = load_tile(ti + 1)
            xb = xnT[ti % 2]
            for si, (k, s0, ns) in enumerate(subs):
                sl = (ti * 2 + si) % 4
                self.rstd_of(xr[k][:ns, :], ns, junk, ssb[sl], rsb[sl], [("xr", k)], sl)
                xsb = xs[(ti * 2 + si) % 2]
                xsid = ("xs", (ti * 2 + si) % 2)
                P.op("dve", lambda e, xsb=xsb, k=k, ns=ns, sl=sl: e.tensor_scalar(
                    out=xsb[:ns, :], in0=xr[k][:ns, :], scalar1=rsb[sl][:ns, :], scalar2=None, op0=ALU.mult),
                     reads=[("xr", k), ("rstd", sl)], writes=[xsid])
                for kc in range(KC):
                    P.op("pe", lambda e, xsb=xsb, kc=kc, ns=ns: e.transpose(
                        out=ps_tr[:, kc * 128:kc * 128 + ns], in_=xsb[:ns, kc * 128:(kc + 1) * 128],
                        identity=self.ident[:ns, :ns]),
                         reads=[xsid, "consts"], writes=["ps_tr"])
                P.op("act", lambda e, xb=xb, s0=s0, ns=ns: e.activation(
                    out=xb[:, :, s0:s0 + ns], in_=ps_tr.rearrange("p (a b) -> p a b", a=KC)[:, :, 0:ns],
                    func=AF.Copy),
                     reads=["ps_tr"], writes=[("xnT", ti % 2, si)])
            xn_ids = [("xnT", ti % 2, si) for si in range(len(subs))]
            for f in range(NFF):
                g = gu[gu_ctr % 3]
                gid = ("gu", gu_ctr % 3)
                gu_ctr += 1
                for kc in range(KC):
                    P.op("pe", lambda e, g=g, kc=kc, f=f, xb=xb, nt=nt: e.matmul(
                        g[:, 0:nt], lhsT=Wg[:, kc, f * 128:(f + 1) * 128], rhs=xb[:, kc, 0:nt],
                        start=(kc == 0), stop=(kc == KC - 1)),
                         reads=xn_ids + [("Wg", kc)], writes=[gid])
                for kc in range(KC):
                    P.op("pe", lambda e, g=g, kc=kc, f=f, xb=xb, nt=nt: e.matmul(
                        g[:, 256:256 + nt], lhsT=Wu[:, kc, f * 128:(f + 1) * 128], rhs=xb[:, kc, 0:nt],
                        start=(kc == 0), stop=(kc == KC - 1)),
                         reads=xn_ids + [("Wu", kc)], writes=[gid])
                stb = stmp[st_ctr % 3]
                sid = ("stmp", st_ctr % 3)
                st_ctr += 1
                P.op("act", lambda e, g=g, stb=stb, nt=nt: e.activation(out=stb[:, 0:nt], in_=g[:, 0:nt], func=AF.Silu),
                     reads=[gid], writes=[sid])
                P.op("dve", lambda e, g=g, stb=stb, nt=nt, f=f: e.tensor_tensor(
                    out=actT[:, f, 0:nt], in0=stb[:, 0:nt], in1=g[:, 256:256 + nt], op=ALU.mult),
                     reads=[gid, sid], writes=[("actT", f)])
            for si, (k, s0, ns) in enumerate(subs):
                d0, d1 = dn[si % 2]
                did = ("dn", si % 2)
                for half, dps in enumerate((d0, d1)):
                    for f in range(NFF):
                        P.op("pe", lambda e, dps=dps, f=f, s0=s0, ns=ns, half=half: e.matmul(
                            dps[:ns, :], lhsT=actT[:, f, s0:s0 + ns], rhs=Wd[:, f, half * 512:(half + 1) * 512],
                            start=(f == 0), stop=(f == NFF - 1)),
                             reads=[("actT", f), ("Wd", f)], writes=[did])
                sl = (ti * 2 + si) % 4
                ss2 = ssp[sl]
                P.op("act", lambda e, d0=d0, ns=ns, ss2=ss2: e.activation(
                    out=junk[:ns, 0:512], in_=d0[:ns, :], func=AF.Square, accum_out=ss2[:ns, :]),
                     reads=[did], writes=[("junk", 9), ("ssA", sl)])
                ss3 = ssq[sl]
                P.op("act", lambda e, d1=d1, ns=ns, ss3=ss3: e.activation(
                    out=junk[:ns, 512:1024], in_=d1[:ns, :], func=AF.Square, accum_out=ss3[:ns, :]),
                     reads=[did], writes=[("junk", 10), ("ssB", sl)])
                P.op("pool", lambda e, ss2=ss2, ss3=ss3, ns=ns: e.tensor_tensor(
                    out=ss2[:ns, :], in0=ss2[:ns, :], in1=ss3[:ns, :], op=ALU.add),
                     reads=[("ssA", sl), ("ssB", sl)], writes=[("ssA", sl)])
                P.op("pool", lambda e, ss2=ss2, ns=ns: e.tensor_scalar(
                    out=ss2[:ns, :], in0=ss2[:ns, :], scalar1=1.0 / D, scalar2=EPS, op0=ALU.mult, op1=ALU.add),
                     reads=[("ssA", sl)], writes=[("ssA", sl)])
                P.op("pool", lambda e, ss2=ss2, ss3=ss3, ns=ns: e.tensor_tensor(
                    out=ss3[:ns, :], in0=ss2[:ns, :], in1=self.c_mhalf[:ns, :], op=ALU.pow),
                     reads=[("ssA", sl), "consts"], writes=[("ssB", sl)])
                ob = ost[o_ctr % 2]
                oid = ("ost", o_ctr % 2)
                osem = sname + "o%d" % (o_ctr % int(os.environ.get("NOSEM", "2")))
                o_ctr += 1
                for half, dps in enumerate((d0, d1)):
                    P.op("dve", lambda e, dps=dps, ob=ob, ns=ns, half=half, ss3=ss3: e.scalar_tensor_tensor(
                        out=ob[:ns, half * 512:(half + 1) * 512], in0=dps[:ns, :], scalar=ss3[:ns, :],
                        in1=gph[:ns, half * 512:(half + 1) * 512], op0=ALU.mult, op1=ALU.mult),
                         reads=[did, ("ssB", sl), "consts"], writes=[(oid, half)])
                    P.op("pool", lambda e, ob=ob, ns=ns, half=half, k=k: e.tensor_tensor(
                        out=ob[:ns, half * 512:(half + 1) * 512], in0=ob[:ns, half * 512:(half + 1) * 512],
                        in1=xr[k][:ns, half * 512:(half + 1) * 512], op=ALU.add),
                         reads=[(oid, half), ("xr", k)], writes=[(oid, half)])
                P.dma(osem, lambda e, ob=ob, dst=dst, a=t0 + s0, ns=ns: e.dma_start(
                    out=dst[a:a + ns, :], in_=ob[:ns, :]), reads=[(oid, 0), (oid, 1)])
        P.barrier()
        A.pop()

    def consts(self):
        P, A = self.P, self.A
        ident_d = self.din("ident", [128, 128], BF16)
        self.ident = A.alloc([128], BF16)
        self.c_mhalf = A.alloc([1], F32)
        P.dma("c0", lambda e: e.dma_start(out=self.ident[:, :], in_=ident_d[:, :]), writes=["consts"])
        P.op("pool", lambda e: e.memset(self.c_mhalf[:, :], -0.5), writes=["consts_b"])

    def load_const(self, name, shape, dt=F32):
        d = self.din(name, [128] + list(shape), dt)
        t = self.A.alloc(list(shape), dt)
        self.P.dma("c0", lambda e: e.dma_start(out=t, in_=d), writes=["consts_c"])
        return t


def build_ffn_test(ntok):
    B = Builder(0, ntok)
    P = B.P
    x = B.din("x", [ntok, D])
    wg = B.din("wg", [D, DFF])
    wu = B.din("wu", [D, DFF])
    wd = B.din("wd", [DFF, D])
    y = B.dout("y", [ntok, D])
    B.consts()
    gpre = B.load_const("gpre", [KC])
    gpost = B.load_const("gpost", [D])
    P.barrier()
    B.ffn_stage([(x, y, ntok)], wg, wu, wd, gpre, gpost, "f1")
    B.P.emit()
    return B


def mixer_in_stage(B, seqs, Cn, sname):
    P, A, nc = B.P, B.A, B.nc
    A.push()
    TT = 512
    Win = A.alloc([KC, 2560], BF16)
    Wrb = A.alloc([4, 128], BF16)
    Wib = A.alloc([4, 128], BF16)
    mark = A.off
    wst = [A.alloc([2560], F32) for _ in range(4)]
    wrf = A.alloc([4, 128], F32)
    wif = A.alloc([4, 128], F32)
    P.dma("c0", lambda e: e.dma_start(out=wrf, in_=B.inputs["wr_bd"]), writes=["wrf"])
    P.dma("c0", lambda e: e.dma_start(out=wif, in_=B.inputs["wi_bd"]), writes=["wif"])
    if Cn.get("cache_prep") is not None:
        cache_prep_ops(B, *Cn["cache_prep"])
    B.prep_weight(B.inputs["win"], D, 2560, Win, Cn["gma_col"], wst, "Win")
    P.op("dve", lambda e: e.tensor_copy(out=Wrb, in_=wrf), reads=["wrf"], writes=["Wrb"])
    P.op("dve", lambda e: e.tensor_copy(out=Wib, in_=wif), reads=["wif"], writes=["Wib"])
    P.barrier()
    A.off = mark
    NXR = 6
    xr = [A.alloc([D], F32) for _ in range(NXR)]
    xs = [A.alloc([D], BF16) for _ in range(2)]
    xnT = [A.alloc([KC, TT], BF16) for _ in range(2)]
    junk = A.alloc([D], BF16)
    ssb = [A.alloc([1], F32) for _ in range(4)]
    rsb = [A.alloc([1], F32) for _ in range(4)]
    kst = [A.alloc([TT], BF16) for _ in range(3)]
    vst = [A.alloc([512], F32) for _ in range(2)]
    vbf = [A.alloc([4, 129], BF16) for _ in range(2)]
    for i in range(2):
        P.op("pool", lambda e, i=i: e.memset(vbf[i], 1.0), writes=[("vbf", i)])
    lxb = [[A.alloc([TT + 4], BF16) for _ in range(2)] for _ in range(4)]
    lxl = A.alloc([4, 3], F32)
    lx0 = A.alloc([4, 3], F32)
    dgw = A.alloc([16, 128], BF16)
    xcb = [A.alloc([TT], BF16) for _ in range(4)]
    rb = [A.alloc([TT], F32) for _ in range(4)]
    ib = [A.alloc([TT], F32) for _ in range(4)]
    ab = [A.alloc([TT], F32) for _ in range(4)]
    a2b = [A.alloc([TT], F32) for _ in range(4)]
    hb = [A.alloc([TT], F32) for _ in range(4)]
    gt = [[A.alloc([TT], F32) for _ in range(4)] for _ in range(2)]
    tb = [A.alloc([TT], F32) for _ in range(4)]
    lob = [A.alloc([TT], BF16) for _ in range(4)]
    hstate = A.alloc([4], F32)
    cL = A.alloc([4], F32)
    cL2 = A.alloc([4], F32)
    ps = B.psum
    ps_tr = ps[0][:, :].bitcast(BF16)
    fm = [ps[1], ps[2]]
    cvb = ps[3]
    tm = [ps[4], ps[5]]
    rg = [ps[6], ps[7]]

    if "nocl" not in DBG:
        P.op("act", lambda e: e.activation(out=cL, in_=Cn["lam"], func=AF.Exp, scale=-1.0), reads=["consts"], writes=["cL"])
        P.op("act", lambda e: e.activation(out=cL, in_=cL, func=AF.Ln, bias=1.0), reads=["cL"], writes=["cL"])
    P.op("pool", lambda e: e.tensor_scalar(out=cL2, in0=cL, scalar1=-16.0, scalar2=None, op0=ALU.mult),
         reads=["cL"], writes=["cL2"])
    P.op("pool", lambda e: e.tensor_scalar(out=cL, in0=cL, scalar1=-8.0, scalar2=None, op0=ALU.mult),
         reads=["cL", "cL2"], writes=["cL"])
    for g in range(4):
        for j in range(4):
            P.op("dve", lambda e, g=g, j=j: e.tensor_scalar(out=dgw[:, g * 4 + j, :], in0=B.ident, scalar1=Cn["convw"][:, g, j:j + 1],
                                                            scalar2=None, op0=ALU.mult), reads=["consts"], writes=["dgw"])

    def run_seq(sq, x1_src, T, T_OTH, KT_scr, V_scr, QT_scr, LT_scr, ko, vo, hl_out, cb_out, h0_d, conv0_d, koff):
        sn = sname + str(sq)
        if h0_d is None:
            P.op("pool", lambda e: e.memset(hstate, 0.0), writes=["hstate"])
            for g in range(4):
                P.op("pool", lambda e, g=g: e.memset(lxb[g][0][:, 0:3], 0.0), writes=[("lxh", g, 0)])
        else:
            P.dma(sn + "st", lambda e: e.dma_start(out=hstate, in_=h0_d.rearrange("(g p) -> p g", p=128),
                                                      allow_slow_non_contiguous=True), writes=["hstate"])
            for g in range(4):
                P.dma(sn + "st", lambda e, g=g: e.dma_start(
                    out=lx0[:, g, :], in_=conv0_d[:, g * 128:(g + 1) * 128].rearrange("j p -> p j"),
                    allow_slow_non_contiguous=True), writes=[("lx0", g)])
            for g in range(4):
                P.op("dve", lambda e, g=g: e.tensor_copy(out=lxb[g][0][:, 0:3], in_=lx0[:, g, :]),
                     reads=[("lx0", g)], writes=[("lxh", g, 0)])
        P.barrier()

        tiles = []
        t0 = 0
        while t0 < T:
            lim = T_OTH if t0 < T_OTH else T
            nt = min(TT, lim - t0)
            tiles.append((t0, nt))
            t0 += nt
        sub_ctr = [0]

        def load_tile(ti):
            t0, nt = tiles[ti]
            subs = []
            for s0 in range(0, nt, 128):
                ns = min(128, nt - s0)
                k = sub_ctr[0] % NXR
                sub_ctr[0] += 1
                P.dma(sname + "x%d" % k, lambda e, k=k, a=t0 + s0, ns=ns: e.dma_start(
                    out=xr[k][:ns, :], in_=x1_src[a:a + ns, :]), writes=[("xr", k)])
                subs.append((k, s0, ns))
            return subs

        loaded = {0: load_tile(0)}
        fm_c = [0]
        tm_c = [0]
        ks_c = [0]
        vs_c = [0]
        vb_c = [0]
        def tile_body(ti, t0, nt):
            own = t0 >= T_OTH
            to = t0 - T_OTH
            subs = loaded.pop(ti)
            xb = xnT[ti % 2]
            cur, nxt = ti % 2, (ti + 1) % 2
            for si, (k, s0, ns) in enumerate(subs):
                sl = (ti * 4 + si) % 4
                B.rstd_of(xr[k][:ns, :], ns, junk, ssb[sl], rsb[sl], [("xr", k)], sl)
                xsb = xs[si % 2]
                xsid = ("xs", si % 2)
                P.op("dve", lambda e, xsb=xsb, k=k, ns=ns, sl=sl: e.tensor_scalar(
                    out=xsb[:ns, :], in0=xr[k][:ns, :], scalar1=rsb[sl][:ns, :], scalar2=None, op0=ALU.mult),
                     reads=[("xr", k), ("rstd", sl)], writes=[xsid])
                for kc in range(KC):
                    P.op("pe", lambda e, xsb=xsb, kc=kc, ns=ns: e.transpose(
                        out=ps_tr[:, kc * 128:kc * 128 + ns], in_=xsb[:ns, kc * 128:(kc + 1) * 128],
                        identity=B.ident[:ns, :ns]),
                         reads=[xsid, "consts"], writes=["ps_tr"])
                P.op("act", lambda e, xb=xb, s0=s0, ns=ns: e.activation(
                    out=xb[:, :, s0:s0 + ns], in_=ps_tr.rearrange("p (a b) -> p a b", a=KC)[:, :, 0:ns],
                    func=AF.Copy),
                     reads=["ps_tr"], writes=[("xnT", ti % 2, si)])
                if si % 2 == 1:
                    yield
            xn_ids = [("xnT", ti % 2, si) for si in range(len(subs))]
            if ti + 1 < len(tiles):
                loaded[ti + 1] = load_tile(ti + 1)

            def fm_proj(col0):
                bank = fm[fm_c[0] % 2]
                bid = ("fm", fm_c[0] % 2)
                fm_c[0] += 1
                for kc in range(KC):
                    P.op("pe", lambda e, bank=bank, kc=kc, col0=col0: e.matmul(
                        bank[:, 0:nt], lhsT=Win[:, kc, col0:col0 + 128], rhs=xb[:, kc, 0:nt],
                        start=(kc == 0), stop=(kc == KC - 1)),
                         reads=xn_ids + [("Win", kc)], writes=[bid])
                return bank, bid

            def fm_to_scr(col0, dst_ap):
                bank, bid = fm_proj(col0)
                kb = kst[ks_c[0] % 3]
                kid = ("kst", ks_c[0] % 3)
                ksem = sname + "k%d" % (ks_c[0] % 3)
                ks_c[0] += 1
                P.op("act", lambda e, bank=bank, kb=kb: e.activation(out=kb[:, 0:nt], in_=bank[:, 0:nt], func=AF.Copy),
                     reads=[bid], writes=[kid])
                P.dma(ksem, lambda e, kb=kb, dst_ap=dst_ap: e.dma_start(out=dst_ap, in_=kb[:, 0:nt]), reads=[kid])

            for g in range(4):
                bank, bid = fm_proj(1536 + g * 128)
                P.op("act", lambda e, bank=bank, g=g: e.activation(
                    out=lxb[g][cur][:, 3:3 + nt], in_=bank[:, 0:nt], func=AF.Copy),
                     reads=[bid], writes=[("lx", g, cur)])
                if ti == len(tiles) - 1:
                    P.op("act", lambda e, bank=bank, g=g: e.activation(
                        out=lxl[:, g, :], in_=bank[:, nt - 3:nt], func=AF.Copy), reads=[bid], writes=[("lxl", g)])
            yield
            if own:
                for g in range(4):
                    bank, bid = fm_proj(2048 + g * 128)
                    P.op("act", lambda e, bank=bank, g=g: e.activation(out=gt[cur][g][:, 0:nt], in_=bank[:, 0:nt], func=AF.Copy),
                         reads=[bid], writes=[("gt", cur, g)])
            for h in range(4 if "nofm" not in DBG else 0):
                fm_to_scr(512 + h * 128, KT_scr[h, :, koff + t0:koff + t0 + nt])
            yield
            if own and "nofm" not in DBG:
                for h in range(4):
                    fm_to_scr(h * 128, QT_scr[h, :, to:to + nt])
                yield
            for si, (k, s0, ns) in enumerate(subs if "notm" not in DBG else []):
                for which in (("v", 1024), ("k", 512)):
                    if which[0] == "k" and not own:
                        continue
                    bank = tm[tm_c[0] % 2]
                    bid = ("tm", tm_c[0] % 2)
                    tm_c[0] += 1
                    for kc in range(KC):
                        P.op("pe", lambda e, bank=bank, kc=kc, s0=s0, ns=ns, c0=which[1]: e.matmul(
                            bank[:ns, :], lhsT=xb[:, kc, s0:s0 + ns], rhs=Win[:, kc, c0:c0 + 512],
                            start=(kc == 0), stop=(kc == KC - 1)),
                             reads=xn_ids + [("Win", kc)], writes=[bid])
                    if which[0] == "v" and "nov" not in DBG:
                        vb = vbf[vb_c[0] % 2]
                        vbid = ("vbf", vb_c[0] % 2)
                        vsem = sname + "vb%d" % (vb_c[0] % 2)
                        vb_c[0] += 1
                        P.op("dve", lambda e, bank=bank, vb=vb, ns=ns: e.tensor_copy(
                            out=vb[:ns, :, 0:128], in_=bank[:ns, :].rearrange("p (h d) -> p h d", h=4)),
                             reads=[bid], writes=[vbid])
                        P.dma(vsem, lambda e, vb=vb, a=koff + t0 + s0, ns=ns: e.dma_start(
                            out=V_scr[a:a + ns, :], in_=vb[:ns, :, :].rearrange("p h d -> p (h d)")),
                              reads=[vbid])
                    if own and "noko" not in DBG:
                        vs = vst[vs_c[0] % 2]
                        vsid = ("vst", vs_c[0] % 2)
                        vsem = sname + "vs%d" % (vs_c[0] % 2)
                        vs_c[0] += 1
                        dst = vo if which[0] == "v" else ko
                        P.op("act", lambda e, bank=bank, vs=vs, ns=ns: e.activation(out=vs[:ns, :], in_=bank[:ns, :], func=AF.Copy),
                             reads=[bid], writes=[vsid])
                        P.dma(vsem, lambda e, vs=vs, dst=dst, a=to + s0, ns=ns: e.dma_start(out=dst[a:a + ns, :], in_=vs[:ns, :]),
                              reads=[vsid])
                if si % 2 == 1:
                    yield

        def lru_part(ti, t0, nt, own, to, cur, nxt):
            for g in range(4):
                lx = lxb[g][cur]
                lid = [("lx", g, cur), ("lxh", g, cur)]
                for j in range(4):
                    P.op("pe", lambda e, g=g, lx=lx, j=j: e.matmul(
                        cvb[:, 0:nt], lhsT=dgw[:, g * 4 + j, :], rhs=lx[:, j:j + nt], start=(j == 0), stop=(j == 3)),
                         reads=lid + ["dgw"], writes=[("cv", 0)])
                P.op("dve", lambda e, g=g: e.tensor_scalar(
                    out=xcb[g][:, 0:nt], in0=cvb[:, 0:nt], scalar1=Cn["convb"][:, g:g + 1], scalar2=None, op0=ALU.add),
                     reads=[("cv", 0), "consts"], writes=[("xcb", g)])
                if ti + 1 < len(tiles):
                    boundary = (tiles[ti + 1][0] == T_OTH) and T_OTH > 0
                    if boundary:
                        P.op("pool", lambda e, g=g, lx=lx: e.tensor_scalar(
                            out=lxb[g][nxt][:, 0:3], in0=lx[:, nt:nt + 3], scalar1=Cn["flag"][:, 0:1], scalar2=None,
                            op0=ALU.mult), reads=lid + ["consts"], writes=[("lxh", g, nxt)])
                    else:
                        P.op("pool", lambda e, g=g, lx=lx: e.tensor_copy(out=lxb[g][nxt][:, 0:3], in_=lx[:, nt:nt + 3]),
                             reads=lid, writes=[("lxh", g, nxt)])
            yield
            for g in range(4):
                P.op("pe", lambda e, g=g: e.matmul(rg[0][:, 0:nt], lhsT=Wrb[:, g, :], rhs=xcb[g][:, 0:nt], start=True, stop=True),
                     reads=[("xcb", g), "Wrb"], writes=[("rg", 0)])
                P.op("act", lambda e, g=g: e.activation(out=rb[g][:, 0:nt], in_=rg[0][:, 0:nt], func=AF.Sigmoid,
                                                        bias=Cn["brg"][:, g:g + 1]),
                     reads=[("rg", 0), "consts"], writes=[("rb", g)])
                P.op("pe", lambda e, g=g: e.matmul(rg[1][:, 0:nt], lhsT=Wib[:, g, :], rhs=xcb[g][:, 0:nt], start=True, stop=True),
                     reads=[("xcb", g), "Wib"], writes=[("rg", 1)])
                P.op("act", lambda e, g=g: e.activation(out=ib[g][:, 0:nt], in_=rg[1][:, 0:nt], func=AF.Sigmoid,
                                                        bias=Cn["big"][:, g:g + 1]),
                     reads=[("rg", 1), "consts"], writes=[("ib", g)])
            yield
            if own:
                for g in range(4):
                    P.op("pool", lambda e, g=g: e.tensor_tensor(out=tb[g][:, 0:nt], in0=gt[cur][g][:, 0:nt], in1=gt[cur][g][:, 0:nt], op=ALU.mult),
                         reads=[("gt", cur, g)], writes=[("tb", g)])
                    P.op("pool", lambda e, g=g: e.tensor_scalar(out=tb[g][:, 0:nt], in0=tb[g][:, 0:nt], scalar1=0.044715, scalar2=1.0,
                                                                op0=ALU.mult, op1=ALU.add),
                         reads=[("tb", g)], writes=[("tb", g)])
                    P.op("pool", lambda e, g=g: e.tensor_tensor(out=tb[g][:, 0:nt], in0=tb[g][:, 0:nt], in1=gt[cur][g][:, 0:nt], op=ALU.mult),
                         reads=[("tb", g), ("gt", cur, g)], writes=[("tb", g)])
                    P.op("act", lambda e, g=g: e.activation(out=tb[g][:, 0:nt], in_=tb[g][:, 0:nt], func=AF.Sigmoid, scale=1.5957691216),
                         reads=[("tb", g)], writes=[("tb", g)])
                    P.op("pool", lambda e, g=g: e.tensor_tensor(out=gt[cur][g][:, 0:nt], in0=tb[g][:, 0:nt], in1=gt[cur][g][:, 0:nt], op=ALU.mult),
                         reads=[("tb", g), ("gt", cur, g)], writes=[("gt", cur, g)])
            yield
            for g in range(4):
                P.op("act", lambda e, g=g: e.activation(out=ab[g][:, 0:nt], in_=rb[g][:, 0:nt], func=AF.Exp, scale=cL[:, g:g + 1]),
                     reads=[("rb", g), "cL"], writes=[("ab", g)])
                P.op("act", lambda e, g=g: e.activation(out=a2b[g][:, 0:nt], in_=rb[g][:, 0:nt], func=AF.Exp, scale=cL2[:, g:g + 1]),
                     reads=[("rb", g), "cL2"], writes=[("a2b", g)])
            for g in range(4):
                P.op("act", lambda e, g=g: e.activation(out=a2b[g][:, 0:nt], in_=a2b[g][:, 0:nt], func=AF.Sqrt, scale=-1.0, bias=1.0),
                     reads=[("a2b", g)], writes=[("a2b", g)])
            yield
            for g in range(4):
                P.op("dve", lambda e, g=g: e.tensor_tensor(out=ib[g][:, 0:nt], in0=ib[g][:, 0:nt], in1=xcb[g][:, 0:nt], op=ALU.mult),
                     reads=[("ib", g), ("xcb", g)], writes=[("ib", g)])
                P.op("dve", lambda e, g=g: e.tensor_tensor(out=ib[g][:, 0:nt], in0=ib[g][:, 0:nt], in1=a2b[g][:, 0:nt], op=ALU.mult),
                     reads=[("ib", g), ("a2b", g)], writes=[("ib", g)])
                P.op("dve", lambda e, g=g: e.tensor_tensor_scan(
                    out=hb[g][:, 0:nt], data0=ab[g][:, 0:nt], data1=ib[g][:, 0:nt], initial=hstate[:, g:g + 1],
                    op0=ALU.mult, op1=ALU.add),
                     reads=[("ab", g), ("ib", g), "hstate"], writes=[("hb", g)])
            boundary = (ti + 1 < len(tiles)) and (tiles[ti + 1][0] == T_OTH) and T_OTH > 0
            for g in range(4):
                if boundary:
                    P.op("pool", lambda e, g=g: e.tensor_scalar(out=hstate[:, g:g + 1], in0=hb[g][:, nt - 1:nt],
                                                                scalar1=Cn["flag"][:, 0:1], scalar2=None, op0=ALU.mult),
                         reads=[("hb", g), "consts"], writes=["hstate"])
                else:
                    P.op("pool", lambda e, g=g: e.tensor_copy(out=hstate[:, g:g + 1], in_=hb[g][:, nt - 1:nt]),
                         reads=[("hb", g)], writes=["hstate"])
            if own:
                for g in range(4):
                    P.op("dve", lambda e, g=g: e.tensor_tensor(out=lob[g][:, 0:nt], in0=hb[g][:, 0:nt], in1=gt[cur][g][:, 0:nt], op=ALU.mult),
                         reads=[("hb", g), ("gt", cur, g)], writes=[("lob", g)])
                    P.dma(sname + "lo%d" % g, lambda e, g=g: e.dma_start(out=LT_scr[g, :, to:to + nt], in_=lob[g][:, 0:nt]),
                          reads=[("lob", g)])
        def interleave(ga, gb):
            gens = [g for g in (ga, gb) if g is not None]
            while gens:
                for g in list(gens):
                    try:
                        next(g)
                    except StopIteration:
                        gens.remove(g)

        pending = None
        for ti, (t0, nt) in enumerate(tiles):
            interleave(tile_body(ti, t0, nt), pending)
            pending = lru_part(ti, t0, nt, t0 >= T_OTH, t0 - T_OTH, ti % 2, (ti + 1) % 2)
        interleave(None, pending)
        lt0, lnt = tiles[-1]
        lcur = (len(tiles) - 1) % 2
        if "nofin" not in DBG:
            P.dma(sn + "fin", lambda e: e.dma_start(out=hl_out.rearrange("(g p) -> p g", p=128), in_=hstate,
                                                       allow_slow_non_contiguous=True), reads=["hstate"])
        for g in range(4 if "nofin" not in DBG else 0):
            P.dma(sn + "fin", lambda e, g=g: e.dma_start(
                out=cb_out[:, g * 128:(g + 1) * 128].rearrange("j p -> p j"), in_=lxl[:, g, :],
                allow_slow_non_contiguous=True), reads=[("lxl", g)])

    for sq, q in enumerate(seqs):
        run_seq(sq, q['x1'], q['T'], q['T_OTH'], q['KT'], q['V'], q['QT'], q['LT'], q['ko'], q['vo'], q['hl'], q['cb'],
                q.get('h0'), q.get('conv0'), q.get('koff', 0))
    P.barrier()
    A.pop()


SLOPES = [2.0 ** (-8.0 * (i + 1) / 4) for i in range(4)]
LAMBDA_INIT = 0.8 - 0.6 * 1.0


def attn_consts(B, Cn):
    P, A = B.P, B.A
    for nm, shp in (("pb", [4, 71]), ("db", [4, 128]), ("subg", [128]), ("lq1", [64]), ("lk1", [64]), ("lq2", [64]), ("lk2", [64])):
        Cn[nm] = A.alloc(shp, F32)
        P.dma("c0", lambda e, nm=nm: e.dma_start(out=Cn[nm], in_=B.inputs[nm]), writes=["consts"])
    negl = A.alloc([1], F32)
    t1 = A.alloc([1], F32)
    t2 = A.alloc([1], F32)
    j64 = A.alloc([64], F32)
    P.op("dve", lambda e: e.tensor_tensor(out=j64, in0=Cn["lq1"], in1=Cn["lk1"], op=ALU.mult), reads=["consts"], writes=["j64"])
    P.op("dve", lambda e: e.reduce_sum(out=t1, in_=j64, axis=AX.X), reads=["j64"], writes=["t1"])
    P.op("dve", lambda e: e.tensor_tensor(out=j64, in0=Cn["lq2"], in1=Cn["lk2"], op=ALU.mult), reads=["consts", "t1"], writes=["j64"])
    P.op("dve", lambda e: e.reduce_sum(out=t2, in_=j64, axis=AX.X), reads=["j64"], writes=["t2"])
    P.op("act", lambda e: e.activation(out=t1, in_=t1, func=AF.Exp), reads=["t1"], writes=["t1"])
    P.op("act", lambda e: e.activation(out=t2, in_=t2, func=AF.Exp), reads=["t2"], writes=["t2"])
    P.op("pool", lambda e: e.tensor_tensor(out=negl, in0=t2, in1=t1, op=ALU.subtract), reads=["t1", "t2"], writes=["negl"])
    P.op("pool", lambda e: e.tensor_scalar(out=negl, in0=negl, scalar1=-LAMBDA_INIT, scalar2=None, op0=ALU.add),
         reads=["negl"], writes=["negl"])
    subg8 = A.alloc([128], F32)
    P.op("pool", lambda e: e.tensor_scalar(out=subg8, in0=Cn["subg"], scalar1=1.0 - LAMBDA_INIT, scalar2=None, op0=ALU.mult),
         reads=["consts"], writes=["subg8"])
    pbo = A.alloc([4, 71], F32)
    P.op("pool", lambda e: e.tensor_scalar(out=pbo, in0=Cn["pb"], scalar1=Cn["maskv"][:, 0:1], scalar2=None, op0=ALU.add),
         reads=["consts"], writes=["pbo"])
    Cn["negl"], Cn["subg8"], Cn["pbo"] = negl, subg8, pbo


def attn_stage(B, jobs, Cn, sname, window=(None,) * 4):
    P, A, nc = B.P, B.A, B.nc
    TKmax = max(q["TK"] for q in jobs)
    NBmax = cdiv(TKmax, 128)
    A.push()
    KT = A.alloc([4, NBmax * 128], BF16)
    V1 = A.alloc([NBmax, 516], BF16)
    Wout = A.alloc([KC, D], BF16)
    gmb = A.alloc([D], F32)
    P.dma("c0", lambda e: e.dma_start(out=gmb, in_=B.inputs["gmb"]), writes=["gmb"])
    Cn["gmb"] = gmb
    attn_consts(B, Cn)
    mark = A.off
    wst = [A.alloc([D], F32) for _ in range(4)]
    B.prep_weight(B.inputs["wout"], D, D, Wout, None, wst, "Wout")
    P.barrier()
    A.off = mark
    QTILE = 512
    qt = [A.alloc([4, QTILE], BF16) for _ in range(2)]
    NPT = 4
    pt = [A.alloc([QTILE], BF16) for _ in range(NPT)]
    dtmp = [A.alloc([128], F32) for _ in range(2)]
    atok = [A.alloc([512], BF16) for _ in range(4)]
    attT = A.alloc([4, QTILE], BF16)
    lruT = [A.alloc([4, QTILE], BF16) for _ in range(2)]
    x1r = [A.alloc([D], F32) for _ in range(2)]
    ost = [A.alloc([D], F32)] * 2
    otmp = [A.alloc([128], F32) for _ in range(2)]
    ofin = [A.alloc([2, 129], F32) for _ in range(4)]
    junk = A.alloc([512], BF16)
    rl = [A.alloc([2], F32) for _ in range(4)]
    ssn = [A.alloc([1], F32) for _ in range(4)]
    rsn = [A.alloc([1], F32) for _ in range(4)]
    ssm = [A.alloc([1], F32) for _ in range(2)]
    ssm2 = [A.alloc([1], F32) for _ in range(2)]
    rsm = [A.alloc([1], F32) for _ in range(2)]
    ps = B.psum
    pb, pbo, db = Cn["pb"], Cn["pbo"], Cn["db"]
    st_c = [0]
    pt_c = [0]
    dt_c = [0]
    x_c = [0]
    o_c = [0]
    qb_c = [0]
    VCH = 16

    def run_job(TK, NQ, KT_scr, V_scr, QT_scr, LT_scr, x1_scr, x1_off, x2_dst, mask_other):
        NB = cdiv(TK, 128)
        KOFF = TK - NQ
        assert KOFF % 128 == 0
        nkof = lambda j: min(128, TK - 128 * j)
        for h in range(4):
            P.dma(sname + "K%d" % h, lambda e, h=h: e.dma_start(out=KT[:, h, 0:TK], in_=KT_scr[h, :, 0:TK]), writes=[("KT", h)])
        for ci, j0 in enumerate(range(0, NB, VCH)):
            j1 = min(NB, j0 + VCH)
            jf = min(j1, TK // 128)
            if jf > j0:
                P.dma(sname + "V%d" % (ci % 4), lambda e, j0=j0, jf=jf: e.dma_start(
                    out=V1[:, j0:jf, :], in_=V_scr[j0 * 128:jf * 128, :].rearrange("(j p) c -> p j c", p=128)),
                      writes=[("V1", ci)])
            if jf < j1:
                nk = nkof(jf)
                P.dma(sname + "V%d" % (ci % 4), lambda e, jf=jf, nk=nk: e.dma_start(
                    out=V1[:nk, jf, :], in_=V_scr[jf * 128:jf * 128 + nk, :]), writes=[("V1", ci)])
        vid = lambda j: ("V1", j // VCH)

        tiles = []
        q0 = 0
        while q0 < NQ:
            nq = min(QTILE, NQ - q0)
            tiles.append((q0, nq))
            q0 += nq

        LOOK = int(os.environ.get("K_LOOK", "3"))
        dq = []

        def push2(fn):
            dq.append(fn)
            while len(dq) > LOOK:
                dq.pop(0)()

        def load_q(ti):
            q0, nq = tiles[ti]
            b = qb_c[0] % 2
            qb_c[0] += 1
            for h in range(4):
                P.dma(sname + "q%d" % b, lambda e, b=b, h=h, q0=q0, nq=nq: e.dma_start(
                    out=qt[b][:, h, 0:nq], in_=QT_scr[h, :, q0:q0 + nq]), writes=[("qt", b, h)])
            def ld(b=b, q0=q0, nq=nq):
                P.dma(sname + "l%d" % b, lambda e: e.dma_start(
                    out=lruT[b][:, :, 0:nq], in_=LT_scr[:, :, q0:q0 + nq].rearrange("g p t -> p g t")), writes=[("lruT", b)])
            push2(ld)
            return b

        def tile_stream(ti, q0, nq, b):
            nsb = cdiv(nq, 128)
            nqs_of = lambda s: min(128, nq - 128 * s)
            jb = (KOFF + q0) // 128
            nb_next = load_q(ti + 1) if ti + 1 < len(tiles) else None
            for h in range(4):
                persub = (h == 0)
                W = window[h]
                jlo = 0 if W is None else max(0, jb - W)
                first_in_bank = [True] * 4
                for j in range(jlo, jb + nsb):
                    nk = nkof(j)
                    rel = j - jb
                    s_lo = max(0, rel)
                    c0 = s_lo * 128
                    tab = pbo if (mask_other and j * 128 < KOFF) else pb
                    for c in range(2):
                        bi = st_c[0] % 4
                        st_c[0] += 1
                        stb = ps[bi]
                        bid = ("bank", bi)
                        P.op("pe", lambda e, stb=stb, c=c, h=h, j=j, c0=c0, nq=nq, b=b, nk=nk: e.matmul(
                            stb[:nk, c0:nq], lhsT=KT[c * 64:(c + 1) * 64, h, j * 128:j * 128 + nk],
                            rhs=qt[b][c * 64:(c + 1) * 64, h, c0:nq], start=True, stop=True),
                             reads=[("KT", h), ("qt", b, h)], writes=[bid])
                        pi = pt_c[0] % NPT
                        pt_c[0] += 1
                        ptb = pt[pi]
                        pid = ("pt", pi)
                        c1 = c0
                        if rel >= 0:
                            nqs = nqs_of(rel)
                            di = dt_c[0] % 2
                            dt_c[0] += 1
                            P.op("dve", lambda e, stb=stb, di=di, h=h, c0=c0, nk=nk, nqs=nqs: e.scalar_tensor_tensor(
                                out=dtmp[di][:nk, :nqs], in0=stb[:nk, c0:c0 + nqs], scalar=0.125, in1=db[:nk, h, 0:nqs],
                                op0=ALU.mult, op1=ALU.add), reads=[bid, "consts"], writes=[("dtmp", di)])
                            bconst = 0.0 if persub else SLOPES[h] * 128.0 * rel
                            P.op("act", lambda e, ptb=ptb, di=di, c0=c0, bconst=bconst, nk=nk, nqs=nqs: e.activation(
                                out=ptb[:nk, c0:c0 + nqs], in_=dtmp[di][:nk, :nqs], func=AF.Exp, bias=bconst),
                                 reads=[("dtmp", di)], writes=[pid])
                            c1 = c0 + 128
                        if c1 < nq:
                            if persub:
                                for s in range(c1 // 128, nsb):
                                    dj = jb + s - j
                                    ce = s * 128 + nqs_of(s)
                                    P.op("act", lambda e, ptb=ptb, stb=stb, s=s, ce=ce, dj=dj, tab=tab, h=h, nk=nk: e.activation(
                                        out=ptb[:nk, s * 128:ce], in_=stb[:nk, s * 128:ce], func=AF.Exp,
                                        bias=tab[:nk, h, dj + 3:dj + 4], scale=0.125),
                                         reads=[bid, "pbo"], writes=[pid])
                            else:
                                dj = jb - j
                                P.op("act", lambda e, ptb=ptb, stb=stb, c1=c1, nq=nq, dj=dj, tab=tab, h=h, nk=nk: e.activation(
                                    out=ptb[:nk, c1:nq], in_=stb[:nk, c1:nq], func=AF.Exp,
                                    bias=tab[:nk, h, dj + 3:dj + 4], scale=0.125),
                                     reads=[bid, "pbo"], writes=[pid])

                        def pv(ptb=ptb, pid=pid, c=c, j=j, h=h, nk=nk, s_lo=s_lo, fib=first_in_bank, jb=jb, nsb=nsb, nqs_of=nqs_of):
                            for s in range(s_lo, nsb):
                                ob = ps[4 + s]
                                st_flag = fib[s]
                                fib[s] = False
                                nqs = nqs_of(s)
                                last = (j == jb + s)
                                P.op("pe", lambda e, ob=ob, ptb=ptb, s=s, c=c, j=j, h=h, st_flag=st_flag, nk=nk, nqs=nqs, last=last: e.matmul(
                                    ob[:nqs, c * 256:c * 256 + 129], lhsT=ptb[:nk, s * 128:s * 128 + nqs],
                                    rhs=V1[:nk, j, h * 129:(h + 1) * 129], start=st_flag, stop=last,
                                    skip_group_check=True),
                                     reads=[pid, vid(j)], writes=[("bank", 4 + s)])
                        push2(pv)
                push2(lambda h=h, nsb=nsb, nqs_of=nqs_of: finalize(h, nsb, nqs_of))
            push2(lambda ti=ti, q0=q0, nq=nq, b=b, nsb=nsb, nqs_of=nqs_of: tail(q0, nq, b, nsb, nqs_of))
            return nb_next

        def finalize(h, nsb, nqs_of):
            for s in range(nsb):
                n = nqs_of(s)
                ob = ps[4 + s]
                oid = ("bank", 4 + s)
                of = ofin[s]
                fid = ("ofin", s)
                P.op("dve", lambda e, ob=ob, of=of, n=n: e.tensor_copy(
                    out=of[:n, :, :], in_=ob[:n, :].rearrange("p (c x) -> p c x", c=2)[:, :, 0:129]),
                     reads=[oid], writes=[fid])
                r2 = rl[s]
                P.op("dve", lambda e, of=of, r2=r2, n=n: e.reciprocal(out=r2[:n, :], in_=of[:n, :, 128]),
                     reads=[fid], writes=[("rl", s)])
                P.op("pool", lambda e, r2=r2, n=n: e.tensor_tensor(out=r2[:n, 1:2], in0=r2[:n, 1:2], in1=Cn["negl"][:n, :], op=ALU.mult),
                     reads=[("rl", s), "negl"], writes=[("rl", s)])
                oi = o_c[0] % 2
                o_c[0] += 1
                ot = otmp[oi]
                otid = ("otmp", oi)
                P.op("dve", lambda e, of=of, ot=ot, r2=r2, n=n: e.tensor_scalar(
                    out=ot[:n, :], in0=of[:n, 0, 0:128], scalar1=r2[:n, 0:1], scalar2=None, op0=ALU.mult),
                     reads=[fid, ("rl", s)], writes=[otid])
                P.op("dve", lambda e, of=of, ot=ot, r2=r2, n=n: e.scalar_tensor_tensor(
                    out=ot[:n, :], in0=of[:n, 1, 0:128], scalar=r2[:n, 1:2], in1=ot[:n, :], op0=ALU.mult, op1=ALU.add),
                     reads=[fid, ("rl", s), otid], writes=[otid])
                P.op("act", lambda e, ot=ot, s=s, n=n: e.activation(out=junk[:n, 0:128], in_=ot[:n, :], func=AF.Square,
                                                               accum_out=ssn[s][:n, :]),
                     reads=[otid], writes=[("ssn", s), "junkA"])
                P.op("pool", lambda e, s=s, n=n: e.tensor_scalar(out=ssn[s][:n, :], in0=ssn[s][:n, :], scalar1=1.0 / 128, scalar2=EPS,
                                                            op0=ALU.mult, op1=ALU.add), reads=[("ssn", s)], writes=[("ssn", s)])
                P.op("pool", lambda e, s=s, n=n: e.tensor_tensor(out=rsn[s][:n, :], in0=ssn[s][:n, :], in1=B.c_mhalf[:n, :], op=ALU.pow),
                     reads=[("ssn", s)], writes=[("rsn", s)])
                P.op("dve", lambda e, ot=ot, s=s, h=h, n=n: e.scalar_tensor_tensor(
                    out=atok[s][:n, h * 128:(h + 1) * 128], in0=ot[:n, :], scalar=rsn[s][:n, 0:1], in1=Cn["subg8"][:n, :],
                    op0=ALU.mult, op1=ALU.mult), reads=[otid, ("rsn", s), "subg8"], writes=[("atok", s, h)])

        def tail(q0, nq, b, nsb, nqs_of):
            ps_tr = ps[0][:, :].bitcast(BF16)
            for s in range(nsb):
                n = nqs_of(s)
                for h in range(4):
                    P.op("pe", lambda e, s=s, h=h, n=n: e.transpose(
                        out=ps_tr[:, h * 128:h * 128 + n], in_=atok[s][:n, h * 128:(h + 1) * 128], identity=B.ident[:n, :n]),
                         reads=[("atok", s, h)], writes=[("bank", 0)])
                P.op("act", lambda e, s=s, n=n: e.activation(
                    out=attT[:, :, s * 128:s * 128 + n], in_=ps_tr[:, 0:512].rearrange("p (a b) -> p a b", a=4)[:, :, 0:n],
                    func=AF.Copy), reads=[("bank", 0)], writes=[("attT", s)])
            for s in range(nsb):
                n = nqs_of(s)
                mo = (ps[1], ps[2])
                for half in range(2):
                    for kk in range(8):
                        src = attT if kk < 4 else lruT[b]
                        P.op("pe", lambda e, half=half, kk=kk, src=src, s=s, n=n, mo=mo: e.matmul(
                            mo[half][:n, :], lhsT=src[:, kk % 4, s * 128:s * 128 + n],
                            rhs=Wout[:, kk, half * 512:(half + 1) * 512], start=(kk == 0), stop=(kk == 7)),
                             reads=[("attT", s), ("lruT", b), ("Wout", kk)], writes=[("bank", 1 + half)])
                sl = s % 2
                P.op("act", lambda e, sl=sl, n=n: e.activation(out=junk[:n, :], in_=ps[1][:n, :], func=AF.Square, accum_out=ssm[sl][:n, :]),
                     reads=[("bank", 1)], writes=["junkA", ("ssm", sl)])
                P.op("act", lambda e, sl=sl, n=n: e.activation(out=junk[:n, :], in_=ps[2][:n, :], func=AF.Square, accum_out=ssm2[sl][:n, :]),
                     reads=[("bank", 2)], writes=["junkA", ("ssm2", sl)])
                P.op("pool", lambda e, sl=sl, n=n: e.tensor_tensor(out=ssm[sl][:n, :], in0=ssm[sl][:n, :], in1=ssm2[sl][:n, :], op=ALU.add),
                     reads=[("ssm", sl), ("ssm2", sl)], writes=[("ssm", sl)])
                P.op("pool", lambda e, sl=sl, n=n: e.tensor_scalar(out=ssm[sl][:n, :], in0=ssm[sl][:n, :], scalar1=1.0 / D, scalar2=EPS,
                                                              op0=ALU.mult, op1=ALU.add), reads=[("ssm", sl)], writes=[("ssm", sl)])
                P.op("pool", lambda e, sl=sl, n=n: e.tensor_tensor(out=rsm[sl][:n, :], in0=ssm[sl][:n, :], in1=B.c_mhalf[:n, :], op=ALU.pow),
                     reads=[("ssm", sl)], writes=[("rsm", sl)])
                xi = x_c[0] % 2
                x_c[0] += 1
                a = q0 + s * 128
                P.dma(sname + "x%d" % xi, lambda e, xi=xi, a=a, n=n: e.dma_start(
                    out=x1r[xi][:n, :], in_=x1_scr[x1_off + a:x1_off + a + n, :]), writes=[("x1r", xi)])
                for half in range(2):
                    P.op("dve", lambda e, xi=xi, half=half, sl=sl, n=n: e.scalar_tensor_tensor(
                        out=ost[xi][:n, half * 512:(half + 1) * 512], in0=ps[1 + half][:n, :], scalar=rsm[sl][:n, 0:1],
                        in1=Cn["gmb"][:n, half * 512:(half + 1) * 512], op0=ALU.mult, op1=ALU.mult),
                         reads=[("bank", 1 + half), ("rsm", sl), "consts"], writes=[("ost", 0, half)])
                    P.op("pool", lambda e, xi=xi, half=half, n=n: e.tensor_tensor(
                        out=ost[xi][:n, half * 512:(half + 1) * 512], in0=ost[xi][:n, half * 512:(half + 1) * 512],
                        in1=x1r[xi][:n, half * 512:(half + 1) * 512], op=ALU.add),
                         reads=[("ost", 0, half), ("x1r", xi)], writes=[("ost", 0, half)])
                P.dma(sname + "o0", lambda e, xi=xi, a=a, n=n: e.dma_start(out=x2_dst[a:a + n, :], in_=ost[xi][:n, :]),
                      reads=[("ost", 0, 0), ("ost", 0, 1)])

        bcur = load_q(0)
        for ti, (q0, nq) in enumerate(tiles):
            bcur = tile_stream(ti, q0, nq, bcur)
        while dq:
            dq.pop(0)()

    for q in jobs:
        run_job(q["TK"], q["NQ"], q["KT"], q["V"], q["QT"], q["LT"], q["x1"], q["x1_off"], q["x2"], q["mask_other"])
    P.barrier()
    A.pop()


def attn_stage2(B, jobs, Cn, sname, window=(None,) * 4):
    P, A, nc = B.P, B.A, B.nc
    TKmax = max(q["TK"] for q in jobs)
    NBmax = cdiv(TKmax, 128)
    A.push()
    KT = A.alloc([4, NBmax * 128], BF16)
    V1 = A.alloc([NBmax, 516], BF16)
    Wout = A.alloc([KC, D], BF16)
    VCH = 16
    kv_done = {}

    def issue_kv(TK, KT_scr, V_scr):
        kv_done[id(KT_scr)] = True
        NB = cdiv(TK, 128)
        for h in range(4):
            P.dma(sname + "K%d" % h, lambda e, h=h: e.dma_start(out=KT[:, h, 0:TK], in_=KT_scr[h, :, 0:TK]), writes=[("KT", h)])
        for ci, j0 in enumerate(range(0, NB, VCH)):
            j1 = min(NB, j0 + VCH)
            jf = min(j1, TK // 128)
            if jf > j0:
                P.dma(sname + "V%d" % (ci % 4), lambda e, j0=j0, jf=jf: e.dma_start(
                    out=V1[:, j0:jf, :], in_=V_scr[j0 * 128:jf * 128, :].rearrange("(j p) c -> p j c", p=128)),
                      writes=[("V1", ci)])
            if jf < j1:
                nk = min(128, TK - 128 * jf)
                P.dma(sname + "V%d" % (ci % 4), lambda e, jf=jf, nk=nk: e.dma_start(
                    out=V1[:nk, jf, :], in_=V_scr[jf * 128:jf * 128 + nk, :]), writes=[("V1", ci)])

    issue_kv(jobs[0]["TK"], jobs[0]["KT"], jobs[0]["V"])
    gmb = A.alloc([D], F32)
    P.dma("c0", lambda e: e.dma_start(out=gmb, in_=B.inputs["gmb"]), writes=["gmb"])
    Cn["gmb"] = gmb
    attn_consts(B, Cn)
    g8col = A.alloc([1], F32)
    P.dma("c0", lambda e: e.dma_start(out=g8col, in_=B.inputs["subg_col"]), writes=["g8col"])
    P.op("pool", lambda e: e.tensor_scalar(out=g8col, in0=g8col, scalar1=1.0 - LAMBDA_INIT, scalar2=None, op0=ALU.mult),
         reads=["g8col"], writes=["g8col"])
    ones_bf = A.alloc([128], BF16)
    ones_f = A.alloc([128], F32)
    P.op("pool", lambda e: e.memset(ones_bf, 1.0), writes=["ones_bf"])
    P.op("pool", lambda e: e.memset(ones_f, 1.0), writes=["ones_f"])
    mark = A.off
    wst = [A.alloc([D], F32) for _ in range(4)]
    B.prep_weight(B.inputs["wout"], D, D, Wout, None, wst, "Wout")
    P.barrier()
    A.off = mark
    QTILE = 512
    qt = [A.alloc([4, QTILE], BF16) for _ in range(2)]
    NPT = 3
    pt = [A.alloc([2, QTILE], BF16) for _ in range(NPT)]
    dtmp = [A.alloc([2, 128], F32) for _ in range(2)]
    attT = A.alloc([4, QTILE], BF16)
    lruT = [A.alloc([4, QTILE], BF16) for _ in range(2)]
    x1r = [A.alloc([D], F32) for _ in range(2)]
    ost = A.alloc([D], F32)
    rec0 = A.alloc([QTILE], F32)
    rec1 = A.alloc([QTILE], F32)
    o_sb = A.alloc([QTILE], F32)
    junk = A.alloc([512], BF16)
    ssm = [A.alloc([1], F32) for _ in range(2)]
    ssm2 = [A.alloc([1], F32) for _ in range(2)]
    rsm = [A.alloc([1], F32) for _ in range(2)]
    ps = B.psum
    psall = B.psum_all
    stpair = [psall[:, 0:1024].rearrange("p (c x) -> p c x", c=2), psall[:, 1024:2048].rearrange("p (c x) -> p c x", c=2)]
    OTb = (ps[4], ps[5])
    Lb = (ps[6], ps[7])
    pb, pbo, db = Cn["pb"], Cn["pbo"], Cn["db"]
    st_c = [0]
    pt_c = [0]
    dt_c = [0]
    x_c = [0]
    qb_c = [0]

    def take_pair():
        pi = st_c[0] % 2
        st_c[0] += 1
        return pi, [("bank", 2 * pi), ("bank", 2 * pi + 1)]

    def run_job(TK, NQ, KT_scr, V_scr, QT_scr, LT_scr, x1_scr, x1_off, x2_dst, mask_other):
        NB = cdiv(TK, 128)
        KOFF = TK - NQ
        assert KOFF % 128 == 0
        nkof = lambda j: min(128, TK - 128 * j)
        if not kv_done.get(id(KT_scr)):
            issue_kv(TK, KT_scr, V_scr)
        vid = lambda j: ("V1", j // VCH)
        tiles = []
        q0 = 0
        while q0 < NQ:
            nq = min(QTILE, NQ - q0)
            tiles.append((q0, nq))
            q0 += nq
        LOOK = int(os.environ.get("K_LOOK2", "2"))
        dq = []

        def push2(fn):
            dq.append(fn)
            while len(dq) > LOOK:
                dq.pop(0)()

        def load_q(ti):
            q0, nq = tiles[ti]
            b = qb_c[0] % 2
            qb_c[0] += 1
            for h in range(4):
                P.dma(sname + "q%d" % b, lambda e, b=b, h=h, q0=q0, nq=nq: e.dma_start(
                    out=qt[b][:, h, 0:nq], in_=QT_scr[h, :, q0:q0 + nq]), writes=[("qt", b, h)])

            def ld(b=b, q0=q0, nq=nq):
                P.dma(sname + "l%d" % b, lambda e: e.dma_start(
                    out=lruT[b][:, :, 0:nq], in_=LT_scr[:, :, q0:q0 + nq].rearrange("g p t -> p g t")), writes=[("lruT", b)])
            push2(ld)
            return b

        def tile_stream(ti, q0, nq, b):
            nsb = cdiv(nq, 128)
            nqs_of = lambda s: min(128, nq - 128 * s)
            jb = (KOFF + q0) // 128
            nb_next = load_q(ti + 1) if ti + 1 < len(tiles) else None
            for h in range(4):
                persub = (h == 0)
                W = window[h]
                jlo = 0 if W is None else max(0, jb - W)
                first = [True]
                jlast = jb + nsb - 1
                for j in range(jlo, jb + nsb):
                    nk = nkof(j)
                    rel = j - jb
                    s_lo = max(0, rel)
                    c0 = s_lo * 128
                    tab = pbo if (mask_other and j * 128 < KOFF) else pb
                    pi, bids = take_pair()
                    stp = stpair[pi]
                    for c in range(2):
                        P.op("pe", lambda e, stp=stp, c=c, h=h, j=j, c0=c0, nq=nq, b=b, nk=nk: e.matmul(
                            stp[:nk, c, c0:nq], lhsT=KT[c * 64:(c + 1) * 64, h, j * 128:j * 128 + nk],
                            rhs=qt[b][c * 64:(c + 1) * 64, h, c0:nq], start=True, stop=True),
                             reads=[("KT", h), ("qt", b, h)], writes=[bids[c]])
                    ri = pt_c[0] % NPT
                    pt_c[0] += 1
                    ptb = pt[ri]
                    pid = ("pt", ri)
                    c1 = c0
                    if rel >= 0:
                        nqs = nqs_of(rel)
                        di = dt_c[0] % 2
                        dt_c[0] += 1
                        for c in range(2):
                            P.op("dve", lambda e, stp=stp, di=di, h=h, c0=c0, nk=nk, nqs=nqs, c=c: e.scalar_tensor_tensor(
                                out=dtmp[di][:nk, c, :nqs], in0=stp[:nk, c, c0:c0 + nqs], scalar=0.125, in1=db[:nk, h, 0:nqs],
                                op0=ALU.mult, op1=ALU.add), reads=[bids[c], "consts"], writes=[("dtmp", di, c)])
                        bconst = 0.0 if persub else SLOPES[h] * 128.0 * rel
                        P.op("act", lambda e, ptb=ptb, di=di, c0=c0, bconst=bconst, nk=nk, nqs=nqs: e.activation(
                            out=ptb[:nk, :, c0:c0 + nqs], in_=dtmp[di][:nk, :, :nqs], func=AF.Exp, bias=bconst),
                             reads=[("dtmp", di, 0), ("dtmp", di, 1)], writes=[pid])
                        c1 = c0 + 128
                    if c1 < nq:
                        if persub:
                            for s in range(c1 // 128, nsb):
                                dj = jb + s - j
                                ce = s * 128 + nqs_of(s)
                                P.op("act", lambda e, ptb=ptb, stp=stp, s=s, ce=ce, dj=dj, tab=tab, h=h, nk=nk: e.activation(
                                    out=ptb[:nk, :, s * 128:ce], in_=stp[:nk, :, s * 128:ce], func=AF.Exp,
                                    bias=tab[:nk, h, dj + 3:dj + 4], scale=0.125),
                                     reads=bids + ["pbo"], writes=[pid])
                        else:
                            dj = jb - j
                            P.op("act", lambda e, ptb=ptb, stp=stp, c1=c1, nq=nq, dj=dj, tab=tab, h=h, nk=nk: e.activation(
                                out=ptb[:nk, :, c1:nq], in_=stp[:nk, :, c1:nq], func=AF.Exp,
                                bias=tab[:nk, h, dj + 3:dj + 4], scale=0.125),
                                 reads=bids + ["pbo"], writes=[pid])

                    def pv(ptb=ptb, pid=pid, j=j, h=h, nk=nk, c0=c0, nq=nq, first=first, last=(j == jlast)):
                        st_flag = first[0]
                        first[0] = False
                        for c in range(2):
                            P.op("pe", lambda e, c=c: e.matmul(
                                OTb[c][:, c0:nq], lhsT=V1[:nk, j, h * 129:h * 129 + 128], rhs=ptb[:nk, c, c0:nq],
                                start=st_flag, stop=last), reads=[pid, vid(j)], writes=[("bank", 4 + c)])
                        for c in range(2):
                            P.op("pe", lambda e, c=c: e.matmul(
                                Lb[c][:, c0:nq], lhsT=ones_bf[:nk, :], rhs=ptb[:nk, c, c0:nq],
                                start=st_flag, stop=last), reads=[pid, "ones_bf"], writes=[("bank", 6 + c)])
                    push2(pv)
                push2(lambda h=h, nq=nq: finalize(h, nq))
            push2(lambda q0=q0, nq=nq, b=b, nsb=nsb, nqs_of=nqs_of: tail(q0, nq, b, nsb, nqs_of))
            return nb_next

        def finalize(h, nq):
            P.op("dve", lambda e: e.reciprocal(out=rec0[:, 0:nq], in_=Lb[0][:, 0:nq]), reads=[("bank", 6)], writes=["rec0"])
            P.op("dve", lambda e: e.reciprocal(out=rec1[:, 0:nq], in_=Lb[1][:, 0:nq]), reads=[("bank", 7)], writes=["rec1"])
            P.op("dve", lambda e: e.tensor_tensor(out=rec0[:, 0:nq], in0=OTb[0][:, 0:nq], in1=rec0[:, 0:nq], op=ALU.mult),
                 reads=[("bank", 4), "rec0"], writes=["rec0"])
            P.op("dve", lambda e: e.tensor_tensor(out=rec1[:, 0:nq], in0=OTb[1][:, 0:nq], in1=rec1[:, 0:nq], op=ALU.mult),
                 reads=[("bank", 5), "rec1"], writes=["rec1"])
            P.op("dve", lambda e: e.scalar_tensor_tensor(out=o_sb[:, 0:nq], in0=rec1[:, 0:nq], scalar=Cn["negl"][:, 0:1],
                                                         in1=rec0[:, 0:nq], op0=ALU.mult, op1=ALU.add),
                 reads=["rec0", "rec1", "negl"], writes=["o_sb"])
            P.op("pool", lambda e: e.tensor_tensor(out=rec0[:, 0:nq], in0=o_sb[:, 0:nq], in1=o_sb[:, 0:nq], op=ALU.mult),
                 reads=["o_sb"], writes=["rec0"])
            pi, bids = take_pair()
            ssb = ps[2 * pi]
            P.op("pe", lambda e: e.matmul(ssb[:, 0:nq], lhsT=ones_f[:, :], rhs=rec0[:, 0:nq], start=True, stop=True),
                 reads=["rec0", "ones_f"], writes=bids)
            P.op("act", lambda e: e.activation(out=rec1[:, 0:nq], in_=ssb[:, 0:nq], func=AF.Ln, scale=1.0 / 128, bias=EPS),
                 reads=[bids[0]], writes=["rec1"])
            P.op("act", lambda e: e.activation(out=rec1[:, 0:nq], in_=rec1[:, 0:nq], func=AF.Exp, scale=-0.5),
                 reads=["rec1"], writes=["rec1"])
            P.op("dve", lambda e: e.scalar_tensor_tensor(out=attT[:, h, 0:nq], in0=o_sb[:, 0:nq], scalar=g8col[:, 0:1],
                                                         in1=rec1[:, 0:nq], op0=ALU.mult, op1=ALU.mult),
                 reads=["o_sb", "rec1", "g8col"], writes=[("attT", h)])

        def tail(q0, nq, b, nsb, nqs_of):
            for s in range(nsb):
                n = nqs_of(s)
                pi, bids = take_pair()
                mo = (ps[2 * pi], ps[2 * pi + 1])
                for half in range(2):
                    for kk in range(8):
                        src = attT if kk < 4 else lruT[b]
                        P.op("pe", lambda e, half=half, kk=kk, src=src, s=s, n=n, mo=mo: e.matmul(
                            mo[half][:n, :], lhsT=src[:, kk % 4, s * 128:s * 128 + n],
                            rhs=Wout[:, kk, half * 512:(half + 1) * 512], start=(kk == 0), stop=(kk == 7)),
                             reads=[("attT", kk % 4), ("lruT", b), ("Wout", kk)], writes=[bids[half]])
                sl = s % 2
                P.op("act", lambda e, sl=sl, n=n, mo=mo: e.activation(out=junk[:n, :], in_=mo[0][:n, :], func=AF.Square, accum_out=ssm[sl][:n, :]),
                     reads=[bids[0]], writes=["junkA", ("ssm", sl)])
                P.op("act", lambda e, sl=sl, n=n, mo=mo: e.activation(out=junk[:n, :], in_=mo[1][:n, :], func=AF.Square, accum_out=ssm2[sl][:n, :]),
                     reads=[bids[1]], writes=["junkA", ("ssm2", sl)])
                P.op("pool", lambda e, sl=sl, n=n: e.tensor_tensor(out=ssm[sl][:n, :], in0=ssm[sl][:n, :], in1=ssm2[sl][:n, :], op=ALU.add),
                     reads=[("ssm", sl), ("ssm2", sl)], writes=[("ssm", sl)])
                P.op("pool", lambda e, sl=sl, n=n: e.tensor_scalar(out=ssm[sl][:n, :], in0=ssm[sl][:n, :], scalar1=1.0 / D, scalar2=EPS,
                                                              op0=ALU.mult, op1=ALU.add), reads=[("ssm", sl)], writes=[("ssm", sl)])
                P.op("pool", lambda e, sl=sl, n=n: e.tensor_tensor(out=rsm[sl][:n, :], in0=ssm[sl][:n, :], in1=B.c_mhalf[:n, :], op=ALU.pow),
                     reads=[("ssm", sl)], writes=[("rsm", sl)])
                xi = x_c[0] % 2
                x_c[0] += 1
                a = q0 + s * 128
                P.dma(sname + "x%d" % xi, lambda e, xi=xi, a=a, n=n: e.dma_start(
                    out=x1r[xi][:n, :], in_=x1_scr[x1_off + a:x1_off + a + n, :]), writes=[("x1r", xi)])
                for half in range(2):
                    P.op("dve", lambda e, half=half, sl=sl, n=n, mo=mo: e.scalar_tensor_tensor(
                        out=ost[:n, half * 512:(half + 1) * 512], in0=mo[half][:n, :], scalar=rsm[sl][:n, 0:1],
                        in1=Cn["gmb"][:n, half * 512:(half + 1) * 512], op0=ALU.mult, op1=ALU.mult),
                         reads=[bids[half], ("rsm", sl), "consts"], writes=[("ost", half)])
                    P.op("pool", lambda e, xi=xi, half=half, n=n: e.tensor_tensor(
                        out=ost[:n, half * 512:(half + 1) * 512], in0=ost[:n, half * 512:(half + 1) * 512],
                        in1=x1r[xi][:n, half * 512:(half + 1) * 512], op=ALU.add),
                         reads=[("ost", half), ("x1r", xi)], writes=[("ost", half)])
                P.dma(sname + "o0", lambda e, a=a, n=n: e.dma_start(out=x2_dst[a:a + n, :], in_=ost[:n, :]),
                      reads=[("ost", 0), ("ost", 1)])

        bcur = load_q(0)
        for ti, (q0, nq) in enumerate(tiles):
            bcur = tile_stream(ti, q0, nq, bcur)
        while dq:
            dq.pop(0)()

    for q in jobs:
        run_job(q["TK"], q["NQ"], q["KT"], q["V"], q["QT"], q["LT"], q["x1"], q["x1_off"], q["x2"], q["mask_other"])
    P.barrier()
    A.pop()


def cache_prep_ops(B, ck, cv, KT_s, V_s, PAST, sname):
    P, A = B.P, B.A
    NSTEP = PAST // 512
    kin = [A.alloc([4, 512], F32) for _ in range(2)]
    kbf = [A.alloc([4, 512], BF16) for _ in range(2)]
    kT = [A.alloc([4, 512], BF16) for _ in range(2)]
    vin = [A.alloc([4, 512], F32) for _ in range(2)]
    vb = [A.alloc([4, 516], BF16) for _ in range(2)]
    for i in range(2):
        P.op("pool", lambda e, i=i: e.memset(vb[i], 1.0), writes=[("cvb", i)])
    ps = B.psum
    for st in range(NSTEP):
        r = st % 2
        a = st * 512
        P.dma(sname + "k%d" % r, lambda e, r=r, a=a: e.dma_start(
            out=kin[r], in_=ck[a:a + 512, :].rearrange("(j p) c -> p j c", p=128)), writes=[("ckin", r)])
        P.dma(sname + "v%d" % r, lambda e, r=r, a=a: e.dma_start(
            out=vin[r], in_=cv[a:a + 512, :].rearrange("(j p) c -> p j c", p=128)), writes=[("cvin", r)])
        P.op("dve", lambda e, r=r: e.tensor_copy(out=kbf[r], in_=kin[r]), reads=[("ckin", r)], writes=[("ckbf", r)])
        for jj in range(4):
            bank = ps[4 + (st * 4 + jj) % 4]
            bid = ("bank", 4 + (st * 4 + jj) % 4)
            ps_tr = bank[:, :].bitcast(BF16)
            for h in range(4):
                P.op("pe", lambda e, ps_tr=ps_tr, h=h, r=r, jj=jj: e.transpose(
                    out=ps_tr[:, h * 128:(h + 1) * 128], in_=kbf[r][:, jj, h * 128:(h + 1) * 128], identity=B.ident),
                     reads=[("ckbf", r)], writes=[bid])
            P.op("act", lambda e, ps_tr=ps_tr, r=r, jj=jj: e.activation(
                out=kT[r][:, :, jj * 128:(jj + 1) * 128], in_=ps_tr[:, 0:512].rearrange("p (a b) -> p a b", a=4), func=AF.Copy),
                 reads=[bid], writes=[("ckT", r, jj)])
        P.dma(sname + "ko%d" % r, lambda e, r=r, a=a: e.dma_start(
            out=KT_s[:, :, a:a + 512].rearrange("h p t -> p h t"), in_=kT[r]),
              reads=[("ckT", r, x) for x in range(4)])
        for jj in range(4):
            P.op("pool" if jj % 2 else "dve", lambda e, r=r, jj=jj: e.tensor_copy(
                out=vb[r][:, jj, :].rearrange("p (h d) -> p h d", h=4)[:, :, 0:128],
                in_=vin[r][:, jj, :].rearrange("p (h d) -> p h d", h=4)),
                 reads=[("cvin", r), ("cvb", r)], writes=[("cvb", r, jj)])
        P.dma(sname + "vo%d" % r, lambda e, r=r, a=a: e.dma_start(
            out=V_s[a:a + 512, :].rearrange("(j p) c -> p j c", p=128), in_=vb[r]),
              reads=[("cvb", r, x) for x in range(4)] + [("cvb", r)])


SMALL_INPUTS = [
    ("gma_col", [KC]), ("g1a_col", [KC]), ("g2a_col", [KC]),
    ("convw", [4, 4]), ("convb", [4]), ("brg", [4]), ("big", [4]), ("lam", [4]),
    ("flag", [1]), ("maskv", [1]),
]


def build_main(T_OTH, T_OWN, with_sample=True, window=(None,) * 4, stages=("A1", "A2", "C", "D")):
    if os.environ.get("K_WIN", "1") == "1":
        window = (4, 16, None, None)
    B = Builder(T_OTH, T_OWN)
    P = B.P
    T = T_OTH + T_OWN
    x = B.din("x", [T, D])
    for nm, shp in (("f1g", [D, DFF]), ("f1u", [D, DFF]), ("f1d", [DFF, D]),
                    ("f2g", [D, DFF]), ("f2u", [D, DFF]), ("f2d", [DFF, D]),
                    ("win", [D, 2560]), ("wout", [D, D])):
        B.din(nm, shp)
    for nm, shp in (("g1b_bc", [128, D]), ("g2b_bc", [128, D]), ("gmb", [128, D]),
                    ("wr_bd", [128, 4, 128]), ("wi_bd", [128, 4, 128]),
                    ("pb", [128, 4, 71]), ("db", [128, 4, 128]), ("subg", [128, 128]), ("subg_col", [128, 1]),
                    ("lq1", [128, 64]), ("lk1", [128, 64]), ("lq2", [128, 64]), ("lk2", [128, 64])):
        B.din(nm, shp)
    y = B.dout("y", [T_OWN, D])
    ko = B.dout("ko", [T_OWN, 512])
    vo = B.dout("vo", [T_OWN, 512])
    hl = B.dout("hl", [512])
    cb = B.dout("cb", [3, 512])
    x1_scr = B.dscr("x1_scr", [T, D])
    x2_scr = B.dscr("x2_scr", [T_OWN, D])
    KT_scr = B.dscr("KT_scr", [4, 128, T], BF16)
    V_scr = B.dscr("V_scr", [T, 516], BF16)
    QT_scr = B.dscr("QT_scr", [4, 128, T_OWN], BF16)
    LT_scr = B.dscr("LT_scr", [4, 128, T_OWN], BF16)
    S = {}
    NSAMP, PAST = 32, 4096
    if with_sample:
        S["xs"] = B.din("xs", [NSAMP, D])
        S["ck"] = B.din("ck", [PAST, 512])
        S["cv"] = B.din("cv", [PAST, 512])
        S["sh"] = B.din("sh", [512])
        S["sc"] = B.din("sc", [3, 512])
        S["ys"] = B.dout("ys", [NSAMP, D])
        S["kso"] = B.dout("kso", [NSAMP, 512])
        S["vso"] = B.dout("vso", [NSAMP, 512])
        S["hls"] = B.dout("hls", [512])
        S["cbs"] = B.dout("cbs", [3, 512])
        S["xs1"] = B.dscr("xs1_scr", [NSAMP, D])
        S["xs2"] = B.dscr("xs2_scr", [NSAMP, D])
        S["KT"] = B.dscr("KTs_scr", [4, 128, PAST + NSAMP], BF16)
        S["V"] = B.dscr("Vs_scr", [PAST + NSAMP, 516], BF16)
        S["QT"] = B.dscr("QTs_scr", [4, 128, NSAMP], BF16)
        S["LT"] = B.dscr("LTs_scr", [4, 128, NSAMP], BF16)
    B.consts()
    Cn = {}
    for nm, shp in SMALL_INPUTS:
        Cn[nm] = B.load_const(nm, shp)
    P.barrier()
    inp = B.inputs
    if "A1" in stages:
        segs = [(x, x1_scr, T)] + ([(S["xs"], S["xs1"], NSAMP)] if with_sample else [])
        B.ffn_stage(segs, inp["f1g"], inp["f1u"], inp["f1d"], Cn["g1a_col"], inp["g1b_bc"], "a")
    if with_sample and "A2" in stages:
        Cn["cache_prep"] = (S["ck"], S["cv"], S["KT"], S["V"], PAST, "p")
    if "A2" in stages:
        seqs = [dict(x1=x1_scr, T=T, T_OTH=T_OTH, KT=KT_scr, V=V_scr, QT=QT_scr, LT=LT_scr, ko=ko, vo=vo, hl=hl, cb=cb)]
        if with_sample:
            seqs.append(dict(x1=S["xs1"], T=NSAMP, T_OTH=0, KT=S["KT"], V=S["V"], QT=S["QT"], LT=S["LT"],
                             ko=S["kso"], vo=S["vso"], hl=S["hls"], cb=S["cbs"], h0=S["sh"], conv0=S["sc"], koff=PAST))
        mixer_in_stage(B, seqs, Cn, "m")
    if "C" in stages:
        jobs = [dict(TK=T, NQ=T_OWN, KT=KT_scr, V=V_scr, QT=QT_scr, LT=LT_scr, x1=x1_scr, x1_off=T_OTH, x2=x2_scr,
                     mask_other=True)]
        if with_sample:
            jobs.append(dict(TK=PAST + NSAMP, NQ=NSAMP, KT=S["KT"], V=S["V"], QT=S["QT"], LT=S["LT"], x1=S["xs1"],
                             x1_off=0, x2=S["xs2"], mask_other=False))
        (attn_stage2 if os.environ.get("K_ATT", "2") == "2" else attn_stage)(B, jobs, Cn, "c", window=window)
    if "D" in stages:
        segs = [(x2_scr, y, T_OWN)] + ([(S["xs2"], S["ys"], NSAMP)] if with_sample else [])
        B.ffn_stage(segs, inp["f2g"], inp["f2u"], inp["f2d"], Cn["g2a_col"], inp["g2b_bc"], "d")
    B.P.emit()
    return B


def _col(v, n):
    return np.ascontiguousarray(np.asarray(v, np.float32).reshape(n, 128).T)


def _bc(v):
    v = np.asarray(v, np.float32).reshape(1, -1)
    return np.ascontiguousarray(np.broadcast_to(v, (128, v.shape[1])))


def _block_diag(w):
    out = np.zeros((128, 4, 128), np.float32)
    for g in range(4):
        for hb in range(2):
            out[hb * 64:(hb + 1) * 64, g, hb * 64:(hb + 1) * 64] = w[2 * g + hb]
    return out


def _tables():
    k = np.arange(128, dtype=np.float64)
    pb = np.zeros((128, 4, 71), np.float32)
    db = np.zeros((128, 4, 128), np.float32)
    kk = k[:, None]
    qq = k[None, :]
    for h in range(4):
        sl = SLOPES[h]
        for dj in range(-3, 68):
            pb[:, h, dj + 3] = sl * (k - 128.0 * dj)
        v = np.where(kk <= qq, sl * kk, sl * (2 * qq - kk))
        v = np.where((kk // 64) > (qq // 64), NEG, v)
        db[:, h, :] = v
    return pb, db


def shared_inputs(inputs):
    import ml_dtypes
    g = lambda n: np.asarray(inputs[n], np.float32)
    pb, db = _tables()
    d = {
        "f1g": g("ffn1_w_gate")[0], "f1u": g("ffn1_w_up")[0], "f1d": g("ffn1_w_down")[0],
        "f2g": g("ffn2_w_gate")[0], "f2u": g("ffn2_w_up")[0], "f2d": g("ffn2_w_down")[0],
        "win": g("w_in")[0], "wout": g("w_out")[0],
        "g1b_bc": _bc(g("g_ffn1_post")[0]), "g2b_bc": _bc(g("g_ffn2_post")[0]), "gmb": _bc(g("g_mix_post")[0]),
        "wr_bd": _block_diag(g("w_rgate")[0]), "wi_bd": _block_diag(g("w_igate")[0]),
        "pb": pb, "db": db, "subg": _bc(g("subln_g")[0]), "subg_col": _col(g("subln_g")[0], 1),
        "lq1": _bc(g("lambda_q1")[0]), "lk1": _bc(g("lambda_k1")[0]),
        "lq2": _bc(g("lambda_q2")[0]), "lk2": _bc(g("lambda_k2")[0]),
        "gma_col": _col(g("g_mix_pre")[0], 8), "g1a_col": _col(g("g_ffn1_pre")[0], 8), "g2a_col": _col(g("g_ffn2_pre")[0], 8),
        "convw": np.ascontiguousarray(g("conv_w")[0].reshape(4, 4, 128).transpose(2, 1, 0)),
        "convb": _col(g("conv_b")[0], 4), "brg": _col(g("b_rgate")[0], 4), "big": _col(g("b_igate")[0], 4),
        "lam": _col(g("lru_lambda")[0], 4),
        "ident": np.eye(128).astype(ml_dtypes.bfloat16),
    }
    return d


_CACHE = {}


def kernel(**inputs):
    TH = 4096
    WITH_SAMPLE = bool(int(os.environ.get("K_SAMPLE", "1")))
    key = ("main", TH, WITH_SAMPLE)
    if key not in _CACHE:
        _CACHE[key] = build_main(TH, TH, with_sample=WITH_SAMPLE)
    B = _CACHE[key]
    sh = shared_inputs(inputs)
    xp = np.asarray(inputs["x_prompt"], np.float32)
    maps = []
    for c in range(8):
        b, r = c // 2, c % 2
        own = xp[b, r * TH:(r + 1) * TH]
        oth = xp[b, (1 - r) * TH:(2 - r) * TH]
        m = dict(sh)
        m["x"] = np.ascontiguousarray(np.concatenate([oth, own], 0))
        m["flag"] = np.full((128, 1), float(r), np.float32)
        m["maskv"] = np.full((128, 1), 0.0 if r == 1 else NEG, np.float32)
        if WITH_SAMPLE:
            m["xs"] = np.ascontiguousarray(np.asarray(inputs["x_sample"], np.float32)[c])
            m["ck"] = np.ascontiguousarray(np.asarray(inputs["cache_k"], np.float32)[0, c].reshape(4096, 512))
            m["cv"] = np.ascontiguousarray(np.asarray(inputs["cache_v"], np.float32)[0, c].reshape(4096, 512))
            m["sh"] = np.ascontiguousarray(np.asarray(inputs["state_lru_h"], np.float32)[0, c])
            m["sc"] = np.ascontiguousarray(np.asarray(inputs["state_conv"], np.float32)[0, c])
        m = {k: v for k, v in m.items() if k in B.inputs}
        maps.append(m)
    res = run_bass_kernel_spmd(B.nc, maps, core_ids=list(range(8))).results
    y = np.zeros((4, 8192, 1024), np.float32)
    kp = np.zeros((1, 4, 8192, 4, 2, 64), np.float32)
    vp = np.zeros((1, 4, 8192, 4, 128), np.float32)
    hp = np.zeros((1, 4, 512), np.float32)
    cp = np.zeros((1, 4, 3, 512), np.float32)
    ys = np.zeros((8, 32, 1024), np.float32)
    ks = np.zeros((1, 8, 32, 4, 2, 64), np.float32)
    vs = np.zeros((1, 8, 32, 4, 128), np.float32)
    hs = np.zeros((1, 8, 512), np.float32)
    cs = np.zeros((1, 8, 3, 512), np.float32)
    for c in range(8):
        b, r = c // 2, c % 2
        o = res[c]
        sl = slice(r * TH, (r + 1) * TH)
        y[b, sl] = o["y"]
        kp[0, b, sl] = o["ko"].reshape(TH, 4, 2, 64)
        vp[0, b, sl] = o["vo"].reshape(TH, 4, 128)
        if r == 1:
            hp[0, b] = o["hl"]
            cp[0, b] = o["cb"]
        if WITH_SAMPLE:
            ys[c] = o["ys"]
            ks[0, c] = o["kso"].reshape(32, 4, 2, 64)
            vs[0, c] = o["vso"].reshape(32, 4, 128)
            hs[0, c] = o["hls"]
            cs[0, c] = o["cbs"]
    return (y, ys, kp, vp, hp, cp, ks, vs, hs, cs)
```

```python
import numpy as np
import concourse.bass as bass
import concourse.mybir as mybir
from concourse.bass_utils import run_bass_kernel_spmd
from contextlib import ExitStack

F32 = mybir.dt.float32
BF16 = mybir.dt.bfloat16
AF = mybir.ActivationFunctionType
ALU = mybir.AluOpType
AX = mybir.AxisListType

D = 1024
DFF = 2816
NFF = DFF // 128
KC = D // 128
EPS = 1e-6
NEG = -1e30


class Op:
    __slots__ = ("eng", "fn", "deps", "sem", "inc", "val", "is_dma", "needs_inc")


class Prog:
    ENGS = ("pe", "act", "dve", "pool", "sp")

    def __init__(self, nc, stack):
        self.nc = nc
        self.stack = stack
        self.ops = {e: [] for e in self.ENGS}
        self.last_w = {}
        self.readers = {}
        self.barrier_ops = []
        self.esem = {e: stack.enter_context(nc.semaphore("s_" + e)) for e in ("pe", "act", "dve", "pool")}
        self.dma_sems = {}
        self.dma_last = {}
        self.n_sem = 4

    def dsem(self, name):
        if name not in self.dma_sems:
            self.dma_sems[name] = self.stack.enter_context(self.nc.semaphore("d_" + name))
            self.n_sem += 1
        return name

    PSUM_IDS = ("gu", "dn", "fm", "tm", "rg", "bank", "cv")

    def _is_psum(self, t):
        return t == "ps_tr" or (isinstance(t, tuple) and t[0] in self.PSUM_IDS)

    def _add(self, o, reads, writes):
        xr = [t for t in reads if self._is_psum(t)]
        if xr:
            reads = [t for t in reads if not self._is_psum(t)]
            writes = list(writes) + xr
        deps = set(self.barrier_ops)
        for t in reads:
            w = self.last_w.get(t)
            if w is not None:
                deps.add(w)
        for t in writes:
            w = self.last_w.get(t)
            if w is not None:
                deps.add(w)
            for r in self.readers.get(t, ()):
                deps.add(r)
        if o.eng == "pe" and not o.is_dma:
            deps = {d for d in deps if not (d.eng == "pe" and not d.is_dma)}
        deps = {(self.dma_last[d.sem] if d.is_dma else d) for d in deps}
        o.deps = deps
        for d in deps:
            d.needs_inc = True
        for t in reads:
            self.readers.setdefault(t, []).append(o)
        for t in writes:
            self.last_w[t] = o
            self.readers[t] = []
        self.ops[o.eng].append(o)

    def op(self, eng, fn, reads=(), writes=()):
        o = Op()
        o.eng = eng
        o.fn = fn
        o.is_dma = False
        o.needs_inc = False
        o.sem = None
        o.val = 0
        self._add(o, reads, writes)
        return o

    def dma(self, sem_name, fn, reads=(), writes=(), q="sp"):
        o = Op()
        o.eng = q
        o.fn = fn
        o.is_dma = True
        o.needs_inc = True
        o.sem = self.dsem(sem_name)
        o.val = 0
        self._add(o, reads, writes)
        self.dma_last[sem_name] = o
        return o

    def barrier(self):
        b = []
        for e in self.ENGS:
            for o in reversed(self.ops[e]):
                if not o.is_dma:
                    b.append(o)
                    break
        for o in self.dma_last.values():
            b.append(o)
        for o in b:
            o.needs_inc = True
        self.barrier_ops = b
        self.last_w = {}
        self.readers = {}

    def emit(self):
        nc = self.nc
        dcount = {}
        for e in self.ENGS:
            cnt = 0
            for o in self.ops[e]:
                if o.is_dma:
                    dcount[o.sem] = dcount.get(o.sem, 0) + 16
                    o.val = dcount[o.sem]
                elif o.needs_inc:
                    cnt += 1
                    o.val = cnt
        ops = self.ops
        esem = self.esem
        dsems = self.dma_sems

        def run(ename, eng):
            waited = {}
            for o in ops[ename]:
                need = {}
                for d in o.deps:
                    key = d.sem if d.is_dma else d.eng
                    if d.val > need.get(key, 0):
                        need[key] = d.val
                for key, v in need.items():
                    if waited.get(key, 0) < v:
                        sem = esem[key] if key in esem else dsems[key]
                        eng.wait_ge(sem, v)
                        waited[key] = v
                ins = o.fn(eng)
                if o.is_dma:
                    ins.then_inc(dsems[o.sem], 16)
                elif o.needs_inc:
                    ins.then_inc(esem[ename], 1)
            fin = {}
            for o in ops[ename]:
                if o.is_dma:
                    fin[o.sem] = max(fin.get(o.sem, 0), o.val)
            for s, v in fin.items():
                if waited.get(s, 0) < v:
                    eng.wait_ge(dsems[s], v)

        with nc.Block() as block:
            @block.tensor
            def _(e):
                run("pe", e)

            @block.scalar
            def _(e):
                run("act", e)

            @block.vector
            def _(e):
                run("dve", e)

            @block.gpsimd
            def _(e):
                run("pool", e)

            @block.sync
            def _(e):
                run("sp", e)


class Arena:
    def __init__(self, nc, stack, nbytes):
        self.t = stack.enter_context(nc.sbuf_tensor("arena", [128, nbytes // 4], F32))
        self.cap = nbytes
        self.off = 0
        self.marks = []

    def push(self):
        self.marks.append(self.off)

    def pop(self):
        self.off = self.marks.pop()

    def alloc(self, shape, dt):
        n = 1
        for s in shape:
            n *= s
        esz = 4 if dt == F32 else 2
        nb = (n * esz + 31) // 32 * 32
        assert self.off + nb <= self.cap, ("arena overflow", self.off, nb, self.cap)
        ap = self.t[:, self.off // 4:(self.off + nb) // 4]
        self.off += nb
        self.peak = max(getattr(self, 'peak', 0), self.off)
        if dt != F32:
            ap = ap.bitcast(dt)
        ap = ap[:, 0:n]
        if len(shape) == 2:
            ap = ap.rearrange("p (a b) -> p a b", a=shape[0])
        elif len(shape) == 3:
            ap = ap.rearrange("p (a b c) -> p a b c", a=shape[0], b=shape[1])
        return ap


import os
DBG = set(os.environ.get("KDBG", "").split(","))


def cdiv(a, b):
    return (a + b - 1) // b


class Builder:
    def __init__(self, T_OTH, T_OWN, n_samp=32, past=4096, stages="all"):
        self.T_OTH, self.T_OWN, self.NS, self.PAST = T_OTH, T_OWN, n_samp, past
        self.T = T_OTH + T_OWN
        self.stages = stages
        self.nc = bass.Bass("TRN2", target_bir_lowering=False)
        self.stack = ExitStack()
        self.P = Prog(self.nc, self.stack)
        self.A = Arena(self.nc, self.stack, 211456)
        self.inputs = {}
        self.outputs = {}
        nc = self.nc
        self.psum_all = self.stack.enter_context(nc.psum_tensor("psall", [128, 4096], F32))
        self.psum = [self.psum_all[:, i * 512:(i + 1) * 512] for i in range(8)]

    def din(self, name, shape, dt=F32):
        t = self.nc.dram_tensor(name, list(shape), dt, kind="ExternalInput").ap()
        self.inputs[name] = t
        return t

    def dout(self, name, shape, dt=F32):
        t = self.nc.dram_tensor(name, list(shape), dt, kind="ExternalOutput").ap()
        self.outputs[name] = t
        return t

    def dscr(self, name, shape, dt=F32):
        return self.nc.dram_tensor(name, list(shape), dt, kind="Internal").ap()

    def prep_weight(self, w_dram, K, N, dst, gcol, stage_bufs, tag, eng_cycle=("dve", "pool")):
        P = self.P
        nk = K // 128
        for kc in range(nk):
            sb = stage_bufs[kc % len(stage_bufs)]
            sid = ("wst", kc % len(stage_bufs))
            src = w_dram[kc * 128:(kc + 1) * 128, :]
            P.dma("wst%d" % (kc % len(stage_bufs)),
                  lambda e, sb=sb, src=src, N=N: e.dma_start(out=sb[:, 0:N], in_=src),
                  writes=[sid])
            ec = tuple(os.environ.get("K_PREP", "dve,act").split(","))
            en = ec[kc % len(ec)]
            if en == "act":
                if gcol is None:
                    P.op("act", lambda e, sb=sb, kc=kc, N=N, dst=dst: e.activation(out=dst[:, kc, :], in_=sb[:, 0:N], func=AF.Copy),
                         reads=[sid], writes=[(tag, kc)])
                else:
                    P.op("act", lambda e, sb=sb, kc=kc, N=N, dst=dst, gcol=gcol: e.activation(
                        out=dst[:, kc, :], in_=sb[:, 0:N], func=AF.Copy, scale=gcol[:, kc:kc + 1]),
                         reads=[sid, "consts"], writes=[(tag, kc)])
                continue
            if gcol is None:
                P.op(en, lambda e, sb=sb, kc=kc, N=N, dst=dst: e.tensor_copy(out=dst[:, kc, :], in_=sb[:, 0:N]),
                     reads=[sid], writes=[(tag, kc)])
            else:
                P.op(en, lambda e, sb=sb, kc=kc, N=N, dst=dst, gcol=gcol: e.tensor_scalar(
                    out=dst[:, kc, :], in0=sb[:, 0:N], scalar1=gcol[:, kc:kc + 1], scalar2=None, op0=ALU.mult),
                     reads=[sid, "consts"], writes=[(tag, kc)])

    def rstd_of(self, src_ap, np_, junk, ss, rstd, src_ids, tagid):
        P = self.P
        P.op("act", lambda e: e.activation(out=junk[:np_, :], in_=src_ap, func=AF.Square, accum_out=ss[:np_, :]),
             reads=src_ids, writes=[("junk", tagid), ("ss", tagid)])
        P.op("pool", lambda e: e.tensor_scalar(out=ss[:np_, :], in0=ss[:np_, :], scalar1=1.0 / D, scalar2=EPS,
                                               op0=ALU.mult, op1=ALU.add),
             reads=[("ss", tagid)], writes=[("ss", tagid)])
        P.op("pool", lambda e: e.tensor_tensor(out=rstd[:np_, :], in0=ss[:np_, :], in1=self.c_mhalf[:np_, :], op=ALU.pow),
             reads=[("ss", tagid), "consts"], writes=[("rstd", tagid)])

    def ffn_stage(self, segs, wg_d, wu_d, wd_d, gpre_col, gpost_bc, sname):
        P, A, nc = self.P, self.A, self.nc
        A.push()
        Wg = A.alloc([KC, DFF], BF16)
        Wu = A.alloc([KC, DFF], BF16)
        Wd = A.alloc([NFF, D], BF16)
        gph = A.alloc([D], F32)
        mark_act = A.off
        wst = [A.alloc([DFF], F32) for _ in range(5)]
        P.dma("c0", lambda e: e.dma_start(out=gph[:, :], in_=gpost_bc), writes=["gph"])
        P.op("pool", lambda e: e.tensor_scalar(out=gph[:, :], in0=gph[:, :], scalar1=0.5, scalar2=None,
                                               op0=ALU.mult), reads=["gph"], writes=["gph"])
        self.prep_weight(wg_d, D, DFF, Wg, gpre_col, wst, "Wg")
        self.prep_weight(wu_d, D, DFF, Wu, gpre_col, wst, "Wu")
        self.prep_weight(wd_d, DFF, D, Wd, None, wst, "Wd")
        P.barrier()
        A.off = mark_act
        TT = 256
        NXR = 5
        xr = [A.alloc([D], F32) for _ in range(NXR)]
        xs = [A.alloc([D], BF16) for _ in range(2)]
        xnT = [A.alloc([KC, TT], BF16) for _ in range(2)]
        actT = A.alloc([NFF, TT], BF16)
        stmp = [A.alloc([TT], F32) for _ in range(3)]
        ost = [A.alloc([D], F32) for _ in range(2)]
        junk = A.alloc([D], BF16)
        ssb = [A.alloc([1], F32) for _ in range(4)]
        rsb = [A.alloc([1], F32) for _ in range(4)]
        ssp = [A.alloc([1], F32) for _ in range(4)]
        ssq = [A.alloc([1], F32) for _ in range(4)]
        ps = self.psum
        ps_tr = ps[0][:, :].bitcast(BF16)
        gu = [ps[1], ps[2], ps[3]]
        dn = [(ps[4], ps[5]), (ps[6], ps[7])]

        tiles = []
        for (src, dst, n) in segs:
            t0 = 0
            while t0 < n:
                nt = min(TT, n - t0)
                tiles.append((src, dst, t0, nt))
                t0 += nt
        sub_ctr = [0]

        def load_tile(ti):
            src, dst, t0, nt = tiles[ti]
            subs = []
            for s0 in range(0, nt, 128):
                ns = min(128, nt - s0)
                k = sub_ctr[0] % NXR
                sub_ctr[0] += 1
                P.dma(sname + "x%d" % k, lambda e, k=k, src=src, a=t0 + s0, ns=ns: e.dma_start(
                    out=xr[k][:ns, :], in_=src[a:a + ns, :]), writes=[("xr", k)])
                subs.append((k, s0, ns))
            return subs

        loaded = {0: load_tile(0)}
        gu_ctr = 0
        st_ctr = 0
        o_ctr = 0
        for ti in range(len(tiles)):
            src, dst, t0, nt = tiles[ti]
            subs = loaded.pop(ti)
            if ti + 1 < len(tiles):
                loaded[ti + 1] = load_tile(ti + 1)
            xb = xnT[ti % 2]
            for si, (k, s0, ns) in enumerate(subs):
                sl = (ti * 2 + si) % 4
                self.rstd_of(xr[k][:ns, :], ns, junk, ssb[sl], rsb[sl], [("xr", k)], sl)
                xsb = xs[(ti * 2 + si) % 2]
                xsid = ("xs", (ti * 2 + si) % 2)
                P.op("dve", lambda e, xsb=xsb, k=k, ns=ns, sl=sl: e.tensor_scalar(
                    out=xsb[:ns, :], in0=xr[k][:ns, :], scalar1=rsb[sl][:ns, :], scalar2=None, op0=ALU.mult),
                     reads=[("xr", k), ("rstd", sl)], writes=[xsid])
                for kc in range(KC):
                    P.op("pe", lambda e, xsb=xsb, kc=kc, ns=ns: e.transpose(
                        out=ps_tr[:, kc * 128:kc * 128 + ns], in_=xsb[:ns, kc * 128:(kc + 1) * 128],
                        identity=self.ident[:ns, :ns]),
                         reads=[xsid, "consts"], writes=["ps_tr"])
                P.op("act", lambda e, xb=xb, s0=s0, ns=ns: e.activation(
                    out=xb[:, :, s0:s0 + ns], in_=ps_tr.rearrange("p (a b) -> p a b", a=KC)[:, :, 0:ns],
                    func=AF.Copy),
                     reads=["ps_tr"], writes=[("xnT", ti % 2, si)])
            xn_ids = [("xnT", ti % 2, si) for si in range(len(subs))]
            for f in range(NFF):
                g = gu[gu_ctr % 3]
                gid = ("gu", gu_ctr % 3)
                gu_ctr += 1
                for kc in range(KC):
                    P.op("pe", lambda e, g=g, kc=kc, f=f, xb=xb, nt=nt: e.matmul(
                        g[:, 0:nt], lhsT=Wg[:, kc, f * 128:(f + 1) * 128], rhs=xb[:, kc, 0:nt],
                        start=(kc == 0), stop=(kc == KC - 1)),
                         reads=xn_ids + [("Wg", kc)], writes=[gid])
                for kc in range(KC):
                    P.op("pe", lambda e, g=g, kc=kc, f=f, xb=xb, nt=nt: e.matmul(
                        g[:, 256:256 + nt], lhsT=Wu[:, kc, f * 128:(f + 1) * 128], rhs=xb[:, kc, 0:nt],
                        start=(kc == 0), stop=(kc == KC - 1)),
                         reads=xn_ids + [("Wu", kc)], writes=[gid])
                stb = stmp[st_ctr % 3]
                sid = ("stmp", st_ctr % 3)
                st_ctr += 1
                P.op("act", lambda e, g=g, stb=stb, nt=nt: e.activation(out=stb[:, 0:nt], in_=g[:, 0:nt], func=AF.Silu),
                     reads=[gid], writes=[sid])
                P.op("dve", lambda e, g=g, stb=stb, nt=nt, f=f: e.tensor_tensor(
                    out=actT[:, f, 0:nt], in0=stb[:, 0:nt], in1=g[:, 256:256 + nt], op=ALU.mult),
                     reads=[gid, sid], writes=[("actT", f)])
            for si, (k, s0, ns) in enumerate(subs):
                d0, d1 = dn[si % 2]
                did = ("dn", si % 2)
                for half, dps in enumerate((d0, d1)):
                    for f in range(NFF):
                        P.op("pe", lambda e, dps=dps, f=f, s0=s0, ns=ns, half=half: e.matmul(
                            dps[:ns, :], lhsT=actT[:, f, s0:s0 + ns], rhs=Wd[:, f, half * 512:(half + 1) * 512],
                            start=(f == 0), stop=(f == NFF - 1)),
                             reads=[("actT", f), ("Wd", f)], writes=[did])
                sl = (ti * 2 + si) % 4
                ss2 = ssp[sl]
                P.op("act", lambda e, d0=d0, ns=ns, ss2=ss2: e.activation(
                    out=junk[:ns, 0:512], in_=d0[:ns, :], func=AF.Square, accum_out=ss2[:ns, :]),
                     reads=[did], writes=[("junk", 9), ("ssA", sl)])
                ss3 = ssq[sl]
                P.op("act", lambda e, d1=d1, ns=ns, ss3=ss3: e.activation(
                    out=junk[:ns, 512:1024], in_=d1[:ns, :], func=AF.Square, accum_out=ss3[:ns, :]),
                     reads=[did], writes=[("junk", 10), ("ssB", sl)])
                P.op("pool", lambda e, ss2=ss2, ss3=ss3, ns=ns: e.tensor_tensor(
                    out=ss2[:ns, :], in0=ss2[:ns, :], in1=ss3[:ns, :], op=ALU.add),
                     reads=[("ssA", sl), ("ssB", sl)], writes=[("ssA", sl)])
                P.op("pool", lambda e, ss2=ss2, ns=ns: e.tensor_scalar(
                    out=ss2[:ns, :], in0=ss2[:ns, :], scalar1=1.0 / D, scalar2=EPS, op0=ALU.mult, op1=ALU.add),
                     reads=[("ssA", sl)], writes=[("ssA", sl)])
                P.op("pool", lambda e, ss2=ss2, ss3=ss3, ns=ns: e.tensor_tensor(
                    out=ss3[:ns, :], in0=ss2[:ns, :], in1=self.c_mhalf[:ns, :], op=ALU.pow),
                     reads=[("ssA", sl), "consts"], writes=[("ssB", sl)])
                ob = ost[o_ctr % 2]
                oid = ("ost", o_ctr % 2)
                osem = sname + "o%d" % (o_ctr % int(os.environ.get("NOSEM", "2")))
                o_ctr += 1
                for half, dps in enumerate((d0, d1)):
                    P.op("dve", lambda e, dps=dps, ob=ob, ns=ns, half=half, ss3=ss3: e.scalar_tensor_tensor(
                        out=ob[:ns, half * 512:(half + 1) * 512], in0=dps[:ns, :], scalar=ss3[:ns, :],
                        in1=gph[:ns, half * 512:(half + 1) * 512], op0=ALU.mult, op1=ALU.mult),
                         reads=[did, ("ssB", sl), "consts"], writes=[(oid, half)])
                    P.op("pool", lambda e, ob=ob, ns=ns, half=half, k=k: e.tensor_tensor(
                        out=ob[:ns, half * 512:(half + 1) * 512], in0=ob[:ns, half * 512:(half + 1) * 512],
                        in1=xr[k][:ns, half * 512:(half + 1) * 512], op=ALU.add),
                         reads=[(oid, half), ("xr", k)], writes=[(oid, half)])
                P.dma(osem, lambda e, ob=ob, dst=dst, a=t0 + s0, ns=ns: e.dma_start(
                    out=dst[a:a + ns, :], in_=ob[:ns, :]), reads=[(oid, 0), (oid, 1)])
        P.barrier()
        A.pop()

    def consts(self):
        P, A = self.P, self.A
        ident_d = self.din("ident", [128, 128], BF16)
        self.ident = A.alloc([128], BF16)
        self.c_mhalf = A.alloc([1], F32)
        P.dma("c0", lambda e: e.dma_start(out=self.ident[:, :], in_=ident_d[:, :]), writes=["consts"])
        P.op("pool", lambda e: e.memset(self.c_mhalf[:, :], -0.5), writes=["consts_b"])

    def load_const(self, name, shape, dt=F32):
        d = self.din(name, [128] + list(shape), dt)
        t = self.A.alloc(list(shape), dt)
        self.P.dma("c0", lambda e: e.dma_start(out=t, in_=d), writes=["consts_c"])
        return t


def build_ffn_test(ntok):
    B = Builder(0, ntok)
    P = B.P
    x = B.din("x", [ntok, D])
    wg = B.din("wg", [D, DFF])
    wu = B.din("wu", [D, DFF])
    wd = B.din("wd", [DFF, D])
    y = B.dout("y", [ntok, D])
    B.consts()
    gpre = B.load_const("gpre", [KC])
    gpost = B.load_const("gpost", [D])
    P.barrier()
    B.ffn_stage([(x, y, ntok)], wg, wu, wd, gpre, gpost, "f1")
    B.P.emit()
    return B


def mixer_in_stage(B, seqs, Cn, sname):
    P, A, nc = B.P, B.A, B.nc
    A.push()
    TT = 512
    Wout_p = A.alloc([KC, D], BF16)
    Cn["wout_off"] = A.off
    Win = A.alloc([KC, 2560], BF16)
    Wrb = A.alloc([4, 128], BF16)
    Wib = A.alloc([4, 128], BF16)
    mark = A.off
    wst = [A.alloc([2560], F32) for _ in range(4)]
    wrf = A.alloc([4, 128], F32)
    wif = A.alloc([4, 128], F32)
    P.dma("c0", lambda e: e.dma_start(out=wrf, in_=B.inputs["wr_bd"]), writes=["wrf"])
    P.dma("c0", lambda e: e.dma_start(out=wif, in_=B.inputs["wi_bd"]), writes=["wif"])
    if Cn.get("cache_prep") is not None:
        cache_prep_ops(B, *Cn["cache_prep"])
    B.prep_weight(B.inputs["win"], D, 2560, Win, Cn["gma_col"], wst, "Win")
    B.prep_weight(B.inputs["wout"], D, D, Wout_p, None, wst, "Wout")
    Cn["wout_ready"] = True
    P.op("dve", lambda e: e.tensor_copy(out=Wrb, in_=wrf), reads=["wrf"], writes=["Wrb"])
    P.op("dve", lambda e: e.tensor_copy(out=Wib, in_=wif), reads=["wif"], writes=["Wib"])
    P.barrier()
    A.off = mark
    NXR = 6
    xr = [A.alloc([D], F32) for _ in range(NXR)]
    xs = [A.alloc([D], BF16) for _ in range(2)]
    xnT = [A.alloc([KC, TT], BF16) for _ in range(2)]
    junk = A.alloc([D], BF16)
    ssb = [A.alloc([1], F32) for _ in range(4)]
    rsb = [A.alloc([1], F32) for _ in range(4)]
    kst = [A.alloc([TT], BF16) for _ in range(3)]
    vst = [A.alloc([512], F32) for _ in range(2)]
    vbf = [A.alloc([4, 129], BF16) for _ in range(2)]
    for i in range(2):
        P.op("pool", lambda e, i=i: e.memset(vbf[i], 1.0), writes=[("vbf", i)])
    lxb = [[A.alloc([TT + 4], BF16) for _ in range(2)] for _ in range(4)]
    lxl = A.alloc([4, 3], F32)
    lx0 = A.alloc([4, 3], F32)
    dgw = A.alloc([16, 128], BF16)
    xcb = [A.alloc([TT], BF16) for _ in range(4)]
    rb = [A.alloc([TT], F32) for _ in range(4)]
    ib = [A.alloc([TT], F32) for _ in range(4)]
    ab = [A.alloc([TT], F32) for _ in range(4)]
    a2b = [A.alloc([TT], F32) for _ in range(4)]
    hb = [A.alloc([TT], F32) for _ in range(4)]
    gt = [[A.alloc([TT], F32) for _ in range(4)] for _ in range(2)]
    tb = [A.alloc([TT], F32) for _ in range(4)]
    lob = [A.alloc([TT], BF16) for _ in range(4)]
    hstate = A.alloc([4], F32)
    cL = A.alloc([4], F32)
    cL2 = A.alloc([4], F32)
    ps = B.psum
    ps_tr = ps[0][:, :].bitcast(BF16)
    fm = [ps[1], ps[2]]
    cvb = ps[3]
    tm = [ps[4], ps[5]]
    rg = [ps[6], ps[7]]

    if "nocl" not in DBG:
        P.op("act", lambda e: e.activation(out=cL, in_=Cn["lam"], func=AF.Exp, scale=-1.0), reads=["consts"], writes=["cL"])
        P.op("act", lambda e: e.activation(out=cL, in_=cL, func=AF.Ln, bias=1.0), reads=["cL"], writes=["cL"])
    P.op("pool", lambda e: e.tensor_scalar(out=cL2, in0=cL, scalar1=-16.0, scalar2=None, op0=ALU.mult),
         reads=["cL"], writes=["cL2"])
    P.op("pool", lambda e: e.tensor_scalar(out=cL, in0=cL, scalar1=-8.0, scalar2=None, op0=ALU.mult),
         reads=["cL", "cL2"], writes=["cL"])
    for g in range(4):
        for j in range(4):
            P.op("dve", lambda e, g=g, j=j: e.tensor_scalar(out=dgw[:, g * 4 + j, :], in0=B.ident, scalar1=Cn["convw"][:, g, j:j + 1],
                                                            scalar2=None, op0=ALU.mult), reads=["consts"], writes=["dgw"])

    def run_seq(sq, x1_src, T, T_OTH, KT_scr, V_scr, QT_scr, LT_scr, ko, vo, hl_out, cb_out, h0_d, conv0_d, koff):
        sn = sname + str(sq)
        if h0_d is None:
            P.op("pool", lambda e: e.memset(hstate, 0.0), writes=["hstate"])
            for g in range(4):
                P.op("pool", lambda e, g=g: e.memset(lxb[g][0][:, 0:3], 0.0), writes=[("lxh", g, 0)])
        else:
            P.dma(sn + "st", lambda e: e.dma_start(out=hstate, in_=h0_d.rearrange("(g p) -> p g", p=128),
                                                      allow_slow_non_contiguous=True), writes=["hstate"])
            for g in range(4):
                P.dma(sn + "st", lambda e, g=g: e.dma_start(
                    out=lx0[:, g, :], in_=conv0_d[:, g * 128:(g + 1) * 128].rearrange("j p -> p j"),
                    allow_slow_non_contiguous=True), writes=[("lx0", g)])
            for g in range(4):
                P.op("dve", lambda e, g=g: e.tensor_copy(out=lxb[g][0][:, 0:3], in_=lx0[:, g, :]),
                     reads=[("lx0", g)], writes=[("lxh", g, 0)])
        P.barrier()

        tiles = []
        t0 = 0
        while t0 < T:
            lim = T_OTH if t0 < T_OTH else T
            nt = min(TT, lim - t0)
            tiles.append((t0, nt))
            t0 += nt
        sub_ctr = [0]

        def load_tile(ti):
            t0, nt = tiles[ti]
            subs = []
            for s0 in range(0, nt, 128):
                ns = min(128, nt - s0)
                k = sub_ctr[0] % NXR
                sub_ctr[0] += 1
                P.dma(sname + "x%d" % k, lambda e, k=k, a=t0 + s0, ns=ns: e.dma_start(
                    out=xr[k][:ns, :], in_=x1_src[a:a + ns, :]), writes=[("xr", k)])
                subs.append((k, s0, ns))
            return subs

        loaded = {0: load_tile(0)}
        fm_c = [0]
        tm_c = [0]
        ks_c = [0]
        vs_c = [0]
        vb_c = [0]
        def tile_body(ti, t0, nt):
            own = t0 >= T_OTH
            to = t0 - T_OTH
            subs = loaded.pop(ti)
            xb = xnT[ti % 2]
            cur, nxt = ti % 2, (ti + 1) % 2
            for si, (k, s0, ns) in enumerate(subs):
                sl = (ti * 4 + si) % 4
                B.rstd_of(xr[k][:ns, :], ns, junk, ssb[sl], rsb[sl], [("xr", k)], sl)
                xsb = xs[si % 2]
                xsid = ("xs", si % 2)
                P.op("dve", lambda e, xsb=xsb, k=k, ns=ns, sl=sl: e.tensor_scalar(
                    out=xsb[:ns, :], in0=xr[k][:ns, :], scalar1=rsb[sl][:ns, :], scalar2=None, op0=ALU.mult),
                     reads=[("xr", k), ("rstd", sl)], writes=[xsid])
                for kc in range(KC):
                    P.op("pe", lambda e, xsb=xsb, kc=kc, ns=ns: e.transpose(
                        out=ps_tr[:, kc * 128:kc * 128 + ns], in_=xsb[:ns, kc * 128:(kc + 1) * 128],
                        identity=B.ident[:ns, :ns]),
                         reads=[xsid, "consts"], writes=["ps_tr"])
                P.op("act", lambda e, xb=xb, s0=s0, ns=ns: e.activation(
                    out=xb[:, :, s0:s0 + ns], in_=ps_tr.rearrange("p (a b) -> p a b", a=KC)[:, :, 0:ns],
                    func=AF.Copy),
                     reads=["ps_tr"], writes=[("xnT", ti % 2, si)])
                if si % 2 == 1:
                    yield
            xn_ids = [("xnT", ti % 2, si) for si in range(len(subs))]
            if ti + 1 < len(tiles):
                loaded[ti + 1] = load_tile(ti + 1)

            def fm_proj(col0):
                bank = fm[fm_c[0] % 2]
                bid = ("fm", fm_c[0] % 2)
                fm_c[0] += 1
                for kc in range(KC):
                    P.op("pe", lambda e, bank=bank, kc=kc, col0=col0: e.matmul(
                        bank[:, 0:nt], lhsT=Win[:, kc, col0:col0 + 128], rhs=xb[:, kc, 0:nt],
                        start=(kc == 0), stop=(kc == KC - 1)),
                         reads=xn_ids + [("Win", kc)], writes=[bid])
                return bank, bid

            def fm_to_scr(col0, dst_ap):
                bank, bid = fm_proj(col0)
                kb = kst[ks_c[0] % 3]
                kid = ("kst", ks_c[0] % 3)
                ksem = sname + "k%d" % (ks_c[0] % 3)
                ks_c[0] += 1
                P.op("act", lambda e, bank=bank, kb=kb: e.activation(out=kb[:, 0:nt], in_=bank[:, 0:nt], func=AF.Copy),
                     reads=[bid], writes=[kid])
                P.dma(ksem, lambda e, kb=kb, dst_ap=dst_ap: e.dma_start(out=dst_ap, in_=kb[:, 0:nt]), reads=[kid])

            for g in range(4):
                bank, bid = fm_proj(1536 + g * 128)
                P.op("act", lambda e, bank=bank, g=g: e.activation(
                    out=lxb[g][cur][:, 3:3 + nt], in_=bank[:, 0:nt], func=AF.Copy),
                     reads=[bid], writes=[("lx", g, cur)])
                if ti == len(tiles) - 1:
                    P.op("act", lambda e, bank=bank, g=g: e.activation(
                        out=lxl[:, g, :], in_=bank[:, nt - 3:nt], func=AF.Copy), reads=[bid], writes=[("lxl", g)])
            yield
            if own:
                for g in range(4):
                    bank, bid = fm_proj(2048 + g * 128)
                    P.op("act", lambda e, bank=bank, g=g: e.activation(out=gt[cur][g][:, 0:nt], in_=bank[:, 0:nt], func=AF.Copy),
                         reads=[bid], writes=[("gt", cur, g)])
            for h in range(4 if "nofm" not in DBG else 0):
                fm_to_scr(512 + h * 128, KT_scr[h, :, koff + t0:koff + t0 + nt])
            yield
            if own and "nofm" not in DBG:
                for h in range(4):
                    fm_to_scr(h * 128, QT_scr[h, :, to:to + nt])
                yield
            for si, (k, s0, ns) in enumerate(subs if "notm" not in DBG else []):
                for which in (("v", 1024), ("k", 512)):
                    if which[0] == "k" and not own:
                        continue
                    bank = tm[tm_c[0] % 2]
                    bid = ("tm", tm_c[0] % 2)
                    tm_c[0] += 1
                    for kc in range(KC):
                        P.op("pe", lambda e, bank=bank, kc=kc, s0=s0, ns=ns, c0=which[1]: e.matmul(
                            bank[:ns, :], lhsT=xb[:, kc, s0:s0 + ns], rhs=Win[:, kc, c0:c0 + 512],
                            start=(kc == 0), stop=(kc == KC - 1)),
                             reads=xn_ids + [("Win", kc)], writes=[bid])
                    if which[0] == "v" and "nov" not in DBG:
                        vb = vbf[vb_c[0] % 2]
                        vbid = ("vbf", vb_c[0] % 2)
                        vsem = sname + "vb%d" % (vb_c[0] % 2)
                        vb_c[0] += 1
                        P.op("dve", lambda e, bank=bank, vb=vb, ns=ns: e.tensor_copy(
                            out=vb[:ns, :, 0:128], in_=bank[:ns, :].rearrange("p (h d) -> p h d", h=4)),
                             reads=[bid], writes=[vbid])
                        P.dma(vsem, lambda e, vb=vb, a=koff + t0 + s0, ns=ns: e.dma_start(
                            out=V_scr[a:a + ns, :], in_=vb[:ns, :, :].rearrange("p h d -> p (h d)")),
                              reads=[vbid])
                    if own and "noko" not in DBG:
                        vs = vst[vs_c[0] % 2]
                        vsid = ("vst", vs_c[0] % 2)
                        vsem = sname + "vs%d" % (vs_c[0] % 2)
                        vs_c[0] += 1
                        dst = vo if which[0] == "v" else ko
                        P.op("act", lambda e, bank=bank, vs=vs, ns=ns: e.activation(out=vs[:ns, :], in_=bank[:ns, :], func=AF.Copy),
                             reads=[bid], writes=[vsid])
                        P.dma(vsem, lambda e, vs=vs, dst=dst, a=to + s0, ns=ns: e.dma_start(out=dst[a:a + ns, :], in_=vs[:ns, :]),
                              reads=[vsid])
                if si % 2 == 1:
                    yield

        def lru_part(ti, t0, nt, own, to, cur, nxt):
            for g in range(4):
                lx = lxb[g][cur]
                lid = [("lx", g, cur), ("lxh", g, cur)]
                for j in range(4):
                    P.op("pe", lambda e, g=g, lx=lx, j=j: e.matmul(
                        cvb[:, 0:nt], lhsT=dgw[:, g * 4 + j, :], rhs=lx[:, j:j + nt], start=(j == 0), stop=(j == 3)),
                         reads=lid + ["dgw"], writes=[("cv", 0)])
                P.op("dve", lambda e, g=g: e.tensor_scalar(
                    out=xcb[g][:, 0:nt], in0=cvb[:, 0:nt], scalar1=Cn["convb"][:, g:g + 1], scalar2=None, op0=ALU.add),
                     reads=[("cv", 0), "consts"], writes=[("xcb", g)])
                if ti + 1 < len(tiles):
                    boundary = (tiles[ti + 1][0] == T_OTH) and T_OTH > 0
                    if boundary:
                        P.op("pool", lambda e, g=g, lx=lx: e.tensor_scalar(
                            out=lxb[g][nxt][:, 0:3], in0=lx[:, nt:nt + 3], scalar1=Cn["flag"][:, 0:1], scalar2=None,
                            op0=ALU.mult), reads=lid + ["consts"], writes=[("lxh", g, nxt)])
                    else:
                        P.op("pool", lambda e, g=g, lx=lx: e.tensor_copy(out=lxb[g][nxt][:, 0:3], in_=lx[:, nt:nt + 3]),
                             reads=lid, writes=[("lxh", g, nxt)])
            yield
            for g in range(4):
                P.op("pe", lambda e, g=g: e.matmul(rg[0][:, 0:nt], lhsT=Wrb[:, g, :], rhs=xcb[g][:, 0:nt], start=True, stop=True),
                     reads=[("xcb", g), "Wrb"], writes=[("rg", 0)])
                P.op("act", lambda e, g=g: e.activation(out=rb[g][:, 0:nt], in_=rg[0][:, 0:nt], func=AF.Sigmoid,
                                                        bias=Cn["brg"][:, g:g + 1]),
                     reads=[("rg", 0), "consts"], writes=[("rb", g)])
                P.op("pe", lambda e, g=g: e.matmul(rg[1][:, 0:nt], lhsT=Wib[:, g, :], rhs=xcb[g][:, 0:nt], start=True, stop=True),
                     reads=[("xcb", g), "Wib"], writes=[("rg", 1)])
                P.op("act", lambda e, g=g: e.activation(out=ib[g][:, 0:nt], in_=rg[1][:, 0:nt], func=AF.Sigmoid,
                                                        bias=Cn["big"][:, g:g + 1]),
                     reads=[("rg", 1), "consts"], writes=[("ib", g)])
            yield
            if own:
                for g in range(4):
                    P.op("pool", lambda e, g=g: e.tensor_tensor(out=tb[g][:, 0:nt], in0=gt[cur][g][:, 0:nt], in1=gt[cur][g][:, 0:nt], op=ALU.mult),
                         reads=[("gt", cur, g)], writes=[("tb", g)])
                    P.op("pool", lambda e, g=g: e.tensor_scalar(out=tb[g][:, 0:nt], in0=tb[g][:, 0:nt], scalar1=0.044715, scalar2=1.0,
                                                                op0=ALU.mult, op1=ALU.add),
                         reads=[("tb", g)], writes=[("tb", g)])
                    P.op("pool", lambda e, g=g: e.tensor_tensor(out=tb[g][:, 0:nt], in0=tb[g][:, 0:nt], in1=gt[cur][g][:, 0:nt], op=ALU.mult),
                         reads=[("tb", g), ("gt", cur, g)], writes=[("tb", g)])
                    P.op("act", lambda e, g=g: e.activation(out=tb[g][:, 0:nt], in_=tb[g][:, 0:nt], func=AF.Sigmoid, scale=1.5957691216),
                         reads=[("tb", g)], writes=[("tb", g)])
                    P.op("pool", lambda e, g=g: e.tensor_tensor(out=gt[cur][g][:, 0:nt], in0=tb[g][:, 0:nt], in1=gt[cur][g][:, 0:nt], op=ALU.mult),
                         reads=[("tb", g), ("gt", cur, g)], writes=[("gt", cur, g)])
            yield
            for g in range(4):
                P.op("act", lambda e, g=g: e.activation(out=ab[g][:, 0:nt], in_=rb[g][:, 0:nt], func=AF.Exp, scale=cL[:, g:g + 1]),
                     reads=[("rb", g), "cL"], writes=[("ab", g)])
                P.op("act", lambda e, g=g: e.activation(out=a2b[g][:, 0:nt], in_=rb[g][:, 0:nt], func=AF.Exp, scale=cL2[:, g:g + 1]),
                     reads=[("rb", g), "cL2"], writes=[("a2b", g)])
            for g in range(4):
                P.op("act", lambda e, g=g: e.activation(out=a2b[g][:, 0:nt], in_=a2b[g][:, 0:nt], func=AF.Sqrt, scale=-1.0, bias=1.0),
                     reads=[("a2b", g)], writes=[("a2b", g)])
            yield
            for g in range(4):
                P.op("dve", lambda e, g=g: e.tensor_tensor(out=ib[g][:, 0:nt], in0=ib[g][:, 0:nt], in1=xcb[g][:, 0:nt], op=ALU.mult),
                     reads=[("ib", g), ("xcb", g)], writes=[("ib", g)])
                P.op("dve", lambda e, g=g: e.tensor_tensor(out=ib[g][:, 0:nt], in0=ib[g][:, 0:nt], in1=a2b[g][:, 0:nt], op=ALU.mult),
                     reads=[("ib", g), ("a2b", g)], writes=[("ib", g)])
                P.op("dve", lambda e, g=g: e.tensor_tensor_scan(
                    out=hb[g][:, 0:nt], data0=ab[g][:, 0:nt], data1=ib[g][:, 0:nt], initial=hstate[:, g:g + 1],
                    op0=ALU.mult, op1=ALU.add),
                     reads=[("ab", g), ("ib", g), "hstate"], writes=[("hb", g)])
            boundary = (ti + 1 < len(tiles)) and (tiles[ti + 1][0] == T_OTH) and T_OTH > 0
            for g in range(4):
                if boundary:
                    P.op("pool", lambda e, g=g: e.tensor_scalar(out=hstate[:, g:g + 1], in0=hb[g][:, nt - 1:nt],
                                                                scalar1=Cn["flag"][:, 0:1], scalar2=None, op0=ALU.mult),
                         reads=[("hb", g), "consts"], writes=["hstate"])
                else:
                    P.op("pool", lambda e, g=g: e.tensor_copy(out=hstate[:, g:g + 1], in_=hb[g][:, nt - 1:nt]),
                         reads=[("hb", g)], writes=["hstate"])
            if own:
                for g in range(4):
                    P.op("dve", lambda e, g=g: e.tensor_tensor(out=lob[g][:, 0:nt], in0=hb[g][:, 0:nt], in1=gt[cur][g][:, 0:nt], op=ALU.mult),
                         reads=[("hb", g), ("gt", cur, g)], writes=[("lob", g)])
                    P.dma(sname + "lo%d" % g, lambda e, g=g: e.dma_start(out=LT_scr[g, :, to:to + nt], in_=lob[g][:, 0:nt]),
                          reads=[("lob", g)])
        def interleave(ga, gb):
            gens = [g for g in (ga, gb) if g is not None]
            while gens:
                for g in list(gens):
                    try:
                        next(g)
                    except StopIteration:
                        gens.remove(g)

        pending = None
        for ti, (t0, nt) in enumerate(tiles):
            interleave(tile_body(ti, t0, nt), pending)
            pending = lru_part(ti, t0, nt, t0 >= T_OTH, t0 - T_OTH, ti % 2, (ti + 1) % 2)
        interleave(None, pending)
        lt0, lnt = tiles[-1]
        lcur = (len(tiles) - 1) % 2
        if "nofin" not in DBG:
            P.dma(sn + "fin", lambda e: e.dma_start(out=hl_out.rearrange("(g p) -> p g", p=128), in_=hstate,
                                                       allow_slow_non_contiguous=True), reads=["hstate"])
        for g in range(4 if "nofin" not in DBG else 0):
            P.dma(sn + "fin", lambda e, g=g: e.dma_start(
                out=cb_out[:, g * 128:(g + 1) * 128].rearrange("j p -> p j"), in_=lxl[:, g, :],
                allow_slow_non_contiguous=True), reads=[("lxl", g)])

    for sq, q in enumerate(seqs):
        run_seq(sq, q['x1'], q['T'], q['T_OTH'], q['KT'], q['V'], q['QT'], q['LT'], q['ko'], q['vo'], q['hl'], q['cb'],
                q.get('h0'), q.get('conv0'), q.get('koff', 0))
    P.barrier()
    A.pop()


SLOPES = [2.0 ** (-8.0 * (i + 1) / 4) for i in range(4)]
LAMBDA_INIT = 0.8 - 0.6 * 1.0


def attn_consts(B, Cn):
    P, A = B.P, B.A
    for nm, shp in (("pb", [4, 71]), ("db", [4, 128]), ("subg", [128]), ("lq1", [64]), ("lk1", [64]), ("lq2", [64]), ("lk2", [64])):
        Cn[nm] = A.alloc(shp, F32)
        P.dma("c0", lambda e, nm=nm: e.dma_start(out=Cn[nm], in_=B.inputs[nm]), writes=["consts"])
    negl = A.alloc([1], F32)
    t1 = A.alloc([1], F32)
    t2 = A.alloc([1], F32)
    j64 = A.alloc([64], F32)
    P.op("dve", lambda e: e.tensor_tensor(out=j64, in0=Cn["lq1"], in1=Cn["lk1"], op=ALU.mult), reads=["consts"], writes=["j64"])
    P.op("dve", lambda e: e.reduce_sum(out=t1, in_=j64, axis=AX.X), reads=["j64"], writes=["t1"])
    P.op("dve", lambda e: e.tensor_tensor(out=j64, in0=Cn["lq2"], in1=Cn["lk2"], op=ALU.mult), reads=["consts", "t1"], writes=["j64"])
    P.op("dve", lambda e: e.reduce_sum(out=t2, in_=j64, axis=AX.X), reads=["j64"], writes=["t2"])
    P.op("act", lambda e: e.activation(out=t1, in_=t1, func=AF.Exp), reads=["t1"], writes=["t1"])
    P.op("act", lambda e: e.activation(out=t2, in_=t2, func=AF.Exp), reads=["t2"], writes=["t2"])
    P.op("pool", lambda e: e.tensor_tensor(out=negl, in0=t2, in1=t1, op=ALU.subtract), reads=["t1", "t2"], writes=["negl"])
    P.op("pool", lambda e: e.tensor_scalar(out=negl, in0=negl, scalar1=-LAMBDA_INIT, scalar2=None, op0=ALU.add),
         reads=["negl"], writes=["negl"])
    subg8 = A.alloc([128], F32)
    P.op("pool", lambda e: e.tensor_scalar(out=subg8, in0=Cn["subg"], scalar1=1.0 - LAMBDA_INIT, scalar2=None, op0=ALU.mult),
         reads=["consts"], writes=["subg8"])
    pbo = A.alloc([4, 71], F32)
    P.op("pool", lambda e: e.tensor_scalar(out=pbo, in0=Cn["pb"], scalar1=Cn["maskv"][:, 0:1], scalar2=None, op0=ALU.add),
         reads=["consts"], writes=["pbo"])
    Cn["negl"], Cn["subg8"], Cn["pbo"] = negl, subg8, pbo


def attn_stage(B, jobs, Cn, sname, window=(None,) * 4):
    P, A, nc = B.P, B.A, B.nc
    TKmax = max(q["TK"] for q in jobs)
    NBmax = cdiv(TKmax, 128)
    A.push()
    KT = A.alloc([4, NBmax * 128], BF16)
    V1 = A.alloc([NBmax, 516], BF16)
    Wout = A.alloc([KC, D], BF16)
    gmb = A.alloc([D], F32)
    P.dma("c0", lambda e: e.dma_start(out=gmb, in_=B.inputs["gmb"]), writes=["gmb"])
    Cn["gmb"] = gmb
    attn_consts(B, Cn)
    mark = A.off
    wst = [A.alloc([D], F32) for _ in range(4)]
    B.prep_weight(B.inputs["wout"], D, D, Wout, None, wst, "Wout")
    P.barrier()
    A.off = mark
    QTILE = 512
    qt = [A.alloc([4, QTILE], BF16) for _ in range(2)]
    NPT = 4
    pt = [A.alloc([QTILE], BF16) for _ in range(NPT)]
    dtmp = [A.alloc([128], F32) for _ in range(2)]
    atok = [A.alloc([512], BF16) for _ in range(4)]
    attT = A.alloc([4, QTILE], BF16)
    lruT = [A.alloc([4, QTILE], BF16) for _ in range(2)]
    x1r = [A.alloc([D], F32) for _ in range(2)]
    ost = [A.alloc([D], F32)] * 2
    otmp = [A.alloc([128], F32) for _ in range(2)]
    ofin = [A.alloc([2, 129], F32) for _ in range(4)]
    junk = A.alloc([512], BF16)
    rl = [A.alloc([2], F32) for _ in range(4)]
    ssn = [A.alloc([1], F32) for _ in range(4)]
    rsn = [A.alloc([1], F32) for _ in range(4)]
    ssm = [A.alloc([1], F32) for _ in range(2)]
    ssm2 = [A.alloc([1], F32) for _ in range(2)]
    rsm = [A.alloc([1], F32) for _ in range(2)]
    ps = B.psum
    pb, pbo, db = Cn["pb"], Cn["pbo"], Cn["db"]
    st_c = [0]
    pt_c = [0]
    dt_c = [0]
    x_c = [0]
    o_c = [0]
    qb_c = [0]
    VCH = 16

    def run_job(TK, NQ, KT_scr, V_scr, QT_scr, LT_scr, x1_scr, x1_off, x2_dst, mask_other):
        NB = cdiv(TK, 128)
        KOFF = TK - NQ
        assert KOFF % 128 == 0
        nkof = lambda j: min(128, TK - 128 * j)
        for h in range(4):
            P.dma(sname + "K%d" % h, lambda e, h=h: e.dma_start(out=KT[:, h, 0:TK], in_=KT_scr[h, :, 0:TK]), writes=[("KT", h)])
        for ci, j0 in enumerate(range(0, NB, VCH)):
            j1 = min(NB, j0 + VCH)
            jf = min(j1, TK // 128)
            if jf > j0:
                P.dma(sname + "V%d" % (ci % 4), lambda e, j0=j0, jf=jf: e.dma_start(
                    out=V1[:, j0:jf, :], in_=V_scr[j0 * 128:jf * 128, :].rearrange("(j p) c -> p j c", p=128)),
                      writes=[("V1", ci)])
            if jf < j1:
                nk = nkof(jf)
                P.dma(sname + "V%d" % (ci % 4), lambda e, jf=jf, nk=nk: e.dma_start(
                    out=V1[:nk, jf, :], in_=V_scr[jf * 128:jf * 128 + nk, :]), writes=[("V1", ci)])
        vid = lambda j: ("V1", j // VCH)

        tiles = []
        q0 = 0
        while q0 < NQ:
            nq = min(QTILE, NQ - q0)
            tiles.append((q0, nq))
            q0 += nq

        LOOK = int(os.environ.get("K_LOOK", "3"))
        dq = []

        def push2(fn):
            dq.append(fn)
            while len(dq) > LOOK:
                dq.pop(0)()

        def load_q(ti):
            q0, nq = tiles[ti]
            b = qb_c[0] % 2
            qb_c[0] += 1
            for h in range(4):
                P.dma(sname + "q%d" % b, lambda e, b=b, h=h, q0=q0, nq=nq: e.dma_start(
                    out=qt[b][:, h, 0:nq], in_=QT_scr[h, :, q0:q0 + nq]), writes=[("qt", b, h)])
            def ld(b=b, q0=q0, nq=nq):
                P.dma(sname + "l%d" % b, lambda e: e.dma_start(
                    out=lruT[b][:, :, 0:nq], in_=LT_scr[:, :, q0:q0 + nq].rearrange("g p t -> p g t")), writes=[("lruT", b)])
            push2(ld)
            return b

        def tile_stream(ti, q0, nq, b):
            nsb = cdiv(nq, 128)
            nqs_of = lambda s: min(128, nq - 128 * s)
            jb = (KOFF + q0) // 128
            nb_next = load_q(ti + 1) if ti + 1 < len(tiles) else None
            for h in range(4):
                persub = (h == 0)
                W = window[h]
                jlo = 0 if W is None else max(0, jb - W)
                first_in_bank = [True] * 4
                for j in range(jlo, jb + nsb):
                    nk = nkof(j)
                    rel = j - jb
                    s_lo = max(0, rel)
                    c0 = s_lo * 128
                    tab = pbo if (mask_other and j * 128 < KOFF) else pb
                    for c in range(2):
                        bi = st_c[0] % 4
                        st_c[0] += 1
                        stb = ps[bi]
                        bid = ("bank", bi)
                        P.op("pe", lambda e, stb=stb, c=c, h=h, j=j, c0=c0, nq=nq, b=b, nk=nk: e.matmul(
                            stb[:nk, c0:nq], lhsT=KT[c * 64:(c + 1) * 64, h, j * 128:j * 128 + nk],
                            rhs=qt[b][c * 64:(c + 1) * 64, h, c0:nq], start=True, stop=True),
                             reads=[("KT", h), ("qt", b, h)], writes=[bid])
                        pi = pt_c[0] % NPT
                        pt_c[0] += 1
                        ptb = pt[pi]
                        pid = ("pt", pi)
                        c1 = c0
                        if rel >= 0:
                            nqs = nqs_of(rel)
                            di = dt_c[0] % 2
                            dt_c[0] += 1
                            P.op("dve", lambda e, stb=stb, di=di, h=h, c0=c0, nk=nk, nqs=nqs: e.scalar_tensor_tensor(
                                out=dtmp[di][:nk, :nqs], in0=stb[:nk, c0:c0 + nqs], scalar=0.125, in1=db[:nk, h, 0:nqs],
                                op0=ALU.mult, op1=ALU.add), reads=[bid, "consts"], writes=[("dtmp", di)])
                            bconst = 0.0 if persub else SLOPES[h] * 128.0 * rel
                            P.op("act", lambda e, ptb=ptb, di=di, c0=c0, bconst=bconst, nk=nk, nqs=nqs: e.activation(
                                out=ptb[:nk, c0:c0 + nqs], in_=dtmp[di][:nk, :nqs], func=AF.Exp, bias=bconst),
                                 reads=[("dtmp", di)], writes=[pid])
                            c1 = c0 + 128
                        if c1 < nq:
                            if persub:
                                for s in range(c1 // 128, nsb):
                                    dj = jb + s - j
                                    ce = s * 128 + nqs_of(s)
                                    P.op("act", lambda e, ptb=ptb, stb=stb, s=s, ce=ce, dj=dj, tab=tab, h=h, nk=nk: e.activation(
                                        out=ptb[:nk, s * 128:ce], in_=stb[:nk, s * 128:ce], func=AF.Exp,
                                        bias=tab[:nk, h, dj + 3:dj + 4], scale=0.125),
                                         reads=[bid, "pbo"], writes=[pid])
                            else:
                                dj = jb - j
                                P.op("act", lambda e, ptb=ptb, stb=stb, c1=c1, nq=nq, dj=dj, tab=tab, h=h, nk=nk: e.activation(
                                    out=ptb[:nk, c1:nq], in_=stb[:nk, c1:nq], func=AF.Exp,
                                    bias=tab[:nk, h, dj + 3:dj + 4], scale=0.125),
                                     reads=[bid, "pbo"], writes=[pid])

                        def pv(ptb=ptb, pid=pid, c=c, j=j, h=h, nk=nk, s_lo=s_lo, fib=first_in_bank, jb=jb, nsb=nsb, nqs_of=nqs_of):
                            for s in range(s_lo, nsb):
                                ob = ps[4 + s]
                                st_flag = fib[s]
                                fib[s] = False
                                nqs = nqs_of(s)
                                last = (j == jb + s)
                                P.op("pe", lambda e, ob=ob, ptb=ptb, s=s, c=c, j=j, h=h, st_flag=st_flag, nk=nk, nqs=nqs, last=last: e.matmul(
                                    ob[:nqs, c * 256:c * 256 + 129], lhsT=ptb[:nk, s * 128:s * 128 + nqs],
                                    rhs=V1[:nk, j, h * 129:(h + 1) * 129], start=st_flag, stop=last,
                                    skip_group_check=True),
                                     reads=[pid, vid(j)], writes=[("bank", 4 + s)])
                        push2(pv)
                push2(lambda h=h, nsb=nsb, nqs_of=nqs_of: finalize(h, nsb, nqs_of))
            push2(lambda ti=ti, q0=q0, nq=nq, b=b, nsb=nsb, nqs_of=nqs_of: tail(q0, nq, b, nsb, nqs_of))
            return nb_next

        def finalize(h, nsb, nqs_of):
            for s in range(nsb):
                n = nqs_of(s)
                ob = ps[4 + s]
                oid = ("bank", 4 + s)
                of = ofin[s]
                fid = ("ofin", s)
                P.op("dve", lambda e, ob=ob, of=of, n=n: e.tensor_copy(
                    out=of[:n, :, :], in_=ob[:n, :].rearrange("p (c x) -> p c x", c=2)[:, :, 0:129]),
                     reads=[oid], writes=[fid])
                r2 = rl[s]
                P.op("dve", lambda e, of=of, r2=r2, n=n: e.reciprocal(out=r2[:n, :], in_=of[:n, :, 128]),
                     reads=[fid], writes=[("rl", s)])
                P.op("pool", lambda e, r2=r2, n=n: e.tensor_tensor(out=r2[:n, 1:2], in0=r2[:n, 1:2], in1=Cn["negl"][:n, :], op=ALU.mult),
                     reads=[("rl", s), "negl"], writes=[("rl", s)])
                oi = o_c[0] % 2
                o_c[0] += 1
                ot = otmp[oi]
                otid = ("otmp", oi)
                P.op("dve", lambda e, of=of, ot=ot, r2=r2, n=n: e.tensor_scalar(
                    out=ot[:n, :], in0=of[:n, 0, 0:128], scalar1=r2[:n, 0:1], scalar2=None, op0=ALU.mult),
                     reads=[fid, ("rl", s)], writes=[otid])
                P.op("dve", lambda e, of=of, ot=ot, r2=r2, n=n: e.scalar_tensor_tensor(
                    out=ot[:n, :], in0=of[:n, 1, 0:128], scalar=r2[:n, 1:2], in1=ot[:n, :], op0=ALU.mult, op1=ALU.add),
                     reads=[fid, ("rl", s), otid], writes=[otid])
                P.op("act", lambda e, ot=ot, s=s, n=n: e.activation(out=junk[:n, 0:128], in_=ot[:n, :], func=AF.Square,
                                                               accum_out=ssn[s][:n, :]),
                     reads=[otid], writes=[("ssn", s), "junkA"])
                P.op("pool", lambda e, s=s, n=n: e.tensor_scalar(out=ssn[s][:n, :], in0=ssn[s][:n, :], scalar1=1.0 / 128, scalar2=EPS,
                                                            op0=ALU.mult, op1=ALU.add), reads=[("ssn", s)], writes=[("ssn", s)])
                P.op("pool", lambda e, s=s, n=n: e.tensor_tensor(out=rsn[s][:n, :], in0=ssn[s][:n, :], in1=B.c_mhalf[:n, :], op=ALU.pow),
                     reads=[("ssn", s)], writes=[("rsn", s)])
                P.op("dve", lambda e, ot=ot, s=s, h=h, n=n: e.scalar_tensor_tensor(
                    out=atok[s][:n, h * 128:(h + 1) * 128], in0=ot[:n, :], scalar=rsn[s][:n, 0:1], in1=Cn["subg8"][:n, :],
                    op0=ALU.mult, op1=ALU.mult), reads=[otid, ("rsn", s), "subg8"], writes=[("atok", s, h)])

        def tail(q0, nq, b, nsb, nqs_of):
            ps_tr = ps[0][:, :].bitcast(BF16)
            for s in range(nsb):
                n = nqs_of(s)
                for h in range(4):
                    P.op("pe", lambda e, s=s, h=h, n=n: e.transpose(
                        out=ps_tr[:, h * 128:h * 128 + n], in_=atok[s][:n, h * 128:(h + 1) * 128], identity=B.ident[:n, :n]),
                         reads=[("atok", s, h)], writes=[("bank", 0)])
                P.op("act", lambda e, s=s, n=n: e.activation(
                    out=attT[:, :, s * 128:s * 128 + n], in_=ps_tr[:, 0:512].rearrange("p (a b) -> p a b", a=4)[:, :, 0:n],
                    func=AF.Copy), reads=[("bank", 0)], writes=[("attT", s)])
            for s in range(nsb):
                n = nqs_of(s)
                mo = (ps[1], ps[2])
                for half in range(2):
                    for kk in range(8):
                        src = attT if kk < 4 else lruT[b]
                        P.op("pe", lambda e, half=half, kk=kk, src=src, s=s, n=n, mo=mo: e.matmul(
                            mo[half][:n, :], lhsT=src[:, kk % 4, s * 128:s * 128 + n],
                            rhs=Wout[:, kk, half * 512:(half + 1) * 512], start=(kk == 0), stop=(kk == 7)),
                             reads=[("attT", s), ("lruT", b), ("Wout", kk)], writes=[("bank", 1 + half)])
                sl = s % 2
                P.op("act", lambda e, sl=sl, n=n: e.activation(out=junk[:n, :], in_=ps[1][:n, :], func=AF.Square, accum_out=ssm[sl][:n, :]),
                     reads=[("bank", 1)], writes=["junkA", ("ssm", sl)])
                P.op("act", lambda e, sl=sl, n=n: e.activation(out=junk[:n, :], in_=ps[2][:n, :], func=AF.Square, accum_out=ssm2[sl][:n, :]),
                     reads=[("bank", 2)], writes=["junkA", ("ssm2", sl)])
                P.op("pool", lambda e, sl=sl, n=n: e.tensor_tensor(out=ssm[sl][:n, :], in0=ssm[sl][:n, :], in1=ssm2[sl][:n, :], op=ALU.add),
                     reads=[("ssm", sl), ("ssm2", sl)], writes=[("ssm", sl)])
                P.op("pool", lambda e, sl=sl, n=n: e.tensor_scalar(out=ssm[sl][:n, :], in0=ssm[sl][:n, :], scalar1=1.0 / D, scalar2=EPS,
                                                              op0=ALU.mult, op1=ALU.add), reads=[("ssm", sl)], writes=[("ssm", sl)])
                P.op("pool", lambda e, sl=sl, n=n: e.tensor_tensor(out=rsm[sl][:n, :], in0=ssm[sl][:n, :], in1=B.c_mhalf[:n, :], op=ALU.pow),
                     reads=[("ssm", sl)], writes=[("rsm", sl)])
                xi = x_c[0] % 2
                x_c[0] += 1
                a = q0 + s * 128
                P.dma(sname + "x%d" % xi, lambda e, xi=xi, a=a, n=n: e.dma_start(
                    out=x1r[xi][:n, :], in_=x1_scr[x1_off + a:x1_off + a + n, :]), writes=[("x1r", xi)])
                for half in range(2):
                    P.op("dve", lambda e, xi=xi, half=half, sl=sl, n=n: e.scalar_tensor_tensor(
                        out=ost[xi][:n, half * 512:(half + 1) * 512], in0=ps[1 + half][:n, :], scalar=rsm[sl][:n, 0:1],
                        in1=Cn["gmb"][:n, half * 512:(half + 1) * 512], op0=ALU.mult, op1=ALU.mult),
                         reads=[("bank", 1 + half), ("rsm", sl), "consts"], writes=[("ost", 0, half)])
                    P.op("pool", lambda e, xi=xi, half=half, n=n: e.tensor_tensor(
                        out=ost[xi][:n, half * 512:(half + 1) * 512], in0=ost[xi][:n, half * 512:(half + 1) * 512],
                        in1=x1r[xi][:n, half * 512:(half + 1) * 512], op=ALU.add),
                         reads=[("ost", 0, half), ("x1r", xi)], writes=[("ost", 0, half)])
                P.dma(sname + "o0", lambda e, xi=xi, a=a, n=n: e.dma_start(out=x2_dst[a:a + n, :], in_=ost[xi][:n, :]),
                      reads=[("ost", 0, 0), ("ost", 0, 1)])

        bcur = load_q(0)
        for ti, (q0, nq) in enumerate(tiles):
            bcur = tile_stream(ti, q0, nq, bcur)
        while dq:
            dq.pop(0)()

    for q in jobs:
        run_job(q["TK"], q["NQ"], q["KT"], q["V"], q["QT"], q["LT"], q["x1"], q["x1_off"], q["x2"], q["mask_other"])
    P.barrier()
    A.pop()


def attn_stage2(B, jobs, Cn, sname, window=(None,) * 4):
    P, A, nc = B.P, B.A, B.nc
    TKmax = max(q["TK"] for q in jobs)
    NBmax = cdiv(TKmax, 128)
    A.push()
    Wout = A.alloc([KC, D], BF16)
    if Cn.get("wout_ready"):
        assert A.off == Cn["wout_off"], (A.off, Cn["wout_off"])
    KT = A.alloc([4, NBmax * 128], BF16)
    V1 = A.alloc([NBmax, 516], BF16)
    VCH = 16
    kv_done = {}

    def issue_kv(TK, KT_scr, V_scr):
        kv_done[id(KT_scr)] = True
        NB = cdiv(TK, 128)
        for h in range(4):
            P.dma(sname + "K%d" % h, lambda e, h=h: e.dma_start(out=KT[:, h, 0:TK], in_=KT_scr[h, :, 0:TK]), writes=[("KT", h)])
        for ci, j0 in enumerate(range(0, NB, VCH)):
            j1 = min(NB, j0 + VCH)
            jf = min(j1, TK // 128)
            if jf > j0:
                P.dma(sname + "V%d" % (ci % 4), lambda e, j0=j0, jf=jf: e.dma_start(
                    out=V1[:, j0:jf, :], in_=V_scr[j0 * 128:jf * 128, :].rearrange("(j p) c -> p j c", p=128)),
                      writes=[("V1", ci)])
            if jf < j1:
                nk = min(128, TK - 128 * jf)
                P.dma(sname + "V%d" % (ci % 4), lambda e, jf=jf, nk=nk: e.dma_start(
                    out=V1[:nk, jf, :], in_=V_scr[jf * 128:jf * 128 + nk, :]), writes=[("V1", ci)])

    issue_kv(jobs[0]["TK"], jobs[0]["KT"], jobs[0]["V"])
    gmb = A.alloc([D], F32)
    P.dma("c0", lambda e: e.dma_start(out=gmb, in_=B.inputs["gmb"]), writes=["gmb"])
    Cn["gmb"] = gmb
    attn_consts(B, Cn)
    g8col = A.alloc([1], F32)
    P.dma("c0", lambda e: e.dma_start(out=g8col, in_=B.inputs["subg_col"]), writes=["g8col"])
    P.op("pool", lambda e: e.tensor_scalar(out=g8col, in0=g8col, scalar1=1.0 - LAMBDA_INIT, scalar2=None, op0=ALU.mult),
         reads=["g8col"], writes=["g8col"])
    ones_bf = A.alloc([128], BF16)
    ones_f = A.alloc([128], F32)
    P.op("pool", lambda e: e.memset(ones_bf, 1.0), writes=["ones_bf"])
    P.op("pool", lambda e: e.memset(ones_f, 1.0), writes=["ones_f"])
    if not Cn.get("wout_ready"):
        mark = A.off
        wst = [A.alloc([D], F32) for _ in range(4)]
        B.prep_weight(B.inputs["wout"], D, D, Wout, None, wst, "Wout")
        P.barrier()
        A.off = mark
    QTILE = 512
    qt = [A.alloc([4, QTILE], BF16) for _ in range(2)]
    NPT = 3
    pt = [A.alloc([2, QTILE], BF16) for _ in range(NPT)]
    dtmp = [A.alloc([2, 128], F32) for _ in range(2)]
    attT = A.alloc([4, QTILE], BF16)
    lruT = [A.alloc([4, QTILE], BF16) for _ in range(2)]
    x1r = [A.alloc([D], F32) for _ in range(2)]
    ost = A.alloc([D], F32)
    rec0 = A.alloc([QTILE], F32)
    rec1 = A.alloc([QTILE], F32)
    o_sb = A.alloc([QTILE], F32)
    junk = A.alloc([512], BF16)
    ssm = [A.alloc([1], F32) for _ in range(2)]
    ssm2 = [A.alloc([1], F32) for _ in range(2)]
    rsm = [A.alloc([1], F32) for _ in range(2)]
    ps = B.psum
    psall = B.psum_all
    stpair = [psall[:, 0:1024].rearrange("p (c x) -> p c x", c=2), psall[:, 1024:2048].rearrange("p (c x) -> p c x", c=2)]
    OTb = (ps[4], ps[5])
    Lb = (ps[6], ps[7])
    pb, pbo, db = Cn["pb"], Cn["pbo"], Cn["db"]
    st_c = [0]
    pt_c = [0]
    dt_c = [0]
    x_c = [0]
    qb_c = [0]

    def take_pair():
        pi = st_c[0] % 2
        st_c[0] += 1
        return pi, [("bank", 2 * pi), ("bank", 2 * pi + 1)]

    def run_job(TK, NQ, KT_scr, V_scr, QT_scr, LT_scr, x1_scr, x1_off, x2_dst, mask_other):
        NB = cdiv(TK, 128)
        KOFF = TK - NQ
        assert KOFF % 128 == 0
        nkof = lambda j: min(128, TK - 128 * j)
        if not kv_done.get(id(KT_scr)):
            issue_kv(TK, KT_scr, V_scr)
        vid = lambda j: ("V1", j // VCH)
        tiles = []
        q0 = 0
        while q0 < NQ:
            nq = min(QTILE, NQ - q0)
            tiles.append((q0, nq))
            q0 += nq
        LOOK = int(os.environ.get("K_LOOK2", "2"))
        dq = []

        def push2(fn):
            dq.append(fn)
            while len(dq) > LOOK:
                dq.pop(0)()

        def load_q(ti):
            q0, nq = tiles[ti]
            b = qb_c[0] % 2
            qb_c[0] += 1
            for h in range(4):
                P.dma(sname + "q%d" % b, lambda e, b=b, h=h, q0=q0, nq=nq: e.dma_start(
                    out=qt[b][:, h, 0:nq], in_=QT_scr[h, :, q0:q0 + nq]), writes=[("qt", b, h)])

            def ld(b=b, q0=q0, nq=nq):
                P.dma(sname + "l%d" % b, lambda e: e.dma_start(
                    out=lruT[b][:, :, 0:nq], in_=LT_scr[:, :, q0:q0 + nq].rearrange("g p t -> p g t")), writes=[("lruT", b)])
            push2(ld)
            return b

        def tile_stream(ti, q0, nq, b):
            nsb = cdiv(nq, 128)
            nqs_of = lambda s: min(128, nq - 128 * s)
            jb = (KOFF + q0) // 128
            nb_next = load_q(ti + 1) if ti + 1 < len(tiles) else None
            for h in range(4):
                persub = (h == 0)
                W = window[h]
                jlo = 0 if W is None else max(0, jb - W)
                first = [True]
                jlast = jb + nsb - 1
                for j in range(jlo, jb + nsb):
                    nk = nkof(j)
                    rel = j - jb
                    s_lo = max(0, rel)
                    c0 = s_lo * 128
                    tab = pbo if (mask_other and j * 128 < KOFF) else pb
                    pi, bids = take_pair()
                    stp = stpair[pi]
                    for c in range(2):
                        P.op("pe", lambda e, stp=stp, c=c, h=h, j=j, c0=c0, nq=nq, b=b, nk=nk: e.matmul(
                            stp[:nk, c, c0:nq], lhsT=KT[c * 64:(c + 1) * 64, h, j * 128:j * 128 + nk],
                            rhs=qt[b][c * 64:(c + 1) * 64, h, c0:nq], start=True, stop=True),
                             reads=[("KT", h), ("qt", b, h)], writes=[bids[c]])
                    ri = pt_c[0] % NPT
                    pt_c[0] += 1
                    ptb = pt[ri]
                    pid = ("pt", ri)
                    c1 = c0
                    if rel >= 0:
                        nqs = nqs_of(rel)
                        di = dt_c[0] % 2
                        dt_c[0] += 1
                        for c in range(2):
                            P.op("dve", lambda e, stp=stp, di=di, h=h, c0=c0, nk=nk, nqs=nqs, c=c: e.scalar_tensor_tensor(
                                out=dtmp[di][:nk, c, :nqs], in0=stp[:nk, c, c0:c0 + nqs], scalar=0.125, in1=db[:nk, h, 0:nqs],
                                op0=ALU.mult, op1=ALU.add), reads=[bids[c], "consts"], writes=[("dtmp", di, c)])
                        bconst = 0.0 if persub else SLOPES[h] * 128.0 * rel
                        P.op("act", lambda e, ptb=ptb, di=di, c0=c0, bconst=bconst, nk=nk, nqs=nqs: e.activation(
                            out=ptb[:nk, :, c0:c0 + nqs], in_=dtmp[di][:nk, :, :nqs], func=AF.Exp, bias=bconst),
                             reads=[("dtmp", di, 0), ("dtmp", di, 1)], writes=[pid])
                        c1 = c0 + 128
                    if c1 < nq:
                        if persub:
                            for s in range(c1 // 128, nsb):
                                dj = jb + s - j
                                ce = s * 128 + nqs_of(s)
                                P.op("act", lambda e, ptb=ptb, stp=stp, s=s, ce=ce, dj=dj, tab=tab, h=h, nk=nk: e.activation(
                                    out=ptb[:nk, :, s * 128:ce], in_=stp[:nk, :, s * 128:ce], func=AF.Exp,
                                    bias=tab[:nk, h, dj + 3:dj + 4], scale=0.125),
                                     reads=bids + ["pbo"], writes=[pid])
                        else:
                            dj = jb - j
                            P.op("act", lambda e, ptb=ptb, stp=stp, c1=c1, nq=nq, dj=dj, tab=tab, h=h, nk=nk: e.activation(
                                out=ptb[:nk, :, c1:nq], in_=stp[:nk, :, c1:nq], func=AF.Exp,
                                bias=tab[:nk, h, dj + 3:dj + 4], scale=0.125),
                                 reads=bids + ["pbo"], writes=[pid])

                    def pv(ptb=ptb, pid=pid, j=j, h=h, nk=nk, c0=c0, nq=nq, first=first, last=(j == jlast)):
                        st_flag = first[0]
                        first[0] = False
                        for c in range(2):
                            P.op("pe", lambda e, c=c: e.matmul(
                                OTb[c][:, c0:nq], lhsT=V1[:nk, j, h * 129:h * 129 + 128], rhs=ptb[:nk, c, c0:nq],
                                start=st_flag, stop=last), reads=[pid, vid(j)], writes=[("bank", 4 + c)])
                        for c in range(2):
                            P.op("pe", lambda e, c=c: e.matmul(
                                Lb[c][:, c0:nq], lhsT=ones_bf[:nk, :], rhs=ptb[:nk, c, c0:nq],
                                start=st_flag, stop=last), reads=[pid, "ones_bf"], writes=[("bank", 6 + c)])
                    push2(pv)
                push2(lambda h=h, nq=nq: finalize(h, nq))
            push2(lambda q0=q0, nq=nq, b=b, nsb=nsb, nqs_of=nqs_of: tail(q0, nq, b, nsb, nqs_of))
            return nb_next

        def finalize(h, nq):
            P.op("dve", lambda e: e.reciprocal(out=rec0[:, 0:nq], in_=Lb[0][:, 0:nq]), reads=[("bank", 6)], writes=["rec0"])
            P.op("dve", lambda e: e.reciprocal(out=rec1[:, 0:nq], in_=Lb[1][:, 0:nq]), reads=[("bank", 7)], writes=["rec1"])
            P.op("dve", lambda e: e.tensor_tensor(out=rec0[:, 0:nq], in0=OTb[0][:, 0:nq], in1=rec0[:, 0:nq], op=ALU.mult),
                 reads=[("bank", 4), "rec0"], writes=["rec0"])
            P.op("dve", lambda e: e.tensor_tensor(out=rec1[:, 0:nq], in0=OTb[1][:, 0:nq], in1=rec1[:, 0:nq], op=ALU.mult),
                 reads=[("bank", 5), "rec1"], writes=["rec1"])
            P.op("dve", lambda e: e.scalar_tensor_tensor(out=o_sb[:, 0:nq], in0=rec1[:, 0:nq], scalar=Cn["negl"][:, 0:1],
                                                         in1=rec0[:, 0:nq], op0=ALU.mult, op1=ALU.add),
                 reads=["rec0", "rec1", "negl"], writes=["o_sb"])
            P.op("pool", lambda e: e.tensor_tensor(out=rec0[:, 0:nq], in0=o_sb[:, 0:nq], in1=o_sb[:, 0:nq], op=ALU.mult),
                 reads=["o_sb"], writes=["rec0"])
            pi, bids = take_pair()
            ssb = ps[2 * pi]
            P.op("pe", lambda e: e.matmul(ssb[:, 0:nq], lhsT=ones_f[:, :], rhs=rec0[:, 0:nq], start=True, stop=True),
                 reads=["rec0", "ones_f"], writes=bids)
            P.op("act", lambda e: e.activation(out=rec1[:, 0:nq], in_=ssb[:, 0:nq], func=AF.Ln, scale=1.0 / 128, bias=EPS),
                 reads=[bids[0]], writes=["rec1"])
            P.op("act", lambda e: e.activation(out=rec1[:, 0:nq], in_=rec1[:, 0:nq], func=AF.Exp, scale=-0.5),
                 reads=["rec1"], writes=["rec1"])
            P.op("dve", lambda e: e.scalar_tensor_tensor(out=attT[:, h, 0:nq], in0=o_sb[:, 0:nq], scalar=g8col[:, 0:1],
                                                         in1=rec1[:, 0:nq], op0=ALU.mult, op1=ALU.mult),
                 reads=["o_sb", "rec1", "g8col"], writes=[("attT", h)])

        def tail(q0, nq, b, nsb, nqs_of):
            for s in range(nsb):
                n = nqs_of(s)
                pi, bids = take_pair()
                mo = (ps[2 * pi], ps[2 * pi + 1])
                for half in range(2):
                    for kk in range(8):
                        src = attT if kk < 4 else lruT[b]
                        P.op("pe", lambda e, half=half, kk=kk, src=src, s=s, n=n, mo=mo: e.matmul(
                            mo[half][:n, :], lhsT=src[:, kk % 4, s * 128:s * 128 + n],
                            rhs=Wout[:, kk, half * 512:(half + 1) * 512], start=(kk == 0), stop=(kk == 7)),
                             reads=[("attT", kk % 4), ("lruT", b), ("Wout", kk)], writes=[bids[half]])
                sl = s % 2
                P.op("act", lambda e, sl=sl, n=n, mo=mo: e.activation(out=junk[:n, :], in_=mo[0][:n, :], func=AF.Square, accum_out=ssm[sl][:n, :]),
                     reads=[bids[0]], writes=["junkA", ("ssm", sl)])
                P.op("act", lambda e, sl=sl, n=n, mo=mo: e.activation(out=junk[:n, :], in_=mo[1][:n, :], func=AF.Square, accum_out=ssm2[sl][:n, :]),
                     reads=[bids[1]], writes=["junkA", ("ssm2", sl)])
                P.op("pool", lambda e, sl=sl, n=n: e.tensor_tensor(out=ssm[sl][:n, :], in0=ssm[sl][:n, :], in1=ssm2[sl][:n, :], op=ALU.add),
                     reads=[("ssm", sl), ("ssm2", sl)], writes=[("ssm", sl)])
                P.op("pool", lambda e, sl=sl, n=n: e.tensor_scalar(out=ssm[sl][:n, :], in0=ssm[sl][:n, :], scalar1=1.0 / D, scalar2=EPS,
                                                              op0=ALU.mult, op1=ALU.add), reads=[("ssm", sl)], writes=[("ssm", sl)])
                P.op("pool", lambda e, sl=sl, n=n: e.tensor_tensor(out=rsm[sl][:n, :], in0=ssm[sl][:n, :], in1=B.c_mhalf[:n, :], op=ALU.pow),
                     reads=[("ssm", sl)], writes=[("rsm", sl)])
                xi = x_c[0] % 2
                x_c[0] += 1
                a = q0 + s * 128
                P.dma(sname + "x%d" % xi, lambda e, xi=xi, a=a, n=n: e.dma_start(
                    out=x1r[xi][:n, :], in_=x1_scr[x1_off + a:x1_off + a + n, :]), writes=[("x1r", xi)])
                for half in range(2):
                    P.op("dve", lambda e, half=half, sl=sl, n=n, mo=mo: e.scalar_tensor_tensor(
                        out=ost[:n, half * 512:(half + 1) * 512], in0=mo[half][:n, :], scalar=rsm[sl][:n, 0:1],
                        in1=Cn["gmb"][:n, half * 512:(half + 1) * 512], op0=ALU.mult, op1=ALU.mult),
                         reads=[bids[half], ("rsm", sl), "consts"], writes=[("ost", half)])
                    P.op("pool", lambda e, xi=xi, half=half, n=n: e.tensor_tensor(
                        out=ost[:n, half * 512:(half + 1) * 512], in0=ost[:n, half * 512:(half + 1) * 512],
                        in1=x1r[xi][:n, half * 512:(half + 1) * 512], op=ALU.add),
                         reads=[("ost", half), ("x1r", xi)], writes=[("ost", half)])
                P.dma(sname + "o0", lambda e, a=a, n=n: e.dma_start(out=x2_dst[a:a + n, :], in_=ost[:n, :]),
                      reads=[("ost", 0), ("ost", 1)])

        bcur = load_q(0)
        for ti, (q0, nq) in enumerate(tiles):
            bcur = tile_stream(ti, q0, nq, bcur)
        while dq:
            dq.pop(0)()

    for q in jobs:
        run_job(q["TK"], q["NQ"], q["KT"], q["V"], q["QT"], q["LT"], q["x1"], q["x1_off"], q["x2"], q["mask_other"])
    P.barrier()
    A.pop()


def cache_prep_ops(B, ck, cv, KT_s, V_s, PAST, sname):
    P, A = B.P, B.A
    NSTEP = PAST // 512
    kin = [A.alloc([4, 512], F32) for _ in range(2)]
    kbf = [A.alloc([4, 512], BF16) for _ in range(2)]
    kT = [A.alloc([4, 512], BF16) for _ in range(2)]
    vin = [A.alloc([4, 512], F32) for _ in range(2)]
    vb = [A.alloc([4, 516], BF16) for _ in range(2)]
    for i in range(2):
        P.op("pool", lambda e, i=i: e.memset(vb[i], 1.0), writes=[("cvb", i)])
    ps = B.psum
    for st in range(NSTEP):
        r = st % 2
        a = st * 512
        P.dma(sname + "k%d" % r, lambda e, r=r, a=a: e.dma_start(
            out=kin[r], in_=ck[a:a + 512, :].rearrange("(j p) c -> p j c", p=128)), writes=[("ckin", r)])
        P.dma(sname + "v%d" % r, lambda e, r=r, a=a: e.dma_start(
            out=vin[r], in_=cv[a:a + 512, :].rearrange("(j p) c -> p j c", p=128)), writes=[("cvin", r)])
        P.op("dve", lambda e, r=r: e.tensor_copy(out=kbf[r], in_=kin[r]), reads=[("ckin", r)], writes=[("ckbf", r)])
        for jj in range(4):
            bank = ps[4 + (st * 4 + jj) % 4]
            bid = ("bank", 4 + (st * 4 + jj) % 4)
            ps_tr = bank[:, :].bitcast(BF16)
            for h in range(4):
                P.op("pe", lambda e, ps_tr=ps_tr, h=h, r=r, jj=jj: e.transpose(
                    out=ps_tr[:, h * 128:(h + 1) * 128], in_=kbf[r][:, jj, h * 128:(h + 1) * 128], identity=B.ident),
                     reads=[("ckbf", r)], writes=[bid])
            P.op("act", lambda e, ps_tr=ps_tr, r=r, jj=jj: e.activation(
                out=kT[r][:, :, jj * 128:(jj + 1) * 128], in_=ps_tr[:, 0:512].rearrange("p (a b) -> p a b", a=4), func=AF.Copy),
                 reads=[bid], writes=[("ckT", r, jj)])
        P.dma(sname + "ko%d" % r, lambda e, r=r, a=a: e.dma_start(
            out=KT_s[:, :, a:a + 512].rearrange("h p t -> p h t"), in_=kT[r]),
              reads=[("ckT", r, x) for x in range(4)])
        for jj in range(4):
            P.op("pool" if jj % 2 else "dve", lambda e, r=r, jj=jj: e.tensor_copy(
                out=vb[r][:, jj, :].rearrange("p (h d) -> p h d", h=4)[:, :, 0:128],
                in_=vin[r][:, jj, :].rearrange("p (h d) -> p h d", h=4)),
                 reads=[("cvin", r), ("cvb", r)], writes=[("cvb", r, jj)])
        P.dma(sname + "vo%d" % r, lambda e, r=r, a=a: e.dma_start(
            out=V_s[a:a + 512, :].rearrange("(j p) c -> p j c", p=128), in_=vb[r]),
              reads=[("cvb", r, x) for x in range(4)] + [("cvb", r)])


SMALL_INPUTS = [
    ("gma_col", [KC]), ("g1a_col", [KC]), ("g2a_col", [KC]),
    ("convw", [4, 4]), ("convb", [4]), ("brg", [4]), ("big", [4]), ("lam", [4]),
    ("flag", [1]), ("maskv", [1]),
]


def build_main(T_OTH, T_OWN, with_sample=True, window=(None,) * 4, stages=("A1", "A2", "C", "D")):
    if os.environ.get("K_WIN", "1") == "1":
        window = (4, 16, None, None)
    B = Builder(T_OTH, T_OWN)
    P = B.P
    T = T_OTH + T_OWN
    x = B.din("x", [T, D])
    for nm, shp in (("f1g", [D, DFF]), ("f1u", [D, DFF]), ("f1d", [DFF, D]),
                    ("f2g", [D, DFF]), ("f2u", [D, DFF]), ("f2d", [DFF, D]),
                    ("win", [D, 2560]), ("wout", [D, D])):
        B.din(nm, shp)
    for nm, shp in (("g1b_bc", [128, D]), ("g2b_bc", [128, D]), ("gmb", [128, D]),
                    ("wr_bd", [128, 4, 128]), ("wi_bd", [128, 4, 128]),
                    ("pb", [128, 4, 71]), ("db", [128, 4, 128]), ("subg", [128, 128]), ("subg_col", [128, 1]),
                    ("lq1", [128, 64]), ("lk1", [128, 64]), ("lq2", [128, 64]), ("lk2", [128, 64])):
        B.din(nm, shp)
    y = B.dout("y", [T_OWN, D])
    ko = B.dout("ko", [T_OWN, 512])
    vo = B.dout("vo", [T_OWN, 512])
    hl = B.dout("hl", [512])
    cb = B.dout("cb", [3, 512])
    x1_scr = B.dscr("x1_scr", [T, D])
    x2_scr = B.dscr("x2_scr", [T_OWN, D])
    KT_scr = B.dscr("KT_scr", [4, 128, T], BF16)
    V_scr = B.dscr("V_scr", [T, 516], BF16)
    QT_scr = B.dscr("QT_scr", [4, 128, T_OWN], BF16)
    LT_scr = B.dscr("LT_scr", [4, 128, T_OWN], BF16)
    S = {}
    NSAMP, PAST = 32, 4096
    if with_sample:
        S["xs"] = B.din("xs", [NSAMP, D])
        S["ck"] = B.din("ck", [PAST, 512])
        S["cv"] = B.din("cv", [PAST, 512])
        S["sh"] = B.din("sh", [512])
        S["sc"] = B.din("sc", [3, 512])
        S["ys"] = B.dout("ys", [NSAMP, D])
        S["kso"] = B.dout("kso", [NSAMP, 512])
        S["vso"] = B.dout("vso", [NSAMP, 512])
        S["hls"] = B.dout("hls", [512])
        S["cbs"] = B.dout("cbs", [3, 512])
        S["xs1"] = B.dscr("xs1_scr", [NSAMP, D])
        S["xs2"] = B.dscr("xs2_scr", [NSAMP, D])
        S["KT"] = B.dscr("KTs_scr", [4, 128, PAST + NSAMP], BF16)
        S["V"] = B.dscr("Vs_scr", [PAST + NSAMP, 516], BF16)
        S["QT"] = B.dscr("QTs_scr", [4, 128, NSAMP], BF16)
        S["LT"] = B.dscr("LTs_scr", [4, 128, NSAMP], BF16)
    B.consts()
    Cn = {}
    for nm, shp in SMALL_INPUTS:
        Cn[nm] = B.load_const(nm, shp)
    P.barrier()
    inp = B.inputs
    if "A1" in stages:
        segs = [(x, x1_scr, T)] + ([(S["xs"], S["xs1"], NSAMP)] if with_sample else [])
        B.ffn_stage(segs, inp["f1g"], inp["f1u"], inp["f1d"], Cn["g1a_col"], inp["g1b_bc"], "a")
    if with_sample and "A2" in stages:
        Cn["cache_prep"] = (S["ck"], S["cv"], S["KT"], S["V"], PAST, "p")
    if "A2" in stages:
        seqs = [dict(x1=x1_scr, T=T, T_OTH=T_OTH, KT=KT_scr, V=V_scr, QT=QT_scr, LT=LT_scr, ko=ko, vo=vo, hl=hl, cb=cb)]
        if with_sample:
            seqs.append(dict(x1=S["xs1"], T=NSAMP, T_OTH=0, KT=S["KT"], V=S["V"], QT=S["QT"], LT=S["LT"],
                             ko=S["kso"], vo=S["vso"], hl=S["hls"], cb=S["cbs"], h0=S["sh"], conv0=S["sc"], koff=PAST))
        mixer_in_stage(B, seqs, Cn, "m")
    if "C" in stages:
        jobs = [dict(TK=T, NQ=T_OWN, KT=KT_scr, V=V_scr, QT=QT_scr, LT=LT_scr, x1=x1_scr, x1_off=T_OTH, x2=x2_scr,
                     mask_other=True)]
        if with_sample:
            jobs.append(dict(TK=PAST + NSAMP, NQ=NSAMP, KT=S["KT"], V=S["V"], QT=S["QT"], LT=S["LT"], x1=S["xs1"],
                             x1_off=0, x2=S["xs2"], mask_other=False))
        (attn_stage2 if os.environ.get("K_ATT", "2") == "2" else attn_stage)(B, jobs, Cn, "c", window=window)
    if "D" in stages:
        segs = [(x2_scr, y, T_OWN)] + ([(S["xs2"], S["ys"], NSAMP)] if with_sample else [])
        B.ffn_stage(segs, inp["f2g"], inp["f2u"], inp["f2d"], Cn["g2a_col"], inp["g2b_bc"], "d")
    B.P.emit()
    return B


def _col(v, n):
    return np.ascontiguousarray(np.asarray(v, np.float32).reshape(n, 128).T)


def _bc(v):
    v = np.asarray(v, np.float32).reshape(1, -1)
    return np.ascontiguousarray(np.broadcast_to(v, (128, v.shape[1])))


def _block_diag(w):
    out = np.zeros((128, 4, 128), np.float32)
    for g in range(4):
        for hb in range(2):
            out[hb * 64:(hb + 1) * 64, g, hb * 64:(hb + 1) * 64] = w[2 * g + hb]
    return out


def _tables():
    k = np.arange(128, dtype=np.float64)
    pb = np.zeros((128, 4, 71), np.float32)
    db = np.zeros((128, 4, 128), np.float32)
    kk = k[:, None]
    qq = k[None, :]
    for h in range(4):
        sl = SLOPES[h]
        for dj in range(-3, 68):
            pb[:, h, dj + 3] = sl * (k - 128.0 * dj)
        v = np.where(kk <= qq, sl * kk, sl * (2 * qq - kk))
        v = np.where((kk // 64) > (qq // 64), NEG, v)
        db[:, h, :] = v
    return pb, db


def shared_inputs(inputs):
    import ml_dtypes
    g = lambda n: np.asarray(inputs[n], np.float32)
    pb, db = _tables()
    d = {
        "f1g": g("ffn1_w_gate")[0], "f1u": g("ffn1_w_up")[0], "f1d": g("ffn1_w_down")[0],
        "f2g": g("ffn2_w_gate")[0], "f2u": g("ffn2_w_up")[0], "f2d": g("ffn2_w_down")[0],
        "win": g("w_in")[0], "wout": g("w_out")[0],
        "g1b_bc": _bc(g("g_ffn1_post")[0]), "g2b_bc": _bc(g("g_ffn2_post")[0]), "gmb": _bc(g("g_mix_post")[0]),
        "wr_bd": _block_diag(g("w_rgate")[0]), "wi_bd": _block_diag(g("w_igate")[0]),
        "pb": pb, "db": db, "subg": _bc(g("subln_g")[0]), "subg_col": _col(g("subln_g")[0], 1),
        "lq1": _bc(g("lambda_q1")[0]), "lk1": _bc(g("lambda_k1")[0]),
        "lq2": _bc(g("lambda_q2")[0]), "lk2": _bc(g("lambda_k2")[0]),
        "gma_col": _col(g("g_mix_pre")[0], 8), "g1a_col": _col(g("g_ffn1_pre")[0], 8), "g2a_col": _col(g("g_ffn2_pre")[0], 8),
        "convw": np.ascontiguousarray(g("conv_w")[0].reshape(4, 4, 128).transpose(2, 1, 0)),
        "convb": _col(g("conv_b")[0], 4), "brg": _col(g("b_rgate")[0], 4), "big": _col(g("b_igate")[0], 4),
        "lam": _col(g("lru_lambda")[0], 4),
        "ident": np.eye(128).astype(ml_dtypes.bfloat16),
    }
    return d


_CACHE = {}


def kernel(**inputs):
    TH = 4096
    WITH_SAMPLE = bool(int(os.environ.get("K_SAMPLE", "1")))
    key = ("main", TH, WITH_SAMPLE)
    if key not in _CACHE:
        _CACHE[key] = build_main(TH, TH, with_sample=WITH_SAMPLE)
    B = _CACHE[key]
    sh = shared_inputs(inputs)
    xp = np.asarray(inputs["x_prompt"], np.float32)
    maps = []
    for c in range(8):
        b, r = c // 2, c % 2
        own = xp[b, r * TH:(r + 1) * TH]
        oth = xp[b, (1 - r) * TH:(2 - r) * TH]
        m = dict(sh)
        m["x"] = np.ascontiguousarray(np.concatenate([oth, own], 0))
        m["flag"] = np.full((128, 1), float(r), np.float32)
        m["maskv"] = np.full((128, 1), 0.0 if r == 1 else NEG, np.float32)
        if WITH_SAMPLE:
            m["xs"] = np.ascontiguousarray(np.asarray(inputs["x_sample"], np.float32)[c])
            m["ck"] = np.ascontiguousarray(np.asarray(inputs["cache_k"], np.float32)[0, c].reshape(4096, 512))
            m["cv"] = np.ascontiguousarray(np.asarray(inputs["cache_v"], np.float32)[0, c].reshape(4096, 512))
            m["sh"] = np.ascontiguousarray(np.asarray(inputs["state_lru_h"], np.float32)[0, c])
            m["sc"] = np.ascontiguousarray(np.asarray(inputs["state_conv"], np.float32)[0, c])
        m = {k: v for k, v in m.items() if k in B.inputs}
        maps.append(m)
    res = run_bass_kernel_spmd(B.nc, maps, core_ids=list(range(8))).results
    y = np.zeros((4, 8192, 1024), np.float32)
    kp = np.zeros((1, 4, 8192, 4, 2, 64), np.float32)
    vp = np.zeros((1, 4, 8192, 4, 128), np.float32)
    hp = np.zeros((1, 4, 512), np.float32)
    cp = np.zeros((1, 4, 3, 512), np.float32)
    ys = np.zeros((8, 32, 1024), np.float32)
    ks = np.zeros((1, 8, 32, 4, 2, 64), np.float32)
    vs = np.zeros((1, 8, 32, 4, 128), np.float32)
    hs = np.zeros((1, 8, 512), np.float32)
    cs = np.zeros((1, 8, 3, 512), np.float32)
    for c in range(8):
        b, r = c // 2, c % 2
        o = res[c]
        sl = slice(r * TH, (r + 1) * TH)
        y[b, sl] = o["y"]
        kp[0, b, sl] = o["ko"].reshape(TH, 4, 2, 64)
        vp[0, b, sl] = o["vo"].reshape(TH, 4, 128)
        if r == 1:
            hp[0, b] = o["hl"]
            cp[0, b] = o["cb"]
        if WITH_SAMPLE:
            ys[c] = o["ys"]
            ks[0, c] = o["kso"].reshape(32, 4, 2, 64)
            vs[0, c] = o["vso"].reshape(32, 4, 128)
            hs[0, c] = o["hls"]
            cs[0, c] = o["cbs"]
    return (y, ys, kp, vp, hp, cp, ks, vs, hs, cs)
```

```python
import numpy as np
import concourse.bass as bass
import concourse.mybir as mybir
from concourse.bass_utils import run_bass_kernel_spmd
from contextlib import ExitStack

F32 = mybir.dt.float32
BF16 = mybir.dt.bfloat16
AF = mybir.ActivationFunctionType
ALU = mybir.AluOpType
AX = mybir.AxisListType

D = 1024
DFF = 2816
NFF = DFF // 128
KC = D // 128
EPS = 1e-6
NEG = -1e30


class Op:
    __slots__ = ("eng", "fn", "deps", "sem", "inc", "val", "is_dma", "needs_inc")


class Prog:
    ENGS = ("pe", "act", "dve", "pool", "sp")

    def __init__(self, nc, stack):
        self.nc = nc
        self.stack = stack
        self.ops = {e: [] for e in self.ENGS}
        self.last_w = {}
        self.readers = {}
        self.barrier_ops = []
        self.esem = {e: stack.enter_context(nc.semaphore("s_" + e)) for e in ("pe", "act", "dve", "pool")}
        self.dma_sems = {}
        self.dma_last = {}
        self.n_sem = 4

    def dsem(self, name):
        if name not in self.dma_sems:
            self.dma_sems[name] = self.stack.enter_context(self.nc.semaphore("d_" + name))
            self.n_sem += 1
        return name

    PSUM_IDS = ("gu", "dn", "fm", "tm", "rg", "bank", "cv")

    def _is_psum(self, t):
        return t == "ps_tr" or (isinstance(t, tuple) and t[0] in self.PSUM_IDS)

    def _add(self, o, reads, writes):
        xr = [t for t in reads if self._is_psum(t)]
        if xr:
            reads = [t for t in reads if not self._is_psum(t)]
            writes = list(writes) + xr
        deps = set(self.barrier_ops)
        for t in reads:
            w = self.last_w.get(t)
            if w is not None:
                deps.add(w)
        for t in writes:
            w = self.last_w.get(t)
            if w is not None:
                deps.add(w)
            for r in self.readers.get(t, ()):
                deps.add(r)
        if o.eng == "pe" and not o.is_dma:
            deps = {d for d in deps if not (d.eng == "pe" and not d.is_dma)}
        deps = {(self.dma_last[d.sem] if d.is_dma else d) for d in deps}
        o.deps = deps
        for d in deps:
            d.needs_inc = True
        for t in reads:
            self.readers.setdefault(t, []).append(o)
        for t in writes:
            self.last_w[t] = o
            self.readers[t] = []
        self.ops[o.eng].append(o)

    def op(self, eng, fn, reads=(), writes=()):
        o = Op()
        o.eng = eng
        o.fn = fn
        o.is_dma = False
        o.needs_inc = False
        o.sem = None
        o.val = 0
        self._add(o, reads, writes)
        return o

    def dma(self, sem_name, fn, reads=(), writes=(), q="sp"):
        o = Op()
        o.eng = q
        o.fn = fn
        o.is_dma = True
        o.needs_inc = True
        o.sem = self.dsem(sem_name)
        o.val = 0
        self._add(o, reads, writes)
        self.dma_last[sem_name] = o
        return o

    def barrier(self):
        b = []
        for e in self.ENGS:
            for o in reversed(self.ops[e]):
                if not o.is_dma:
                    b.append(o)
                    break
        for o in self.dma_last.values():
            b.append(o)
        for o in b:
            o.needs_inc = True
        self.barrier_ops = b
        self.last_w = {}
        self.readers = {}

    def emit(self):
        nc = self.nc
        dcount = {}
        for e in self.ENGS:
            cnt = 0
            for o in self.ops[e]:
                if o.is_dma:
                    dcount[o.sem] = dcount.get(o.sem, 0) + 16
                    o.val = dcount[o.sem]
                elif o.needs_inc:
                    cnt += 1
                    o.val = cnt
        ops = self.ops
        esem = self.esem
        dsems = self.dma_sems

        def run(ename, eng):
            waited = {}
            for o in ops[ename]:
                need = {}
                for d in o.deps:
                    key = d.sem if d.is_dma else d.eng
                    if d.val > need.get(key, 0):
                        need[key] = d.val
                for key, v in need.items():
                    if waited.get(key, 0) < v:
                        sem = esem[key] if key in esem else dsems[key]
                        eng.wait_ge(sem, v)
                        waited[key] = v
                ins = o.fn(eng)
                if o.is_dma:
                    ins.then_inc(dsems[o.sem], 16)
                elif o.needs_inc:
                    ins.then_inc(esem[ename], 1)
            fin = {}
            for o in ops[ename]:
                if o.is_dma:
                    fin[o.sem] = max(fin.get(o.sem, 0), o.val)
            for s, v in fin.items():
                if waited.get(s, 0) < v:
                    eng.wait_ge(dsems[s], v)

        with nc.Block() as block:
            @block.tensor
            def _(e):
                run("pe", e)

            @block.scalar
            def _(e):
                run("act", e)

            @block.vector
            def _(e):
                run("dve", e)

            @block.gpsimd
            def _(e):
                run("pool", e)

            @block.sync
            def _(e):
                run("sp", e)


class Arena:
    def __init__(self, nc, stack, nbytes):
        self.t = stack.enter_context(nc.sbuf_tensor("arena", [128, nbytes // 4], F32))
        self.cap = nbytes
        self.off = 0
        self.marks = []

    def push(self):
        self.marks.append(self.off)

    def pop(self):
        self.off = self.marks.pop()

    def alloc(self, shape, dt):
        n = 1
        for s in shape:
            n *= s
        esz = 4 if dt == F32 else 2
        nb = (n * esz + 31) // 32 * 32
        assert self.off + nb <= self.cap, ("arena overflow", self.off, nb, self.cap)
        ap = self.t[:, self.off // 4:(self.off + nb) // 4]
        self.off += nb
        self.peak = max(getattr(self, 'peak', 0), self.off)
        if dt != F32:
            ap = ap.bitcast(dt)
        ap = ap[:, 0:n]
        if len(shape) == 2:
            ap = ap.rearrange("p (a b) -> p a b", a=shape[0])
        elif len(shape) == 3:
            ap = ap.rearrange("p (a b c) -> p a b c", a=shape[0], b=shape[1])
        return ap


import os
DBG = set(os.environ.get("KDBG", "").split(","))


def cdiv(a, b):
    return (a + b - 1) // b


class Builder:
    def __init__(self, T_OTH, T_OWN, n_samp=32, past=4096, stages="all"):
        self.T_OTH, self.T_OWN, self.NS, self.PAST = T_OTH, T_OWN, n_samp, past
        self.T = T_OTH + T_OWN
        self.stages = stages
        self.nc = bass.Bass("TRN2", target_bir_lowering=False)
        self.stack = ExitStack()
        self.P = Prog(self.nc, self.stack)
        self.A = Arena(self.nc, self.stack, 211456)
        self.inputs = {}
        self.outputs = {}
        nc = self.nc
        self.psum_all = self.stack.enter_context(nc.psum_tensor("psall", [128, 4096], F32))
        self.psum = [self.psum_all[:, i * 512:(i + 1) * 512] for i in range(8)]

    def din(self, name, shape, dt=F32):
        t = self.nc.dram_tensor(name, list(shape), dt, kind="ExternalInput").ap()
        self.inputs[name] = t
        return t

    def dout(self, name, shape, dt=F32):
        t = self.nc.dram_tensor(name, list(shape), dt, kind="ExternalOutput").ap()
        self.outputs[name] = t
        return t

    def dscr(self, name, shape, dt=F32):
        return self.nc.dram_tensor(name, list(shape), dt, kind="Internal").ap()

    def prep_weight(self, w_dram, K, N, dst, gcol, stage_bufs, tag, eng_cycle=("dve", "pool")):
        P = self.P
        nk = K // 128
        for kc in range(nk):
            sb = stage_bufs[kc % len(stage_bufs)]
            sid = ("wst", kc % len(stage_bufs))
            src = w_dram[kc * 128:(kc + 1) * 128, :]
            P.dma("wst%d" % (kc % len(stage_bufs)),
                  lambda e, sb=sb, src=src, N=N: e.dma_start(out=sb[:, 0:N], in_=src),
                  writes=[sid])
            ec = tuple(os.environ.get("K_PREP", "dve,act").split(","))
            en = ec[kc % len(ec)]
            if en == "act":
                if gcol is None:
                    P.op("act", lambda e, sb=sb, kc=kc, N=N, dst=dst: e.activation(out=dst[:, kc, :], in_=sb[:, 0:N], func=AF.Copy),
                         reads=[sid], writes=[(tag, kc)])
                else:
                    P.op("act", lambda e, sb=sb, kc=kc, N=N, dst=dst, gcol=gcol: e.activation(
                        out=dst[:, kc, :], in_=sb[:, 0:N], func=AF.Copy, scale=gcol[:, kc:kc + 1]),
                         reads=[sid, "consts"], writes=[(tag, kc)])
                continue
            if gcol is None:
                P.op(en, lambda e, sb=sb, kc=kc, N=N, dst=dst: e.tensor_copy(out=dst[:, kc, :], in_=sb[:, 0:N]),
                     reads=[sid], writes=[(tag, kc)])
            else:
                P.op(en, lambda e, sb=sb, kc=kc, N=N, dst=dst, gcol=gcol: e.tensor_scalar(
                    out=dst[:, kc, :], in0=sb[:, 0:N], scalar1=gcol[:, kc:kc + 1], scalar2=None, op0=ALU.mult),
                     reads=[sid, "consts"], writes=[(tag, kc)])

    def rstd_of(self, src_ap, np_, junk, ss, rstd, src_ids, tagid):
        P = self.P
        P.op("act", lambda e: e.activation(out=junk[:np_, :], in_=src_ap, func=AF.Square, accum_out=ss[:np_, :]),
             reads=src_ids, writes=[("junk", tagid), ("ss", tagid)])
        P.op("pool", lambda e: e.tensor_scalar(out=ss[:np_, :], in0=ss[:np_, :], scalar1=1.0 / D, scalar2=EPS,
                                               op0=ALU.mult, op1=ALU.add),
             reads=[("ss", tagid)], writes=[("ss", tagid)])
        P.op("pool", lambda e: e.tensor_tensor(out=rstd[:np_, :], in0=ss[:np_, :], in1=self.c_mhalf[:np_, :], op=ALU.pow),
             reads=[("ss", tagid), "consts"], writes=[("rstd", tagid)])

    def ffn_stage(self, segs, wg_d, wu_d, wd_d, gpre_col, gpost_bc, sname):
        P, A, nc = self.P, self.A, self.nc
        A.push()
        Wg = A.alloc([KC, DFF], BF16)
        Wu = A.alloc([KC, DFF], BF16)
        Wd = A.alloc([NFF, D], BF16)
        gph = A.alloc([D], F32)
        mark_act = A.off
        wst = [A.alloc([DFF], F32) for _ in range(5)]
        P.dma("c0", lambda e: e.dma_start(out=gph[:, :], in_=gpost_bc), writes=["gph"])
        P.op("pool", lambda e: e.tensor_scalar(out=gph[:, :], in0=gph[:, :], scalar1=0.5, scalar2=None,
                                               op0=ALU.mult), reads=["gph"], writes=["gph"])
        self.prep_weight(wg_d, D, DFF, Wg, gpre_col, wst, "Wg")
        self.prep_weight(wu_d, D, DFF, Wu, gpre_col, wst, "Wu")
        self.prep_weight(wd_d, DFF, D, Wd, None, wst, "Wd")
        P.barrier()
        A.off = mark_act
        TT = 256
        NXR = 5
        xr = [A.alloc([D], F32) for _ in range(NXR)]
        xs = [A.alloc([D], BF16) for _ in range(2)]
        xnT = [A.alloc([KC, TT], BF16) for _ in range(2)]
        actT = A.alloc([NFF, TT], BF16)
        stmp = [A.alloc([TT], F32) for _ in range(3)]
        ost = [A.alloc([D], F32) for _ in range(2)]
        junk = A.alloc([D], BF16)
        ssb = [A.alloc([1], F32) for _ in range(4)]
        rsb = [A.alloc([1], F32) for _ in range(4)]
        ssp = [A.alloc([1], F32) for _ in range(4)]
        ssq = [A.alloc([1], F32) for _ in range(4)]
        ps = self.psum
        ps_tr = ps[0][:, :].bitcast(BF16)
        gu = [ps[1], ps[2], ps[3]]
        dn = [(ps[4], ps[5]), (ps[6], ps[7])]

        tiles = []
        for (src, dst, n) in segs:
            t0 = 0
            while t0 < n:
                nt = min(TT, n - t0)
                tiles.append((src, dst, t0, nt))
                t0 += nt
        sub_ctr = [0]

        def load_tile(ti):
            src, dst, t0, nt = tiles[ti]
            subs = []
            for s0 in range(0, nt, 128):
                ns = min(128, nt - s0)
                k = sub_ctr[0] % NXR
                sub_ctr[0] += 1
                P.dma(sname + "x%d" % k, lambda e, k=k, src=src, a=t0 + s0, ns=ns: e.dma_start(
                    out=xr[k][:ns, :], in_=src[a:a + ns, :]), writes=[("xr", k)])
                subs.append((k, s0, ns))
            return subs

        loaded = {0: load_tile(0)}
        gu_ctr = 0
        st_ctr = 0
        o_ctr = 0
        subs_of = {}
        gu_state = {'gu': 0, 'st': 0, 'o': 0}

        def prenorm(ti):
            src, dst, t0, nt = tiles[ti]
            subs = loaded.pop(ti)
            subs_of[ti] = subs
            xb = xnT[ti % 2]
            for si, (k, s0, ns) in enumerate(subs):
                sl = (ti * 2 + si) % 4
                self.rstd_of(xr[k][:ns, :], ns, junk, ssb[sl], rsb[sl], [("xr", k)], sl)
                xsb = xs[(ti * 2 + si) % 2]
                xsid = ("xs", (ti * 2 + si) % 2)
                P.op("dve", lambda e, xsb=xsb, k=k, ns=ns, sl=sl: e.tensor_scalar(
                    out=xsb[:ns, :], in0=xr[k][:ns, :], scalar1=rsb[sl][:ns, :], scalar2=None, op0=ALU.mult),
                     reads=[("xr", k), ("rstd", sl)], writes=[xsid])
                for kc in range(KC):
                    P.op("pe", lambda e, xsb=xsb, kc=kc, ns=ns: e.transpose(
                        out=ps_tr[:, kc * 128:kc * 128 + ns], in_=xsb[:ns, kc * 128:(kc + 1) * 128],
                        identity=self.ident[:ns, :ns]),
                         reads=[xsid, "consts"], writes=["ps_tr"])
                P.op("act", lambda e, xb=xb, s0=s0, ns=ns: e.activation(
                    out=xb[:, :, s0:s0 + ns], in_=ps_tr.rearrange("p (a b) -> p a b", a=KC)[:, :, 0:ns],
                    func=AF.Copy),
                     reads=["ps_tr"], writes=[("xnT", ti % 2, si)])
            xn_ids = [("xnT", ti % 2, si) for si in range(len(subs))]

        def gateup(ti):
            nonlocal gu_ctr, st_ctr
            src, dst, t0, nt = tiles[ti]
            subs = subs_of[ti]
            xb = xnT[ti % 2]
            xn_ids = [("xnT", ti % 2, si) for si in range(len(subs))]
            for f in range(NFF):
                g = gu[gu_ctr % 3]
                gid = ("gu", gu_ctr % 3)
                gu_ctr += 1
                for kc in range(KC):
                    P.op("pe", lambda e, g=g, kc=kc, f=f, xb=xb, nt=nt: e.matmul(
                        g[:, 0:nt], lhsT=Wg[:, kc, f * 128:(f + 1) * 128], rhs=xb[:, kc, 0:nt],
                        start=(kc == 0), stop=(kc == KC - 1)),
                         reads=xn_ids + [("Wg", kc)], writes=[gid])
                for kc in range(KC):
                    P.op("pe", lambda e, g=g, kc=kc, f=f, xb=xb, nt=nt: e.matmul(
                        g[:, 256:256 + nt], lhsT=Wu[:, kc, f * 128:(f + 1) * 128], rhs=xb[:, kc, 0:nt],
                        start=(kc == 0), stop=(kc == KC - 1)),
                         reads=xn_ids + [("Wu", kc)], writes=[gid])
                stb = stmp[st_ctr % 3]
                sid = ("stmp", st_ctr % 3)
                st_ctr += 1
                P.op("act", lambda e, g=g, stb=stb, nt=nt: e.activation(out=stb[:, 0:nt], in_=g[:, 0:nt], func=AF.Silu),
                     reads=[gid], writes=[sid])
                P.op("dve", lambda e, g=g, stb=stb, nt=nt, f=f: e.tensor_tensor(
                    out=actT[:, f, 0:nt], in0=stb[:, 0:nt], in1=g[:, 256:256 + nt], op=ALU.mult),
                     reads=[gid, sid], writes=[("actT", f)])

        def down_post(ti):
            nonlocal o_ctr
            src, dst, t0, nt = tiles[ti]
            subs = subs_of.pop(ti)
            for si, (k, s0, ns) in enumerate(subs):
                d0, d1 = dn[si % 2]
                did = ("dn", si % 2)
                for half, dps in enumerate((d0, d1)):
                    for f in range(NFF):
                        P.op("pe", lambda e, dps=dps, f=f, s0=s0, ns=ns, half=half: e.matmul(
                            dps[:ns, :], lhsT=actT[:, f, s0:s0 + ns], rhs=Wd[:, f, half * 512:(half + 1) * 512],
                            start=(f == 0), stop=(f == NFF - 1)),
                             reads=[("actT", f), ("Wd", f)], writes=[did])
                sl = (ti * 2 + si) % 4
                ss2 = ssp[sl]
                P.op("act", lambda e, d0=d0, ns=ns, ss2=ss2: e.activation(
                    out=junk[:ns, 0:512], in_=d0[:ns, :], func=AF.Square, accum_out=ss2[:ns, :]),
                     reads=[did], writes=[("junk", 9), ("ssA", sl)])
                ss3 = ssq[sl]
                P.op("act", lambda e, d1=d1, ns=ns, ss3=ss3: e.activation(
                    out=junk[:ns, 512:1024], in_=d1[:ns, :], func=AF.Square, accum_out=ss3[:ns, :]),
                     reads=[did], writes=[("junk", 10), ("ssB", sl)])
                P.op("pool", lambda e, ss2=ss2, ss3=ss3, ns=ns: e.tensor_tensor(
                    out=ss2[:ns, :], in0=ss2[:ns, :], in1=ss3[:ns, :], op=ALU.add),
                     reads=[("ssA", sl), ("ssB", sl)], writes=[("ssA", sl)])
                P.op("pool", lambda e, ss2=ss2, ns=ns: e.tensor_scalar(
                    out=ss2[:ns, :], in0=ss2[:ns, :], scalar1=1.0 / D, scalar2=EPS, op0=ALU.mult, op1=ALU.add),
                     reads=[("ssA", sl)], writes=[("ssA", sl)])
                P.op("pool", lambda e, ss2=ss2, ss3=ss3, ns=ns: e.tensor_tensor(
                    out=ss3[:ns, :], in0=ss2[:ns, :], in1=self.c_mhalf[:ns, :], op=ALU.pow),
                     reads=[("ssA", sl), "consts"], writes=[("ssB", sl)])
                ob = ost[o_ctr % 2]
                oid = ("ost", o_ctr % 2)
                osem = sname + "o%d" % (o_ctr % int(os.environ.get("NOSEM", "2")))
                o_ctr += 1
                for half, dps in enumerate((d0, d1)):
                    P.op("dve", lambda e, dps=dps, ob=ob, ns=ns, half=half, ss3=ss3: e.scalar_tensor_tensor(
                        out=ob[:ns, half * 512:(half + 1) * 512], in0=dps[:ns, :], scalar=ss3[:ns, :],
                        in1=gph[:ns, half * 512:(half + 1) * 512], op0=ALU.mult, op1=ALU.mult),
                         reads=[did, ("ssB", sl), "consts"], writes=[(oid, half)])
                    P.op("pool", lambda e, ob=ob, ns=ns, half=half, k=k: e.tensor_tensor(
                        out=ob[:ns, half * 512:(half + 1) * 512], in0=ob[:ns, half * 512:(half + 1) * 512],
                        in1=xr[k][:ns, half * 512:(half + 1) * 512], op=ALU.add),
                         reads=[(oid, half), ("xr", k)], writes=[(oid, half)])
                P.dma(osem, lambda e, ob=ob, dst=dst, a=t0 + s0, ns=ns: e.dma_start(
                    out=dst[a:a + ns, :], in_=ob[:ns, :]), reads=[(oid, 0), (oid, 1)])

        if len(tiles) > 1:
            loaded[1] = load_tile(1)
        prenorm(0)
        for ti in range(len(tiles)):
            gateup(ti)
            if ti + 1 < len(tiles):
                prenorm(ti + 1)
            down_post(ti)
            if ti + 2 < len(tiles):
                loaded[ti + 2] = load_tile(ti + 2)
        P.barrier()
        A.pop()

    def consts(self):
        P, A = self.P, self.A
        ident_d = self.din("ident", [128, 128], BF16)
        self.ident = A.alloc([128], BF16)
        self.c_mhalf = A.alloc([1], F32)
        P.dma("c0", lambda e: e.dma_start(out=self.ident[:, :], in_=ident_d[:, :]), writes=["consts"])
        P.op("pool", lambda e: e.memset(self.c_mhalf[:, :], -0.5), writes=["consts_b"])

    def load_const(self, name, shape, dt=F32):
        d = self.din(name, [128] + list(shape), dt)
        t = self.A.alloc(list(shape), dt)
        self.P.dma("c0", lambda e: e.dma_start(out=t, in_=d), writes=["consts_c"])
        return t


def build_ffn_test(ntok):
    B = Builder(0, ntok)
    P = B.P
    x = B.din("x", [ntok, D])
    wg = B.din("wg", [D, DFF])
    wu = B.din("wu", [D, DFF])
    wd = B.din("wd", [DFF, D])
    y = B.dout("y", [ntok, D])
    B.consts()
    gpre = B.load_const("gpre", [KC])
    gpost = B.load_const("gpost", [D])
    P.barrier()
    B.ffn_stage([(x, y, ntok)], wg, wu, wd, gpre, gpost, "f1")
    B.P.emit()
    return B


def mixer_in_stage(B, seqs, Cn, sname):
    P, A, nc = B.P, B.A, B.nc
    A.push()
    TT = 512
    Wout_p = A.alloc([KC, D], BF16)
    Cn["wout_off"] = A.off
    Win = A.alloc([KC, 2560], BF16)
    Wrb = A.alloc([4, 128], BF16)
    Wib = A.alloc([4, 128], BF16)
    mark = A.off
    wst = [A.alloc([2560], F32) for _ in range(4)]
    wrf = A.alloc([4, 128], F32)
    wif = A.alloc([4, 128], F32)
    P.dma("c0", lambda e: e.dma_start(out=wrf, in_=B.inputs["wr_bd"]), writes=["wrf"])
    P.dma("c0", lambda e: e.dma_start(out=wif, in_=B.inputs["wi_bd"]), writes=["wif"])
    if Cn.get("cache_prep") is not None:
        cache_prep_ops(B, *Cn["cache_prep"])
    B.prep_weight(B.inputs["win"], D, 2560, Win, Cn["gma_col"], wst, "Win")
    B.prep_weight(B.inputs["wout"], D, D, Wout_p, None, wst, "Wout")
    Cn["wout_ready"] = True
    P.op("dve", lambda e: e.tensor_copy(out=Wrb, in_=wrf), reads=["wrf"], writes=["Wrb"])
    P.op("dve", lambda e: e.tensor_copy(out=Wib, in_=wif), reads=["wif"], writes=["Wib"])
    P.barrier()
    A.off = mark
    NXR = 6
    xr = [A.alloc([D], F32) for _ in range(NXR)]
    xs = [A.alloc([D], BF16) for _ in range(2)]
    xnT = [A.alloc([KC, TT], BF16) for _ in range(2)]
    junk = A.alloc([D], BF16)
    ssb = [A.alloc([1], F32) for _ in range(4)]
    rsb = [A.alloc([1], F32) for _ in range(4)]
    kst = [A.alloc([TT], BF16) for _ in range(3)]
    vst = [A.alloc([512], F32) for _ in range(2)]
    vbf = [A.alloc([4, 129], BF16) for _ in range(2)]
    for i in range(2):
        P.op("pool", lambda e, i=i: e.memset(vbf[i], 1.0), writes=[("vbf", i)])
    lxb = [[A.alloc([TT + 4], BF16) for _ in range(2)] for _ in range(4)]
    lxl = A.alloc([4, 3], F32)
    lx0 = A.alloc([4, 3], F32)
    dgw = A.alloc([16, 128], BF16)
    xcb = [A.alloc([TT], BF16) for _ in range(4)]
    rb = [A.alloc([TT], F32) for _ in range(4)]
    ib = [A.alloc([TT], F32) for _ in range(4)]
    ab = [A.alloc([TT], F32) for _ in range(4)]
    a2b = [A.alloc([TT], F32) for _ in range(4)]
    hb = [A.alloc([TT], F32) for _ in range(4)]
    gt = [[A.alloc([TT], F32) for _ in range(4)] for _ in range(2)]
    tb = [A.alloc([TT], F32) for _ in range(4)]
    lob = [A.alloc([TT], BF16) for _ in range(4)]
    hstate = A.alloc([4], F32)
    cL = A.alloc([4], F32)
    cL2 = A.alloc([4], F32)
    ps = B.psum
    ps_tr = ps[0][:, :].bitcast(BF16)
    fm = [ps[1], ps[2]]
    cvb = ps[3]
    tm = [ps[4], ps[5]]
    rg = [ps[6], ps[7]]

    if "nocl" not in DBG:
        P.op("act", lambda e: e.activation(out=cL, in_=Cn["lam"], func=AF.Exp, scale=-1.0), reads=["consts"], writes=["cL"])
        P.op("act", lambda e: e.activation(out=cL, in_=cL, func=AF.Ln, bias=1.0), reads=["cL"], writes=["cL"])
    P.op("pool", lambda e: e.tensor_scalar(out=cL2, in0=cL, scalar1=-16.0, scalar2=None, op0=ALU.mult),
         reads=["cL"], writes=["cL2"])
    P.op("pool", lambda e: e.tensor_scalar(out=cL, in0=cL, scalar1=-8.0, scalar2=None, op0=ALU.mult),
         reads=["cL", "cL2"], writes=["cL"])
    for g in range(4):
        for j in range(4):
            P.op("dve", lambda e, g=g, j=j: e.tensor_scalar(out=dgw[:, g * 4 + j, :], in0=B.ident, scalar1=Cn["convw"][:, g, j:j + 1],
                                                            scalar2=None, op0=ALU.mult), reads=["consts"], writes=["dgw"])

    def run_seq(sq, x1_src, T, T_OTH, KT_scr, V_scr, QT_scr, LT_scr, ko, vo, hl_out, cb_out, h0_d, conv0_d, koff):
        sn = sname + str(sq)
        if h0_d is None:
            P.op("pool", lambda e: e.memset(hstate, 0.0), writes=["hstate"])
            for g in range(4):
                P.op("pool", lambda e, g=g: e.memset(lxb[g][0][:, 0:3], 0.0), writes=[("lxh", g, 0)])
        else:
            P.dma(sn + "st", lambda e: e.dma_start(out=hstate, in_=h0_d.rearrange("(g p) -> p g", p=128),
                                                      allow_slow_non_contiguous=True), writes=["hstate"])
            for g in range(4):
                P.dma(sn + "st", lambda e, g=g: e.dma_start(
                    out=lx0[:, g, :], in_=conv0_d[:, g * 128:(g + 1) * 128].rearrange("j p -> p j"),
                    allow_slow_non_contiguous=True), writes=[("lx0", g)])
            for g in range(4):
                P.op("dve", lambda e, g=g: e.tensor_copy(out=lxb[g][0][:, 0:3], in_=lx0[:, g, :]),
                     reads=[("lx0", g)], writes=[("lxh", g, 0)])
        P.barrier()

        tiles = []
        t0 = 0
        while t0 < T:
            lim = T_OTH if t0 < T_OTH else T
            nt = min(TT, lim - t0)
            tiles.append((t0, nt))
            t0 += nt
        sub_ctr = [0]

        def load_tile(ti):
            t0, nt = tiles[ti]
            subs = []
            for s0 in range(0, nt, 128):
                ns = min(128, nt - s0)
                k = sub_ctr[0] % NXR
                sub_ctr[0] += 1
                P.dma(sname + "x%d" % k, lambda e, k=k, a=t0 + s0, ns=ns: e.dma_start(
                    out=xr[k][:ns, :], in_=x1_src[a:a + ns, :]), writes=[("xr", k)])
                subs.append((k, s0, ns))
            return subs

        loaded = {0: load_tile(0)}
        fm_c = [0]
        tm_c = [0]
        ks_c = [0]
        vs_c = [0]
        vb_c = [0]
        def tile_body(ti, t0, nt):
            own = t0 >= T_OTH
            to = t0 - T_OTH
            subs = loaded.pop(ti)
            xb = xnT[ti % 2]
            cur, nxt = ti % 2, (ti + 1) % 2
            for si, (k, s0, ns) in enumerate(subs):
                sl = (ti * 4 + si) % 4
                B.rstd_of(xr[k][:ns, :], ns, junk, ssb[sl], rsb[sl], [("xr", k)], sl)
                xsb = xs[si % 2]
                xsid = ("xs", si % 2)
                P.op("dve", lambda e, xsb=xsb, k=k, ns=ns, sl=sl: e.tensor_scalar(
                    out=xsb[:ns, :], in0=xr[k][:ns, :], scalar1=rsb[sl][:ns, :], scalar2=None, op0=ALU.mult),
                     reads=[("xr", k), ("rstd", sl)], writes=[xsid])
                for kc in range(KC):
                    P.op("pe", lambda e, xsb=xsb, kc=kc, ns=ns: e.transpose(
                        out=ps_tr[:, kc * 128:kc * 128 + ns], in_=xsb[:ns, kc * 128:(kc + 1) * 128],
                        identity=B.ident[:ns, :ns]),
                         reads=[xsid, "consts"], writes=["ps_tr"])
                P.op("act", lambda e, xb=xb, s0=s0, ns=ns: e.activation(
                    out=xb[:, :, s0:s0 + ns], in_=ps_tr.rearrange("p (a b) -> p a b", a=KC)[:, :, 0:ns],
                    func=AF.Copy),
                     reads=["ps_tr"], writes=[("xnT", ti % 2, si)])
                if si % 2 == 1:
                    yield
            xn_ids = [("xnT", ti % 2, si) for si in range(len(subs))]
            if ti + 1 < len(tiles):
                loaded[ti + 1] = load_tile(ti + 1)

            def fm_proj(col0):
                bank = fm[fm_c[0] % 2]
                bid = ("fm", fm_c[0] % 2)
                fm_c[0] += 1
                for kc in range(KC):
                    P.op("pe", lambda e, bank=bank, kc=kc, col0=col0: e.matmul(
                        bank[:, 0:nt], lhsT=Win[:, kc, col0:col0 + 128], rhs=xb[:, kc, 0:nt],
                        start=(kc == 0), stop=(kc == KC - 1)),
                         reads=xn_ids + [("Win", kc)], writes=[bid])
                return bank, bid

            def fm_to_scr(col0, dst_ap):
                bank, bid = fm_proj(col0)
                kb = kst[ks_c[0] % 3]
                kid = ("kst", ks_c[0] % 3)
                ksem = sname + "k%d" % (ks_c[0] % 3)
                ks_c[0] += 1
                P.op("act", lambda e, bank=bank, kb=kb: e.activation(out=kb[:, 0:nt], in_=bank[:, 0:nt], func=AF.Copy),
                     reads=[bid], writes=[kid])
                P.dma(ksem, lambda e, kb=kb, dst_ap=dst_ap: e.dma_start(out=dst_ap, in_=kb[:, 0:nt]), reads=[kid])

            for g in range(4):
                bank, bid = fm_proj(1536 + g * 128)
                P.op("act", lambda e, bank=bank, g=g: e.activation(
                    out=lxb[g][cur][:, 3:3 + nt], in_=bank[:, 0:nt], func=AF.Copy),
                     reads=[bid], writes=[("lx", g, cur)])
                if ti == len(tiles) - 1:
                    P.op("act", lambda e, bank=bank, g=g: e.activation(
                        out=lxl[:, g, :], in_=bank[:, nt - 3:nt], func=AF.Copy), reads=[bid], writes=[("lxl", g)])
            yield
            if own:
                for g in range(4):
                    bank, bid = fm_proj(2048 + g * 128)
                    P.op("act", lambda e, bank=bank, g=g: e.activation(out=gt[cur][g][:, 0:nt], in_=bank[:, 0:nt], func=AF.Copy),
                         reads=[bid], writes=[("gt", cur, g)])
            for h in range(4 if "nofm" not in DBG else 0):
                fm_to_scr(512 + h * 128, KT_scr[h, :, koff + t0:koff + t0 + nt])
            yield
            if own and "nofm" not in DBG:
                for h in range(4):
                    fm_to_scr(h * 128, QT_scr[h, :, to:to + nt])
                yield
            for si, (k, s0, ns) in enumerate(subs if "notm" not in DBG else []):
                for which in (("v", 1024), ("k", 512)):
                    if which[0] == "k" and not own:
                        continue
                    bank = tm[tm_c[0] % 2]
                    bid = ("tm", tm_c[0] % 2)
                    tm_c[0] += 1
                    for kc in range(KC):
                        P.op("pe", lambda e, bank=bank, kc=kc, s0=s0, ns=ns, c0=which[1]: e.matmul(
                            bank[:ns, :], lhsT=xb[:, kc, s0:s0 + ns], rhs=Win[:, kc, c0:c0 + 512],
                            start=(kc == 0), stop=(kc == KC - 1)),
                             reads=xn_ids + [("Win", kc)], writes=[bid])
                    if which[0] == "v" and "nov" not in DBG:
                        vb = vbf[vb_c[0] % 2]
                        vbid = ("vbf", vb_c[0] % 2)
                        vsem = sname + "vb%d" % (vb_c[0] % 2)
                        vb_c[0] += 1
                        P.op("dve", lambda e, bank=bank, vb=vb, ns=ns: e.tensor_copy(
                            out=vb[:ns, :, 0:128], in_=bank[:ns, :].rearrange("p (h d) -> p h d", h=4)),
                             reads=[bid], writes=[vbid])
                        P.dma(vsem, lambda e, vb=vb, a=koff + t0 + s0, ns=ns: e.dma_start(
                            out=V_scr[a:a + ns, :], in_=vb[:ns, :, :].rearrange("p h d -> p (h d)")),
                              reads=[vbid])
                    if own and "noko" not in DBG:
                        vs = vst[vs_c[0] % 2]
                        vsid = ("vst", vs_c[0] % 2)
                        vsem = sname + "vs%d" % (vs_c[0] % 2)
                        vs_c[0] += 1
                        dst = vo if which[0] == "v" else ko
                        P.op("act", lambda e, bank=bank, vs=vs, ns=ns: e.activation(out=vs[:ns, :], in_=bank[:ns, :], func=AF.Copy),
                             reads=[bid], writes=[vsid])
                        P.dma(vsem, lambda e, vs=vs, dst=dst, a=to + s0, ns=ns: e.dma_start(out=dst[a:a + ns, :], in_=vs[:ns, :]),
                              reads=[vsid])
                if si % 2 == 1:
                    yield

        def lru_part(ti, t0, nt, own, to, cur, nxt):
            for g in range(4):
                lx = lxb[g][cur]
                lid = [("lx", g, cur), ("lxh", g, cur)]
                for j in range(4):
                    P.op("pe", lambda e, g=g, lx=lx, j=j: e.matmul(
                        cvb[:, 0:nt], lhsT=dgw[:, g * 4 + j, :], rhs=lx[:, j:j + nt], start=(j == 0), stop=(j == 3)),
                         reads=lid + ["dgw"], writes=[("cv", 0)])
                P.op("dve", lambda e, g=g: e.tensor_scalar(
                    out=xcb[g][:, 0:nt], in0=cvb[:, 0:nt], scalar1=Cn["convb"][:, g:g + 1], scalar2=None, op0=ALU.add),
                     reads=[("cv", 0), "consts"], writes=[("xcb", g)])
                if ti + 1 < len(tiles):
                    boundary = (tiles[ti + 1][0] == T_OTH) and T_OTH > 0
                    if boundary:
                        P.op("pool", lambda e, g=g, lx=lx: e.tensor_scalar(
                            out=lxb[g][nxt][:, 0:3], in0=lx[:, nt:nt + 3], scalar1=Cn["flag"][:, 0:1], scalar2=None,
                            op0=ALU.mult), reads=lid + ["consts"], writes=[("lxh", g, nxt)])
                    else:
                        P.op("pool", lambda e, g=g, lx=lx: e.tensor_copy(out=lxb[g][nxt][:, 0:3], in_=lx[:, nt:nt + 3]),
                             reads=lid, writes=[("lxh", g, nxt)])
            yield
            for g in range(4):
                P.op("pe", lambda e, g=g: e.matmul(rg[0][:, 0:nt], lhsT=Wrb[:, g, :], rhs=xcb[g][:, 0:nt], start=True, stop=True),
                     reads=[("xcb", g), "Wrb"], writes=[("rg", 0)])
                P.op("act", lambda e, g=g: e.activation(out=rb[g][:, 0:nt], in_=rg[0][:, 0:nt], func=AF.Sigmoid,
                                                        bias=Cn["brg"][:, g:g + 1]),
                     reads=[("rg", 0), "consts"], writes=[("rb", g)])
                P.op("pe", lambda e, g=g: e.matmul(rg[1][:, 0:nt], lhsT=Wib[:, g, :], rhs=xcb[g][:, 0:nt], start=True, stop=True),
                     reads=[("xcb", g), "Wib"], writes=[("rg", 1)])
                P.op("act", lambda e, g=g: e.activation(out=ib[g][:, 0:nt], in_=rg[1][:, 0:nt], func=AF.Sigmoid,
                                                        bias=Cn["big"][:, g:g + 1]),
                     reads=[("rg", 1), "consts"], writes=[("ib", g)])
            yield
            if own:
                for g in range(4):
                    P.op("pool", lambda e, g=g: e.tensor_tensor(out=tb[g][:, 0:nt], in0=gt[cur][g][:, 0:nt], in1=gt[cur][g][:, 0:nt], op=ALU.mult),
                         reads=[("gt", cur, g)], writes=[("tb", g)])
                    P.op("pool", lambda e, g=g: e.tensor_scalar(out=tb[g][:, 0:nt], in0=tb[g][:, 0:nt], scalar1=0.044715, scalar2=1.0,
                                                                op0=ALU.mult, op1=ALU.add),
                         reads=[("tb", g)], writes=[("tb", g)])
                    P.op("pool", lambda e, g=g: e.tensor_tensor(out=tb[g][:, 0:nt], in0=tb[g][:, 0:nt], in1=gt[cur][g][:, 0:nt], op=ALU.mult),
                         reads=[("tb", g), ("gt", cur, g)], writes=[("tb", g)])
                    P.op("act", lambda e, g=g: e.activation(out=tb[g][:, 0:nt], in_=tb[g][:, 0:nt], func=AF.Sigmoid, scale=1.5957691216),
                         reads=[("tb", g)], writes=[("tb", g)])
                    P.op("pool", lambda e, g=g: e.tensor_tensor(out=gt[cur][g][:, 0:nt], in0=tb[g][:, 0:nt], in1=gt[cur][g][:, 0:nt], op=ALU.mult),
                         reads=[("tb", g), ("gt", cur, g)], writes=[("gt", cur, g)])
            yield
            for g in range(4):
                P.op("act", lambda e, g=g: e.activation(out=ab[g][:, 0:nt], in_=rb[g][:, 0:nt], func=AF.Exp, scale=cL[:, g:g + 1]),
                     reads=[("rb", g), "cL"], writes=[("ab", g)])
                P.op("act", lambda e, g=g: e.activation(out=a2b[g][:, 0:nt], in_=rb[g][:, 0:nt], func=AF.Exp, scale=cL2[:, g:g + 1]),
                     reads=[("rb", g), "cL2"], writes=[("a2b", g)])
            for g in range(4):
                P.op("act", lambda e, g=g: e.activation(out=a2b[g][:, 0:nt], in_=a2b[g][:, 0:nt], func=AF.Sqrt, scale=-1.0, bias=1.0),
                     reads=[("a2b", g)], writes=[("a2b", g)])
            yield
            for g in range(4):
                P.op("dve", lambda e, g=g: e.tensor_tensor(out=ib[g][:, 0:nt], in0=ib[g][:, 0:nt], in1=xcb[g][:, 0:nt], op=ALU.mult),
                     reads=[("ib", g), ("xcb", g)], writes=[("ib", g)])
                P.op("dve", lambda e, g=g: e.tensor_tensor(out=ib[g][:, 0:nt], in0=ib[g][:, 0:nt], in1=a2b[g][:, 0:nt], op=ALU.mult),
                     reads=[("ib", g), ("a2b", g)], writes=[("ib", g)])
                P.op("dve", lambda e, g=g: e.tensor_tensor_scan(
                    out=hb[g][:, 0:nt], data0=ab[g][:, 0:nt], data1=ib[g][:, 0:nt], initial=hstate[:, g:g + 1],
                    op0=ALU.mult, op1=ALU.add),
                     reads=[("ab", g), ("ib", g), "hstate"], writes=[("hb", g)])
            boundary = (ti + 1 < len(tiles)) and (tiles[ti + 1][0] == T_OTH) and T_OTH > 0
            for g in range(4):
                if boundary:
                    P.op("pool", lambda e, g=g: e.tensor_scalar(out=hstate[:, g:g + 1], in0=hb[g][:, nt - 1:nt],
                                                                scalar1=Cn["flag"][:, 0:1], scalar2=None, op0=ALU.mult),
                         reads=[("hb", g), "consts"], writes=["hstate"])
                else:
                    P.op("pool", lambda e, g=g: e.tensor_copy(out=hstate[:, g:g + 1], in_=hb[g][:, nt - 1:nt]),
                         reads=[("hb", g)], writes=["hstate"])
            if own:
                for g in range(4):
                    P.op("dve", lambda e, g=g: e.tensor_tensor(out=lob[g][:, 0:nt], in0=hb[g][:, 0:nt], in1=gt[cur][g][:, 0:nt], op=ALU.mult),
                         reads=[("hb", g), ("gt", cur, g)], writes=[("lob", g)])
                    P.dma(sname + "lo%d" % g, lambda e, g=g: e.dma_start(out=LT_scr[g, :, to:to + nt], in_=lob[g][:, 0:nt]),
                          reads=[("lob", g)])
        def interleave(ga, gb):
            gens = [g for g in (ga, gb) if g is not None]
            while gens:
                for g in list(gens):
                    try:
                        next(g)
                    except StopIteration:
                        gens.remove(g)

        pending = None
        for ti, (t0, nt) in enumerate(tiles):
            interleave(tile_body(ti, t0, nt), pending)
            pending = lru_part(ti, t0, nt, t0 >= T_OTH, t0 - T_OTH, ti % 2, (ti + 1) % 2)
        interleave(None, pending)
        lt0, lnt = tiles[-1]
        lcur = (len(tiles) - 1) % 2
        if "nofin" not in DBG:
            P.dma(sn + "fin", lambda e: e.dma_start(out=hl_out.rearrange("(g p) -> p g", p=128), in_=hstate,
                                                       allow_slow_non_contiguous=True), reads=["hstate"])
        for g in range(4 if "nofin" not in DBG else 0):
            P.dma(sn + "fin", lambda e, g=g: e.dma_start(
                out=cb_out[:, g * 128:(g + 1) * 128].rearrange("j p -> p j"), in_=lxl[:, g, :],
                allow_slow_non_contiguous=True), reads=[("lxl", g)])

    for sq, q in enumerate(seqs):
        run_seq(sq, q['x1'], q['T'], q['T_OTH'], q['KT'], q['V'], q['QT'], q['LT'], q['ko'], q['vo'], q['hl'], q['cb'],
                q.get('h0'), q.get('conv0'), q.get('koff', 0))
    P.barrier()
    A.pop()


SLOPES = [2.0 ** (-8.0 * (i + 1) / 4) for i in range(4)]
LAMBDA_INIT = 0.8 - 0.6 * 1.0


def attn_consts(B, Cn):
    P, A = B.P, B.A
    for nm, shp in (("pb", [4, 71]), ("db", [4, 128]), ("subg", [128]), ("lq1", [64]), ("lk1", [64]), ("lq2", [64]), ("lk2", [64])):
        Cn[nm] = A.alloc(shp, F32)
        P.dma("c0", lambda e, nm=nm: e.dma_start(out=Cn[nm], in_=B.inputs[nm]), writes=["consts"])
    negl = A.alloc([1], F32)
    t1 = A.alloc([1], F32)
    t2 = A.alloc([1], F32)
    j64 = A.alloc([64], F32)
    P.op("dve", lambda e: e.tensor_tensor(out=j64, in0=Cn["lq1"], in1=Cn["lk1"], op=ALU.mult), reads=["consts"], writes=["j64"])
    P.op("dve", lambda e: e.reduce_sum(out=t1, in_=j64, axis=AX.X), reads=["j64"], writes=["t1"])
    P.op("dve", lambda e: e.tensor_tensor(out=j64, in0=Cn["lq2"], in1=Cn["lk2"], op=ALU.mult), reads=["consts", "t1"], writes=["j64"])
    P.op("dve", lambda e: e.reduce_sum(out=t2, in_=j64, axis=AX.X), reads=["j64"], writes=["t2"])
    P.op("act", lambda e: e.activation(out=t1, in_=t1, func=AF.Exp), reads=["t1"], writes=["t1"])
    P.op("act", lambda e: e.activation(out=t2, in_=t2, func=AF.Exp), reads=["t2"], writes=["t2"])
    P.op("pool", lambda e: e.tensor_tensor(out=negl, in0=t2, in1=t1, op=ALU.subtract), reads=["t1", "t2"], writes=["negl"])
    P.op("pool", lambda e: e.tensor_scalar(out=negl, in0=negl, scalar1=-LAMBDA_INIT, scalar2=None, op0=ALU.add),
         reads=["negl"], writes=["negl"])
    subg8 = A.alloc([128], F32)
    P.op("pool", lambda e: e.tensor_scalar(out=subg8, in0=Cn["subg"], scalar1=1.0 - LAMBDA_INIT, scalar2=None, op0=ALU.mult),
         reads=["consts"], writes=["subg8"])
    pbo = A.alloc([4, 71], F32)
    P.op("pool", lambda e: e.tensor_scalar(out=pbo, in0=Cn["pb"], scalar1=Cn["maskv"][:, 0:1], scalar2=None, op0=ALU.add),
         reads=["consts"], writes=["pbo"])
    Cn["negl"], Cn["subg8"], Cn["pbo"] = negl, subg8, pbo


def attn_stage(B, jobs, Cn, sname, window=(None,) * 4):
    P, A, nc = B.P, B.A, B.nc
    TKmax = max(q["TK"] for q in jobs)
    NBmax = cdiv(TKmax, 128)
    A.push()
    KT = A.alloc([4, NBmax * 128], BF16)
    V1 = A.alloc([NBmax, 516], BF16)
    Wout = A.alloc([KC, D], BF16)
    gmb = A.alloc([D], F32)
    P.dma("c0", lambda e: e.dma_start(out=gmb, in_=B.inputs["gmb"]), writes=["gmb"])
    Cn["gmb"] = gmb
    attn_consts(B, Cn)
    mark = A.off
    wst = [A.alloc([D], F32) for _ in range(4)]
    B.prep_weight(B.inputs["wout"], D, D, Wout, None, wst, "Wout")
    P.barrier()
    A.off = mark
    QTILE = 512
    qt = [A.alloc([4, QTILE], BF16) for _ in range(2)]
    NPT = 4
    pt = [A.alloc([QTILE], BF16) for _ in range(NPT)]
    dtmp = [A.alloc([128], F32) for _ in range(2)]
    atok = [A.alloc([512], BF16) for _ in range(4)]
    attT = A.alloc([4, QTILE], BF16)
    lruT = [A.alloc([4, QTILE], BF16) for _ in range(2)]
    x1r = [A.alloc([D], F32) for _ in range(2)]
    ost = [A.alloc([D], F32)] * 2
    otmp = [A.alloc([128], F32) for _ in range(2)]
    ofin = [A.alloc([2, 129], F32) for _ in range(4)]
    junk = A.alloc([512], BF16)
    rl = [A.alloc([2], F32) for _ in range(4)]
    ssn = [A.alloc([1], F32) for _ in range(4)]
    rsn = [A.alloc([1], F32) for _ in range(4)]
    ssm = [A.alloc([1], F32) for _ in range(2)]
    ssm2 = [A.alloc([1], F32) for _ in range(2)]
    rsm = [A.alloc([1], F32) for _ in range(2)]
    ps = B.psum
    pb, pbo, db = Cn["pb"], Cn["pbo"], Cn["db"]
    st_c = [0]
    pt_c = [0]
    dt_c = [0]
    x_c = [0]
    o_c = [0]
    qb_c = [0]
    VCH = 16

    def run_job(TK, NQ, KT_scr, V_scr, QT_scr, LT_scr, x1_scr, x1_off, x2_dst, mask_other):
        NB = cdiv(TK, 128)
        KOFF = TK - NQ
        assert KOFF % 128 == 0
        nkof = lambda j: min(128, TK - 128 * j)
        for h in range(4):
            P.dma(sname + "K%d" % h, lambda e, h=h: e.dma_start(out=KT[:, h, 0:TK], in_=KT_scr[h, :, 0:TK]), writes=[("KT", h)])
        for ci, j0 in enumerate(range(0, NB, VCH)):
            j1 = min(NB, j0 + VCH)
            jf = min(j1, TK // 128)
            if jf > j0:
                P.dma(sname + "V%d" % (ci % 4), lambda e, j0=j0, jf=jf: e.dma_start(
                    out=V1[:, j0:jf, :], in_=V_scr[j0 * 128:jf * 128, :].rearrange("(j p) c -> p j c", p=128)),
                      writes=[("V1", ci)])
            if jf < j1:
                nk = nkof(jf)
                P.dma(sname + "V%d" % (ci % 4), lambda e, jf=jf, nk=nk: e.dma_start(
                    out=V1[:nk, jf, :], in_=V_scr[jf * 128:jf * 128 + nk, :]), writes=[("V1", ci)])
        vid = lambda j: ("V1", j // VCH)

        tiles = []
        q0 = 0
        while q0 < NQ:
            nq = min(QTILE, NQ - q0)
            tiles.append((q0, nq))
            q0 += nq

        LOOK = int(os.environ.get("K_LOOK", "3"))
        dq = []

        def push2(fn):
            dq.append(fn)
            while len(dq) > LOOK:
                dq.pop(0)()

        def load_q(ti):
            q0, nq = tiles[ti]
            b = qb_c[0] % 2
            qb_c[0] += 1
            for h in range(4):
                P.dma(sname + "q%d" % b, lambda e, b=b, h=h, q0=q0, nq=nq: e.dma_start(
                    out=qt[b][:, h, 0:nq], in_=QT_scr[h, :, q0:q0 + nq]), writes=[("qt", b, h)])
            def ld(b=b, q0=q0, nq=nq):
                P.dma(sname + "l%d" % b, lambda e: e.dma_start(
                    out=lruT[b][:, :, 0:nq], in_=LT_scr[:, :, q0:q0 + nq].rearrange("g p t -> p g t")), writes=[("lruT", b)])
            push2(ld)
            return b

        def tile_stream(ti, q0, nq, b):
            nsb = cdiv(nq, 128)
            nqs_of = lambda s: min(128, nq - 128 * s)
            jb = (KOFF + q0) // 128
            nb_next = load_q(ti + 1) if ti + 1 < len(tiles) else None
            for h in range(4):
                persub = (h == 0)
                W = window[h]
                jlo = 0 if W is None else max(0, jb - W)
                first_in_bank = [True] * 4
                for j in range(jlo, jb + nsb):
                    nk = nkof(j)
                    rel = j - jb
                    s_lo = max(0, rel)
                    c0 = s_lo * 128
                    tab = pbo if (mask_other and j * 128 < KOFF) else pb
                    for c in range(2):
                        bi = st_c[0] % 4
                        st_c[0] += 1
                        stb = ps[bi]
                        bid = ("bank", bi)
                        P.op("pe", lambda e, stb=stb, c=c, h=h, j=j, c0=c0, nq=nq, b=b, nk=nk: e.matmul(
                            stb[:nk, c0:nq], lhsT=KT[c * 64:(c + 1) * 64, h, j * 128:j * 128 + nk],
                            rhs=qt[b][c * 64:(c + 1) * 64, h, c0:nq], start=True, stop=True),
                             reads=[("KT", h), ("qt", b, h)], writes=[bid])
                        pi = pt_c[0] % NPT
                        pt_c[0] += 1
                        ptb = pt[pi]
                        pid = ("pt", pi)
                        c1 = c0
                        if rel >= 0:
                            nqs = nqs_of(rel)
                            di = dt_c[0] % 2
                            dt_c[0] += 1
                            P.op("dve", lambda e, stb=stb, di=di, h=h, c0=c0, nk=nk, nqs=nqs: e.scalar_tensor_tensor(
                                out=dtmp[di][:nk, :nqs], in0=stb[:nk, c0:c0 + nqs], scalar=0.125, in1=db[:nk, h, 0:nqs],
                                op0=ALU.mult, op1=ALU.add), reads=[bid, "consts"], writes=[("dtmp", di)])
                            bconst = 0.0 if persub else SLOPES[h] * 128.0 * rel
                            P.op("act", lambda e, ptb=ptb, di=di, c0=c0, bconst=bconst, nk=nk, nqs=nqs: e.activation(
                                out=ptb[:nk, c0:c0 + nqs], in_=dtmp[di][:nk, :nqs], func=AF.Exp, bias=bconst),
                                 reads=[("dtmp", di)], writes=[pid])
                            c1 = c0 + 128
                        if c1 < nq:
                            if persub:
                                for s in range(c1 // 128, nsb):
                                    dj = jb + s - j
                                    ce = s * 128 + nqs_of(s)
                                    P.op("act", lambda e, ptb=ptb, stb=stb, s=s, ce=ce, dj=dj, tab=tab, h=h, nk=nk: e.activation(
                                        out=ptb[:nk, s * 128:ce], in_=stb[:nk, s * 128:ce], func=AF.Exp,
                                        bias=tab[:nk, h, dj + 3:dj + 4], scale=0.125),
                                         reads=[bid, "pbo"], writes=[pid])
                            else:
                                dj = jb - j
                                P.op("act", lambda e, ptb=ptb, stb=stb, c1=c1, nq=nq, dj=dj, tab=tab, h=h, nk=nk: e.activation(
                                    out=ptb[:nk, c1:nq], in_=stb[:nk, c1:nq], func=AF.Exp,
                                    bias=tab[:nk, h, dj + 3:dj + 4], scale=0.125),
                                     reads=[bid, "pbo"], writes=[pid])

                        def pv(ptb=ptb, pid=pid, c=c, j=j, h=h, nk=nk, s_lo=s_lo, fib=first_in_bank, jb=jb, nsb=nsb, nqs_of=nqs_of):
                            for s in range(s_lo, nsb):
                                ob = ps[4 + s]
                                st_flag = fib[s]
                                fib[s] = False
                                nqs = nqs_of(s)
                                last = (j == jb + s)
                                P.op("pe", lambda e, ob=ob, ptb=ptb, s=s, c=c, j=j, h=h, st_flag=st_flag, nk=nk, nqs=nqs, last=last: e.matmul(
                                    ob[:nqs, c * 256:c * 256 + 129], lhsT=ptb[:nk, s * 128:s * 128 + nqs],
                                    rhs=V1[:nk, j, h * 129:(h + 1) * 129], start=st_flag, stop=last,
                                    skip_group_check=True),
                                     reads=[pid, vid(j)], writes=[("bank", 4 + s)])
                        push2(pv)
                push2(lambda h=h, nsb=nsb, nqs_of=nqs_of: finalize(h, nsb, nqs_of))
            push2(lambda ti=ti, q0=q0, nq=nq, b=b, nsb=nsb, nqs_of=nqs_of: tail(q0, nq, b, nsb, nqs_of))
            return nb_next

        def finalize(h, nsb, nqs_of):
            for s in range(nsb):
                n = nqs_of(s)
                ob = ps[4 + s]
                oid = ("bank", 4 + s)
                of = ofin[s]
                fid = ("ofin", s)
                P.op("dve", lambda e, ob=ob, of=of, n=n: e.tensor_copy(
                    out=of[:n, :, :], in_=ob[:n, :].rearrange("p (c x) -> p c x", c=2)[:, :, 0:129]),
                     reads=[oid], writes=[fid])
                r2 = rl[s]
                P.op("dve", lambda e, of=of, r2=r2, n=n: e.reciprocal(out=r2[:n, :], in_=of[:n, :, 128]),
                     reads=[fid], writes=[("rl", s)])
                P.op("pool", lambda e, r2=r2, n=n: e.tensor_tensor(out=r2[:n, 1:2], in0=r2[:n, 1:2], in1=Cn["negl"][:n, :], op=ALU.mult),
                     reads=[("rl", s), "negl"], writes=[("rl", s)])
                oi = o_c[0] % 2
                o_c[0] += 1
                ot = otmp[oi]
                otid = ("otmp", oi)
                P.op("dve", lambda e, of=of, ot=ot, r2=r2, n=n: e.tensor_scalar(
                    out=ot[:n, :], in0=of[:n, 0, 0:128], scalar1=r2[:n, 0:1], scalar2=None, op0=ALU.mult),
                     reads=[fid, ("rl", s)], writes=[otid])
                P.op("dve", lambda e, of=of, ot=ot, r2=r2, n=n: e.scalar_tensor_tensor(
                    out=ot[:n, :], in0=of[:n, 1, 0:128], scalar=r2[:n, 1:2], in1=ot[:n, :], op0=ALU.mult, op1=ALU.add),
                     reads=[fid, ("rl", s), otid], writes=[otid])
                P.op("act", lambda e, ot=ot, s=s, n=n: e.activation(out=junk[:n, 0:128], in_=ot[:n, :], func=AF.Square,
                                                               accum_out=ssn[s][:n, :]),
                     reads=[otid], writes=[("ssn", s), "junkA"])
                P.op("pool", lambda e, s=s, n=n: e.tensor_scalar(out=ssn[s][:n, :], in0=ssn[s][:n, :], scalar1=1.0 / 128, scalar2=EPS,
                                                            op0=ALU.mult, op1=ALU.add), reads=[("ssn", s)], writes=[("ssn", s)])
                P.op("pool", lambda e, s=s, n=n: e.tensor_tensor(out=rsn[s][:n, :], in0=ssn[s][:n, :], in1=B.c_mhalf[:n, :], op=ALU.pow),
                     reads=[("ssn", s)], writes=[("rsn", s)])
                P.op("dve", lambda e, ot=ot, s=s, h=h, n=n: e.scalar_tensor_tensor(
                    out=atok[s][:n, h * 128:(h + 1) * 128], in0=ot[:n, :], scalar=rsn[s][:n, 0:1], in1=Cn["subg8"][:n, :],
                    op0=ALU.mult, op1=ALU.mult), reads=[otid, ("rsn", s), "subg8"], writes=[("atok", s, h)])

        def tail(q0, nq, b, nsb, nqs_of):
            ps_tr = ps[0][:, :].bitcast(BF16)
            for s in range(nsb):
                n = nqs_of(s)
                for h in range(4):
                    P.op("pe", lambda e, s=s, h=h, n=n: e.transpose(
                        out=ps_tr[:, h * 128:h * 128 + n], in_=atok[s][:n, h * 128:(h + 1) * 128], identity=B.ident[:n, :n]),
                         reads=[("atok", s, h)], writes=[("bank", 0)])
                P.op("act", lambda e, s=s, n=n: e.activation(
                    out=attT[:, :, s * 128:s * 128 + n], in_=ps_tr[:, 0:512].rearrange("p (a b) -> p a b", a=4)[:, :, 0:n],
                    func=AF.Copy), reads=[("bank", 0)], writes=[("attT", s)])
            for s in range(nsb):
                n = nqs_of(s)
                mo = (ps[1], ps[2])
                for half in range(2):
                    for kk in range(8):
                        src = attT if kk < 4 else lruT[b]
                        P.op("pe", lambda e, half=half, kk=kk, src=src, s=s, n=n, mo=mo: e.matmul(
                            mo[half][:n, :], lhsT=src[:, kk % 4, s * 128:s * 128 + n],
                            rhs=Wout[:, kk, half * 512:(half + 1) * 512], start=(kk == 0), stop=(kk == 7)),
                             reads=[("attT", s), ("lruT", b), ("Wout", kk)], writes=[("bank", 1 + half)])
                sl = s % 2
                P.op("act", lambda e, sl=sl, n=n: e.activation(out=junk[:n, :], in_=ps[1][:n, :], func=AF.Square, accum_out=ssm[sl][:n, :]),
                     reads=[("bank", 1)], writes=["junkA", ("ssm", sl)])
                P.op("act", lambda e, sl=sl, n=n: e.activation(out=junk[:n, :], in_=ps[2][:n, :], func=AF.Square, accum_out=ssm2[sl][:n, :]),
                     reads=[("bank", 2)], writes=["junkA", ("ssm2", sl)])
                P.op("pool", lambda e, sl=sl, n=n: e.tensor_tensor(out=ssm[sl][:n, :], in0=ssm[sl][:n, :], in1=ssm2[sl][:n, :], op=ALU.add),
                     reads=[("ssm", sl), ("ssm2", sl)], writes=[("ssm", sl)])
                P.op("pool", lambda e, sl=sl, n=n: e.tensor_scalar(out=ssm[sl][:n, :], in0=ssm[sl][:n, :], scalar1=1.0 / D, scalar2=EPS,
                                                              op0=ALU.mult, op1=ALU.add), reads=[("ssm", sl)], writes=[("ssm", sl)])
                P.op("pool", lambda e, sl=sl, n=n: e.tensor_tensor(out=rsm[sl][:n, :], in0=ssm[sl][:n, :], in1=B.c_mhalf[:n, :], op=ALU.pow),
                     reads=[("ssm", sl)], writes=[("rsm", sl)])
                xi = x_c[0] % 2
                x_c[0] += 1
                a = q0 + s * 128
                P.dma(sname + "x%d" % xi, lambda e, xi=xi, a=a, n=n: e.dma_start(
                    out=x1r[xi][:n, :], in_=x1_scr[x1_off + a:x1_off + a + n, :]), writes=[("x1r", xi)])
                for half in range(2):
                    P.op("dve", lambda e, xi=xi, half=half, sl=sl, n=n: e.scalar_tensor_tensor(
                        out=ost[xi][:n, half * 512:(half + 1) * 512], in0=ps[1 + half][:n, :], scalar=rsm[sl][:n, 0:1],
                        in1=Cn["gmb"][:n, half * 512:(half + 1) * 512], op0=ALU.mult, op1=ALU.mult),
                         reads=[("bank", 1 + half), ("rsm", sl), "consts"], writes=[("ost", 0, half)])
                    P.op("pool", lambda e, xi=xi, half=half, n=n: e.tensor_tensor(
                        out=ost[xi][:n, half * 512:(half + 1) * 512], in0=ost[xi][:n, half * 512:(half + 1) * 512],
                        in1=x1r[xi][:n, half * 512:(half + 1) * 512], op=ALU.add),
                         reads=[("ost", 0, half), ("x1r", xi)], writes=[("ost", 0, half)])
                P.dma(sname + "o0", lambda e, xi=xi, a=a, n=n: e.dma_start(out=x2_dst[a:a + n, :], in_=ost[xi][:n, :]),
                      reads=[("ost", 0, 0), ("ost", 0, 1)])

        bcur = load_q(0)
        for ti, (q0, nq) in enumerate(tiles):
            bcur = tile_stream(ti, q0, nq, bcur)
        while dq:
            dq.pop(0)()

    for q in jobs:
        run_job(q["TK"], q["NQ"], q["KT"], q["V"], q["QT"], q["LT"], q["x1"], q["x1_off"], q["x2"], q["mask_other"])
    P.barrier()
    A.pop()


def attn_stage2(B, jobs, Cn, sname, window=(None,) * 4):
    P, A, nc = B.P, B.A, B.nc
    TKmax = max(q["TK"] for q in jobs)
    NBmax = cdiv(TKmax, 128)
    A.push()
    Wout = A.alloc([KC, D], BF16)
    if Cn.get("wout_ready"):
        assert A.off == Cn["wout_off"], (A.off, Cn["wout_off"])
    KT = A.alloc([4, NBmax * 128], BF16)
    V1 = A.alloc([NBmax, 516], BF16)
    VCH = 16
    kv_done = {}

    def issue_kv(TK, KT_scr, V_scr):
        kv_done[id(KT_scr)] = True
        NB = cdiv(TK, 128)
        for h in range(4):
            P.dma(sname + "K%d" % h, lambda e, h=h: e.dma_start(out=KT[:, h, 0:TK], in_=KT_scr[h, :, 0:TK]), writes=[("KT", h)])
        for ci, j0 in enumerate(range(0, NB, VCH)):
            j1 = min(NB, j0 + VCH)
            jf = min(j1, TK // 128)
            if jf > j0:
                P.dma(sname + "V%d" % (ci % 4), lambda e, j0=j0, jf=jf: e.dma_start(
                    out=V1[:, j0:jf, :], in_=V_scr[j0 * 128:jf * 128, :].rearrange("(j p) c -> p j c", p=128)),
                      writes=[("V1", ci)])
            if jf < j1:
                nk = min(128, TK - 128 * jf)
                P.dma(sname + "V%d" % (ci % 4), lambda e, jf=jf, nk=nk: e.dma_start(
                    out=V1[:nk, jf, :], in_=V_scr[jf * 128:jf * 128 + nk, :]), writes=[("V1", ci)])

    issue_kv(jobs[0]["TK"], jobs[0]["KT"], jobs[0]["V"])
    gmb = A.alloc([D], F32)
    P.dma("c0", lambda e: e.dma_start(out=gmb, in_=B.inputs["gmb"]), writes=["gmb"])
    Cn["gmb"] = gmb
    attn_consts(B, Cn)
    g8col = A.alloc([1], F32)
    P.dma("c0", lambda e: e.dma_start(out=g8col, in_=B.inputs["subg_col"]), writes=["g8col"])
    P.op("pool", lambda e: e.tensor_scalar(out=g8col, in0=g8col, scalar1=1.0 - LAMBDA_INIT, scalar2=None, op0=ALU.mult),
         reads=["g8col"], writes=["g8col"])
    ones_bf = A.alloc([128], BF16)
    ones_f = A.alloc([128], F32)
    P.op("pool", lambda e: e.memset(ones_bf, 1.0), writes=["ones_bf"])
    P.op("pool", lambda e: e.memset(ones_f, 1.0), writes=["ones_f"])
    if not Cn.get("wout_ready"):
        mark = A.off
        wst = [A.alloc([D], F32) for _ in range(4)]
        B.prep_weight(B.inputs["wout"], D, D, Wout, None, wst, "Wout")
        P.barrier()
        A.off = mark
    QTILE = 512
    qt = [A.alloc([4, QTILE], BF16) for _ in range(2)]
    NPT = 3
    pt = [A.alloc([2, QTILE], BF16) for _ in range(NPT)]
    dtmp = [A.alloc([2, 128], F32) for _ in range(2)]
    attT = A.alloc([4, QTILE], BF16)
    lruT = [A.alloc([4, QTILE], BF16) for _ in range(2)]
    x1r = [A.alloc([D], F32) for _ in range(2)]
    ost = A.alloc([D], F32)
    rec0 = A.alloc([QTILE], F32)
    rec1 = A.alloc([QTILE], F32)
    o_sb = A.alloc([QTILE], F32)
    junk = A.alloc([512], BF16)
    ssm = [A.alloc([1], F32) for _ in range(2)]
    ssm2 = [A.alloc([1], F32) for _ in range(2)]
    rsm = [A.alloc([1], F32) for _ in range(2)]
    ps = B.psum
    psall = B.psum_all
    stpair = [psall[:, 0:1024].rearrange("p (c x) -> p c x", c=2), psall[:, 1024:2048].rearrange("p (c x) -> p c x", c=2)]
    OTb = (ps[4], ps[5])
    Lb = (ps[6], ps[7])
    pb, pbo, db = Cn["pb"], Cn["pbo"], Cn["db"]
    st_c = [0]
    pt_c = [0]
    dt_c = [0]
    x_c = [0]
    qb_c = [0]

    def take_pair():
        pi = st_c[0] % 2
        st_c[0] += 1
        return pi, [("bank", 2 * pi), ("bank", 2 * pi + 1)]

    def run_job(TK, NQ, KT_scr, V_scr, QT_scr, LT_scr, x1_scr, x1_off, x2_dst, mask_other):
        NB = cdiv(TK, 128)
        KOFF = TK - NQ
        assert KOFF % 128 == 0
        nkof = lambda j: min(128, TK - 128 * j)
        if not kv_done.get(id(KT_scr)):
            issue_kv(TK, KT_scr, V_scr)
        vid = lambda j: ("V1", j // VCH)
        tiles = []
        q0 = 0
        while q0 < NQ:
            nq = min(QTILE, NQ - q0)
            tiles.append((q0, nq))
            q0 += nq
        LOOK = int(os.environ.get("K_LOOK2", "2"))
        dq = []

        def push2(fn):
            dq.append(fn)
            while len(dq) > LOOK:
                dq.pop(0)()

        def load_q(ti):
            q0, nq = tiles[ti]
            b = qb_c[0] % 2
            qb_c[0] += 1
            for h in range(4):
                P.dma(sname + "q%d" % b, lambda e, b=b, h=h, q0=q0, nq=nq: e.dma_start(
                    out=qt[b][:, h, 0:nq], in_=QT_scr[h, :, q0:q0 + nq]), writes=[("qt", b, h)])

            def ld(b=b, q0=q0, nq=nq):
                P.dma(sname + "l%d" % b, lambda e: e.dma_start(
                    out=lruT[b][:, :, 0:nq], in_=LT_scr[:, :, q0:q0 + nq].rearrange("g p t -> p g t")), writes=[("lruT", b)])
            push2(ld)
            return b

        def tile_stream(ti, q0, nq, b):
            nsb = cdiv(nq, 128)
            nqs_of = lambda s: min(128, nq - 128 * s)
            jb = (KOFF + q0) // 128
            nb_next = load_q(ti + 1) if ti + 1 < len(tiles) else None
            for h in range(4):
                persub = (h == 0)
                W = window[h]
                jlo = 0 if W is None else max(0, jb - W)
                first = [True]
                jlast = jb + nsb - 1
                for j in range(jlo, jb + nsb):
                    nk = nkof(j)
                    rel = j - jb
                    s_lo = max(0, rel)
                    c0 = s_lo * 128
                    tab = pbo if (mask_other and j * 128 < KOFF) else pb
                    pi, bids = take_pair()
                    stp = stpair[pi]
                    for c in range(2):
                        P.op("pe", lambda e, stp=stp, c=c, h=h, j=j, c0=c0, nq=nq, b=b, nk=nk: e.matmul(
                            stp[:nk, c, c0:nq], lhsT=KT[c * 64:(c + 1) * 64, h, j * 128:j * 128 + nk],
                            rhs=qt[b][c * 64:(c + 1) * 64, h, c0:nq], start=True, stop=True),
                             reads=[("KT", h), ("qt", b, h)], writes=[bids[c]])
                    ri = pt_c[0] % NPT
                    pt_c[0] += 1
                    ptb = pt[ri]
                    pid = ("pt", ri)
                    c1 = c0
                    if rel >= 0:
                        nqs = nqs_of(rel)
                        di = dt_c[0] % 2
                        dt_c[0] += 1
                        for c in range(2):
                            P.op("dve", lambda e, stp=stp, di=di, h=h, c0=c0, nk=nk, nqs=nqs, c=c: e.scalar_tensor_tensor(
                                out=dtmp[di][:nk, c, :nqs], in0=stp[:nk, c, c0:c0 + nqs], scalar=0.125, in1=db[:nk, h, 0:nqs],
                                op0=ALU.mult, op1=ALU.add), reads=[bids[c], "consts"], writes=[("dtmp", di, c)])
                        bconst = 0.0 if persub else SLOPES[h] * 128.0 * rel
                        P.op("act", lambda e, ptb=ptb, di=di, c0=c0, bconst=bconst, nk=nk, nqs=nqs: e.activation(
                            out=ptb[:nk, :, c0:c0 + nqs], in_=dtmp[di][:nk, :, :nqs], func=AF.Exp, bias=bconst),
                             reads=[("dtmp", di, 0), ("dtmp", di, 1)], writes=[pid])
                        c1 = c0 + 128
                    if c1 < nq:
                        if persub:
                            for s in range(c1 // 128, nsb):
                                dj = jb + s - j
                                ce = s * 128 + nqs_of(s)
                                P.op("act", lambda e, ptb=ptb, stp=stp, s=s, ce=ce, dj=dj, tab=tab, h=h, nk=nk: e.activation(
                                    out=ptb[:nk, :, s * 128:ce], in_=stp[:nk, :, s * 128:ce], func=AF.Exp,
                                    bias=tab[:nk, h, dj + 3:dj + 4], scale=0.125),
                                     reads=bids + ["pbo"], writes=[pid])
                        else:
                            dj = jb - j
                            P.op("act", lambda e, ptb=ptb, stp=stp, c1=c1, nq=nq, dj=dj, tab=tab, h=h, nk=nk: e.activation(
                                out=ptb[:nk, :, c1:nq], in_=stp[:nk, :, c1:nq], func=AF.Exp,
                                bias=tab[:nk, h, dj + 3:dj + 4], scale=0.125),
                                 reads=bids + ["pbo"], writes=[pid])

                    def pv(ptb=ptb, pid=pid, j=j, h=h, nk=nk, c0=c0, nq=nq, first=first, last=(j == jlast)):
                        st_flag = first[0]
                        first[0] = False
                        for c in range(2):
                            P.op("pe", lambda e, c=c: e.matmul(
                                OTb[c][:, c0:nq], lhsT=V1[:nk, j, h * 129:h * 129 + 128], rhs=ptb[:nk, c, c0:nq],
                                start=st_flag, stop=last), reads=[pid, vid(j)], writes=[("bank", 4 + c)])
                        for c in range(2):
                            P.op("pe", lambda e, c=c: e.matmul(
                                Lb[c][:, c0:nq], lhsT=ones_bf[:nk, :], rhs=ptb[:nk, c, c0:nq],
                                start=st_flag, stop=last), reads=[pid, "ones_bf"], writes=[("bank", 6 + c)])
                    push2(pv)
                push2(lambda h=h, nq=nq: finalize(h, nq))
            push2(lambda q0=q0, nq=nq, b=b, nsb=nsb, nqs_of=nqs_of: tail(q0, nq, b, nsb, nqs_of))
            return nb_next

        def finalize(h, nq):
            P.op("dve", lambda e: e.reciprocal(out=rec0[:, 0:nq], in_=Lb[0][:, 0:nq]), reads=[("bank", 6)], writes=["rec0"])
            P.op("dve", lambda e: e.reciprocal(out=rec1[:, 0:nq], in_=Lb[1][:, 0:nq]), reads=[("bank", 7)], writes=["rec1"])
            P.op("dve", lambda e: e.tensor_tensor(out=rec0[:, 0:nq], in0=OTb[0][:, 0:nq], in1=rec0[:, 0:nq], op=ALU.mult),
                 reads=[("bank", 4), "rec0"], writes=["rec0"])
            P.op("dve", lambda e: e.tensor_tensor(out=rec1[:, 0:nq], in0=OTb[1][:, 0:nq], in1=rec1[:, 0:nq], op=ALU.mult),
                 reads=[("bank", 5), "rec1"], writes=["rec1"])
            P.op("dve", lambda e: e.scalar_tensor_tensor(out=o_sb[:, 0:nq], in0=rec1[:, 0:nq], scalar=Cn["negl"][:, 0:1],
                                                         in1=rec0[:, 0:nq], op0=ALU.mult, op1=ALU.add),
                 reads=["rec0", "rec1", "negl"], writes=["o_sb"])
            P.op("pool", lambda e: e.tensor_tensor(out=rec0[:, 0:nq], in0=o_sb[:, 0:nq], in1=o_sb[:, 0:nq], op=ALU.mult),
                 reads=["o_sb"], writes=["rec0"])
            pi, bids = take_pair()
            ssb = ps[2 * pi]
            P.op("pe", lambda e: e.matmul(ssb[:, 0:nq], lhsT=ones_f[:, :], rhs=rec0[:, 0:nq], start=True, stop=True),
                 reads=["rec0", "ones_f"], writes=bids)
            P.op("act", lambda e: e.activation(out=rec1[:, 0:nq], in_=ssb[:, 0:nq], func=AF.Ln, scale=1.0 / 128, bias=EPS),
                 reads=[bids[0]], writes=["rec1"])
            P.op("act", lambda e: e.activation(out=rec1[:, 0:nq], in_=rec1[:, 0:nq], func=AF.Exp, scale=-0.5),
                 reads=["rec1"], writes=["rec1"])
            P.op("dve", lambda e: e.scalar_tensor_tensor(out=attT[:, h, 0:nq], in0=o_sb[:, 0:nq], scalar=g8col[:, 0:1],
                                                         in1=rec1[:, 0:nq], op0=ALU.mult, op1=ALU.mult),
                 reads=["o_sb", "rec1", "g8col"], writes=[("attT", h)])

        def tail(q0, nq, b, nsb, nqs_of):
            for s in range(nsb):
                n = nqs_of(s)
                pi, bids = take_pair()
                mo = (ps[2 * pi], ps[2 * pi + 1])
                for half in range(2):
                    for kk in range(8):
                        src = attT if kk < 4 else lruT[b]
                        P.op("pe", lambda e, half=half, kk=kk, src=src, s=s, n=n, mo=mo: e.matmul(
                            mo[half][:n, :], lhsT=src[:, kk % 4, s * 128:s * 128 + n],
                            rhs=Wout[:, kk, half * 512:(half + 1) * 512], start=(kk == 0), stop=(kk == 7)),
                             reads=[("attT", kk % 4), ("lruT", b), ("Wout", kk)], writes=[bids[half]])
                sl = s % 2
                P.op("act", lambda e, sl=sl, n=n, mo=mo: e.activation(out=junk[:n, :], in_=mo[0][:n, :], func=AF.Square, accum_out=ssm[sl][:n, :]),
                     reads=[bids[0]], writes=["junkA", ("ssm", sl)])
                P.op("act", lambda e, sl=sl, n=n, mo=mo: e.activation(out=junk[:n, :], in_=mo[1][:n, :], func=AF.Square, accum_out=ssm2[sl][:n, :]),
                     reads=[bids[1]], writes=["junkA", ("ssm2", sl)])
                P.op("pool", lambda e, sl=sl, n=n: e.tensor_tensor(out=ssm[sl][:n, :], in0=ssm[sl][:n, :], in1=ssm2[sl][:n, :], op=ALU.add),
                     reads=[("ssm", sl), ("ssm2", sl)], writes=[("ssm", sl)])
                P.op("pool", lambda e, sl=sl, n=n: e.tensor_scalar(out=ssm[sl][:n, :], in0=ssm[sl][:n, :], scalar1=1.0 / D, scalar2=EPS,
                                                              op0=ALU.mult, op1=ALU.add), reads=[("ssm", sl)], writes=[("ssm", sl)])
                P.op("pool", lambda e, sl=sl, n=n: e.tensor_tensor(out=rsm[sl][:n, :], in0=ssm[sl][:n, :], in1=B.c_mhalf[:n, :], op=ALU.pow),
                     reads=[("ssm", sl)], writes=[("rsm", sl)])
                xi = x_c[0] % 2
                x_c[0] += 1
                a = q0 + s * 128
                P.dma(sname + "x%d" % xi, lambda e, xi=xi, a=a, n=n: e.dma_start(
                    out=x1r[xi][:n, :], in_=x1_scr[x1_off + a:x1_off + a + n, :]), writes=[("x1r", xi)])
                for half in range(2):
                    P.op("dve", lambda e, half=half, sl=sl, n=n, mo=mo: e.scalar_tensor_tensor(
                        out=ost[:n, half * 512:(half + 1) * 512], in0=mo[half][:n, :], scalar=rsm[sl][:n, 0:1],
                        in1=Cn["gmb"][:n, half * 512:(half + 1) * 512], op0=ALU.mult, op1=ALU.mult),
                         reads=[bids[half], ("rsm", sl), "consts"], writes=[("ost", half)])
                    P.op("pool", lambda e, xi=xi, half=half, n=n: e.tensor_tensor(
                        out=ost[:n, half * 512:(half + 1) * 512], in0=ost[:n, half * 512:(half + 1) * 512],
                        in1=x1r[xi][:n, half * 512:(half + 1) * 512], op=ALU.add),
                         reads=[("ost", half), ("x1r", xi)], writes=[("ost", half)])
                P.dma(sname + "o0", lambda e, a=a, n=n: e.dma_start(out=x2_dst[a:a + n, :], in_=ost[:n, :]),
                      reads=[("ost", 0), ("ost", 1)])

        bcur = load_q(0)
        for ti, (q0, nq) in enumerate(tiles):
            bcur = tile_stream(ti, q0, nq, bcur)
        while dq:
            dq.pop(0)()

    for q in jobs:
        run_job(q["TK"], q["NQ"], q["KT"], q["V"], q["QT"], q["LT"], q["x1"], q["x1_off"], q["x2"], q["mask_other"])
    P.barrier()
    A.pop()


def cache_prep_ops(B, ck, cv, KT_s, V_s, PAST, sname):
    P, A = B.P, B.A
    NSTEP = PAST // 512
    kin = [A.alloc([4, 512], F32) for _ in range(2)]
    kbf = [A.alloc([4, 512], BF16) for _ in range(2)]
    kT = [A.alloc([4, 512], BF16) for _ in range(2)]
    vin = [A.alloc([4, 512], F32) for _ in range(2)]
    vb = [A.alloc([4, 516], BF16) for _ in range(2)]
    for i in range(2):
        P.op("pool", lambda e, i=i: e.memset(vb[i], 1.0), writes=[("cvb", i)])
    ps = B.psum
    for st in range(NSTEP):
        r = st % 2
        a = st * 512
        P.dma(sname + "k%d" % r, lambda e, r=r, a=a: e.dma_start(
            out=kin[r], in_=ck[a:a + 512, :].rearrange("(j p) c -> p j c", p=128)), writes=[("ckin", r)])
        P.dma(sname + "v%d" % r, lambda e, r=r, a=a: e.dma_start(
            out=vin[r], in_=cv[a:a + 512, :].rearrange("(j p) c -> p j c", p=128)), writes=[("cvin", r)])
        P.op("dve", lambda e, r=r: e.tensor_copy(out=kbf[r], in_=kin[r]), reads=[("ckin", r)], writes=[("ckbf", r)])
        for jj in range(4):
            bank = ps[4 + (st * 4 + jj) % 4]
            bid = ("bank", 4 + (st * 4 + jj) % 4)
            ps_tr = bank[:, :].bitcast(BF16)
            for h in range(4):
                P.op("pe", lambda e, ps_tr=ps_tr, h=h, r=r, jj=jj: e.transpose(
                    out=ps_tr[:, h * 128:(h + 1) * 128], in_=kbf[r][:, jj, h * 128:(h + 1) * 128], identity=B.ident),
                     reads=[("ckbf", r)], writes=[bid])
            P.op("act", lambda e, ps_tr=ps_tr, r=r, jj=jj: e.activation(
                out=kT[r][:, :, jj * 128:(jj + 1) * 128], in_=ps_tr[:, 0:512].rearrange("p (a b) -> p a b", a=4), func=AF.Copy),
                 reads=[bid], writes=[("ckT", r, jj)])
        P.dma(sname + "ko%d" % r, lambda e, r=r, a=a: e.dma_start(
            out=KT_s[:, :, a:a + 512].rearrange("h p t -> p h t"), in_=kT[r]),
              reads=[("ckT", r, x) for x in range(4)])
        for jj in range(4):
            P.op("pool" if jj % 2 else "dve", lambda e, r=r, jj=jj: e.tensor_copy(
                out=vb[r][:, jj, :].rearrange("p (h d) -> p h d", h=4)[:, :, 0:128],
                in_=vin[r][:, jj, :].rearrange("p (h d) -> p h d", h=4)),
                 reads=[("cvin", r), ("cvb", r)], writes=[("cvb", r, jj)])
        P.dma(sname + "vo%d" % r, lambda e, r=r, a=a: e.dma_start(
            out=V_s[a:a + 512, :].rearrange("(j p) c -> p j c", p=128), in_=vb[r]),
              reads=[("cvb", r, x) for x in range(4)] + [("cvb", r)])


SMALL_INPUTS = [
    ("gma_col", [KC]), ("g1a_col", [KC]), ("g2a_col", [KC]),
    ("convw", [4, 4]), ("convb", [4]), ("brg", [4]), ("big", [4]), ("lam", [4]),
    ("flag", [1]), ("maskv", [1]),
]


def build_main(T_OTH, T_OWN, with_sample=True, window=(None,) * 4, stages=("A1", "A2", "C", "D")):
    if os.environ.get("K_WIN", "1") == "1":
        window = (4, 16, None, None)
    B = Builder(T_OTH, T_OWN)
    P = B.P
    T = T_OTH + T_OWN
    x = B.din("x", [T, D])
    for nm, shp in (("f1g", [D, DFF]), ("f1u", [D, DFF]), ("f1d", [DFF, D]),
                    ("f2g", [D, DFF]), ("f2u", [D, DFF]), ("f2d", [DFF, D]),
                    ("win", [D, 2560]), ("wout", [D, D])):
        B.din(nm, shp)
    for nm, shp in (("g1b_bc", [128, D]), ("g2b_bc", [128, D]), ("gmb", [128, D]),
                    ("wr_bd", [128, 4, 128]), ("wi_bd", [128, 4, 128]),
                    ("pb", [128, 4, 71]), ("db", [128, 4, 128]), ("subg", [128, 128]), ("subg_col", [128, 1]),
                    ("lq1", [128, 64]), ("lk1", [128, 64]), ("lq2", [128, 64]), ("lk2", [128, 64])):
        B.din(nm, shp)
    y = B.dout("y", [T_OWN, D])
    ko = B.dout("ko", [T_OWN, 512])
    vo = B.dout("vo", [T_OWN, 512])
    hl = B.dout("hl", [512])
    cb = B.dout("cb", [3, 512])
    x1_scr = B.dscr("x1_scr", [T, D])
    x2_scr = B.dscr("x2_scr", [T_OWN, D])
    KT_scr = B.dscr("KT_scr", [4, 128, T], BF16)
    V_scr = B.dscr("V_scr", [T, 516], BF16)
    QT_scr = B.dscr("QT_scr", [4, 128, T_OWN], BF16)
    LT_scr = B.dscr("LT_scr", [4, 128, T_OWN], BF16)
    S = {}
    NSAMP, PAST = 32, 4096
    if with_sample:
        S["xs"] = B.din("xs", [NSAMP, D])
        S["ck"] = B.din("ck", [PAST, 512])
        S["cv"] = B.din("cv", [PAST, 512])
        S["sh"] = B.din("sh", [512])
        S["sc"] = B.din("sc", [3, 512])
        S["ys"] = B.dout("ys", [NSAMP, D])
        S["kso"] = B.dout("kso", [NSAMP, 512])
        S["vso"] = B.dout("vso", [NSAMP, 512])
        S["hls"] = B.dout("hls", [512])
        S["cbs"] = B.dout("cbs", [3, 512])
        S["xs1"] = B.dscr("xs1_scr", [NSAMP, D])
        S["xs2"] = B.dscr("xs2_scr", [NSAMP, D])
        S["KT"] = B.dscr("KTs_scr", [4, 128, PAST + NSAMP], BF16)
        S["V"] = B.dscr("Vs_scr", [PAST + NSAMP, 516], BF16)
        S["QT"] = B.dscr("QTs_scr", [4, 128, NSAMP], BF16)
        S["LT"] = B.dscr("LTs_scr", [4, 128, NSAMP], BF16)
    B.consts()
    Cn = {}
    for nm, shp in SMALL_INPUTS:
        Cn[nm] = B.load_const(nm, shp)
    P.barrier()
    inp = B.inputs
    if "A1" in stages:
        segs = [(x, x1_scr, T)] + ([(S["xs"], S["xs1"], NSAMP)] if with_sample else [])
        B.ffn_stage(segs, inp["f1g"], inp["f1u"], inp["f1d"], Cn["g1a_col"], inp["g1b_bc"], "a")
    if with_sample and "A2" in stages:
        Cn["cache_prep"] = (S["ck"], S["cv"], S["KT"], S["V"], PAST, "p")
    if "A2" in stages:
        seqs = [dict(x1=x1_scr, T=T, T_OTH=T_OTH, KT=KT_scr, V=V_scr, QT=QT_scr, LT=LT_scr, ko=ko, vo=vo, hl=hl, cb=cb)]
        if with_sample:
            seqs.append(dict(x1=S["xs1"], T=NSAMP, T_OTH=0, KT=S["KT"], V=S["V"], QT=S["QT"], LT=S["LT"],
                             ko=S["kso"], vo=S["vso"], hl=S["hls"], cb=S["cbs"], h0=S["sh"], conv0=S["sc"], koff=PAST))
        mixer_in_stage(B, seqs, Cn, "m")
    if "C" in stages:
        jobs = [dict(TK=T, NQ=T_OWN, KT=KT_scr, V=V_scr, QT=QT_scr, LT=LT_scr, x1=x1_scr, x1_off=T_OTH, x2=x2_scr,
                     mask_other=True)]
        if with_sample:
            jobs.append(dict(TK=PAST + NSAMP, NQ=NSAMP, KT=S["KT"], V=S["V"], QT=S["QT"], LT=S["LT"], x1=S["xs1"],
                             x1_off=0, x2=S["xs2"], mask_other=False))
        (attn_stage2 if os.environ.get("K_ATT", "2") == "2" else attn_stage)(B, jobs, Cn, "c", window=window)
    if "D" in stages:
        segs = [(x2_scr, y, T_OWN)] + ([(S["xs2"], S["ys"], NSAMP)] if with_sample else [])
        B.ffn_stage(segs, inp["f2g"], inp["f2u"], inp["f2d"], Cn["g2a_col"], inp["g2b_bc"], "d")
    B.P.emit()
    return B


def _col(v, n):
    return np.ascontiguousarray(np.asarray(v, np.float32).reshape(n, 128).T)


def _bc(v):
    v = np.asarray(v, np.float32).reshape(1, -1)
    return np.ascontiguousarray(np.broadcast_to(v, (128, v.shape[1])))


def _block_diag(w):
    out = np.zeros((128, 4, 128), np.float32)
    for g in range(4):
        for hb in range(2):
            out[hb * 64:(hb + 1) * 64, g, hb * 64:(hb + 1) * 64] = w[2 * g + hb]
    return out


def _tables():
    k = np.arange(128, dtype=np.float64)
    pb = np.zeros((128, 4, 71), np.float32)
    db = np.zeros((128, 4, 128), np.float32)
    kk = k[:, None]
    qq = k[None, :]
    for h in range(4):
        sl = SLOPES[h]
        for dj in range(-3, 68):
            pb[:, h, dj + 3] = sl * (k - 128.0 * dj)
        v = np.where(kk <= qq, sl * kk, sl * (2 * qq - kk))
        v = np.where((kk // 64) > (qq // 64), NEG, v)
        db[:, h, :] = v
    return pb, db


def shared_inputs(inputs):
    import ml_dtypes
    g = lambda n: np.asarray(inputs[n], np.float32)
    pb, db = _tables()
    d = {
        "f1g": g("ffn1_w_gate")[0], "f1u": g("ffn1_w_up")[0], "f1d": g("ffn1_w_down")[0],
        "f2g": g("ffn2_w_gate")[0], "f2u": g("ffn2_w_up")[0], "f2d": g("ffn2_w_down")[0],
        "win": g("w_in")[0], "wout": g("w_out")[0],
        "g1b_bc": _bc(g("g_ffn1_post")[0]), "g2b_bc": _bc(g("g_ffn2_post")[0]), "gmb": _bc(g("g_mix_post")[0]),
        "wr_bd": _block_diag(g("w_rgate")[0]), "wi_bd": _block_diag(g("w_igate")[0]),
        "pb": pb, "db": db, "subg": _bc(g("subln_g")[0]), "subg_col": _col(g("subln_g")[0], 1),
        "lq1": _bc(g("lambda_q1")[0]), "lk1": _bc(g("lambda_k1")[0]),
        "lq2": _bc(g("lambda_q2")[0]), "lk2": _bc(g("lambda_k2")[0]),
        "gma_col": _col(g("g_mix_pre")[0], 8), "g1a_col": _col(g("g_ffn1_pre")[0], 8), "g2a_col": _col(g("g_ffn2_pre")[0], 8),
        "convw": np.ascontiguousarray(g("conv_w")[0].reshape(4, 4, 128).transpose(2, 1, 0)),
        "convb": _col(g("conv_b")[0], 4), "brg": _col(g("b_rgate")[0], 4), "big": _col(g("b_igate")[0], 4),
        "lam": _col(g("lru_lambda")[0], 4),
        "ident": np.eye(128).astype(ml_dtypes.bfloat16),
    }
    return d


_CACHE = {}


def kernel(**inputs):
    TH = 4096
    WITH_SAMPLE = bool(int(os.environ.get("K_SAMPLE", "1")))
    key = ("main", TH, WITH_SAMPLE)
    if key not in _CACHE:
        _CACHE[key] = build_main(TH, TH, with_sample=WITH_SAMPLE)
    B = _CACHE[key]
    sh = shared_inputs(inputs)
    xp = np.asarray(inputs["x_prompt"], np.float32)
    maps = []
    for c in range(8):
        b, r = c // 2, c % 2
        own = xp[b, r * TH:(r + 1) * TH]
        oth = xp[b, (1 - r) * TH:(2 - r) * TH]
        m = dict(sh)
        m["x"] = np.ascontiguousarray(np.concatenate([oth, own], 0))
        m["flag"] = np.full((128, 1), float(r), np.float32)
        m["maskv"] = np.full((128, 1), 0.0 if r == 1 else NEG, np.float32)
        if WITH_SAMPLE:
            m["xs"] = np.ascontiguousarray(np.asarray(inputs["x_sample"], np.float32)[c])
            m["ck"] = np.ascontiguousarray(np.asarray(inputs["cache_k"], np.float32)[0, c].reshape(4096, 512))
            m["cv"] = np.ascontiguousarray(np.asarray(inputs["cache_v"], np.float32)[0, c].reshape(4096, 512))
            m["sh"] = np.ascontiguousarray(np.asarray(inputs["state_lru_h"], np.float32)[0, c])
            m["sc"] = np.ascontiguousarray(np.asarray(inputs["state_conv"], np.float32)[0, c])
        m = {k: v for k, v in m.items() if k in B.inputs}
        maps.append(m)
    res = run_bass_kernel_spmd(B.nc, maps, core_ids=list(range(8))).results
    y = np.zeros((4, 8192, 1024), np.float32)
    kp = np.zeros((1, 4, 8192, 4, 2, 64), np.float32)
    vp = np.zeros((1, 4, 8192, 4, 128), np.float32)
    hp = np.zeros((1, 4, 512), np.float32)
    cp = np.zeros((1, 4, 3, 512), np.float32)
    ys = np.zeros((8, 32, 1024), np.float32)
    ks = np.zeros((1, 8, 32, 4, 2, 64), np.float32)
    vs = np.zeros((1, 8, 32, 4, 128), np.float32)
    hs = np.zeros((1, 8, 512), np.float32)
    cs = np.zeros((1, 8, 3, 512), np.float32)
    for c in range(8):
        b, r = c // 2, c % 2
        o = res[c]
        sl = slice(r * TH, (r + 1) * TH)
        y[b, sl] = o["y"]
        kp[0, b, sl] = o["ko"].reshape(TH, 4, 2, 64)
        vp[0, b, sl] = o["vo"].reshape(TH, 4, 128)
        if r == 1:
            hp[0, b] = o["hl"]
            cp[0, b] = o["cb"]
        if WITH_SAMPLE:
            ys[c] = o["ys"]
            ks[0, c] = o["kso"].reshape(32, 4, 2, 64)
            vs[0, c] = o["vso"].reshape(32, 4, 128)
            hs[0, c] = o["hls"]
            cs[0, c] = o["cbs"]
    return (y, ys, kp, vp, hp, cp, ks, vs, hs, cs)
```

```python
import numpy as np
import concourse.bass as bass
import concourse.mybir as mybir
from concourse.bass_utils import run_bass_kernel_spmd
from contextlib import ExitStack

F32 = mybir.dt.float32
BF16 = mybir.dt.bfloat16
AF = mybir.ActivationFunctionType
ALU = mybir.AluOpType
AX = mybir.AxisListType

D = 1024
DFF = 2816
NFF = DFF // 128
KC = D // 128
EPS = 1e-6
NEG = -1e30


class Op:
    __slots__ = ("eng", "fn", "deps", "sem", "inc", "val", "is_dma", "needs_inc")


class Prog:
    ENGS = ("pe", "act", "dve", "pool", "sp")

    def __init__(self, nc, stack):
        self.nc = nc
        self.stack = stack
        self.ops = {e: [] for e in self.ENGS}
        self.last_w = {}
        self.readers = {}
        self.barrier_ops = []
        self.esem = {e: stack.enter_context(nc.semaphore("s_" + e)) for e in ("pe", "act", "dve", "pool")}
        self.dma_sems = {}
        self.dma_last = {}
        self.n_sem = 4

    def dsem(self, name):
        if name not in self.dma_sems:
            self.dma_sems[name] = self.stack.enter_context(self.nc.semaphore("d_" + name))
            self.n_sem += 1
        return name

    PSUM_IDS = ("gu", "dn", "fm", "tm", "rg", "bank", "cv")

    def _is_psum(self, t):
        return t == "ps_tr" or (isinstance(t, tuple) and t[0] in self.PSUM_IDS)

    def _add(self, o, reads, writes):
        xr = [t for t in reads if self._is_psum(t)]
        if xr:
            reads = [t for t in reads if not self._is_psum(t)]
            writes = list(writes) + xr
        deps = set(self.barrier_ops)
        for t in reads:
            w = self.last_w.get(t)
            if w is not None:
                deps.add(w)
        for t in writes:
            w = self.last_w.get(t)
            if w is not None:
                deps.add(w)
            for r in self.readers.get(t, ()):
                deps.add(r)
        if o.eng == "pe" and not o.is_dma:
            deps = {d for d in deps if not (d.eng == "pe" and not d.is_dma)}
        deps = {(self.dma_last[d.sem] if d.is_dma else d) for d in deps}
        o.deps = deps
        for d in deps:
            d.needs_inc = True
        for t in reads:
            self.readers.setdefault(t, []).append(o)
        for t in writes:
            self.last_w[t] = o
            self.readers[t] = []
        self.ops[o.eng].append(o)

    def op(self, eng, fn, reads=(), writes=()):
        o = Op()
        o.eng = eng
        o.fn = fn
        o.is_dma = False
        o.needs_inc = False
        o.sem = None
        o.val = 0
        self._add(o, reads, writes)
        return o

    def dma(self, sem_name, fn, reads=(), writes=(), q="sp"):
        o = Op()
        o.eng = q
        o.fn = fn
        o.is_dma = True
        o.needs_inc = True
        o.sem = self.dsem(sem_name)
        o.val = 0
        self._add(o, reads, writes)
        self.dma_last[sem_name] = o
        return o

    def barrier(self):
        b = []
        for e in self.ENGS:
            for o in reversed(self.ops[e]):
                if not o.is_dma:
                    b.append(o)
                    break
        for o in self.dma_last.values():
            b.append(o)
        for o in b:
            o.needs_inc = True
        self.barrier_ops = b
        self.last_w = {}
        self.readers = {}

    def emit(self):
        nc = self.nc
        dcount = {}
        for e in self.ENGS:
            cnt = 0
            for o in self.ops[e]:
                if o.is_dma:
                    dcount[o.sem] = dcount.get(o.sem, 0) + 16
                    o.val = dcount[o.sem]
                elif o.needs_inc:
                    cnt += 1
                    o.val = cnt
        ops = self.ops
        esem = self.esem
        dsems = self.dma_sems

        def run(ename, eng):
            waited = {}
            for o in ops[ename]:
                need = {}
                for d in o.deps:
                    key = d.sem if d.is_dma else d.eng
                    if d.val > need.get(key, 0):
                        need[key] = d.val
                for key, v in need.items():
                    if waited.get(key, 0) < v:
                        sem = esem[key] if key in esem else dsems[key]
                        eng.wait_ge(sem, v)
                        waited[key] = v
                ins = o.fn(eng)
                if o.is_dma:
                    ins.then_inc(dsems[o.sem], 16)
                elif o.needs_inc:
                    ins.then_inc(esem[ename], 1)
            fin = {}
            for o in ops[ename]:
                if o.is_dma:
                    fin[o.sem] = max(fin.get(o.sem, 0), o.val)
            for s, v in fin.items():
                if waited.get(s, 0) < v:
                    eng.wait_ge(dsems[s], v)

        with nc.Block() as block:
            @block.tensor
            def _(e):
                run("pe", e)

            @block.scalar
            def _(e):
                run("act", e)

            @block.vector
            def _(e):
                run("dve", e)

            @block.gpsimd
            def _(e):
                run("pool", e)

            @block.sync
            def _(e):
                run("sp", e)


class Arena:
    def __init__(self, nc, stack, nbytes):
        self.t = stack.enter_context(nc.sbuf_tensor("arena", [128, nbytes // 4], F32))
        self.cap = nbytes
        self.off = 0
        self.marks = []

    def push(self):
        self.marks.append(self.off)

    def pop(self):
        self.off = self.marks.pop()

    def alloc(self, shape, dt):
        n = 1
        for s in shape:
            n *= s
        esz = 4 if dt == F32 else 2
        nb = (n * esz + 31) // 32 * 32
        assert self.off + nb <= self.cap, ("arena overflow", self.off, nb, self.cap)
        ap = self.t[:, self.off // 4:(self.off + nb) // 4]
        self.off += nb
        self.peak = max(getattr(self, 'peak', 0), self.off)
        if dt != F32:
            ap = ap.bitcast(dt)
        ap = ap[:, 0:n]
        if len(shape) == 2:
            ap = ap.rearrange("p (a b) -> p a b", a=shape[0])
        elif len(shape) == 3:
            ap = ap.rearrange("p (a b c) -> p a b c", a=shape[0], b=shape[1])
        return ap


import os
DBG = set(os.environ.get("KDBG", "").split(","))


def cdiv(a, b):
    return (a + b - 1) // b


class Builder:
    def __init__(self, T_OTH, T_OWN, n_samp=32, past=4096, stages="all"):
        self.T_OTH, self.T_OWN, self.NS, self.PAST = T_OTH, T_OWN, n_samp, past
        self.T = T_OTH + T_OWN
        self.stages = stages
        self.nc = bass.Bass("TRN2", target_bir_lowering=False)
        self.stack = ExitStack()
        self.P = Prog(self.nc, self.stack)
        self.A = Arena(self.nc, self.stack, 211456)
        self.inputs = {}
        self.outputs = {}
        nc = self.nc
        self.psum_all = self.stack.enter_context(nc.psum_tensor("psall", [128, 4096], F32))
        self.psum = [self.psum_all[:, i * 512:(i + 1) * 512] for i in range(8)]

    def din(self, name, shape, dt=F32):
        t = self.nc.dram_tensor(name, list(shape), dt, kind="ExternalInput").ap()
        self.inputs[name] = t
        return t

    def dout(self, name, shape, dt=F32):
        t = self.nc.dram_tensor(name, list(shape), dt, kind="ExternalOutput").ap()
        self.outputs[name] = t
        return t

    def dscr(self, name, shape, dt=F32):
        return self.nc.dram_tensor(name, list(shape), dt, kind="Internal").ap()

    def prep_weight(self, w_dram, K, N, dst, gcol, stage_bufs, tag, eng_cycle=("dve", "pool")):
        P = self.P
        nk = K // 128
        for kc in range(nk):
            sb = stage_bufs[kc % len(stage_bufs)]
            sid = ("wst", kc % len(stage_bufs))
            src = w_dram[kc * 128:(kc + 1) * 128, :]
            P.dma("wst%d" % (kc % len(stage_bufs)),
                  lambda e, sb=sb, src=src, N=N: e.dma_start(out=sb[:, 0:N], in_=src),
                  writes=[sid])
            ec = tuple(os.environ.get("K_PREP", "dve,act").split(","))
            en = ec[kc % len(ec)]
            if en == "act":
                if gcol is None:
                    P.op("act", lambda e, sb=sb, kc=kc, N=N, dst=dst: e.activation(out=dst[:, kc, :], in_=sb[:, 0:N], func=AF.Copy),
                         reads=[sid], writes=[(tag, kc)])
                else:
                    P.op("act", lambda e, sb=sb, kc=kc, N=N, dst=dst, gcol=gcol: e.activation(
                        out=dst[:, kc, :], in_=sb[:, 0:N], func=AF.Copy, scale=gcol[:, kc:kc + 1]),
                         reads=[sid, "consts"], writes=[(tag, kc)])
                continue
            if gcol is None:
                P.op(en, lambda e, sb=sb, kc=kc, N=N, dst=dst: e.tensor_copy(out=dst[:, kc, :], in_=sb[:, 0:N]),
                     reads=[sid], writes=[(tag, kc)])
            else:
                P.op(en, lambda e, sb=sb, kc=kc, N=N, dst=dst, gcol=gcol: e.tensor_scalar(
                    out=dst[:, kc, :], in0=sb[:, 0:N], scalar1=gcol[:, kc:kc + 1], scalar2=None, op0=ALU.mult),
                     reads=[sid, "consts"], writes=[(tag, kc)])

    def rstd_of(self, src_ap, np_, junk, ss, rstd, src_ids, tagid):
        P = self.P
        P.op("act", lambda e: e.activation(out=junk[:np_, :], in_=src_ap, func=AF.Square, accum_out=ss[:np_, :]),
             reads=src_ids, writes=[("junk", tagid), ("ss", tagid)])
        P.op("pool", lambda e: e.tensor_scalar(out=ss[:np_, :], in0=ss[:np_, :], scalar1=1.0 / D, scalar2=EPS,
                                               op0=ALU.mult, op1=ALU.add),
             reads=[("ss", tagid)], writes=[("ss", tagid)])
        P.op("pool", lambda e: e.tensor_tensor(out=rstd[:np_, :], in0=ss[:np_, :], in1=self.c_mhalf[:np_, :], op=ALU.pow),
             reads=[("ss", tagid), "consts"], writes=[("rstd", tagid)])

    def ffn_stage(self, segs, wg_d, wu_d, wd_d, gpre_col, gpost_bc, sname):
        P, A, nc = self.P, self.A, self.nc
        A.push()
        Wg = A.alloc([KC, DFF], BF16)
        Wu = A.alloc([KC, DFF], BF16)
        Wd = A.alloc([NFF, D], BF16)
        gph = A.alloc([D], F32)
        mark_act = A.off
        wst = [A.alloc([DFF], F32) for _ in range(5)]
        P.dma("c0", lambda e: e.dma_start(out=gph[:, :], in_=gpost_bc), writes=["gph"])
        P.op("pool", lambda e: e.tensor_scalar(out=gph[:, :], in0=gph[:, :], scalar1=0.5, scalar2=None,
                                               op0=ALU.mult), reads=["gph"], writes=["gph"])
        self.prep_weight(wg_d, D, DFF, Wg, gpre_col, wst, "Wg")
        self.prep_weight(wu_d, D, DFF, Wu, gpre_col, wst, "Wu")
        self.prep_weight(wd_d, DFF, D, Wd, None, wst, "Wd")
        P.barrier()
        A.off = mark_act
        TT = 256
        NXR = 5
        xr = [A.alloc([D], F32) for _ in range(NXR)]
        xs = [A.alloc([D], BF16) for _ in range(2)]
        xnT = [A.alloc([KC, TT], BF16) for _ in range(2)]
        actT = A.alloc([NFF, TT], BF16)
        stmp = [A.alloc([TT], F32) for _ in range(3)]
        ost = [A.alloc([D], F32) for _ in range(2)]
        junk = A.alloc([D], BF16)
        ssb = [A.alloc([1], F32) for _ in range(4)]
        rsb = [A.alloc([1], F32) for _ in range(4)]
        ssp = [A.alloc([1], F32) for _ in range(4)]
        ssq = [A.alloc([1], F32) for _ in range(4)]
        ps = self.psum
        ps_tr = ps[0][:, :].bitcast(BF16)
        gu = [ps[1], ps[2], ps[3]]
        dn = [(ps[4], ps[5]), (ps[6], ps[7])]

        tiles = []
        for (src, dst, n) in segs:
            t0 = 0
            while t0 < n:
                nt = min(TT, n - t0)
                tiles.append((src, dst, t0, nt))
                t0 += nt
        sub_ctr = [0]

        def load_tile(ti):
            src, dst, t0, nt = tiles[ti]
            subs = []
            for s0 in range(0, nt, 128):
                ns = min(128, nt - s0)
                k = sub_ctr[0] % NXR
                sub_ctr[0] += 1
                P.dma(sname + "x%d" % k, lambda e, k=k, src=src, a=t0 + s0, ns=ns: e.dma_start(
                    out=xr[k][:ns, :], in_=src[a:a + ns, :]), writes=[("xr", k)])
                subs.append((k, s0, ns))
            return subs

        loaded = {0: load_tile(0)}
        gu_ctr = 0
        st_ctr = 0
        o_ctr = 0
        subs_of = {}
        gu_state = {'gu': 0, 'st': 0, 'o': 0}

        def prenorm(ti):
            src, dst, t0, nt = tiles[ti]
            subs = loaded.pop(ti)
            subs_of[ti] = subs
            xb = xnT[ti % 2]
            for si, (k, s0, ns) in enumerate(subs):
                sl = (ti * 2 + si) % 4
                self.rstd_of(xr[k][:ns, :], ns, junk, ssb[sl], rsb[sl], [("xr", k)], sl)
                xsb = xs[(ti * 2 + si) % 2]
                xsid = ("xs", (ti * 2 + si) % 2)
                P.op("dve", lambda e, xsb=xsb, k=k, ns=ns, sl=sl: e.tensor_scalar(
                    out=xsb[:ns, :], in0=xr[k][:ns, :], scalar1=rsb[sl][:ns, :], scalar2=None, op0=ALU.mult),
                     reads=[("xr", k), ("rstd", sl)], writes=[xsid])
                for kc in range(KC):
                    P.op("pe", lambda e, xsb=xsb, kc=kc, ns=ns: e.transpose(
                        out=ps_tr[:, kc * 128:kc * 128 + ns], in_=xsb[:ns, kc * 128:(kc + 1) * 128],
                        identity=self.ident[:ns, :ns]),
                         reads=[xsid, "consts"], writes=["ps_tr"])
                P.op("act", lambda e, xb=xb, s0=s0, ns=ns: e.activation(
                    out=xb[:, :, s0:s0 + ns], in_=ps_tr.rearrange("p (a b) -> p a b", a=KC)[:, :, 0:ns],
                    func=AF.Copy),
                     reads=["ps_tr"], writes=[("xnT", ti % 2, si)])
            xn_ids = [("xnT", ti % 2, si) for si in range(len(subs))]

        def gateup(ti):
            nonlocal gu_ctr, st_ctr
            src, dst, t0, nt = tiles[ti]
            subs = subs_of[ti]
            xb = xnT[ti % 2]
            xn_ids = [("xnT", ti % 2, si) for si in range(len(subs))]
            for f in range(NFF):
                g = gu[gu_ctr % 3]
                gid = ("gu", gu_ctr % 3)
                gu_ctr += 1
                for kc in range(KC):
                    P.op("pe", lambda e, g=g, kc=kc, f=f, xb=xb, nt=nt: e.matmul(
                        g[:, 0:nt], lhsT=Wg[:, kc, f * 128:(f + 1) * 128], rhs=xb[:, kc, 0:nt],
                        start=(kc == 0), stop=(kc == KC - 1)),
                         reads=xn_ids + [("Wg", kc)], writes=[gid])
                for kc in range(KC):
                    P.op("pe", lambda e, g=g, kc=kc, f=f, xb=xb, nt=nt: e.matmul(
                        g[:, 256:256 + nt], lhsT=Wu[:, kc, f * 128:(f + 1) * 128], rhs=xb[:, kc, 0:nt],
                        start=(kc == 0), stop=(kc == KC - 1)),
                         reads=xn_ids + [("Wu", kc)], writes=[gid])
                stb = stmp[st_ctr % 3]
                sid = ("stmp", st_ctr % 3)
                st_ctr += 1
                P.op("act", lambda e, g=g, stb=stb, nt=nt: e.activation(out=stb[:, 0:nt], in_=g[:, 0:nt], func=AF.Silu),
                     reads=[gid], writes=[sid])
                P.op("dve", lambda e, g=g, stb=stb, nt=nt, f=f: e.tensor_tensor(
                    out=actT[:, f, 0:nt], in0=stb[:, 0:nt], in1=g[:, 256:256 + nt], op=ALU.mult),
                     reads=[gid, sid], writes=[("actT", f)])

        def down_post(ti):
            nonlocal o_ctr
            src, dst, t0, nt = tiles[ti]
            subs = subs_of.pop(ti)
            for si, (k, s0, ns) in enumerate(subs):
                d0, d1 = dn[si % 2]
                did = ("dn", si % 2)
                for half, dps in enumerate((d0, d1)):
                    for f in range(NFF):
                        P.op("pe", lambda e, dps=dps, f=f, s0=s0, ns=ns, half=half: e.matmul(
                            dps[:ns, :], lhsT=actT[:, f, s0:s0 + ns], rhs=Wd[:, f, half * 512:(half + 1) * 512],
                            start=(f == 0), stop=(f == NFF - 1)),
                             reads=[("actT", f), ("Wd", f)], writes=[did])
                sl = (ti * 2 + si) % 4
                ss2 = ssp[sl]
                P.op("act", lambda e, d0=d0, ns=ns, ss2=ss2: e.activation(
                    out=junk[:ns, 0:512], in_=d0[:ns, :], func=AF.Square, accum_out=ss2[:ns, :]),
                     reads=[did], writes=[("junk", 9), ("ssA", sl)])
                ss3 = ssq[sl]
                P.op("act", lambda e, d1=d1, ns=ns, ss3=ss3: e.activation(
                    out=junk[:ns, 512:1024], in_=d1[:ns, :], func=AF.Square, accum_out=ss3[:ns, :]),
                     reads=[did], writes=[("junk", 10), ("ssB", sl)])
                P.op("pool", lambda e, ss2=ss2, ss3=ss3, ns=ns: e.tensor_tensor(
                    out=ss2[:ns, :], in0=ss2[:ns, :], in1=ss3[:ns, :], op=ALU.add),
                     reads=[("ssA", sl), ("ssB", sl)], writes=[("ssA", sl)])
                P.op("pool", lambda e, ss2=ss2, ns=ns: e.tensor_scalar(
                    out=ss2[:ns, :], in0=ss2[:ns, :], scalar1=1.0 / D, scalar2=EPS, op0=ALU.mult, op1=ALU.add),
                     reads=[("ssA", sl)], writes=[("ssA", sl)])
                P.op("pool", lambda e, ss2=ss2, ss3=ss3, ns=ns: e.tensor_tensor(
                    out=ss3[:ns, :], in0=ss2[:ns, :], in1=self.c_mhalf[:ns, :], op=ALU.pow),
                     reads=[("ssA", sl), "consts"], writes=[("ssB", sl)])
                ob = ost[o_ctr % 2]
                oid = ("ost", o_ctr % 2)
                osem = sname + "o%d" % (o_ctr % int(os.environ.get("NOSEM", "2")))
                o_ctr += 1
                for half, dps in enumerate((d0, d1)):
                    P.op("dve", lambda e, dps=dps, ob=ob, ns=ns, half=half, ss3=ss3: e.scalar_tensor_tensor(
                        out=ob[:ns, half * 512:(half + 1) * 512], in0=dps[:ns, :], scalar=ss3[:ns, :],
                        in1=gph[:ns, half * 512:(half + 1) * 512], op0=ALU.mult, op1=ALU.mult),
                         reads=[did, ("ssB", sl), "consts"], writes=[(oid, half)])
                    P.op("pool", lambda e, ob=ob, ns=ns, half=half, k=k: e.tensor_tensor(
                        out=ob[:ns, half * 512:(half + 1) * 512], in0=ob[:ns, half * 512:(half + 1) * 512],
                        in1=xr[k][:ns, half * 512:(half + 1) * 512], op=ALU.add),
                         reads=[(oid, half), ("xr", k)], writes=[(oid, half)])
                P.dma(osem, lambda e, ob=ob, dst=dst, a=t0 + s0, ns=ns: e.dma_start(
                    out=dst[a:a + ns, :], in_=ob[:ns, :]), reads=[(oid, 0), (oid, 1)])

        if len(tiles) > 1:
            loaded[1] = load_tile(1)
        prenorm(0)
        for ti in range(len(tiles)):
            gateup(ti)
            if ti + 1 < len(tiles):
                prenorm(ti + 1)
            down_post(ti)
            if ti + 2 < len(tiles):
                loaded[ti + 2] = load_tile(ti + 2)
        P.barrier()
        A.pop()

    def consts(self):
        P, A = self.P, self.A
        ident_d = self.din("ident", [128, 128], BF16)
        self.ident = A.alloc([128], BF16)
        self.c_mhalf = A.alloc([1], F32)
        P.dma("c0", lambda e: e.dma_start(out=self.ident[:, :], in_=ident_d[:, :]), writes=["consts"])
        P.op("pool", lambda e: e.memset(self.c_mhalf[:, :], -0.5), writes=["consts_b"])

    def load_const(self, name, shape, dt=F32):
        d = self.din(name, [128] + list(shape), dt)
        t = self.A.alloc(list(shape), dt)
        self.P.dma("c0", lambda e: e.dma_start(out=t, in_=d), writes=["consts_c"])
        return t


def build_ffn_test(ntok):
    B = Builder(0, ntok)
    P = B.P
    x = B.din("x", [ntok, D])
    wg = B.din("wg", [D, DFF])
    wu = B.din("wu", [D, DFF])
    wd = B.din("wd", [DFF, D])
    y = B.dout("y", [ntok, D])
    B.consts()
    gpre = B.load_const("gpre", [KC])
    gpost = B.load_const("gpost", [D])
    P.barrier()
    B.ffn_stage([(x, y, ntok)], wg, wu, wd, gpre, gpost, "f1")
    B.P.emit()
    return B


def mixer_in_stage(B, seqs, Cn, sname):
    P, A, nc = B.P, B.A, B.nc
    A.push()
    TT = 512
    Wout_p = A.alloc([KC, D], BF16)
    Cn["wout_off"] = A.off
    Win = A.alloc([KC, 2560], BF16)
    Wrb = A.alloc([4, 128], BF16)
    Wib = A.alloc([4, 128], BF16)
    mark = A.off
    wst = [A.alloc([2560], F32) for _ in range(4)]
    wrf = A.alloc([4, 128], F32)
    wif = A.alloc([4, 128], F32)
    P.dma("c0", lambda e: e.dma_start(out=wrf, in_=B.inputs["wr_bd"]), writes=["wrf"])
    P.dma("c0", lambda e: e.dma_start(out=wif, in_=B.inputs["wi_bd"]), writes=["wif"])
    if Cn.get("cache_prep") is not None:
        cache_prep_ops(B, *Cn["cache_prep"])
    B.prep_weight(B.inputs["win"], D, 2560, Win, Cn["gma_col"], wst, "Win")
    B.prep_weight(B.inputs["wout"], D, D, Wout_p, None, wst, "Wout")
    Cn["wout_ready"] = True
    P.op("dve", lambda e: e.tensor_copy(out=Wrb, in_=wrf), reads=["wrf"], writes=["Wrb"])
    P.op("dve", lambda e: e.tensor_copy(out=Wib, in_=wif), reads=["wif"], writes=["Wib"])
    P.barrier()
    A.off = mark
    NXR = 6
    xr = [A.alloc([D], F32) for _ in range(NXR)]
    xs = [A.alloc([D], BF16) for _ in range(2)]
    xnT = [A.alloc([KC, TT], BF16) for _ in range(2)]
    junk = A.alloc([D], BF16)
    ssb = [A.alloc([1], F32) for _ in range(4)]
    rsb = [A.alloc([1], F32) for _ in range(4)]
    kst = [A.alloc([TT], BF16) for _ in range(3)]
    vst = [A.alloc([512], F32) for _ in range(2)]
    vbf = [A.alloc([4, 129], BF16) for _ in range(2)]
    for i in range(2):
        P.op("pool", lambda e, i=i: e.memset(vbf[i], 1.0), writes=[("vbf", i)])
    lxb = [[A.alloc([TT + 4], BF16) for _ in range(2)] for _ in range(4)]
    lxl = A.alloc([4, 3], F32)
    lx0 = A.alloc([4, 3], F32)
    dgw = A.alloc([16, 128], BF16)
    xcb = [A.alloc([TT], BF16) for _ in range(4)]
    rb = [A.alloc([TT], F32) for _ in range(4)]
    ib = [A.alloc([TT], F32) for _ in range(4)]
    ab = [A.alloc([TT], F32) for _ in range(4)]
    a2b = [A.alloc([TT], F32) for _ in range(4)]
    hb = [A.alloc([TT], F32) for _ in range(4)]
    gt = [[A.alloc([TT], F32) for _ in range(4)] for _ in range(2)]
    tb = [A.alloc([TT], F32) for _ in range(4)]
    lob = [A.alloc([TT], BF16) for _ in range(4)]
    hstate = A.alloc([4], F32)
    cL = A.alloc([4], F32)
    cL2 = A.alloc([4], F32)
    ps = B.psum
    ps_tr = ps[0][:, :].bitcast(BF16)
    fm = [ps[1], ps[2]]
    cvb = ps[3]
    tm = [ps[4], ps[5]]
    rg = [ps[6], ps[7]]

    if "nocl" not in DBG:
        P.op("act", lambda e: e.activation(out=cL, in_=Cn["lam"], func=AF.Exp, scale=-1.0), reads=["consts"], writes=["cL"])
        P.op("act", lambda e: e.activation(out=cL, in_=cL, func=AF.Ln, bias=1.0), reads=["cL"], writes=["cL"])
    P.op("pool", lambda e: e.tensor_scalar(out=cL2, in0=cL, scalar1=-16.0, scalar2=None, op0=ALU.mult),
         reads=["cL"], writes=["cL2"])
    P.op("pool", lambda e: e.tensor_scalar(out=cL, in0=cL, scalar1=-8.0, scalar2=None, op0=ALU.mult),
         reads=["cL", "cL2"], writes=["cL"])
    for g in range(4):
        for j in range(4):
            P.op("dve", lambda e, g=g, j=j: e.tensor_scalar(out=dgw[:, g * 4 + j, :], in0=B.ident, scalar1=Cn["convw"][:, g, j:j + 1],
                                                            scalar2=None, op0=ALU.mult), reads=["consts"], writes=["dgw"])

    def run_seq(sq, x1_src, T, T_OTH, KT_scr, V_scr, QT_scr, LT_scr, ko, vo, hl_out, cb_out, h0_d, conv0_d, koff):
        sn = sname + str(sq)
        if h0_d is None:
            P.op("pool", lambda e: e.memset(hstate, 0.0), writes=["hstate"])
            for g in range(4):
                P.op("pool", lambda e, g=g: e.memset(lxb[g][0][:, 0:3], 0.0), writes=[("lxh", g, 0)])
        else:
            P.dma(sn + "st", lambda e: e.dma_start(out=hstate, in_=h0_d.rearrange("(g p) -> p g", p=128),
                                                      allow_slow_non_contiguous=True), writes=["hstate"])
            for g in range(4):
                P.dma(sn + "st", lambda e, g=g: e.dma_start(
                    out=lx0[:, g, :], in_=conv0_d[:, g * 128:(g + 1) * 128].rearrange("j p -> p j"),
                    allow_slow_non_contiguous=True), writes=[("lx0", g)])
            for g in range(4):
                P.op("dve", lambda e, g=g: e.tensor_copy(out=lxb[g][0][:, 0:3], in_=lx0[:, g, :]),
                     reads=[("lx0", g)], writes=[("lxh", g, 0)])
        P.barrier()

        tiles = []
        t0 = 0
        while t0 < T:
            lim = T_OTH if t0 < T_OTH else T
            nt = min(TT, lim - t0)
            tiles.append((t0, nt))
            t0 += nt
        sub_ctr = [0]

        def load_tile(ti):
            t0, nt = tiles[ti]
            subs = []
            for s0 in range(0, nt, 128):
                ns = min(128, nt - s0)
                k = sub_ctr[0] % NXR
                sub_ctr[0] += 1
                P.dma(sname + "x%d" % k, lambda e, k=k, a=t0 + s0, ns=ns: e.dma_start(
                    out=xr[k][:ns, :], in_=x1_src[a:a + ns, :]), writes=[("xr", k)])
                subs.append((k, s0, ns))
            return subs

        loaded = {0: load_tile(0)}
        fm_c = [0]
        tm_c = [0]
        ks_c = [0]
        vs_c = [0]
        vb_c = [0]
        def tile_body(ti, t0, nt):
            own = t0 >= T_OTH
            to = t0 - T_OTH
            subs = loaded.pop(ti)
            xb = xnT[ti % 2]
            cur, nxt = ti % 2, (ti + 1) % 2
            for si, (k, s0, ns) in enumerate(subs):
                sl = (ti * 4 + si) % 4
                B.rstd_of(xr[k][:ns, :], ns, junk, ssb[sl], rsb[sl], [("xr", k)], sl)
                xsb = xs[si % 2]
                xsid = ("xs", si % 2)
                P.op("dve", lambda e, xsb=xsb, k=k, ns=ns, sl=sl: e.tensor_scalar(
                    out=xsb[:ns, :], in0=xr[k][:ns, :], scalar1=rsb[sl][:ns, :], scalar2=None, op0=ALU.mult),
                     reads=[("xr", k), ("rstd", sl)], writes=[xsid])
                for kc in range(KC):
                    P.op("pe", lambda e, xsb=xsb, kc=kc, ns=ns: e.transpose(
                        out=ps_tr[:, kc * 128:kc * 128 + ns], in_=xsb[:ns, kc * 128:(kc + 1) * 128],
                        identity=B.ident[:ns, :ns]),
                         reads=[xsid, "consts"], writes=["ps_tr"])
                P.op("act", lambda e, xb=xb, s0=s0, ns=ns: e.activation(
                    out=xb[:, :, s0:s0 + ns], in_=ps_tr.rearrange("p (a b) -> p a b", a=KC)[:, :, 0:ns],
                    func=AF.Copy),
                     reads=["ps_tr"], writes=[("xnT", ti % 2, si)])
                if si % 2 == 1:
                    yield
            xn_ids = [("xnT", ti % 2, si) for si in range(len(subs))]
            if ti + 1 < len(tiles):
                loaded[ti + 1] = load_tile(ti + 1)

            def fm_proj(col0):
                bank = fm[fm_c[0] % 2]
                bid = ("fm", fm_c[0] % 2)
                fm_c[0] += 1
                for kc in range(KC):
                    P.op("pe", lambda e, bank=bank, kc=kc, col0=col0: e.matmul(
                        bank[:, 0:nt], lhsT=Win[:, kc, col0:col0 + 128], rhs=xb[:, kc, 0:nt],
                        start=(kc == 0), stop=(kc == KC - 1)),
                         reads=xn_ids + [("Win", kc)], writes=[bid])
                return bank, bid

            def fm_to_scr(col0, dst_ap):
                bank, bid = fm_proj(col0)
                kb = kst[ks_c[0] % 3]
                kid = ("kst", ks_c[0] % 3)
                ksem = sname + "k%d" % (ks_c[0] % 3)
                ks_c[0] += 1
                P.op("act", lambda e, bank=bank, kb=kb: e.activation(out=kb[:, 0:nt], in_=bank[:, 0:nt], func=AF.Copy),
                     reads=[bid], writes=[kid])
                P.dma(ksem, lambda e, kb=kb, dst_ap=dst_ap: e.dma_start(out=dst_ap, in_=kb[:, 0:nt]), reads=[kid])

            for g in range(4):
                bank, bid = fm_proj(1536 + g * 128)
                P.op("act", lambda e, bank=bank, g=g: e.activation(
                    out=lxb[g][cur][:, 3:3 + nt], in_=bank[:, 0:nt], func=AF.Copy),
                     reads=[bid], writes=[("lx", g, cur)])
                if ti == len(tiles) - 1:
                    P.op("act", lambda e, bank=bank, g=g: e.activation(
                        out=lxl[:, g, :], in_=bank[:, nt - 3:nt], func=AF.Copy), reads=[bid], writes=[("lxl", g)])
            yield
            if own:
                for g in range(4):
                    bank, bid = fm_proj(2048 + g * 128)
                    P.op("act", lambda e, bank=bank, g=g: e.activation(out=gt[cur][g][:, 0:nt], in_=bank[:, 0:nt], func=AF.Copy),
                         reads=[bid], writes=[("gt", cur, g)])
            for h in range(4 if "nofm" not in DBG else 0):
                fm_to_scr(512 + h * 128, KT_scr[h, :, koff + t0:koff + t0 + nt])
            yield
            if own and "nofm" not in DBG:
                for h in range(4):
                    fm_to_scr(h * 128, QT_scr[h, :, to:to + nt])
                yield
            for si, (k, s0, ns) in enumerate(subs if "notm" not in DBG else []):
                for which in (("v", 1024), ("k", 512)):
                    if which[0] == "k" and not own:
                        continue
                    bank = tm[tm_c[0] % 2]
                    bid = ("tm", tm_c[0] % 2)
                    tm_c[0] += 1
                    for kc in range(KC):
                        P.op("pe", lambda e, bank=bank, kc=kc, s0=s0, ns=ns, c0=which[1]: e.matmul(
                            bank[:ns, :], lhsT=xb[:, kc, s0:s0 + ns], rhs=Win[:, kc, c0:c0 + 512],
                            start=(kc == 0), stop=(kc == KC - 1)),
                             reads=xn_ids + [("Win", kc)], writes=[bid])
                    if which[0] == "v" and "nov" not in DBG:
                        vb = vbf[vb_c[0] % 2]
                        vbid = ("vbf", vb_c[0] % 2)
                        vsem = sname + "vb%d" % (vb_c[0] % 2)
                        vb_c[0] += 1
                        P.op("dve", lambda e, bank=bank, vb=vb, ns=ns: e.tensor_copy(
                            out=vb[:ns, :, 0:128], in_=bank[:ns, :].rearrange("p (h d) -> p h d", h=4)),
                             reads=[bid], writes=[vbid])
                        P.dma(vsem, lambda e, vb=vb, a=koff + t0 + s0, ns=ns: e.dma_start(
                            out=V_scr[a:a + ns, :], in_=vb[:ns, :, :].rearrange("p h d -> p (h d)")),
                              reads=[vbid])
                    if own and "noko" not in DBG:
                        vs = vst[vs_c[0] % 2]
                        vsid = ("vst", vs_c[0] % 2)
                        vsem = sname + "vs%d" % (vs_c[0] % 2)
                        vs_c[0] += 1
                        dst = vo if which[0] == "v" else ko
                        P.op("act", lambda e, bank=bank, vs=vs, ns=ns: e.activation(out=vs[:ns, :], in_=bank[:ns, :], func=AF.Copy),
                             reads=[bid], writes=[vsid])
                        P.dma(vsem, lambda e, vs=vs, dst=dst, a=to + s0, ns=ns: e.dma_start(out=dst[a:a + ns, :], in_=vs[:ns, :]),
                              reads=[vsid])
                if si % 2 == 1:
                    yield

        def lru_part(ti, t0, nt, own, to, cur, nxt):
            for g in range(4):
                lx = lxb[g][cur]
                lid = [("lx", g, cur), ("lxh", g, cur)]
                for j in range(4):
                    P.op("pe", lambda e, g=g, lx=lx, j=j: e.matmul(
                        cvb[:, 0:nt], lhsT=dgw[:, g * 4 + j, :], rhs=lx[:, j:j + nt], start=(j == 0), stop=(j == 3)),
                         reads=lid + ["dgw"], writes=[("cv", 0)])
                P.op("dve", lambda e, g=g: e.tensor_scalar(
                    out=xcb[g][:, 0:nt], in0=cvb[:, 0:nt], scalar1=Cn["convb"][:, g:g + 1], scalar2=None, op0=ALU.add),
                     reads=[("cv", 0), "consts"], writes=[("xcb", g)])
                if ti + 1 < len(tiles):
                    boundary = (tiles[ti + 1][0] == T_OTH) and T_OTH > 0
                    if boundary:
                        P.op("pool", lambda e, g=g, lx=lx: e.tensor_scalar(
                            out=lxb[g][nxt][:, 0:3], in0=lx[:, nt:nt + 3], scalar1=Cn["flag"][:, 0:1], scalar2=None,
                            op0=ALU.mult), reads=lid + ["consts"], writes=[("lxh", g, nxt)])
                    else:
                        P.op("pool", lambda e, g=g, lx=lx: e.tensor_copy(out=lxb[g][nxt][:, 0:3], in_=lx[:, nt:nt + 3]),
                             reads=lid, writes=[("lxh", g, nxt)])
            yield
            for g in range(4):
                P.op("pe", lambda e, g=g: e.matmul(rg[0][:, 0:nt], lhsT=Wrb[:, g, :], rhs=xcb[g][:, 0:nt], start=True, stop=True),
                     reads=[("xcb", g), "Wrb"], writes=[("rg", 0)])
                P.op("act", lambda e, g=g: e.activation(out=rb[g][:, 0:nt], in_=rg[0][:, 0:nt], func=AF.Sigmoid,
                                                        bias=Cn["brg"][:, g:g + 1]),
                     reads=[("rg", 0), "consts"], writes=[("rb", g)])
                P.op("pe", lambda e, g=g: e.matmul(rg[1][:, 0:nt], lhsT=Wib[:, g, :], rhs=xcb[g][:, 0:nt], start=True, stop=True),
                     reads=[("xcb", g), "Wib"], writes=[("rg", 1)])
                P.op("act", lambda e, g=g: e.activation(out=ib[g][:, 0:nt], in_=rg[1][:, 0:nt], func=AF.Sigmoid,
                                                        bias=Cn["big"][:, g:g + 1]),
                     reads=[("rg", 1), "consts"], writes=[("ib", g)])
            yield
            if own:
                for g in range(4):
                    P.op("pool", lambda e, g=g: e.tensor_tensor(out=tb[g][:, 0:nt], in0=gt[cur][g][:, 0:nt], in1=gt[cur][g][:, 0:nt], op=ALU.mult),
                         reads=[("gt", cur, g)], writes=[("tb", g)])
                    P.op("pool", lambda e, g=g: e.tensor_scalar(out=tb[g][:, 0:nt], in0=tb[g][:, 0:nt], scalar1=0.044715, scalar2=1.0,
                                                                op0=ALU.mult, op1=ALU.add),
                         reads=[("tb", g)], writes=[("tb", g)])
                    P.op("pool", lambda e, g=g: e.tensor_tensor(out=tb[g][:, 0:nt], in0=tb[g][:, 0:nt], in1=gt[cur][g][:, 0:nt], op=ALU.mult),
                         reads=[("tb", g), ("gt", cur, g)], writes=[("tb", g)])
                    P.op("act", lambda e, g=g: e.activation(out=tb[g][:, 0:nt], in_=tb[g][:, 0:nt], func=AF.Sigmoid, scale=1.5957691216),
                         reads=[("tb", g)], writes=[("tb", g)])
                    P.op("pool", lambda e, g=g: e.tensor_tensor(out=gt[cur][g][:, 0:nt], in0=tb[g][:, 0:nt], in1=gt[cur][g][:, 0:nt], op=ALU.mult),
                         reads=[("tb", g), ("gt", cur, g)], writes=[("gt", cur, g)])
            yield
            for g in range(4):
                P.op("act", lambda e, g=g: e.activation(out=ab[g][:, 0:nt], in_=rb[g][:, 0:nt], func=AF.Exp, scale=cL[:, g:g + 1]),
                     reads=[("rb", g), "cL"], writes=[("ab", g)])
                P.op("act", lambda e, g=g: e.activation(out=a2b[g][:, 0:nt], in_=rb[g][:, 0:nt], func=AF.Exp, scale=cL2[:, g:g + 1]),
                     reads=[("rb", g), "cL2"], writes=[("a2b", g)])
            for g in range(4):
                P.op("act", lambda e, g=g: e.activation(out=a2b[g][:, 0:nt], in_=a2b[g][:, 0:nt], func=AF.Sqrt, scale=-1.0, bias=1.0),
                     reads=[("a2b", g)], writes=[("a2b", g)])
            yield
            for g in range(4):
                P.op("dve", lambda e, g=g: e.tensor_tensor(out=ib[g][:, 0:nt], in0=ib[g][:, 0:nt], in1=xcb[g][:, 0:nt], op=ALU.mult),
                     reads=[("ib", g), ("xcb", g)], writes=[("ib", g)])
                P.op("dve", lambda e, g=g: e.tensor_tensor(out=ib[g][:, 0:nt], in0=ib[g][:, 0:nt], in1=a2b[g][:, 0:nt], op=ALU.mult),
                     reads=[("ib", g), ("a2b", g)], writes=[("ib", g)])
                P.op("dve", lambda e, g=g: e.tensor_tensor_scan(
                    out=hb[g][:, 0:nt], data0=ab[g][:, 0:nt], data1=ib[g][:, 0:nt], initial=hstate[:, g:g + 1],
                    op0=ALU.mult, op1=ALU.add),
                     reads=[("ab", g), ("ib", g), "hstate"], writes=[("hb", g)])
            boundary = (ti + 1 < len(tiles)) and (tiles[ti + 1][0] == T_OTH) and T_OTH > 0
            for g in range(4):
                if boundary:
                    P.op("pool", lambda e, g=g: e.tensor_scalar(out=hstate[:, g:g + 1], in0=hb[g][:, nt - 1:nt],
                                                                scalar1=Cn["flag"][:, 0:1], scalar2=None, op0=ALU.mult),
                         reads=[("hb", g), "consts"], writes=["hstate"])
                else:
                    P.op("pool", lambda e, g=g: e.tensor_copy(out=hstate[:, g:g + 1], in_=hb[g][:, nt - 1:nt]),
                         reads=[("hb", g)], writes=["hstate"])
            if own:
                for g in range(4):
                    P.op("dve", lambda e, g=g: e.tensor_tensor(out=lob[g][:, 0:nt], in0=hb[g][:, 0:nt], in1=gt[cur][g][:, 0:nt], op=ALU.mult),
                         reads=[("hb", g), ("gt", cur, g)], writes=[("lob", g)])
                    P.dma(sname + "lo%d" % g, lambda e, g=g: e.dma_start(out=LT_scr[g, :, to:to + nt], in_=lob[g][:, 0:nt]),
                          reads=[("lob", g)])
        def interleave(ga, gb):
            gens = [g for g in (ga, gb) if g is not None]
            while gens:
                for g in list(gens):
                    try:
                        next(g)
                    except StopIteration:
                        gens.remove(g)

        pending = None
        for ti, (t0, nt) in enumerate(tiles):
            interleave(tile_body(ti, t0, nt), pending)
            pending = lru_part(ti, t0, nt, t0 >= T_OTH, t0 - T_OTH, ti % 2, (ti + 1) % 2)
        interleave(None, pending)
        lt0, lnt = tiles[-1]
        lcur = (len(tiles) - 1) % 2
        if "nofin" not in DBG:
            P.dma(sn + "fin", lambda e: e.dma_start(out=hl_out.rearrange("(g p) -> p g", p=128), in_=hstate,
                                                       allow_slow_non_contiguous=True), reads=["hstate"])
        for g in range(4 if "nofin" not in DBG else 0):
            P.dma(sn + "fin", lambda e, g=g: e.dma_start(
                out=cb_out[:, g * 128:(g + 1) * 128].rearrange("j p -> p j"), in_=lxl[:, g, :],
                allow_slow_non_contiguous=True), reads=[("lxl", g)])

    for sq, q in enumerate(seqs):
        run_seq(sq, q['x1'], q['T'], q['T_OTH'], q['KT'], q['V'], q['QT'], q['LT'], q['ko'], q['vo'], q['hl'], q['cb'],
                q.get('h0'), q.get('conv0'), q.get('koff', 0))
    P.barrier()
    A.pop()


SLOPES = [2.0 ** (-8.0 * (i + 1) / 4) for i in range(4)]
LAMBDA_INIT = 0.8 - 0.6 * 1.0


def attn_consts(B, Cn):
    P, A = B.P, B.A
    for nm, shp in (("pb", [4, 71]), ("db", [4, 128]), ("subg", [128]), ("lq1", [64]), ("lk1", [64]), ("lq2", [64]), ("lk2", [64])):
        Cn[nm] = A.alloc(shp, F32)
        P.dma("c0", lambda e, nm=nm: e.dma_start(out=Cn[nm], in_=B.inputs[nm]), writes=["consts"])
    negl = A.alloc([1], F32)
    t1 = A.alloc([1], F32)
    t2 = A.alloc([1], F32)
    j64 = A.alloc([64], F32)
    P.op("dve", lambda e: e.tensor_tensor(out=j64, in0=Cn["lq1"], in1=Cn["lk1"], op=ALU.mult), reads=["consts"], writes=["j64"])
    P.op("dve", lambda e: e.reduce_sum(out=t1, in_=j64, axis=AX.X), reads=["j64"], writes=["t1"])
    P.op("dve", lambda e: e.tensor_tensor(out=j64, in0=Cn["lq2"], in1=Cn["lk2"], op=ALU.mult), reads=["consts", "t1"], writes=["j64"])
    P.op("dve", lambda e: e.reduce_sum(out=t2, in_=j64, axis=AX.X), reads=["j64"], writes=["t2"])
    P.op("act", lambda e: e.activation(out=t1, in_=t1, func=AF.Exp), reads=["t1"], writes=["t1"])
    P.op("act", lambda e: e.activation(out=t2, in_=t2, func=AF.Exp), reads=["t2"], writes=["t2"])
    P.op("pool", lambda e: e.tensor_tensor(out=negl, in0=t2, in1=t1, op=ALU.subtract), reads=["t1", "t2"], writes=["negl"])
    P.op("pool", lambda e: e.tensor_scalar(out=negl, in0=negl, scalar1=-LAMBDA_INIT, scalar2=None, op0=ALU.add),
         reads=["negl"], writes=["negl"])
    subg8 = A.alloc([128], F32)
    P.op("pool", lambda e: e.tensor_scalar(out=subg8, in0=Cn["subg"], scalar1=1.0 - LAMBDA_INIT, scalar2=None, op0=ALU.mult),
         reads=["consts"], writes=["subg8"])
    pbo = A.alloc([4, 71], F32)
    P.op("pool", lambda e: e.tensor_scalar(out=pbo, in0=Cn["pb"], scalar1=Cn["maskv"][:, 0:1], scalar2=None, op0=ALU.add),
         reads=["consts"], writes=["pbo"])
    Cn["negl"], Cn["subg8"], Cn["pbo"] = negl, subg8, pbo


def attn_stage(B, jobs, Cn, sname, window=(None,) * 4):
    P, A, nc = B.P, B.A, B.nc
    TKmax = max(q["TK"] for q in jobs)
    NBmax = cdiv(TKmax, 128)
    A.push()
    KT = A.alloc([4, NBmax * 128], BF16)
    V1 = A.alloc([NBmax, 516], BF16)
    Wout = A.alloc([KC, D], BF16)
    gmb = A.alloc([D], F32)
    P.dma("c0", lambda e: e.dma_start(out=gmb, in_=B.inputs["gmb"]), writes=["gmb"])
    Cn["gmb"] = gmb
    attn_consts(B, Cn)
    mark = A.off
    wst = [A.alloc([D], F32) for _ in range(4)]
    B.prep_weight(B.inputs["wout"], D, D, Wout, None, wst, "Wout")
    P.barrier()
    A.off = mark
    QTILE = 512
    qt = [A.alloc([4, QTILE], BF16) for _ in range(2)]
    NPT = 4
    pt = [A.alloc([QTILE], BF16) for _ in range(NPT)]
    dtmp = [A.alloc([128], F32) for _ in range(2)]
    atok = [A.alloc([512], BF16) for _ in range(4)]
    attT = A.alloc([4, QTILE], BF16)
    lruT = [A.alloc([4, QTILE], BF16) for _ in range(2)]
    x1r = [A.alloc([D], F32) for _ in range(2)]
    ost = [A.alloc([D], F32)] * 2
    otmp = [A.alloc([128], F32) for _ in range(2)]
    ofin = [A.alloc([2, 129], F32) for _ in range(4)]
    junk = A.alloc([512], BF16)
    rl = [A.alloc([2], F32) for _ in range(4)]
    ssn = [A.alloc([1], F32) for _ in range(4)]
    rsn = [A.alloc([1], F32) for _ in range(4)]
    ssm = [A.alloc([1], F32) for _ in range(2)]
    ssm2 = [A.alloc([1], F32) for _ in range(2)]
    rsm = [A.alloc([1], F32) for _ in range(2)]
    ps = B.psum
    pb, pbo, db = Cn["pb"], Cn["pbo"], Cn["db"]
    st_c = [0]
    pt_c = [0]
    dt_c = [0]
    x_c = [0]
    o_c = [0]
    qb_c = [0]
    VCH = 16

    def run_job(TK, NQ, KT_scr, V_scr, QT_scr, LT_scr, x1_scr, x1_off, x2_dst, mask_other):
        NB = cdiv(TK, 128)
        KOFF = TK - NQ
        assert KOFF % 128 == 0
        nkof = lambda j: min(128, TK - 128 * j)
        for h in range(4):
            P.dma(sname + "K%d" % h, lambda e, h=h: e.dma_start(out=KT[:, h, 0:TK], in_=KT_scr[h, :, 0:TK]), writes=[("KT", h)])
        for ci, j0 in enumerate(range(0, NB, VCH)):
            j1 = min(NB, j0 + VCH)
            jf = min(j1, TK // 128)
            if jf > j0:
                P.dma(sname + "V%d" % (ci % 4), lambda e, j0=j0, jf=jf: e.dma_start(
                    out=V1[:, j0:jf, :], in_=V_scr[j0 * 128:jf * 128, :].rearrange("(j p) c -> p j c", p=128)),
                      writes=[("V1", ci)])
            if jf < j1:
                nk = nkof(jf)
                P.dma(sname + "V%d" % (ci % 4), lambda e, jf=jf, nk=nk: e.dma_start(
                    out=V1[:nk, jf, :], in_=V_scr[jf * 128:jf * 128 + nk, :]), writes=[("V1", ci)])
        vid = lambda j: ("V1", j // VCH)

        tiles = []
        q0 = 0
        while q0 < NQ:
            nq = min(QTILE, NQ - q0)
            tiles.append((q0, nq))
            q0 += nq

        LOOK = int(os.environ.get("K_LOOK", "3"))
        dq = []

        def push2(fn):
            dq.append(fn)
            while len(dq) > LOOK:
                dq.pop(0)()

        def load_q(ti):
            q0, nq = tiles[ti]
            b = qb_c[0] % 2
            qb_c[0] += 1
            for h in range(4):
                P.dma(sname + "q%d" % b, lambda e, b=b, h=h, q0=q0, nq=nq: e.dma_start(
                    out=qt[b][:, h, 0:nq], in_=QT_scr[h, :, q0:q0 + nq]), writes=[("qt", b, h)])
            def ld(b=b, q0=q0, nq=nq):
                P.dma(sname + "l%d" % b, lambda e: e.dma_start(
                    out=lruT[b][:, :, 0:nq], in_=LT_scr[:, :, q0:q0 + nq].rearrange("g p t -> p g t")), writes=[("lruT", b)])
            push2(ld)
            return b

        def tile_stream(ti, q0, nq, b):
            nsb = cdiv(nq, 128)
            nqs_of = lambda s: min(128, nq - 128 * s)
            jb = (KOFF + q0) // 128
            nb_next = load_q(ti + 1) if ti + 1 < len(tiles) else None
            for h in range(4):
                persub = (h == 0)
                W = window[h]
                jlo = 0 if W is None else max(0, jb - W)
                first_in_bank = [True] * 4
                for j in range(jlo, jb + nsb):
                    nk = nkof(j)
                    rel = j - jb
                    s_lo = max(0, rel)
                    c0 = s_lo * 128
                    tab = pbo if (mask_other and j * 128 < KOFF) else pb
                    for c in range(2):
                        bi = st_c[0] % 4
                        st_c[0] += 1
                        stb = ps[bi]
                        bid = ("bank", bi)
                        P.op("pe", lambda e, stb=stb, c=c, h=h, j=j, c0=c0, nq=nq, b=b, nk=nk: e.matmul(
                            stb[:nk, c0:nq], lhsT=KT[c * 64:(c + 1) * 64, h, j * 128:j * 128 + nk],
                            rhs=qt[b][c * 64:(c + 1) * 64, h, c0:nq], start=True, stop=True),
                             reads=[("KT", h), ("qt", b, h)], writes=[bid])
                        pi = pt_c[0] % NPT
                        pt_c[0] += 1
                        ptb = pt[pi]
                        pid = ("pt", pi)
                        c1 = c0
                        if rel >= 0:
                            nqs = nqs_of(rel)
                            di = dt_c[0] % 2
                            dt_c[0] += 1
                            P.op("dve", lambda e, stb=stb, di=di, h=h, c0=c0, nk=nk, nqs=nqs: e.scalar_tensor_tensor(
                                out=dtmp[di][:nk, :nqs], in0=stb[:nk, c0:c0 + nqs], scalar=0.125, in1=db[:nk, h, 0:nqs],
                                op0=ALU.mult, op1=ALU.add), reads=[bid, "consts"], writes=[("dtmp", di)])
                            bconst = 0.0 if persub else SLOPES[h] * 128.0 * rel
                            P.op("act", lambda e, ptb=ptb, di=di, c0=c0, bconst=bconst, nk=nk, nqs=nqs: e.activation(
                                out=ptb[:nk, c0:c0 + nqs], in_=dtmp[di][:nk, :nqs], func=AF.Exp, bias=bconst),
                                 reads=[("dtmp", di)], writes=[pid])
                            c1 = c0 + 128
                        if c1 < nq:
                            if persub:
                                for s in range(c1 // 128, nsb):
                                    dj = jb + s - j
                                    ce = s * 128 + nqs_of(s)
                                    P.op("act", lambda e, ptb=ptb, stb=stb, s=s, ce=ce, dj=dj, tab=tab, h=h, nk=nk: e.activation(
                                        out=ptb[:nk, s * 128:ce], in_=stb[:nk, s * 128:ce], func=AF.Exp,
                                        bias=tab[:nk, h, dj + 3:dj + 4], scale=0.125),
                                         reads=[bid, "pbo"], writes=[pid])
                            else:
                                dj = jb - j
                                P.op("act", lambda e, ptb=ptb, stb=stb, c1=c1, nq=nq, dj=dj, tab=tab, h=h, nk=nk: e.activation(
                                    out=ptb[:nk, c1:nq], in_=stb[:nk, c1:nq], func=AF.Exp,
                                    bias=tab[:nk, h, dj + 3:dj + 4], scale=0.125),
                                     reads=[bid, "pbo"], writes=[pid])

                        def pv(ptb=ptb, pid=pid, c=c, j=j, h=h, nk=nk, s_lo=s_lo, fib=first_in_bank, jb=jb, nsb=nsb, nqs_of=nqs_of):
                            for s in range(s_lo, nsb):
                                ob = ps[4 + s]
                                st_flag = fib[s]
                                fib[s] = False
                                nqs = nqs_of(s)
                                last = (j == jb + s)
                                P.op("pe", lambda e, ob=ob, ptb=ptb, s=s, c=c, j=j, h=h, st_flag=st_flag, nk=nk, nqs=nqs, last=last: e.matmul(
                                    ob[:nqs, c * 256:c * 256 + 129], lhsT=ptb[:nk, s * 128:s * 128 + nqs],
                                    rhs=V1[:nk, j, h * 129:(h + 1) * 129], start=st_flag, stop=last,
                                    skip_group_check=True),
                                     reads=[pid, vid(j)], writes=[("bank", 4 + s)])
                        push2(pv)
                push2(lambda h=h, nsb=nsb, nqs_of=nqs_of: finalize(h, nsb, nqs_of))
            push2(lambda ti=ti, q0=q0, nq=nq, b=b, nsb=nsb, nqs_of=nqs_of: tail(q0, nq, b, nsb, nqs_of))
            return nb_next

        def finalize(h, nsb, nqs_of):
            for s in range(nsb):
                n = nqs_of(s)
                ob = ps[4 + s]
                oid = ("bank", 4 + s)
                of = ofin[s]
                fid = ("ofin", s)
                P.op("dve", lambda e, ob=ob, of=of, n=n: e.tensor_copy(
                    out=of[:n, :, :], in_=ob[:n, :].rearrange("p (c x) -> p c x", c=2)[:, :, 0:129]),
                     reads=[oid], writes=[fid])
                r2 = rl[s]
                P.op("dve", lambda e, of=of, r2=r2, n=n: e.reciprocal(out=r2[:n, :], in_=of[:n, :, 128]),
                     reads=[fid], writes=[("rl", s)])
                P.op("pool", lambda e, r2=r2, n=n: e.tensor_tensor(out=r2[:n, 1:2], in0=r2[:n, 1:2], in1=Cn["negl"][:n, :], op=ALU.mult),
                     reads=[("rl", s), "negl"], writes=[("rl", s)])
                oi = o_c[0] % 2
                o_c[0] += 1
                ot = otmp[oi]
                otid = ("otmp", oi)
                P.op("dve", lambda e, of=of, ot=ot, r2=r2, n=n: e.tensor_scalar(
                    out=ot[:n, :], in0=of[:n, 0, 0:128], scalar1=r2[:n, 0:1], scalar2=None, op0=ALU.mult),
                     reads=[fid, ("rl", s)], writes=[otid])
                P.op("dve", lambda e, of=of, ot=ot, r2=r2, n=n: e.scalar_tensor_tensor(
                    out=ot[:n, :], in0=of[:n, 1, 0:128], scalar=r2[:n, 1:2], in1=ot[:n, :], op0=ALU.mult, op1=ALU.add),
                     reads=[fid, ("rl", s), otid], writes=[otid])
                P.op("act", lambda e, ot=ot, s=s, n=n: e.activation(out=junk[:n, 0:128], in_=ot[:n, :], func=AF.Square,
                                                               accum_out=ssn[s][:n, :]),
                     reads=[otid], writes=[("ssn", s), "junkA"])
                P.op("pool", lambda e, s=s, n=n: e.tensor_scalar(out=ssn[s][:n, :], in0=ssn[s][:n, :], scalar1=1.0 / 128, scalar2=EPS,
                                                            op0=ALU.mult, op1=ALU.add), reads=[("ssn", s)], writes=[("ssn", s)])
                P.op("pool", lambda e, s=s, n=n: e.tensor_tensor(out=rsn[s][:n, :], in0=ssn[s][:n, :], in1=B.c_mhalf[:n, :], op=ALU.pow),
                     reads=[("ssn", s)], writes=[("rsn", s)])
                P.op("dve", lambda e, ot=ot, s=s, h=h, n=n: e.scalar_tensor_tensor(
                    out=atok[s][:n, h * 128:(h + 1) * 128], in0=ot[:n, :], scalar=rsn[s][:n, 0:1], in1=Cn["subg8"][:n, :],
                    op0=ALU.mult, op1=ALU.mult), reads=[otid, ("rsn", s), "subg8"], writes=[("atok", s, h)])

        def tail(q0, nq, b, nsb, nqs_of):
            ps_tr = ps[0][:, :].bitcast(BF16)
            for s in range(nsb):
                n = nqs_of(s)
                for h in range(4):
                    P.op("pe", lambda e, s=s, h=h, n=n: e.transpose(
                        out=ps_tr[:, h * 128:h * 128 + n], in_=atok[s][:n, h * 128:(h + 1) * 128], identity=B.ident[:n, :n]),
                         reads=[("atok", s, h)], writes=[("bank", 0)])
                P.op("act", lambda e, s=s, n=n: e.activation(
                    out=attT[:, :, s * 128:s * 128 + n], in_=ps_tr[:, 0:512].rearrange("p (a b) -> p a b", a=4)[:, :, 0:n],
                    func=AF.Copy), reads=[("bank", 0)], writes=[("attT", s)])
            for s in range(nsb):
                n = nqs_of(s)
                mo = (ps[1], ps[2])
                for half in range(2):
                    for kk in range(8):
                        src = attT if kk < 4 else lruT[b]
                        P.op("pe", lambda e, half=half, kk=kk, src=src, s=s, n=n, mo=mo: e.matmul(
                            mo[half][:n, :], lhsT=src[:, kk % 4, s * 128:s * 128 + n],
                            rhs=Wout[:, kk, half * 512:(half + 1) * 512], start=(kk == 0), stop=(kk == 7)),
                             reads=[("attT", s), ("lruT", b), ("Wout", kk)], writes=[("bank", 1 + half)])
                sl = s % 2
                P.op("act", lambda e, sl=sl, n=n: e.activation(out=junk[:n, :], in_=ps[1][:n, :], func=AF.Square, accum_out=ssm[sl][:n, :]),
                     reads=[("bank", 1)], writes=["junkA", ("ssm", sl)])
                P.op("act", lambda e, sl=sl, n=n: e.activation(out=junk[:n, :], in_=ps[2][:n, :], func=AF.Square, accum_out=ssm2[sl][:n, :]),
                     reads=[("bank", 2)], writes=["junkA", ("ssm2", sl)])
                P.op("pool", lambda e, sl=sl, n=n: e.tensor_tensor(out=ssm[sl][:n, :], in0=ssm[sl][:n, :], in1=ssm2[sl][:n, :], op=ALU.add),
                     reads=[("ssm", sl), ("ssm2", sl)], writes=[("ssm", sl)])
                P.op("pool", lambda e, sl=sl, n=n: e.tensor_scalar(out=ssm[sl][:n, :], in0=ssm[sl][:n, :], scalar1=1.0 / D, scalar2=EPS,
                                                              op0=ALU.mult, op1=ALU.add), reads=[("ssm", sl)], writes=[("ssm", sl)])
                P.op("pool", lambda e, sl=sl, n=n: e.tensor_tensor(out=rsm[sl][:n, :], in0=ssm[sl][:n, :], in1=B.c_mhalf[:n, :], op=ALU.pow),
                     reads=[("ssm", sl)], writes=[("rsm", sl)])
                xi = x_c[0] % 2
                x_c[0] += 1
                a = q0 + s * 128
                P.dma(sname + "x%d" % xi, lambda e, xi=xi, a=a, n=n: e.dma_start(
                    out=x1r[xi][:n, :], in_=x1_scr[x1_off + a:x1_off + a + n, :]), writes=[("x1r", xi)])
                for half in range(2):
                    P.op("dve", lambda e, xi=xi, half=half, sl=sl, n=n: e.scalar_tensor_tensor(
                        out=ost[xi][:n, half * 512:(half + 1) * 512], in0=ps[1 + half][:n, :], scalar=rsm[sl][:n, 0:1],
                        in1=Cn["gmb"][:n, half * 512:(half + 1) * 512], op0=ALU.mult, op1=ALU.mult),
                         reads=[("bank", 1 + half), ("rsm", sl), "consts"], writes=[("ost", 0, half)])
                    P.op("pool", lambda e, xi=xi, half=half, n=n: e.tensor_tensor(
                        out=ost[xi][:n, half * 512:(half + 1) * 512], in0=ost[xi][:n, half * 512:(half + 1) * 512],
                        in1=x1r[xi][:n, half * 512:(half + 1) * 512], op=ALU.add),
                         reads=[("ost", 0, half), ("x1r", xi)], writes=[("ost", 0, half)])
                P.dma(sname + "o0", lambda e, xi=xi, a=a, n=n: e.dma_start(out=x2_dst[a:a + n, :], in_=ost[xi][:n, :]),
                      reads=[("ost", 0, 0), ("ost", 0, 1)])

        bcur = load_q(0)
        for ti, (q0, nq) in enumerate(tiles):
            bcur = tile_stream(ti, q0, nq, bcur)
        while dq:
            dq.pop(0)()

    for q in jobs:
        run_job(q["TK"], q["NQ"], q["KT"], q["V"], q["QT"], q["LT"], q["x1"], q["x1_off"], q["x2"], q["mask_other"])
    P.barrier()
    A.pop()


def attn_stage2(B, jobs, Cn, sname, window=(None,) * 4):
    P, A, nc = B.P, B.A, B.nc
    TKmax = max(q["TK"] for q in jobs)
    NBmax = cdiv(TKmax, 128)
    A.push()
    Wout = A.alloc([KC, D], BF16)
    if Cn.get("wout_ready"):
        assert A.off == Cn["wout_off"], (A.off, Cn["wout_off"])
    KT = A.alloc([4, NBmax * 128], BF16)
    V1 = A.alloc([NBmax, 516], BF16)
    VCH = 16
    kv_done = {}

    def issue_kv(TK, KT_scr, V_scr):
        kv_done[id(KT_scr)] = True
        NB = cdiv(TK, 128)
        for h in range(4):
            P.dma(sname + "K%d" % h, lambda e, h=h: e.dma_start(out=KT[:, h, 0:TK], in_=KT_scr[h, :, 0:TK]), writes=[("KT", h)])
        for ci, j0 in enumerate(range(0, NB, VCH)):
            j1 = min(NB, j0 + VCH)
            jf = min(j1, TK // 128)
            if jf > j0:
                P.dma(sname + "V%d" % (ci % 4), lambda e, j0=j0, jf=jf: e.dma_start(
                    out=V1[:, j0:jf, :], in_=V_scr[j0 * 128:jf * 128, :].rearrange("(j p) c -> p j c", p=128)),
                      writes=[("V1", ci)])
            if jf < j1:
                nk = min(128, TK - 128 * jf)
                P.dma(sname + "V%d" % (ci % 4), lambda e, jf=jf, nk=nk: e.dma_start(
                    out=V1[:nk, jf, :], in_=V_scr[jf * 128:jf * 128 + nk, :]), writes=[("V1", ci)])

    issue_kv(jobs[0]["TK"], jobs[0]["KT"], jobs[0]["V"])
    gmb = A.alloc([D], F32)
    P.dma("c0", lambda e: e.dma_start(out=gmb, in_=B.inputs["gmb"]), writes=["gmb"])
    Cn["gmb"] = gmb
    attn_consts(B, Cn)
    g8col = A.alloc([1], F32)
    P.dma("c0", lambda e: e.dma_start(out=g8col, in_=B.inputs["subg_col"]), writes=["g8col"])
    P.op("pool", lambda e: e.tensor_scalar(out=g8col, in0=g8col, scalar1=1.0 - LAMBDA_INIT, scalar2=None, op0=ALU.mult),
         reads=["g8col"], writes=["g8col"])
    ones_bf = A.alloc([128], BF16)
    ones_f = A.alloc([128], F32)
    P.op("pool", lambda e: e.memset(ones_bf, 1.0), writes=["ones_bf"])
    P.op("pool", lambda e: e.memset(ones_f, 1.0), writes=["ones_f"])
    if not Cn.get("wout_ready"):
        mark = A.off
        wst = [A.alloc([D], F32) for _ in range(4)]
        B.prep_weight(B.inputs["wout"], D, D, Wout, None, wst, "Wout")
        P.barrier()
        A.off = mark
    QTILE = 512
    qt = [A.alloc([4, QTILE], BF16) for _ in range(2)]
    NPT = 3
    pt = [A.alloc([2, QTILE], BF16) for _ in range(NPT)]
    dtmp = [A.alloc([2, 128], F32) for _ in range(2)]
    attT = A.alloc([4, QTILE], BF16)
    lruT = [A.alloc([4, QTILE], BF16) for _ in range(2)]
    x1r = [A.alloc([D], F32) for _ in range(2)]
    ost = A.alloc([D], F32)
    rec0 = A.alloc([QTILE], F32)
    rec1 = A.alloc([QTILE], F32)
    o_sb = A.alloc([QTILE], F32)
    o_1 = A.alloc([QTILE], F32)
    junk = A.alloc([512], BF16)
    ssm = [A.alloc([1], F32) for _ in range(2)]
    ssm2 = [A.alloc([1], F32) for _ in range(2)]
    rsm = [A.alloc([1], F32) for _ in range(2)]
    ps = B.psum
    psall = B.psum_all
    stpair = [psall[:, 0:1024].rearrange("p (c x) -> p c x", c=2), psall[:, 1024:2048].rearrange("p (c x) -> p c x", c=2)]
    OTb = (ps[4], ps[5])
    Lb = (ps[6], ps[7])
    pb, pbo, db = Cn["pb"], Cn["pbo"], Cn["db"]
    st_c = [0]
    pt_c = [0]
    dt_c = [0]
    x_c = [0]
    qb_c = [0]

    def take_pair():
        pi = st_c[0] % 2
        st_c[0] += 1
        return pi, [("bank", 2 * pi), ("bank", 2 * pi + 1)]

    def run_job(TK, NQ, KT_scr, V_scr, QT_scr, LT_scr, x1_scr, x1_off, x2_dst, mask_other):
        NB = cdiv(TK, 128)
        KOFF = TK - NQ
        assert KOFF % 128 == 0
        nkof = lambda j: min(128, TK - 128 * j)
        if not kv_done.get(id(KT_scr)):
            issue_kv(TK, KT_scr, V_scr)
        vid = lambda j: ("V1", j // VCH)
        tiles = []
        q0 = 0
        while q0 < NQ:
            nq = min(QTILE, NQ - q0)
            tiles.append((q0, nq))
            q0 += nq
        LOOK = int(os.environ.get("K_LOOK2", "2"))
        dq = []

        late = []

        def tick_late(force=False):
            for it in late:
                it[0] -= 1
            while late and (force or late[0][0] <= 0):
                late.pop(0)[1]()

        def push2(fn):
            dq.append(fn)
            while len(dq) > LOOK:
                dq.pop(0)()
                tick_late()

        def load_q(ti):
            q0, nq = tiles[ti]
            b = qb_c[0] % 2
            qb_c[0] += 1
            for h in range(4):
                P.dma(sname + "q%d" % b, lambda e, b=b, h=h, q0=q0, nq=nq: e.dma_start(
                    out=qt[b][:, h, 0:nq], in_=QT_scr[h, :, q0:q0 + nq]), writes=[("qt", b, h)])

            def ld(b=b, q0=q0, nq=nq):
                P.dma(sname + "l%d" % b, lambda e: e.dma_start(
                    out=lruT[b][:, :, 0:nq], in_=LT_scr[:, :, q0:q0 + nq].rearrange("g p t -> p g t")), writes=[("lruT", b)])
            push2(ld)
            return b

        def tile_stream(ti, q0, nq, b):
            nsb = cdiv(nq, 128)
            nqs_of = lambda s: min(128, nq - 128 * s)
            jb = (KOFF + q0) // 128
            nb_next = load_q(ti + 1) if ti + 1 < len(tiles) else None
            for h in range(4):
                persub = (h == 0)
                W = window[h]
                jlo = 0 if W is None else max(0, jb - W)
                first = [True]
                jlast = jb + nsb - 1
                for j in range(jlo, jb + nsb):
                    nk = nkof(j)
                    rel = j - jb
                    s_lo = max(0, rel)
                    c0 = s_lo * 128
                    tab = pbo if (mask_other and j * 128 < KOFF) else pb
                    pi, bids = take_pair()
                    stp = stpair[pi]
                    for c in range(2):
                        P.op("pe", lambda e, stp=stp, c=c, h=h, j=j, c0=c0, nq=nq, b=b, nk=nk: e.matmul(
                            stp[:nk, c, c0:nq], lhsT=KT[c * 64:(c + 1) * 64, h, j * 128:j * 128 + nk],
                            rhs=qt[b][c * 64:(c + 1) * 64, h, c0:nq], start=True, stop=True),
                             reads=[("KT", h), ("qt", b, h)], writes=[bids[c]])
                    ri = pt_c[0] % NPT
                    pt_c[0] += 1
                    ptb = pt[ri]
                    pid = ("pt", ri)
                    c1 = c0
                    if rel >= 0:
                        nqs = nqs_of(rel)
                        di = dt_c[0] % 2
                        dt_c[0] += 1
                        for c in range(2):
                            P.op("dve", lambda e, stp=stp, di=di, h=h, c0=c0, nk=nk, nqs=nqs, c=c: e.scalar_tensor_tensor(
                                out=dtmp[di][:nk, c, :nqs], in0=stp[:nk, c, c0:c0 + nqs], scalar=0.125, in1=db[:nk, h, 0:nqs],
                                op0=ALU.mult, op1=ALU.add), reads=[bids[c], "consts"], writes=[("dtmp", di, c)])
                        bconst = 0.0 if persub else SLOPES[h] * 128.0 * rel
                        P.op("act", lambda e, ptb=ptb, di=di, c0=c0, bconst=bconst, nk=nk, nqs=nqs: e.activation(
                            out=ptb[:nk, :, c0:c0 + nqs], in_=dtmp[di][:nk, :, :nqs], func=AF.Exp, bias=bconst),
                             reads=[("dtmp", di, 0), ("dtmp", di, 1)], writes=[pid])
                        c1 = c0 + 128
                    if c1 < nq:
                        if persub:
                            for s in range(c1 // 128, nsb):
                                dj = jb + s - j
                                ce = s * 128 + nqs_of(s)
                                P.op("act", lambda e, ptb=ptb, stp=stp, s=s, ce=ce, dj=dj, tab=tab, h=h, nk=nk: e.activation(
                                    out=ptb[:nk, :, s * 128:ce], in_=stp[:nk, :, s * 128:ce], func=AF.Exp,
                                    bias=tab[:nk, h, dj + 3:dj + 4], scale=0.125),
                                     reads=bids + ["pbo"], writes=[pid])
                        else:
                            dj = jb - j
                            P.op("act", lambda e, ptb=ptb, stp=stp, c1=c1, nq=nq, dj=dj, tab=tab, h=h, nk=nk: e.activation(
                                out=ptb[:nk, :, c1:nq], in_=stp[:nk, :, c1:nq], func=AF.Exp,
                                bias=tab[:nk, h, dj + 3:dj + 4], scale=0.125),
                                 reads=bids + ["pbo"], writes=[pid])

                    def pv(ptb=ptb, pid=pid, j=j, h=h, nk=nk, c0=c0, nq=nq, first=first, last=(j == jlast)):
                        st_flag = first[0]
                        first[0] = False
                        for c in range(2):
                            P.op("pe", lambda e, c=c: e.matmul(
                                OTb[c][:, c0:nq], lhsT=V1[:nk, j, h * 129:h * 129 + 128], rhs=ptb[:nk, c, c0:nq],
                                start=st_flag, stop=last), reads=[pid, vid(j)], writes=[("bank", 4 + c)])
                        for c in range(2):
                            P.op("pe", lambda e, c=c: e.matmul(
                                Lb[c][:, c0:nq], lhsT=ones_bf[:nk, :], rhs=ptb[:nk, c, c0:nq],
                                start=st_flag, stop=last), reads=[pid, "ones_bf"], writes=[("bank", 6 + c)])
                    push2(pv)
                push2(lambda h=h, nq=nq: finalize(h, nq))
            def tail_flush(q0=q0, nq=nq, b=b, nsb=nsb, nqs_of=nqs_of):
                tick_late(force=True)
                tail(q0, nq, b, nsb, nqs_of)
            push2(tail_flush)
            return nb_next

        def finalize(h, nq):
            tick_late(force=True)
            P.op("dve", lambda e: e.tensor_copy(out=rec0[:, 0:nq], in_=Lb[0][:, 0:nq]), reads=[("bank", 6)], writes=["rec0"])
            P.op("act", lambda e: e.activation(out=rec1[:, 0:nq], in_=Lb[1][:, 0:nq], func=AF.Copy), reads=[("bank", 7)], writes=["rec1"])
            P.op("dve", lambda e: e.tensor_copy(out=o_sb[:, 0:nq], in_=OTb[0][:, 0:nq]), reads=[("bank", 4)], writes=["o_sb"])
            P.op("act", lambda e: e.activation(out=o_1[:, 0:nq], in_=OTb[1][:, 0:nq], func=AF.Copy), reads=[("bank", 5)], writes=["o_1"])
            P.op("dve", lambda e: e.reciprocal(out=rec0[:, 0:nq], in_=rec0[:, 0:nq]), reads=["rec0"], writes=["rec0"])
            P.op("dve", lambda e: e.reciprocal(out=rec1[:, 0:nq], in_=rec1[:, 0:nq]), reads=["rec1"], writes=["rec1"])
            P.op("dve", lambda e: e.tensor_tensor(out=rec0[:, 0:nq], in0=o_sb[:, 0:nq], in1=rec0[:, 0:nq], op=ALU.mult),
                 reads=["o_sb", "rec0"], writes=["rec0"])
            P.op("dve", lambda e: e.tensor_tensor(out=rec1[:, 0:nq], in0=o_1[:, 0:nq], in1=rec1[:, 0:nq], op=ALU.mult),
                 reads=["o_1", "rec1"], writes=["rec1"])
            P.op("dve", lambda e: e.scalar_tensor_tensor(out=o_sb[:, 0:nq], in0=rec1[:, 0:nq], scalar=Cn["negl"][:, 0:1],
                                                         in1=rec0[:, 0:nq], op0=ALU.mult, op1=ALU.add),
                 reads=["rec0", "rec1", "negl"], writes=["o_sb"])
            P.op("pool", lambda e: e.tensor_tensor(out=rec0[:, 0:nq], in0=o_sb[:, 0:nq], in1=o_sb[:, 0:nq], op=ALU.mult),
                 reads=["o_sb"], writes=["rec0"])
            late.append([int(os.environ.get("K_LATE", "5")), lambda: finalize_b(h, nq)])

        def finalize_b(h, nq):
            pi, bids = take_pair()
            ssb = ps[2 * pi]
            P.op("pe", lambda e: e.matmul(ssb[:, 0:nq], lhsT=ones_f[:, :], rhs=rec0[:, 0:nq], start=True, stop=True),
                 reads=["rec0", "ones_f"], writes=bids)
            P.op("act", lambda e: e.activation(out=rec1[:, 0:nq], in_=ssb[:, 0:nq], func=AF.Ln, scale=1.0 / 128, bias=EPS),
                 reads=[bids[0]], writes=["rec1"])
            P.op("act", lambda e: e.activation(out=rec1[:, 0:nq], in_=rec1[:, 0:nq], func=AF.Exp, scale=-0.5),
                 reads=["rec1"], writes=["rec1"])
            P.op("dve", lambda e: e.scalar_tensor_tensor(out=attT[:, h, 0:nq], in0=o_sb[:, 0:nq], scalar=g8col[:, 0:1],
                                                         in1=rec1[:, 0:nq], op0=ALU.mult, op1=ALU.mult),
                 reads=["o_sb", "rec1", "g8col"], writes=[("attT", h)])

        def tail(q0, nq, b, nsb, nqs_of):
            for s in range(nsb):
                n = nqs_of(s)
                pi, bids = take_pair()
                mo = (ps[2 * pi], ps[2 * pi + 1])
                for half in range(2):
                    for kk in range(8):
                        src = attT if kk < 4 else lruT[b]
                        P.op("pe", lambda e, half=half, kk=kk, src=src, s=s, n=n, mo=mo: e.matmul(
                            mo[half][:n, :], lhsT=src[:, kk % 4, s * 128:s * 128 + n],
                            rhs=Wout[:, kk, half * 512:(half + 1) * 512], start=(kk == 0), stop=(kk == 7)),
                             reads=[("attT", kk % 4), ("lruT", b), ("Wout", kk)], writes=[bids[half]])
                sl = s % 2
                P.op("act", lambda e, sl=sl, n=n, mo=mo: e.activation(out=junk[:n, :], in_=mo[0][:n, :], func=AF.Square, accum_out=ssm[sl][:n, :]),
                     reads=[bids[0]], writes=["junkA", ("ssm", sl)])
                P.op("act", lambda e, sl=sl, n=n, mo=mo: e.activation(out=junk[:n, :], in_=mo[1][:n, :], func=AF.Square, accum_out=ssm2[sl][:n, :]),
                     reads=[bids[1]], writes=["junkA", ("ssm2", sl)])
                P.op("pool", lambda e, sl=sl, n=n: e.tensor_tensor(out=ssm[sl][:n, :], in0=ssm[sl][:n, :], in1=ssm2[sl][:n, :], op=ALU.add),
                     reads=[("ssm", sl), ("ssm2", sl)], writes=[("ssm", sl)])
                P.op("pool", lambda e, sl=sl, n=n: e.tensor_scalar(out=ssm[sl][:n, :], in0=ssm[sl][:n, :], scalar1=1.0 / D, scalar2=EPS,
                                                              op0=ALU.mult, op1=ALU.add), reads=[("ssm", sl)], writes=[("ssm", sl)])
                P.op("pool", lambda e, sl=sl, n=n: e.tensor_tensor(out=rsm[sl][:n, :], in0=ssm[sl][:n, :], in1=B.c_mhalf[:n, :], op=ALU.pow),
                     reads=[("ssm", sl)], writes=[("rsm", sl)])
                xi = x_c[0] % 2
                x_c[0] += 1
                a = q0 + s * 128
                P.dma(sname + "x%d" % xi, lambda e, xi=xi, a=a, n=n: e.dma_start(
                    out=x1r[xi][:n, :], in_=x1_scr[x1_off + a:x1_off + a + n, :]), writes=[("x1r", xi)])
                for half in range(2):
                    P.op("dve", lambda e, half=half, sl=sl, n=n, mo=mo: e.scalar_tensor_tensor(
                        out=ost[:n, half * 512:(half + 1) * 512], in0=mo[half][:n, :], scalar=rsm[sl][:n, 0:1],
                        in1=Cn["gmb"][:n, half * 512:(half + 1) * 512], op0=ALU.mult, op1=ALU.mult),
                         reads=[bids[half], ("rsm", sl), "consts"], writes=[("ost", half)])
                    P.op("pool", lambda e, xi=xi, half=half, n=n: e.tensor_tensor(
                        out=ost[:n, half * 512:(half + 1) * 512], in0=ost[:n, half * 512:(half + 1) * 512],
                        in1=x1r[xi][:n, half * 512:(half + 1) * 512], op=ALU.add),
                         reads=[("ost", half), ("x1r", xi)], writes=[("ost", half)])
                P.dma(sname + "o0", lambda e, a=a, n=n: e.dma_start(out=x2_dst[a:a + n, :], in_=ost[:n, :]),
                      reads=[("ost", 0), ("ost", 1)])

        bcur = load_q(0)
        for ti, (q0, nq) in enumerate(tiles):
            bcur = tile_stream(ti, q0, nq, bcur)
        while dq:
            dq.pop(0)()
        tick_late(force=True)

    for q in jobs:
        run_job(q["TK"], q["NQ"], q["KT"], q["V"], q["QT"], q["LT"], q["x1"], q["x1_off"], q["x2"], q["mask_other"])
    P.barrier()
    A.pop()


def cache_prep_ops(B, ck, cv, KT_s, V_s, PAST, sname):
    P, A = B.P, B.A
    NSTEP = PAST // 512
    kin = [A.alloc([4, 512], F32) for _ in range(2)]
    kbf = [A.alloc([4, 512], BF16) for _ in range(2)]
    kT = [A.alloc([4, 512], BF16) for _ in range(2)]
    vin = [A.alloc([4, 512], F32) for _ in range(2)]
    vb = [A.alloc([4, 516], BF16) for _ in range(2)]
    for i in range(2):
        P.op("pool", lambda e, i=i: e.memset(vb[i], 1.0), writes=[("cvb", i)])
    ps = B.psum
    for st in range(NSTEP):
        r = st % 2
        a = st * 512
        P.dma(sname + "k%d" % r, lambda e, r=r, a=a: e.dma_start(
            out=kin[r], in_=ck[a:a + 512, :].rearrange("(j p) c -> p j c", p=128)), writes=[("ckin", r)])
        P.dma(sname + "v%d" % r, lambda e, r=r, a=a: e.dma_start(
            out=vin[r], in_=cv[a:a + 512, :].rearrange("(j p) c -> p j c", p=128)), writes=[("cvin", r)])
        P.op("dve", lambda e, r=r: e.tensor_copy(out=kbf[r], in_=kin[r]), reads=[("ckin", r)], writes=[("ckbf", r)])
        for jj in range(4):
            bank = ps[4 + (st * 4 + jj) % 4]
            bid = ("bank", 4 + (st * 4 + jj) % 4)
            ps_tr = bank[:, :].bitcast(BF16)
            for h in range(4):
                P.op("pe", lambda e, ps_tr=ps_tr, h=h, r=r, jj=jj: e.transpose(
                    out=ps_tr[:, h * 128:(h + 1) * 128], in_=kbf[r][:, jj, h * 128:(h + 1) * 128], identity=B.ident),
                     reads=[("ckbf", r)], writes=[bid])
            P.op("act", lambda e, ps_tr=ps_tr, r=r, jj=jj: e.activation(
                out=kT[r][:, :, jj * 128:(jj + 1) * 128], in_=ps_tr[:, 0:512].rearrange("p (a b) -> p a b", a=4), func=AF.Copy),
                 reads=[bid], writes=[("ckT", r, jj)])
        P.dma(sname + "ko%d" % r, lambda e, r=r, a=a: e.dma_start(
            out=KT_s[:, :, a:a + 512].rearrange("h p t -> p h t"), in_=kT[r]),
              reads=[("ckT", r, x) for x in range(4)])
        for jj in range(4):
            P.op("pool" if jj % 2 else "dve", lambda e, r=r, jj=jj: e.tensor_copy(
                out=vb[r][:, jj, :].rearrange("p (h d) -> p h d", h=4)[:, :, 0:128],
                in_=vin[r][:, jj, :].rearrange("p (h d) -> p h d", h=4)),
                 reads=[("cvin", r), ("cvb", r)], writes=[("cvb", r, jj)])
        P.dma(sname + "vo%d" % r, lambda e, r=r, a=a: e.dma_start(
            out=V_s[a:a + 512, :].rearrange("(j p) c -> p j c", p=128), in_=vb[r]),
              reads=[("cvb", r, x) for x in range(4)] + [("cvb", r)])


SMALL_INPUTS = [
    ("gma_col", [KC]), ("g1a_col", [KC]), ("g2a_col", [KC]),
    ("convw", [4, 4]), ("convb", [4]), ("brg", [4]), ("big", [4]), ("lam", [4]),
    ("flag", [1]), ("maskv", [1]),
]


def build_main(T_OTH, T_OWN, with_sample=True, window=(None,) * 4, stages=("A1", "A2", "C", "D")):
    if os.environ.get("K_WIN", "1") == "1":
        window = (4, 16, None, None)
    B = Builder(T_OTH, T_OWN)
    P = B.P
    T = T_OTH + T_OWN
    x = B.din("x", [T, D])
    for nm, shp in (("f1g", [D, DFF]), ("f1u", [D, DFF]), ("f1d", [DFF, D]),
                    ("f2g", [D, DFF]), ("f2u", [D, DFF]), ("f2d", [DFF, D]),
                    ("win", [D, 2560]), ("wout", [D, D])):
        B.din(nm, shp)
    for nm, shp in (("g1b_bc", [128, D]), ("g2b_bc", [128, D]), ("gmb", [128, D]),
                    ("wr_bd", [128, 4, 128]), ("wi_bd", [128, 4, 128]),
                    ("pb", [128, 4, 71]), ("db", [128, 4, 128]), ("subg", [128, 128]), ("subg_col", [128, 1]),
                    ("lq1", [128, 64]), ("lk1", [128, 64]), ("lq2", [128, 64]), ("lk2", [128, 64])):
        B.din(nm, shp)
    y = B.dout("y", [T_OWN, D])
    ko = B.dout("ko", [T_OWN, 512])
    vo = B.dout("vo", [T_OWN, 512])
    hl = B.dout("hl", [512])
    cb = B.dout("cb", [3, 512])
    x1_scr = B.dscr("x1_scr", [T, D])
    x2_scr = B.dscr("x2_scr", [T_OWN, D])
    KT_scr = B.dscr("KT_scr", [4, 128, T], BF16)
    V_scr = B.dscr("V_scr", [T, 516], BF16)
    QT_scr = B.dscr("QT_scr", [4, 128, T_OWN], BF16)
    LT_scr = B.dscr("LT_scr", [4, 128, T_OWN], BF16)
    S = {}
    NSAMP, PAST = 32, 4096
    if with_sample:
        S["xs"] = B.din("xs", [NSAMP, D])
        S["ck"] = B.din("ck", [PAST, 512])
        S["cv"] = B.din("cv", [PAST, 512])
        S["sh"] = B.din("sh", [512])
        S["sc"] = B.din("sc", [3, 512])
        S["ys"] = B.dout("ys", [NSAMP, D])
        S["kso"] = B.dout("kso", [NSAMP, 512])
        S["vso"] = B.dout("vso", [NSAMP, 512])
        S["hls"] = B.dout("hls", [512])
        S["cbs"] = B.dout("cbs", [3, 512])
        S["xs1"] = B.dscr("xs1_scr", [NSAMP, D])
        S["xs2"] = B.dscr("xs2_scr", [NSAMP, D])
        S["KT"] = B.dscr("KTs_scr", [4, 128, PAST + NSAMP], BF16)
        S["V"] = B.dscr("Vs_scr", [PAST + NSAMP, 516], BF16)
        S["QT"] = B.dscr("QTs_scr", [4, 128, NSAMP], BF16)
        S["LT"] = B.dscr("LTs_scr", [4, 128, NSAMP], BF16)
    B.consts()
    Cn = {}
    for nm, shp in SMALL_INPUTS:
        Cn[nm] = B.load_const(nm, shp)
    P.barrier()
    inp = B.inputs
    if "A1" in stages:
        segs = [(x, x1_scr, T)] + ([(S["xs"], S["xs1"], NSAMP)] if with_sample else [])
        B.ffn_stage(segs, inp["f1g"], inp["f1u"], inp["f1d"], Cn["g1a_col"], inp["g1b_bc"], "a")
    if with_sample and "A2" in stages:
        Cn["cache_prep"] = (S["ck"], S["cv"], S["KT"], S["V"], PAST, "p")
    if "A2" in stages:
        seqs = [dict(x1=x1_scr, T=T, T_OTH=T_OTH, KT=KT_scr, V=V_scr, QT=QT_scr, LT=LT_scr, ko=ko, vo=vo, hl=hl, cb=cb)]
        if with_sample:
            seqs.append(dict(x1=S["xs1"], T=NSAMP, T_OTH=0, KT=S["KT"], V=S["V"], QT=S["QT"], LT=S["LT"],
                             ko=S["kso"], vo=S["vso"], hl=S["hls"], cb=S["cbs"], h0=S["sh"], conv0=S["sc"], koff=PAST))
        mixer_in_stage(B, seqs, Cn, "m")
    if "C" in stages:
        jobs = [dict(TK=T, NQ=T_OWN, KT=KT_scr, V=V_scr, QT=QT_scr, LT=LT_scr, x1=x1_scr, x1_off=T_OTH, x2=x2_scr,
                     mask_other=True)]
        if with_sample:
            jobs.append(dict(TK=PAST + NSAMP, NQ=NSAMP, KT=S["KT"], V=S["V"], QT=S["QT"], LT=S["LT"], x1=S["xs1"],
                             x1_off=0, x2=S["xs2"], mask_other=False))
        (attn_stage2 if os.environ.get("K_ATT", "2") == "2" else attn_stage)(B, jobs, Cn, "c", window=window)
    if "D" in stages:
        segs = [(x2_scr, y, T_OWN)] + ([(S["xs2"], S["ys"], NSAMP)] if with_sample else [])
        B.ffn_stage(segs, inp["f2g"], inp["f2u"], inp["f2d"], Cn["g2a_col"], inp["g2b_bc"], "d")
    B.P.emit()
    return B


def _col(v, n):
    return np.ascontiguousarray(np.asarray(v, np.float32).reshape(n, 128).T)


def _bc(v):
    v = np.asarray(v, np.float32).reshape(1, -1)
    return np.ascontiguousarray(np.broadcast_to(v, (128, v.shape[1])))


def _block_diag(w):
    out = np.zeros((128, 4, 128), np.float32)
    for g in range(4):
        for hb in range(2):
            out[hb * 64:(hb + 1) * 64, g, hb * 64:(hb + 1) * 64] = w[2 * g + hb]
    return out


def _tables():
    k = np.arange(128, dtype=np.float64)
    pb = np.zeros((128, 4, 71), np.float32)
    db = np.zeros((128, 4, 128), np.float32)
    kk = k[:, None]
    qq = k[None, :]
    for h in range(4):
        sl = SLOPES[h]
        for dj in range(-3, 68):
            pb[:, h, dj + 3] = sl * (k - 128.0 * dj)
        v = np.where(kk <= qq, sl * kk, sl * (2 * qq - kk))
        v = np.where((kk // 64) > (qq // 64), NEG, v)
        db[:, h, :] = v
    return pb, db


def shared_inputs(inputs):
    import ml_dtypes
    g = lambda n: np.asarray(inputs[n], np.float32)
    pb, db = _tables()
    d = {
        "f1g": g("ffn1_w_gate")[0], "f1u": g("ffn1_w_up")[0], "f1d": g("ffn1_w_down")[0],
        "f2g": g("ffn2_w_gate")[0], "f2u": g("ffn2_w_up")[0], "f2d": g("ffn2_w_down")[0],
        "win": g("w_in")[0], "wout": g("w_out")[0],
        "g1b_bc": _bc(g("g_ffn1_post")[0]), "g2b_bc": _bc(g("g_ffn2_post")[0]), "gmb": _bc(g("g_mix_post")[0]),
        "wr_bd": _block_diag(g("w_rgate")[0]), "wi_bd": _block_diag(g("w_igate")[0]),
        "pb": pb, "db": db, "subg": _bc(g("subln_g")[0]), "subg_col": _col(g("subln_g")[0], 1),
        "lq1": _bc(g("lambda_q1")[0]), "lk1": _bc(g("lambda_k1")[0]),
        "lq2": _bc(g("lambda_q2")[0]), "lk2": _bc(g("lambda_k2")[0]),
        "gma_col": _col(g("g_mix_pre")[0], 8), "g1a_col": _col(g("g_ffn1_pre")[0], 8), "g2a_col": _col(g("g_ffn2_pre")[0], 8),
        "convw": np.ascontiguousarray(g("conv_w")[0].reshape(4, 4, 128).transpose(2, 1, 0)),
        "convb": _col(g("conv_b")[0], 4), "brg": _col(g("b_rgate")[0], 4), "big": _col(g("b_igate")[0], 4),
        "lam": _col(g("lru_lambda")[0], 4),
        "ident": np.eye(128).astype(ml_dtypes.bfloat16),
    }
    return d


_CACHE = {}


def kernel(**inputs):
    TH = 4096
    WITH_SAMPLE = bool(int(os.environ.get("K_SAMPLE", "1")))
    key = ("main", TH, WITH_SAMPLE)
    if key not in _CACHE:
        _CACHE[key] = build_main(TH, TH, with_sample=WITH_SAMPLE)
    B = _CACHE[key]
    sh = shared_inputs(inputs)
    xp = np.asarray(inputs["x_prompt"], np.float32)
    maps = []
    for c in range(8):
        b, r = c // 2, c % 2
        own = xp[b, r * TH:(r + 1) * TH]
        oth = xp[b, (1 - r) * TH:(2 - r) * TH]
        m = dict(sh)
        m["x"] = np.ascontiguousarray(np.concatenate([oth, own], 0))
        m["flag"] = np.full((128, 1), float(r), np.float32)
        m["maskv"] = np.full((128, 1), 0.0 if r == 1 else NEG, np.float32)
        if WITH_SAMPLE:
            m["xs"] = np.ascontiguousarray(np.asarray(inputs["x_sample"], np.float32)[c])
            m["ck"] = np.ascontiguousarray(np.asarray(inputs["cache_k"], np.float32)[0, c].reshape(4096, 512))
            m["cv"] = np.ascontiguousarray(np.asarray(inputs["cache_v"], np.float32)[0, c].reshape(4096, 512))
            m["sh"] = np.ascontiguousarray(np.asarray(inputs["state_lru_h"], np.float32)[0, c])
            m["sc"] = np.ascontiguousarray(np.asarray(inputs["state_conv"], np.float32)[0, c])
        m = {k: v for k, v in m.items() if k in B.inputs}
        maps.append(m)
    res = run_bass_kernel_spmd(B.nc, maps, core_ids=list(range(8))).results
    y = np.zeros((4, 8192, 1024), np.float32)
    kp = np.zeros((1, 4, 8192, 4, 2, 64), np.float32)
    vp = np.zeros((1, 4, 8192, 4, 128), np.float32)
    hp = np.zeros((1, 4, 512), np.float32)
    cp = np.zeros((1, 4, 3, 512), np.float32)
    ys = np.zeros((8, 32, 1024), np.float32)
    ks = np.zeros((1, 8, 32, 4, 2, 64), np.float32)
    vs = np.zeros((1, 8, 32, 4, 128), np.float32)
    hs = np.zeros((1, 8, 512), np.float32)
    cs = np.zeros((1, 8, 3, 512), np.float32)
    for c in range(8):
        b, r = c // 2, c % 2
        o = res[c]
        sl = slice(r * TH, (r + 1) * TH)
        y[b, sl] = o["y"]
        kp[0, b, sl] = o["ko"].reshape(TH, 4, 2, 64)
        vp[0, b, sl] = o["vo"].reshape(TH, 4, 128)
        if r == 1:
            hp[0, b] = o["hl"]
            cp[0, b] = o["cb"]
        if WITH_SAMPLE:
            ys[c] = o["ys"]
            ks[0, c] = o["kso"].reshape(32, 4, 2, 64)
            vs[0, c] = o["vso"].reshape(32, 4, 128)
            hs[0, c] = o["hls"]
            cs[0, c] = o["cbs"]
    return (y, ys, kp, vp, hp, cp, ks, vs, hs, cs)
```

```python
import numpy as np
import concourse.bass as bass
import concourse.mybir as mybir
from concourse.bass_utils import run_bass_kernel_spmd
from contextlib import ExitStack

F32 = mybir.dt.float32
BF16 = mybir.dt.bfloat16
AF = mybir.ActivationFunctionType
ALU = mybir.AluOpType
AX = mybir.AxisListType

D = 1024
DFF = 2816
NFF = DFF // 128
KC = D // 128
EPS = 1e-6
NEG = -1e30


class Op:
    __slots__ = ("eng", "fn", "deps", "sem", "inc", "val", "is_dma", "needs_inc")


class Prog:
    ENGS = ("pe", "act", "dve", "pool", "sp")

    def __init__(self, nc, stack):
        self.nc = nc
        self.stack = stack
        self.ops = {e: [] for e in self.ENGS}
        self.last_w = {}
        self.readers = {}
        self.barrier_ops = []
        self.esem = {e: stack.enter_context(nc.semaphore("s_" + e)) for e in ("pe", "act", "dve", "pool")}
        self.dma_sems = {}
        self.dma_last = {}
        self.n_sem = 4

    def dsem(self, name):
        if name not in self.dma_sems:
            self.dma_sems[name] = self.stack.enter_context(self.nc.semaphore("d_" + name))
            self.n_sem += 1
        return name

    PSUM_IDS = ("gu", "dn", "fm", "tm", "rg", "bank", "cv")

    def _is_psum(self, t):
        return t == "ps_tr" or (isinstance(t, tuple) and t[0] in self.PSUM_IDS)

    def _add(self, o, reads, writes):
        xr = [t for t in reads if self._is_psum(t)]
        if xr:
            reads = [t for t in reads if not self._is_psum(t)]
            writes = list(writes) + xr
        deps = set(self.barrier_ops)
        for t in reads:
            w = self.last_w.get(t)
            if w is not None:
                deps.add(w)
        for t in writes:
            w = self.last_w.get(t)
            if w is not None:
                deps.add(w)
            for r in self.readers.get(t, ()):
                deps.add(r)
        if o.eng == "pe" and not o.is_dma:
            deps = {d for d in deps if not (d.eng == "pe" and not d.is_dma)}
        deps = {(self.dma_last[d.sem] if d.is_dma else d) for d in deps}
        o.deps = deps
        for d in deps:
            d.needs_inc = True
        for t in reads:
            self.readers.setdefault(t, []).append(o)
        for t in writes:
            self.last_w[t] = o
            self.readers[t] = []
        self.ops[o.eng].append(o)

    def op(self, eng, fn, reads=(), writes=()):
        o = Op()
        o.eng = eng
        o.fn = fn
        o.is_dma = False
        o.needs_inc = False
        o.sem = None
        o.val = 0
        self._add(o, reads, writes)
        return o

    def dma(self, sem_name, fn, reads=(), writes=(), q="sp"):
        o = Op()
        o.eng = q
        o.fn = fn
        o.is_dma = True
        o.needs_inc = True
        o.sem = self.dsem(sem_name)
        o.val = 0
        self._add(o, reads, writes)
        self.dma_last[sem_name] = o
        return o

    def barrier(self):
        b = []
        for e in self.ENGS:
            for o in reversed(self.ops[e]):
                if not o.is_dma:
                    b.append(o)
                    break
        for o in self.dma_last.values():
            b.append(o)
        for o in b:
            o.needs_inc = True
        self.barrier_ops = b
        self.last_w = {}
        self.readers = {}

    def emit(self):
        nc = self.nc
        dcount = {}
        for e in self.ENGS:
            cnt = 0
            for o in self.ops[e]:
                if o.is_dma:
                    dcount[o.sem] = dcount.get(o.sem, 0) + 16
                    o.val = dcount[o.sem]
                elif o.needs_inc:
                    cnt += 1
                    o.val = cnt
        ops = self.ops
        esem = self.esem
        dsems = self.dma_sems

        def run(ename, eng):
            waited = {}
            for o in ops[ename]:
                need = {}
                for d in o.deps:
                    key = d.sem if d.is_dma else d.eng
                    if d.val > need.get(key, 0):
                        need[key] = d.val
                for key, v in need.items():
                    if waited.get(key, 0) < v:
                        sem = esem[key] if key in esem else dsems[key]
                        eng.wait_ge(sem, v)
                        waited[key] = v
                ins = o.fn(eng)
                if o.is_dma:
                    ins.then_inc(dsems[o.sem], 16)
                elif o.needs_inc:
                    ins.then_inc(esem[ename], 1)
            fin = {}
            for o in ops[ename]:
                if o.is_dma:
                    fin[o.sem] = max(fin.get(o.sem, 0), o.val)
            for s, v in fin.items():
                if waited.get(s, 0) < v:
                    eng.wait_ge(dsems[s], v)

        with nc.Block() as block:
            @block.tensor
            def _(e):
                run("pe", e)

            @block.scalar
            def _(e):
                run("act", e)

            @block.vector
            def _(e):
                run("dve", e)

            @block.gpsimd
            def _(e):
                run("pool", e)

            @block.sync
            def _(e):
                run("sp", e)


class Arena:
    def __init__(self, nc, stack, nbytes):
        self.t = stack.enter_context(nc.sbuf_tensor("arena", [128, nbytes // 4], F32))
        self.cap = nbytes
        self.off = 0
        self.marks = []

    def push(self):
        self.marks.append(self.off)

    def pop(self):
        self.off = self.marks.pop()

    def alloc(self, shape, dt):
        n = 1
        for s in shape:
            n *= s
        esz = 4 if dt == F32 else 2
        nb = (n * esz + 31) // 32 * 32
        assert self.off + nb <= self.cap, ("arena overflow", self.off, nb, self.cap)
        ap = self.t[:, self.off // 4:(self.off + nb) // 4]
        self.off += nb
        self.peak = max(getattr(self, 'peak', 0), self.off)
        if dt != F32:
            ap = ap.bitcast(dt)
        ap = ap[:, 0:n]
        if len(shape) == 2:
            ap = ap.rearrange("p (a b) -> p a b", a=shape[0])
        elif len(shape) == 3:
            ap = ap.rearrange("p (a b c) -> p a b c", a=shape[0], b=shape[1])
        return ap


import os
DBG = set(os.environ.get("KDBG", "").split(","))


def cdiv(a, b):
    return (a + b - 1) // b


class Builder:
    def __init__(self, T_OTH, T_OWN, n_samp=32, past=4096, stages="all"):
        self.T_OTH, self.T_OWN, self.NS, self.PAST = T_OTH, T_OWN, n_samp, past
        self.T = T_OTH + T_OWN
        self.stages = stages
        self.nc = bass.Bass("TRN2", target_bir_lowering=False)
        self.stack = ExitStack()
        self.P = Prog(self.nc, self.stack)
        self.A = Arena(self.nc, self.stack, 211456)
        self.inputs = {}
        self.outputs = {}
        nc = self.nc
        self.psum_all = self.stack.enter_context(nc.psum_tensor("psall", [128, 4096], F32))
        self.psum = [self.psum_all[:, i * 512:(i + 1) * 512] for i in range(8)]

    def din(self, name, shape, dt=F32):
        t = self.nc.dram_tensor(name, list(shape), dt, kind="ExternalInput").ap()
        self.inputs[name] = t
        return t

    def dout(self, name, shape, dt=F32):
        t = self.nc.dram_tensor(name, list(shape), dt, kind="ExternalOutput").ap()
        self.outputs[name] = t
        return t

    def dscr(self, name, shape, dt=F32):
        return self.nc.dram_tensor(name, list(shape), dt, kind="Internal").ap()

    def prep_weight(self, w_dram, K, N, dst, gcol, stage_bufs, tag, eng_cycle=("dve", "pool")):
        P = self.P
        nk = K // 128
        for kc in range(nk):
            sb = stage_bufs[kc % len(stage_bufs)]
            sid = ("wst", kc % len(stage_bufs))
            src = w_dram[kc * 128:(kc + 1) * 128, :]
            P.dma("wst%d" % (kc % len(stage_bufs)),
                  lambda e, sb=sb, src=src, N=N: e.dma_start(out=sb[:, 0:N], in_=src),
                  writes=[sid])
            ec = tuple(os.environ.get("K_PREP", "dve,act").split(","))
            en = ec[kc % len(ec)]
            if en == "act":
                if gcol is None:
                    P.op("act", lambda e, sb=sb, kc=kc, N=N, dst=dst: e.activation(out=dst[:, kc, :], in_=sb[:, 0:N], func=AF.Copy),
                         reads=[sid], writes=[(tag, kc)])
                else:
                    P.op("act", lambda e, sb=sb, kc=kc, N=N, dst=dst, gcol=gcol: e.activation(
                        out=dst[:, kc, :], in_=sb[:, 0:N], func=AF.Copy, scale=gcol[:, kc:kc + 1]),
                         reads=[sid, "consts"], writes=[(tag, kc)])
                continue
            if gcol is None:
                P.op(en, lambda e, sb=sb, kc=kc, N=N, dst=dst: e.tensor_copy(out=dst[:, kc, :], in_=sb[:, 0:N]),
                     reads=[sid], writes=[(tag, kc)])
            else:
                P.op(en, lambda e, sb=sb, kc=kc, N=N, dst=dst, gcol=gcol: e.tensor_scalar(
                    out=dst[:, kc, :], in0=sb[:, 0:N], scalar1=gcol[:, kc:kc + 1], scalar2=None, op0=ALU.mult),
                     reads=[sid, "consts"], writes=[(tag, kc)])

    def rstd_of(self, src_ap, np_, junk, ss, rstd, src_ids, tagid):
        P = self.P
        P.op("act", lambda e: e.activation(out=junk[:np_, :], in_=src_ap, func=AF.Square, accum_out=ss[:np_, :]),
             reads=src_ids, writes=[("junk", tagid), ("ss", tagid)])
        P.op("pool", lambda e: e.tensor_scalar(out=ss[:np_, :], in0=ss[:np_, :], scalar1=1.0 / D, scalar2=EPS,
                                               op0=ALU.mult, op1=ALU.add),
             reads=[("ss", tagid)], writes=[("ss", tagid)])
        P.op("pool", lambda e: e.tensor_tensor(out=rstd[:np_, :], in0=ss[:np_, :], in1=self.c_mhalf[:np_, :], op=ALU.pow),
             reads=[("ss", tagid), "consts"], writes=[("rstd", tagid)])

    def ffn_stage(self, segs, wg_d, wu_d, wd_d, gpre_col, gpost_bc, sname):
        P, A, nc = self.P, self.A, self.nc
        A.push()
        Wg = A.alloc([KC, DFF], BF16)
        Wu = A.alloc([KC, DFF], BF16)
        Wd = A.alloc([NFF, D], BF16)
        gph = A.alloc([D], F32)
        mark_act = A.off
        wst = [A.alloc([DFF], F32) for _ in range(5)]
        P.dma("c0", lambda e: e.dma_start(out=gph[:, :], in_=gpost_bc), writes=["gph"])
        P.op("pool", lambda e: e.tensor_scalar(out=gph[:, :], in0=gph[:, :], scalar1=0.5, scalar2=None,
                                               op0=ALU.mult), reads=["gph"], writes=["gph"])
        self.prep_weight(wg_d, D, DFF, Wg, gpre_col, wst, "Wg")
        self.prep_weight(wu_d, D, DFF, Wu, gpre_col, wst, "Wu")
        self.prep_weight(wd_d, DFF, D, Wd, None, wst, "Wd")
        P.barrier()
        A.off = mark_act
        TT = 256
        NXR = 5
        xr = [A.alloc([D], F32) for _ in range(NXR)]
        xs = [A.alloc([D], BF16) for _ in range(2)]
        xnT = [A.alloc([KC, TT], BF16) for _ in range(2)]
        actT = A.alloc([NFF, TT], BF16)
        stmp = [A.alloc([TT], F32) for _ in range(3)]
        ost = [A.alloc([D], F32) for _ in range(2)]
        junk = A.alloc([D], BF16)
        ssb = [A.alloc([1], F32) for _ in range(4)]
        rsb = [A.alloc([1], F32) for _ in range(4)]
        ssp = [A.alloc([1], F32) for _ in range(4)]
        ssq = [A.alloc([1], F32) for _ in range(4)]
        ps = self.psum
        ps_tr = ps[0][:, :].bitcast(BF16)
        gu = [ps[1], ps[2], ps[3]]
        dn = [(ps[4], ps[5]), (ps[6], ps[7])]

        tiles = []
        for (src, dst, n) in segs:
            t0 = 0
            while t0 < n:
                nt = min(TT, n - t0)
                tiles.append((src, dst, t0, nt))
                t0 += nt
        sub_ctr = [0]

        def load_tile(ti):
            src, dst, t0, nt = tiles[ti]
            subs = []
            for s0 in range(0, nt, 128):
                ns = min(128, nt - s0)
                k = sub_ctr[0] % NXR
                sub_ctr[0] += 1
                P.dma(sname + "x%d" % k, lambda e, k=k, src=src, a=t0 + s0, ns=ns: e.dma_start(
                    out=xr[k][:ns, :], in_=src[a:a + ns, :]), writes=[("xr", k)])
                subs.append((k, s0, ns))
            return subs

        loaded = {0: load_tile(0)}
        gu_ctr = 0
        st_ctr = 0
        o_ctr = 0
        subs_of = {}
        gu_state = {'gu': 0, 'st': 0, 'o': 0}

        def prenorm(ti):
            src, dst, t0, nt = tiles[ti]
            subs = loaded.pop(ti)
            subs_of[ti] = subs
            xb = xnT[ti % 2]
            for si, (k, s0, ns) in enumerate(subs):
                sl = (ti * 2 + si) % 4
                self.rstd_of(xr[k][:ns, :], ns, junk, ssb[sl], rsb[sl], [("xr", k)], sl)
                xsb = xs[(ti * 2 + si) % 2]
                xsid = ("xs", (ti * 2 + si) % 2)
                P.op("dve", lambda e, xsb=xsb, k=k, ns=ns, sl=sl: e.tensor_scalar(
                    out=xsb[:ns, :], in0=xr[k][:ns, :], scalar1=rsb[sl][:ns, :], scalar2=None, op0=ALU.mult),
                     reads=[("xr", k), ("rstd", sl)], writes=[xsid])
                for kc in range(KC):
                    P.op("pe", lambda e, xsb=xsb, kc=kc, ns=ns: e.transpose(
                        out=ps_tr[:, kc * 128:kc * 128 + ns], in_=xsb[:ns, kc * 128:(kc + 1) * 128],
                        identity=self.ident[:ns, :ns]),
                         reads=[xsid, "consts"], writes=["ps_tr"])
                P.op("act", lambda e, xb=xb, s0=s0, ns=ns: e.activation(
                    out=xb[:, :, s0:s0 + ns], in_=ps_tr.rearrange("p (a b) -> p a b", a=KC)[:, :, 0:ns],
                    func=AF.Copy),
                     reads=["ps_tr"], writes=[("xnT", ti % 2, si)])
            xn_ids = [("xnT", ti % 2, si) for si in range(len(subs))]

        def gateup(ti):
            nonlocal gu_ctr, st_ctr
            src, dst, t0, nt = tiles[ti]
            subs = subs_of[ti]
            xb = xnT[ti % 2]
            xn_ids = [("xnT", ti % 2, si) for si in range(len(subs))]
            for f in range(NFF):
                g = gu[gu_ctr % 3]
                gid = ("gu", gu_ctr % 3)
                gu_ctr += 1
                for kc in range(KC):
                    P.op("pe", lambda e, g=g, kc=kc, f=f, xb=xb, nt=nt: e.matmul(
                        g[:, 0:nt], lhsT=Wg[:, kc, f * 128:(f + 1) * 128], rhs=xb[:, kc, 0:nt],
                        start=(kc == 0), stop=(kc == KC - 1)),
                         reads=xn_ids + [("Wg", kc)], writes=[gid])
                for kc in range(KC):
                    P.op("pe", lambda e, g=g, kc=kc, f=f, xb=xb, nt=nt: e.matmul(
                        g[:, 256:256 + nt], lhsT=Wu[:, kc, f * 128:(f + 1) * 128], rhs=xb[:, kc, 0:nt],
                        start=(kc == 0), stop=(kc == KC - 1)),
                         reads=xn_ids + [("Wu", kc)], writes=[gid])
                stb = stmp[st_ctr % 3]
                sid = ("stmp", st_ctr % 3)
                st_ctr += 1
                P.op("act", lambda e, g=g, stb=stb, nt=nt: e.activation(out=stb[:, 0:nt], in_=g[:, 0:nt], func=AF.Silu),
                     reads=[gid], writes=[sid])
                P.op("dve", lambda e, g=g, stb=stb, nt=nt, f=f: e.tensor_tensor(
                    out=actT[:, f, 0:nt], in0=stb[:, 0:nt], in1=g[:, 256:256 + nt], op=ALU.mult),
                     reads=[gid, sid], writes=[("actT", f)])

        def down_post(ti):
            nonlocal o_ctr
            src, dst, t0, nt = tiles[ti]
            subs = subs_of.pop(ti)
            for si, (k, s0, ns) in enumerate(subs):
                d0, d1 = dn[si % 2]
                did = ("dn", si % 2)
                for half, dps in enumerate((d0, d1)):
                    for f in range(NFF):
                        P.op("pe", lambda e, dps=dps, f=f, s0=s0, ns=ns, half=half: e.matmul(
                            dps[:ns, :], lhsT=actT[:, f, s0:s0 + ns], rhs=Wd[:, f, half * 512:(half + 1) * 512],
                            start=(f == 0), stop=(f == NFF - 1)),
                             reads=[("actT", f), ("Wd", f)], writes=[did])
                sl = (ti * 2 + si) % 4
                ss2 = ssp[sl]
                P.op("act", lambda e, d0=d0, ns=ns, ss2=ss2: e.activation(
                    out=junk[:ns, 0:512], in_=d0[:ns, :], func=AF.Square, accum_out=ss2[:ns, :]),
                     reads=[did], writes=[("junk", 9), ("ssA", sl)])
                ss3 = ssq[sl]
                P.op("act", lambda e, d1=d1, ns=ns, ss3=ss3: e.activation(
                    out=junk[:ns, 512:1024], in_=d1[:ns, :], func=AF.Square, accum_out=ss3[:ns, :]),
                     reads=[did], writes=[("junk", 10), ("ssB", sl)])
                P.op("pool", lambda e, ss2=ss2, ss3=ss3, ns=ns: e.tensor_tensor(
                    out=ss2[:ns, :], in0=ss2[:ns, :], in1=ss3[:ns, :], op=ALU.add),
                     reads=[("ssA", sl), ("ssB", sl)], writes=[("ssA", sl)])
                P.op("pool", lambda e, ss2=ss2, ns=ns: e.tensor_scalar(
                    out=ss2[:ns, :], in0=ss2[:ns, :], scalar1=1.0 / D, scalar2=EPS, op0=ALU.mult, op1=ALU.add),
                     reads=[("ssA", sl)], writes=[("ssA", sl)])
                P.op("pool", lambda e, ss2=ss2, ss3=ss3, ns=ns: e.tensor_tensor(
                    out=ss3[:ns, :], in0=ss2[:ns, :], in1=self.c_mhalf[:ns, :], op=ALU.pow),
                     reads=[("ssA", sl), "consts"], writes=[("ssB", sl)])
                ob = ost[o_ctr % 2]
                oid = ("ost", o_ctr % 2)
                osem = sname + "o%d" % (o_ctr % int(os.environ.get("NOSEM", "2")))
                o_ctr += 1
                for half, dps in enumerate((d0, d1)):
                    P.op("dve", lambda e, dps=dps, ob=ob, ns=ns, half=half, ss3=ss3: e.scalar_tensor_tensor(
                        out=ob[:ns, half * 512:(half + 1) * 512], in0=dps[:ns, :], scalar=ss3[:ns, :],
                        in1=gph[:ns, half * 512:(half + 1) * 512], op0=ALU.mult, op1=ALU.mult),
                         reads=[did, ("ssB", sl), "consts"], writes=[(oid, half)])
                    P.op("pool", lambda e, ob=ob, ns=ns, half=half, k=k: e.tensor_tensor(
                        out=ob[:ns, half * 512:(half + 1) * 512], in0=ob[:ns, half * 512:(half + 1) * 512],
                        in1=xr[k][:ns, half * 512:(half + 1) * 512], op=ALU.add),
                         reads=[(oid, half), ("xr", k)], writes=[(oid, half)])
                P.dma(osem, lambda e, ob=ob, dst=dst, a=t0 + s0, ns=ns: e.dma_start(
                    out=dst[a:a + ns, :], in_=ob[:ns, :]), reads=[(oid, 0), (oid, 1)])

        if len(tiles) > 1:
            loaded[1] = load_tile(1)
        prenorm(0)
        for ti in range(len(tiles)):
            gateup(ti)
            if ti + 1 < len(tiles):
                prenorm(ti + 1)
            down_post(ti)
            if ti + 2 < len(tiles):
                loaded[ti + 2] = load_tile(ti + 2)
        P.barrier()
        A.pop()

    def consts(self):
        P, A = self.P, self.A
        ident_d = self.din("ident", [128, 128], BF16)
        self.ident = A.alloc([128], BF16)
        self.c_mhalf = A.alloc([1], F32)
        P.dma("c0", lambda e: e.dma_start(out=self.ident[:, :], in_=ident_d[:, :]), writes=["consts"])
        P.op("pool", lambda e: e.memset(self.c_mhalf[:, :], -0.5), writes=["consts_b"])

    def load_const(self, name, shape, dt=F32):
        d = self.din(name, [128] + list(shape), dt)
        t = self.A.alloc(list(shape), dt)
        self.P.dma("c0", lambda e: e.dma_start(out=t, in_=d), writes=["consts_c"])
        return t


def build_ffn_test(ntok):
    B = Builder(0, ntok)
    P = B.P
    x = B.din("x", [ntok, D])
    wg = B.din("wg", [D, DFF])
    wu = B.din("wu", [D, DFF])
    wd = B.din("wd", [DFF, D])
    y = B.dout("y", [ntok, D])
    B.consts()
    gpre = B.load_const("gpre", [KC])
    gpost = B.load_const("gpost", [D])
    P.barrier()
    B.ffn_stage([(x, y, ntok)], wg, wu, wd, gpre, gpost, "f1")
    B.P.emit()
    return B


def mixer_in_stage(B, seqs, Cn, sname):
    P, A, nc = B.P, B.A, B.nc
    A.push()
    TT = 512
    Wout_p = A.alloc([KC, D], BF16)
    Cn["wout_off"] = A.off
    Win = A.alloc([KC, 2560], BF16)
    Wrb = A.alloc([4, 128], BF16)
    Wib = A.alloc([4, 128], BF16)
    mark = A.off
    wst = [A.alloc([2560], F32) for _ in range(4)]
    wrf = A.alloc([4, 128], F32)
    wif = A.alloc([4, 128], F32)
    P.dma("c0", lambda e: e.dma_start(out=wrf, in_=B.inputs["wr_bd"]), writes=["wrf"])
    P.dma("c0", lambda e: e.dma_start(out=wif, in_=B.inputs["wi_bd"]), writes=["wif"])
    if Cn.get("cache_prep") is not None:
        cache_prep_ops(B, *Cn["cache_prep"])
    B.prep_weight(B.inputs["win"], D, 2560, Win, Cn["gma_col"], wst, "Win")
    B.prep_weight(B.inputs["wout"], D, D, Wout_p, None, wst, "Wout")
    Cn["wout_ready"] = True
    P.op("dve", lambda e: e.tensor_copy(out=Wrb, in_=wrf), reads=["wrf"], writes=["Wrb"])
    P.op("dve", lambda e: e.tensor_copy(out=Wib, in_=wif), reads=["wif"], writes=["Wib"])
    P.barrier()
    A.off = mark
    NXR = 6
    xr = [A.alloc([D], F32) for _ in range(NXR)]
    xs = [A.alloc([D], BF16) for _ in range(2)]
    xnT = [A.alloc([KC, TT], BF16) for _ in range(2)]
    junk = A.alloc([D], BF16)
    ssb = [A.alloc([1], F32) for _ in range(4)]
    rsb = [A.alloc([1], F32) for _ in range(4)]
    kst = [A.alloc([TT], BF16) for _ in range(3)]
    vst = [A.alloc([512], F32) for _ in range(2)]
    vbf = [A.alloc([4, 129], BF16) for _ in range(2)]
    for i in range(2):
        P.op("pool", lambda e, i=i: e.memset(vbf[i], 1.0), writes=[("vbf", i)])
    lxb = [[A.alloc([TT + 4], BF16) for _ in range(2)] for _ in range(4)]
    lxl = A.alloc([4, 3], F32)
    lx0 = A.alloc([4, 3], F32)
    dgw = A.alloc([16, 128], BF16)
    xcb = [A.alloc([TT], BF16) for _ in range(4)]
    rb = [A.alloc([TT], F32) for _ in range(4)]
    ib = [A.alloc([TT], F32) for _ in range(4)]
    ab = [A.alloc([TT], F32) for _ in range(4)]
    a2b = [A.alloc([TT], F32) for _ in range(4)]
    hb = [A.alloc([TT], F32) for _ in range(4)]
    gt = [[A.alloc([TT], F32) for _ in range(4)] for _ in range(2)]
    tb = [A.alloc([TT], F32) for _ in range(4)]
    lob = [A.alloc([TT], BF16) for _ in range(4)]
    hstate = A.alloc([4], F32)
    cL = A.alloc([4], F32)
    cL2 = A.alloc([4], F32)
    ps = B.psum
    ps_tr = ps[0][:, :].bitcast(BF16)
    fm = [ps[1], ps[2]]
    cvb = ps[3]
    tm = [ps[4], ps[5]]
    rg = [ps[6], ps[7]]

    if "nocl" not in DBG:
        P.op("act", lambda e: e.activation(out=cL, in_=Cn["lam"], func=AF.Exp, scale=-1.0), reads=["consts"], writes=["cL"])
        P.op("act", lambda e: e.activation(out=cL, in_=cL, func=AF.Ln, bias=1.0), reads=["cL"], writes=["cL"])
    P.op("pool", lambda e: e.tensor_scalar(out=cL2, in0=cL, scalar1=-16.0, scalar2=None, op0=ALU.mult),
         reads=["cL"], writes=["cL2"])
    P.op("pool", lambda e: e.tensor_scalar(out=cL, in0=cL, scalar1=-8.0, scalar2=None, op0=ALU.mult),
         reads=["cL", "cL2"], writes=["cL"])
    for g in range(4):
        for j in range(4):
            P.op("dve", lambda e, g=g, j=j: e.tensor_scalar(out=dgw[:, g * 4 + j, :], in0=B.ident, scalar1=Cn["convw"][:, g, j:j + 1],
                                                            scalar2=None, op0=ALU.mult), reads=["consts"], writes=["dgw"])

    def run_seq(sq, x1_src, T, T_OTH, KT_scr, V_scr, QT_scr, LT_scr, ko, vo, hl_out, cb_out, h0_d, conv0_d, koff):
        sn = sname + str(sq)
        if h0_d is None:
            P.op("pool", lambda e: e.memset(hstate, 0.0), writes=["hstate"])
            for g in range(4):
                P.op("pool", lambda e, g=g: e.memset(lxb[g][0][:, 0:3], 0.0), writes=[("lxh", g, 0)])
        else:
            P.dma(sn + "st", lambda e: e.dma_start(out=hstate, in_=h0_d.rearrange("(g p) -> p g", p=128),
                                                      allow_slow_non_contiguous=True), writes=["hstate"])
            for g in range(4):
                P.dma(sn + "st", lambda e, g=g: e.dma_start(
                    out=lx0[:, g, :], in_=conv0_d[:, g * 128:(g + 1) * 128].rearrange("j p -> p j"),
                    allow_slow_non_contiguous=True), writes=[("lx0", g)])
            for g in range(4):
                P.op("dve", lambda e, g=g: e.tensor_copy(out=lxb[g][0][:, 0:3], in_=lx0[:, g, :]),
                     reads=[("lx0", g)], writes=[("lxh", g, 0)])
        P.barrier()

        tiles = []
        t0 = 0
        while t0 < T:
            lim = T_OTH if t0 < T_OTH else T
            nt = min(TT, lim - t0)
            tiles.append((t0, nt))
            t0 += nt
        sub_ctr = [0]

        def load_tile(ti):
            t0, nt = tiles[ti]
            subs = []
            for s0 in range(0, nt, 128):
                ns = min(128, nt - s0)
                k = sub_ctr[0] % NXR
                sub_ctr[0] += 1
                P.dma(sname + "x%d" % k, lambda e, k=k, a=t0 + s0, ns=ns: e.dma_start(
                    out=xr[k][:ns, :], in_=x1_src[a:a + ns, :]), writes=[("xr", k)])
                subs.append((k, s0, ns))
            return subs

        loaded = {0: load_tile(0)}
        fm_c = [0]
        tm_c = [0]
        ks_c = [0]
        vs_c = [0]
        vb_c = [0]
        def tile_body(ti, t0, nt):
            own = t0 >= T_OTH
            to = t0 - T_OTH
            subs = loaded.pop(ti)
            xb = xnT[ti % 2]
            cur, nxt = ti % 2, (ti + 1) % 2
            for si, (k, s0, ns) in enumerate(subs):
                sl = (ti * 4 + si) % 4
                B.rstd_of(xr[k][:ns, :], ns, junk, ssb[sl], rsb[sl], [("xr", k)], sl)
                xsb = xs[si % 2]
                xsid = ("xs", si % 2)
                P.op("dve", lambda e, xsb=xsb, k=k, ns=ns, sl=sl: e.tensor_scalar(
                    out=xsb[:ns, :], in0=xr[k][:ns, :], scalar1=rsb[sl][:ns, :], scalar2=None, op0=ALU.mult),
                     reads=[("xr", k), ("rstd", sl)], writes=[xsid])
                for kc in range(KC):
                    P.op("pe", lambda e, xsb=xsb, kc=kc, ns=ns: e.transpose(
                        out=ps_tr[:, kc * 128:kc * 128 + ns], in_=xsb[:ns, kc * 128:(kc + 1) * 128],
                        identity=B.ident[:ns, :ns]),
                         reads=[xsid, "consts"], writes=["ps_tr"])
                P.op("act", lambda e, xb=xb, s0=s0, ns=ns: e.activation(
                    out=xb[:, :, s0:s0 + ns], in_=ps_tr.rearrange("p (a b) -> p a b", a=KC)[:, :, 0:ns],
                    func=AF.Copy),
                     reads=["ps_tr"], writes=[("xnT", ti % 2, si)])
                if si % 2 == 1:
                    yield
            xn_ids = [("xnT", ti % 2, si) for si in range(len(subs))]
            if ti + 1 < len(tiles):
                loaded[ti + 1] = load_tile(ti + 1)

            def fm_proj(col0):
                bank = fm[fm_c[0] % 2]
                bid = ("fm", fm_c[0] % 2)
                fm_c[0] += 1
                for kc in range(KC):
                    P.op("pe", lambda e, bank=bank, kc=kc, col0=col0: e.matmul(
                        bank[:, 0:nt], lhsT=Win[:, kc, col0:col0 + 128], rhs=xb[:, kc, 0:nt],
                        start=(kc == 0), stop=(kc == KC - 1)),
                         reads=xn_ids + [("Win", kc)], writes=[bid])
                return bank, bid

            def fm_to_scr(col0, dst_ap):
                bank, bid = fm_proj(col0)
                kb = kst[ks_c[0] % 3]
                kid = ("kst", ks_c[0] % 3)
                ksem = sname + "k%d" % (ks_c[0] % 3)
                ks_c[0] += 1
                P.op("act", lambda e, bank=bank, kb=kb: e.activation(out=kb[:, 0:nt], in_=bank[:, 0:nt], func=AF.Copy),
                     reads=[bid], writes=[kid])
                P.dma(ksem, lambda e, kb=kb, dst_ap=dst_ap: e.dma_start(out=dst_ap, in_=kb[:, 0:nt]), reads=[kid])

            for g in range(4):
                bank, bid = fm_proj(1536 + g * 128)
                P.op("act", lambda e, bank=bank, g=g: e.activation(
                    out=lxb[g][cur][:, 3:3 + nt], in_=bank[:, 0:nt], func=AF.Copy),
                     reads=[bid], writes=[("lx", g, cur)])
                if ti == len(tiles) - 1:
                    P.op("act", lambda e, bank=bank, g=g: e.activation(
                        out=lxl[:, g, :], in_=bank[:, nt - 3:nt], func=AF.Copy), reads=[bid], writes=[("lxl", g)])
                yield
            if own:
                for g in range(4):
                    bank, bid = fm_proj(2048 + g * 128)
                    P.op("act", lambda e, bank=bank, g=g: e.activation(out=gt[cur][g][:, 0:nt], in_=bank[:, 0:nt], func=AF.Copy),
                         reads=[bid], writes=[("gt", cur, g)])
            for h in range(4 if "nofm" not in DBG else 0):
                fm_to_scr(512 + h * 128, KT_scr[h, :, koff + t0:koff + t0 + nt])
                yield
            if own and "nofm" not in DBG:
                for h in range(4):
                    fm_to_scr(h * 128, QT_scr[h, :, to:to + nt])
                yield
            for si, (k, s0, ns) in enumerate(subs if "notm" not in DBG else []):
                for which in (("v", 1024), ("k", 512)):
                    if which[0] == "k" and not own:
                        continue
                    bank = tm[tm_c[0] % 2]
                    bid = ("tm", tm_c[0] % 2)
                    tm_c[0] += 1
                    for kc in range(KC):
                        P.op("pe", lambda e, bank=bank, kc=kc, s0=s0, ns=ns, c0=which[1]: e.matmul(
                            bank[:ns, :], lhsT=xb[:, kc, s0:s0 + ns], rhs=Win[:, kc, c0:c0 + 512],
                            start=(kc == 0), stop=(kc == KC - 1)),
                             reads=xn_ids + [("Win", kc)], writes=[bid])
                    if which[0] == "v" and "nov" not in DBG:
                        vb = vbf[vb_c[0] % 2]
                        vbid = ("vbf", vb_c[0] % 2)
                        vsem = sname + "vb%d" % (vb_c[0] % 2)
                        vb_c[0] += 1
                        P.op("dve", lambda e, bank=bank, vb=vb, ns=ns: e.tensor_copy(
                            out=vb[:ns, :, 0:128], in_=bank[:ns, :].rearrange("p (h d) -> p h d", h=4)),
                             reads=[bid], writes=[vbid])
                        P.dma(vsem, lambda e, vb=vb, a=koff + t0 + s0, ns=ns: e.dma_start(
                            out=V_scr[a:a + ns, :], in_=vb[:ns, :, :].rearrange("p h d -> p (h d)")),
                              reads=[vbid])
                    if own and "noko" not in DBG:
                        vs = vst[vs_c[0] % 2]
                        vsid = ("vst", vs_c[0] % 2)
                        vsem = sname + "vs%d" % (vs_c[0] % 2)
                        vs_c[0] += 1
                        dst = vo if which[0] == "v" else ko
                        P.op("act", lambda e, bank=bank, vs=vs, ns=ns: e.activation(out=vs[:ns, :], in_=bank[:ns, :], func=AF.Copy),
                             reads=[bid], writes=[vsid])
                        P.dma(vsem, lambda e, vs=vs, dst=dst, a=to + s0, ns=ns: e.dma_start(out=dst[a:a + ns, :], in_=vs[:ns, :]),
                              reads=[vsid])
                if si % 2 == 1:
                    yield

        def lru_part(ti, t0, nt, own, to, cur, nxt):
            for g in range(4):
                lx = lxb[g][cur]
                lid = [("lx", g, cur), ("lxh", g, cur)]
                for j in range(4):
                    P.op("pe", lambda e, g=g, lx=lx, j=j: e.matmul(
                        cvb[:, 0:nt], lhsT=dgw[:, g * 4 + j, :], rhs=lx[:, j:j + nt], start=(j == 0), stop=(j == 3)),
                         reads=lid + ["dgw"], writes=[("cv", 0)])
                P.op("dve", lambda e, g=g: e.tensor_scalar(
                    out=xcb[g][:, 0:nt], in0=cvb[:, 0:nt], scalar1=Cn["convb"][:, g:g + 1], scalar2=None, op0=ALU.add),
                     reads=[("cv", 0), "consts"], writes=[("xcb", g)])
                if ti + 1 < len(tiles):
                    boundary = (tiles[ti + 1][0] == T_OTH) and T_OTH > 0
                    if boundary:
                        P.op("pool", lambda e, g=g, lx=lx: e.tensor_scalar(
                            out=lxb[g][nxt][:, 0:3], in0=lx[:, nt:nt + 3], scalar1=Cn["flag"][:, 0:1], scalar2=None,
                            op0=ALU.mult), reads=lid + ["consts"], writes=[("lxh", g, nxt)])
                    else:
                        P.op("pool", lambda e, g=g, lx=lx: e.tensor_copy(out=lxb[g][nxt][:, 0:3], in_=lx[:, nt:nt + 3]),
                             reads=lid, writes=[("lxh", g, nxt)])
                yield
            for g in range(4):
                P.op("pe", lambda e, g=g: e.matmul(rg[0][:, 0:nt], lhsT=Wrb[:, g, :], rhs=xcb[g][:, 0:nt], start=True, stop=True),
                     reads=[("xcb", g), "Wrb"], writes=[("rg", 0)])
                P.op("act", lambda e, g=g: e.activation(out=rb[g][:, 0:nt], in_=rg[0][:, 0:nt], func=AF.Sigmoid,
                                                        bias=Cn["brg"][:, g:g + 1]),
                     reads=[("rg", 0), "consts"], writes=[("rb", g)])
                P.op("pe", lambda e, g=g: e.matmul(rg[1][:, 0:nt], lhsT=Wib[:, g, :], rhs=xcb[g][:, 0:nt], start=True, stop=True),
                     reads=[("xcb", g), "Wib"], writes=[("rg", 1)])
                P.op("act", lambda e, g=g: e.activation(out=ib[g][:, 0:nt], in_=rg[1][:, 0:nt], func=AF.Sigmoid,
                                                        bias=Cn["big"][:, g:g + 1]),
                     reads=[("rg", 1), "consts"], writes=[("ib", g)])
                yield
            if own:
                for g in range(4):
                    P.op("pool", lambda e, g=g: e.tensor_tensor(out=tb[g][:, 0:nt], in0=gt[cur][g][:, 0:nt], in1=gt[cur][g][:, 0:nt], op=ALU.mult),
                         reads=[("gt", cur, g)], writes=[("tb", g)])
                    P.op("pool", lambda e, g=g: e.tensor_scalar(out=tb[g][:, 0:nt], in0=tb[g][:, 0:nt], scalar1=0.044715, scalar2=1.0,
                                                                op0=ALU.mult, op1=ALU.add),
                         reads=[("tb", g)], writes=[("tb", g)])
                    P.op("pool", lambda e, g=g: e.tensor_tensor(out=tb[g][:, 0:nt], in0=tb[g][:, 0:nt], in1=gt[cur][g][:, 0:nt], op=ALU.mult),
                         reads=[("tb", g), ("gt", cur, g)], writes=[("tb", g)])
                    P.op("act", lambda e, g=g: e.activation(out=tb[g][:, 0:nt], in_=tb[g][:, 0:nt], func=AF.Sigmoid, scale=1.5957691216),
                         reads=[("tb", g)], writes=[("tb", g)])
                    P.op("pool", lambda e, g=g: e.tensor_tensor(out=gt[cur][g][:, 0:nt], in0=tb[g][:, 0:nt], in1=gt[cur][g][:, 0:nt], op=ALU.mult),
                         reads=[("tb", g), ("gt", cur, g)], writes=[("gt", cur, g)])
            yield
            for g in range(4):
                P.op("act", lambda e, g=g: e.activation(out=ab[g][:, 0:nt], in_=rb[g][:, 0:nt], func=AF.Exp, scale=cL[:, g:g + 1]),
                     reads=[("rb", g), "cL"], writes=[("ab", g)])
                P.op("act", lambda e, g=g: e.activation(out=a2b[g][:, 0:nt], in_=rb[g][:, 0:nt], func=AF.Exp, scale=cL2[:, g:g + 1]),
                     reads=[("rb", g), "cL2"], writes=[("a2b", g)])
                yield
            for g in range(4):
                P.op("act", lambda e, g=g: e.activation(out=a2b[g][:, 0:nt], in_=a2b[g][:, 0:nt], func=AF.Sqrt, scale=-1.0, bias=1.0),
                     reads=[("a2b", g)], writes=[("a2b", g)])
            yield
            for g in range(4):
                P.op("dve", lambda e, g=g: e.tensor_tensor(out=ib[g][:, 0:nt], in0=ib[g][:, 0:nt], in1=xcb[g][:, 0:nt], op=ALU.mult),
                     reads=[("ib", g), ("xcb", g)], writes=[("ib", g)])
                P.op("dve", lambda e, g=g: e.tensor_tensor(out=ib[g][:, 0:nt], in0=ib[g][:, 0:nt], in1=a2b[g][:, 0:nt], op=ALU.mult),
                     reads=[("ib", g), ("a2b", g)], writes=[("ib", g)])
                P.op("dve", lambda e, g=g: e.tensor_tensor_scan(
                    out=hb[g][:, 0:nt], data0=ab[g][:, 0:nt], data1=ib[g][:, 0:nt], initial=hstate[:, g:g + 1],
                    op0=ALU.mult, op1=ALU.add),
                     reads=[("ab", g), ("ib", g), "hstate"], writes=[("hb", g)])
                yield
            boundary = (ti + 1 < len(tiles)) and (tiles[ti + 1][0] == T_OTH) and T_OTH > 0
            for g in range(4):
                if boundary:
                    P.op("pool", lambda e, g=g: e.tensor_scalar(out=hstate[:, g:g + 1], in0=hb[g][:, nt - 1:nt],
                                                                scalar1=Cn["flag"][:, 0:1], scalar2=None, op0=ALU.mult),
                         reads=[("hb", g), "consts"], writes=["hstate"])
                else:
                    P.op("pool", lambda e, g=g: e.tensor_copy(out=hstate[:, g:g + 1], in_=hb[g][:, nt - 1:nt]),
                         reads=[("hb", g)], writes=["hstate"])
            if own:
                for g in range(4):
                    P.op("dve", lambda e, g=g: e.tensor_tensor(out=lob[g][:, 0:nt], in0=hb[g][:, 0:nt], in1=gt[cur][g][:, 0:nt], op=ALU.mult),
                         reads=[("hb", g), ("gt", cur, g)], writes=[("lob", g)])
                    P.dma(sname + "lo%d" % g, lambda e, g=g: e.dma_start(out=LT_scr[g, :, to:to + nt], in_=lob[g][:, 0:nt]),
                          reads=[("lob", g)])
        def interleave(ga, gb):
            gens = [g for g in (ga, gb) if g is not None]
            while gens:
                for g in list(gens):
                    try:
                        next(g)
                    except StopIteration:
                        gens.remove(g)

        pending = None
        for ti, (t0, nt) in enumerate(tiles):
            interleave(tile_body(ti, t0, nt), pending)
            pending = lru_part(ti, t0, nt, t0 >= T_OTH, t0 - T_OTH, ti % 2, (ti + 1) % 2)
        interleave(None, pending)
        lt0, lnt = tiles[-1]
        lcur = (len(tiles) - 1) % 2
        if "nofin" not in DBG:
            P.dma(sn + "fin", lambda e: e.dma_start(out=hl_out.rearrange("(g p) -> p g", p=128), in_=hstate,
                                                       allow_slow_non_contiguous=True), reads=["hstate"])
        for g in range(4 if "nofin" not in DBG else 0):
            P.dma(sn + "fin", lambda e, g=g: e.dma_start(
                out=cb_out[:, g * 128:(g + 1) * 128].rearrange("j p -> p j"), in_=lxl[:, g, :],
                allow_slow_non_contiguous=True), reads=[("lxl", g)])

    for sq, q in enumerate(seqs):
        run_seq(sq, q['x1'], q['T'], q['T_OTH'], q['KT'], q['V'], q['QT'], q['LT'], q['ko'], q['vo'], q['hl'], q['cb'],
                q.get('h0'), q.get('conv0'), q.get('koff', 0))
    P.barrier()
    A.pop()


SLOPES = [2.0 ** (-8.0 * (i + 1) / 4) for i in range(4)]
LAMBDA_INIT = 0.8 - 0.6 * 1.0


def attn_consts(B, Cn):
    P, A = B.P, B.A
    for nm, shp in (("pb", [4, 71]), ("db", [4, 128]), ("subg", [128]), ("lq1", [64]), ("lk1", [64]), ("lq2", [64]), ("lk2", [64])):
        Cn[nm] = A.alloc(shp, F32)
        P.dma("c0", lambda e, nm=nm: e.dma_start(out=Cn[nm], in_=B.inputs[nm]), writes=["consts"])
    negl = A.alloc([1], F32)
    t1 = A.alloc([1], F32)
    t2 = A.alloc([1], F32)
    j64 = A.alloc([64], F32)
    P.op("dve", lambda e: e.tensor_tensor(out=j64, in0=Cn["lq1"], in1=Cn["lk1"], op=ALU.mult), reads=["consts"], writes=["j64"])
    P.op("dve", lambda e: e.reduce_sum(out=t1, in_=j64, axis=AX.X), reads=["j64"], writes=["t1"])
    P.op("dve", lambda e: e.tensor_tensor(out=j64, in0=Cn["lq2"], in1=Cn["lk2"], op=ALU.mult), reads=["consts", "t1"], writes=["j64"])
    P.op("dve", lambda e: e.reduce_sum(out=t2, in_=j64, axis=AX.X), reads=["j64"], writes=["t2"])
    P.op("act", lambda e: e.activation(out=t1, in_=t1, func=AF.Exp), reads=["t1"], writes=["t1"])
    P.op("act", lambda e: e.activation(out=t2, in_=t2, func=AF.Exp), reads=["t2"], writes=["t2"])
    P.op("pool", lambda e: e.tensor_tensor(out=negl, in0=t2, in1=t1, op=ALU.subtract), reads=["t1", "t2"], writes=["negl"])
    P.op("pool", lambda e: e.tensor_scalar(out=negl, in0=negl, scalar1=-LAMBDA_INIT, scalar2=None, op0=ALU.add),
         reads=["negl"], writes=["negl"])
    subg8 = A.alloc([128], F32)
    P.op("pool", lambda e: e.tensor_scalar(out=subg8, in0=Cn["subg"], scalar1=1.0 - LAMBDA_INIT, scalar2=None, op0=ALU.mult),
         reads=["consts"], writes=["subg8"])
    pbo = A.alloc([4, 71], F32)
    P.op("pool", lambda e: e.tensor_scalar(out=pbo, in0=Cn["pb"], scalar1=Cn["maskv"][:, 0:1], scalar2=None, op0=ALU.add),
         reads=["consts"], writes=["pbo"])
    Cn["negl"], Cn["subg8"], Cn["pbo"] = negl, subg8, pbo


def attn_stage(B, jobs, Cn, sname, window=(None,) * 4):
    P, A, nc = B.P, B.A, B.nc
    TKmax = max(q["TK"] for q in jobs)
    NBmax = cdiv(TKmax, 128)
    A.push()
    KT = A.alloc([4, NBmax * 128], BF16)
    V1 = A.alloc([NBmax, 516], BF16)
    Wout = A.alloc([KC, D], BF16)
    gmb = A.alloc([D], F32)
    P.dma("c0", lambda e: e.dma_start(out=gmb, in_=B.inputs["gmb"]), writes=["gmb"])
    Cn["gmb"] = gmb
    attn_consts(B, Cn)
    mark = A.off
    wst = [A.alloc([D], F32) for _ in range(4)]
    B.prep_weight(B.inputs["wout"], D, D, Wout, None, wst, "Wout")
    P.barrier()
    A.off = mark
    QTILE = 512
    qt = [A.alloc([4, QTILE], BF16) for _ in range(2)]
    NPT = 4
    pt = [A.alloc([QTILE], BF16) for _ in range(NPT)]
    dtmp = [A.alloc([128], F32) for _ in range(2)]
    atok = [A.alloc([512], BF16) for _ in range(4)]
    attT = A.alloc([4, QTILE], BF16)
    lruT = [A.alloc([4, QTILE], BF16) for _ in range(2)]
    x1r = [A.alloc([D], F32) for _ in range(2)]
    ost = [A.alloc([D], F32)] * 2
    otmp = [A.alloc([128], F32) for _ in range(2)]
    ofin = [A.alloc([2, 129], F32) for _ in range(4)]
    junk = A.alloc([512], BF16)
    rl = [A.alloc([2], F32) for _ in range(4)]
    ssn = [A.alloc([1], F32) for _ in range(4)]
    rsn = [A.alloc([1], F32) for _ in range(4)]
    ssm = [A.alloc([1], F32) for _ in range(2)]
    ssm2 = [A.alloc([1], F32) for _ in range(2)]
    rsm = [A.alloc([1], F32) for _ in range(2)]
    ps = B.psum
    pb, pbo, db = Cn["pb"], Cn["pbo"], Cn["db"]
    st_c = [0]
    pt_c = [0]
    dt_c = [0]
    x_c = [0]
    o_c = [0]
    qb_c = [0]
    VCH = 16

    def run_job(TK, NQ, KT_scr, V_scr, QT_scr, LT_scr, x1_scr, x1_off, x2_dst, mask_other):
        NB = cdiv(TK, 128)
        KOFF = TK - NQ
        assert KOFF % 128 == 0
        nkof = lambda j: min(128, TK - 128 * j)
        for h in range(4):
            P.dma(sname + "K%d" % h, lambda e, h=h: e.dma_start(out=KT[:, h, 0:TK], in_=KT_scr[h, :, 0:TK]), writes=[("KT", h)])
        for ci, j0 in enumerate(range(0, NB, VCH)):
            j1 = min(NB, j0 + VCH)
            jf = min(j1, TK // 128)
            if jf > j0:
                P.dma(sname + "V%d" % (ci % 4), lambda e, j0=j0, jf=jf: e.dma_start(
                    out=V1[:, j0:jf, :], in_=V_scr[j0 * 128:jf * 128, :].rearrange("(j p) c -> p j c", p=128)),
                      writes=[("V1", ci)])
            if jf < j1:
                nk = nkof(jf)
                P.dma(sname + "V%d" % (ci % 4), lambda e, jf=jf, nk=nk: e.dma_start(
                    out=V1[:nk, jf, :], in_=V_scr[jf * 128:jf * 128 + nk, :]), writes=[("V1", ci)])
        vid = lambda j: ("V1", j // VCH)

        tiles = []
        q0 = 0
        while q0 < NQ:
            nq = min(QTILE, NQ - q0)
            tiles.append((q0, nq))
            q0 += nq

        LOOK = int(os.environ.get("K_LOOK", "3"))
        dq = []

        def push2(fn):
            dq.append(fn)
            while len(dq) > LOOK:
                dq.pop(0)()

        def load_q(ti):
            q0, nq = tiles[ti]
            b = qb_c[0] % 2
            qb_c[0] += 1
            for h in range(4):
                P.dma(sname + "q%d" % b, lambda e, b=b, h=h, q0=q0, nq=nq: e.dma_start(
                    out=qt[b][:, h, 0:nq], in_=QT_scr[h, :, q0:q0 + nq]), writes=[("qt", b, h)])
            def ld(b=b, q0=q0, nq=nq):
                P.dma(sname + "l%d" % b, lambda e: e.dma_start(
                    out=lruT[b][:, :, 0:nq], in_=LT_scr[:, :, q0:q0 + nq].rearrange("g p t -> p g t")), writes=[("lruT", b)])
            push2(ld)
            return b

        def tile_stream(ti, q0, nq, b):
            nsb = cdiv(nq, 128)
            nqs_of = lambda s: min(128, nq - 128 * s)
            jb = (KOFF + q0) // 128
            nb_next = load_q(ti + 1) if ti + 1 < len(tiles) else None
            for h in range(4):
                persub = (h == 0)
                W = window[h]
                jlo = 0 if W is None else max(0, jb - W)
                first_in_bank = [True] * 4
                for j in range(jlo, jb + nsb):
                    nk = nkof(j)
                    rel = j - jb
                    s_lo = max(0, rel)
                    c0 = s_lo * 128
                    tab = pbo if (mask_other and j * 128 < KOFF) else pb
                    for c in range(2):
                        bi = st_c[0] % 4
                        st_c[0] += 1
                        stb = ps[bi]
                        bid = ("bank", bi)
                        P.op("pe", lambda e, stb=stb, c=c, h=h, j=j, c0=c0, nq=nq, b=b, nk=nk: e.matmul(
                            stb[:nk, c0:nq], lhsT=KT[c * 64:(c + 1) * 64, h, j * 128:j * 128 + nk],
                            rhs=qt[b][c * 64:(c + 1) * 64, h, c0:nq], start=True, stop=True),
                             reads=[("KT", h), ("qt", b, h)], writes=[bid])
                        pi = pt_c[0] % NPT
                        pt_c[0] += 1
                        ptb = pt[pi]
                        pid = ("pt", pi)
                        c1 = c0
                        if rel >= 0:
                            nqs = nqs_of(rel)
                            di = dt_c[0] % 2
                            dt_c[0] += 1
                            P.op("dve", lambda e, stb=stb, di=di, h=h, c0=c0, nk=nk, nqs=nqs: e.scalar_tensor_tensor(
                                out=dtmp[di][:nk, :nqs], in0=stb[:nk, c0:c0 + nqs], scalar=0.125, in1=db[:nk, h, 0:nqs],
                                op0=ALU.mult, op1=ALU.add), reads=[bid, "consts"], writes=[("dtmp", di)])
                            bconst = 0.0 if persub else SLOPES[h] * 128.0 * rel
                            P.op("act", lambda e, ptb=ptb, di=di, c0=c0, bconst=bconst, nk=nk, nqs=nqs: e.activation(
                                out=ptb[:nk, c0:c0 + nqs], in_=dtmp[di][:nk, :nqs], func=AF.Exp, bias=bconst),
                                 reads=[("dtmp", di)], writes=[pid])
                            c1 = c0 + 128
                        if c1 < nq:
                            if persub:
                                for s in range(c1 // 128, nsb):
                                    dj = jb + s - j
                                    ce = s * 128 + nqs_of(s)
                                    P.op("act", lambda e, ptb=ptb, stb=stb, s=s, ce=ce, dj=dj, tab=tab, h=h, nk=nk: e.activation(
                                        out=ptb[:nk, s * 128:ce], in_=stb[:nk, s * 128:ce], func=AF.Exp,
                                        bias=tab[:nk, h, dj + 3:dj + 4], scale=0.125),
                                         reads=[bid, "pbo"], writes=[pid])
                            else:
                                dj = jb - j
                                P.op("act", lambda e, ptb=ptb, stb=stb, c1=c1, nq=nq, dj=dj, tab=tab, h=h, nk=nk: e.activation(
                                    out=ptb[:nk, c1:nq], in_=stb[:nk, c1:nq], func=AF.Exp,
                                    bias=tab[:nk, h, dj + 3:dj + 4], scale=0.125),
                                     reads=[bid, "pbo"], writes=[pid])

                        def pv(ptb=ptb, pid=pid, c=c, j=j, h=h, nk=nk, s_lo=s_lo, fib=first_in_bank, jb=jb, nsb=nsb, nqs_of=nqs_of):
                            for s in range(s_lo, nsb):
                                ob = ps[4 + s]
                                st_flag = fib[s]
                                fib[s] = False
                                nqs = nqs_of(s)
                                last = (j == jb + s)
                                P.op("pe", lambda e, ob=ob, ptb=ptb, s=s, c=c, j=j, h=h, st_flag=st_flag, nk=nk, nqs=nqs, last=last: e.matmul(
                                    ob[:nqs, c * 256:c * 256 + 129], lhsT=ptb[:nk, s * 128:s * 128 + nqs],
                                    rhs=V1[:nk, j, h * 129:(h + 1) * 129], start=st_flag, stop=last,
                                    skip_group_check=True),
                                     reads=[pid, vid(j)], writes=[("bank", 4 + s)])
                        push2(pv)
                push2(lambda h=h, nsb=nsb, nqs_of=nqs_of: finalize(h, nsb, nqs_of))
            push2(lambda ti=ti, q0=q0, nq=nq, b=b, nsb=nsb, nqs_of=nqs_of: tail(q0, nq, b, nsb, nqs_of))
            return nb_next

        def finalize(h, nsb, nqs_of):
            for s in range(nsb):
                n = nqs_of(s)
                ob = ps[4 + s]
                oid = ("bank", 4 + s)
                of = ofin[s]
                fid = ("ofin", s)
                P.op("dve", lambda e, ob=ob, of=of, n=n: e.tensor_copy(
                    out=of[:n, :, :], in_=ob[:n, :].rearrange("p (c x) -> p c x", c=2)[:, :, 0:129]),
                     reads=[oid], writes=[fid])
                r2 = rl[s]
                P.op("dve", lambda e, of=of, r2=r2, n=n: e.reciprocal(out=r2[:n, :], in_=of[:n, :, 128]),
                     reads=[fid], writes=[("rl", s)])
                P.op("pool", lambda e, r2=r2, n=n: e.tensor_tensor(out=r2[:n, 1:2], in0=r2[:n, 1:2], in1=Cn["negl"][:n, :], op=ALU.mult),
                     reads=[("rl", s), "negl"], writes=[("rl", s)])
                oi = o_c[0] % 2
                o_c[0] += 1
                ot = otmp[oi]
                otid = ("otmp", oi)
                P.op("dve", lambda e, of=of, ot=ot, r2=r2, n=n: e.tensor_scalar(
                    out=ot[:n, :], in0=of[:n, 0, 0:128], scalar1=r2[:n, 0:1], scalar2=None, op0=ALU.mult),
                     reads=[fid, ("rl", s)], writes=[otid])
                P.op("dve", lambda e, of=of, ot=ot, r2=r2, n=n: e.scalar_tensor_tensor(
                    out=ot[:n, :], in0=of[:n, 1, 0:128], scalar=r2[:n, 1:2], in1=ot[:n, :], op0=ALU.mult, op1=ALU.add),
                     reads=[fid, ("rl", s), otid], writes=[otid])
                P.op("act", lambda e, ot=ot, s=s, n=n: e.activation(out=junk[:n, 0:128], in_=ot[:n, :], func=AF.Square,
                                                               accum_out=ssn[s][:n, :]),
                     reads=[otid], writes=[("ssn", s), "junkA"])
                P.op("pool", lambda e, s=s, n=n: e.tensor_scalar(out=ssn[s][:n, :], in0=ssn[s][:n, :], scalar1=1.0 / 128, scalar2=EPS,
                                                            op0=ALU.mult, op1=ALU.add), reads=[("ssn", s)], writes=[("ssn", s)])
                P.op("pool", lambda e, s=s, n=n: e.tensor_tensor(out=rsn[s][:n, :], in0=ssn[s][:n, :], in1=B.c_mhalf[:n, :], op=ALU.pow),
                     reads=[("ssn", s)], writes=[("rsn", s)])
                P.op("dve", lambda e, ot=ot, s=s, h=h, n=n: e.scalar_tensor_tensor(
                    out=atok[s][:n, h * 128:(h + 1) * 128], in0=ot[:n, :], scalar=rsn[s][:n, 0:1], in1=Cn["subg8"][:n, :],
                    op0=ALU.mult, op1=ALU.mult), reads=[otid, ("rsn", s), "subg8"], writes=[("atok", s, h)])

        def tail(q0, nq, b, nsb, nqs_of):
            ps_tr = ps[0][:, :].bitcast(BF16)
            for s in range(nsb):
                n = nqs_of(s)
                for h in range(4):
                    P.op("pe", lambda e, s=s, h=h, n=n: e.transpose(
                        out=ps_tr[:, h * 128:h * 128 + n], in_=atok[s][:n, h * 128:(h + 1) * 128], identity=B.ident[:n, :n]),
                         reads=[("atok", s, h)], writes=[("bank", 0)])
                P.op("act", lambda e, s=s, n=n: e.activation(
                    out=attT[:, :, s * 128:s * 128 + n], in_=ps_tr[:, 0:512].rearrange("p (a b) -> p a b", a=4)[:, :, 0:n],
                    func=AF.Copy), reads=[("bank", 0)], writes=[("attT", s)])
            for s in range(nsb):
                n = nqs_of(s)
                mo = (ps[1], ps[2])
                for half in range(2):
                    for kk in range(8):
                        src = attT if kk < 4 else lruT[b]
                        P.op("pe", lambda e, half=half, kk=kk, src=src, s=s, n=n, mo=mo: e.matmul(
                            mo[half][:n, :], lhsT=src[:, kk % 4, s * 128:s * 128 + n],
                            rhs=Wout[:, kk, half * 512:(half + 1) * 512], start=(kk == 0), stop=(kk == 7)),
                             reads=[("attT", s), ("lruT", b), ("Wout", kk)], writes=[("bank", 1 + half)])
                sl = s % 2
                P.op("act", lambda e, sl=sl, n=n: e.activation(out=junk[:n, :], in_=ps[1][:n, :], func=AF.Square, accum_out=ssm[sl][:n, :]),
                     reads=[("bank", 1)], writes=["junkA", ("ssm", sl)])
                P.op("act", lambda e, sl=sl, n=n: e.activation(out=junk[:n, :], in_=ps[2][:n, :], func=AF.Square, accum_out=ssm2[sl][:n, :]),
                     reads=[("bank", 2)], writes=["junkA", ("ssm2", sl)])
                P.op("pool", lambda e, sl=sl, n=n: e.tensor_tensor(out=ssm[sl][:n, :], in0=ssm[sl][:n, :], in1=ssm2[sl][:n, :], op=ALU.add),
                     reads=[("ssm", sl), ("ssm2", sl)], writes=[("ssm", sl)])
                P.op("pool", lambda e, sl=sl, n=n: e.tensor_scalar(out=ssm[sl][:n, :], in0=ssm[sl][:n, :], scalar1=1.0 / D, scalar2=EPS,
                                                              op0=ALU.mult, op1=ALU.add), reads=[("ssm", sl)], writes=[("ssm", sl)])
                P.op("pool", lambda e, sl=sl, n=n: e.tensor_tensor(out=rsm[sl][:n, :], in0=ssm[sl][:n, :], in1=B.c_mhalf[:n, :], op=ALU.pow),
                     reads=[("ssm", sl)], writes=[("rsm", sl)])
                xi = x_c[0] % 2
                x_c[0] += 1
                a = q0 + s * 128
                P.dma(sname + "x%d" % xi, lambda e, xi=xi, a=a, n=n: e.dma_start(
                    out=x1r[xi][:n, :], in_=x1_scr[x1_off + a:x1_off + a + n, :]), writes=[("x1r", xi)])
                for half in range(2):
                    P.op("dve", lambda e, xi=xi, half=half, sl=sl, n=n: e.scalar_tensor_tensor(
                        out=ost[xi][:n, half * 512:(half + 1) * 512], in0=ps[1 + half][:n, :], scalar=rsm[sl][:n, 0:1],
                        in1=Cn["gmb"][:n, half * 512:(half + 1) * 512], op0=ALU.mult, op1=ALU.mult),
                         reads=[("bank", 1 + half), ("rsm", sl), "consts"], writes=[("ost", 0, half)])
                    P.op("pool", lambda e, xi=xi, half=half, n=n: e.tensor_tensor(
                        out=ost[xi][:n, half * 512:(half + 1) * 512], in0=ost[xi][:n, half * 512:(half + 1) * 512],
                        in1=x1r[xi][:n, half * 512:(half + 1) * 512], op=ALU.add),
                         reads=[("ost", 0, half), ("x1r", xi)], writes=[("ost", 0, half)])
                P.dma(sname + "o0", lambda e, xi=xi, a=a, n=n: e.dma_start(out=x2_dst[a:a + n, :], in_=ost[xi][:n, :]),
                      reads=[("ost", 0, 0), ("ost", 0, 1)])

        bcur = load_q(0)
        for ti, (q0, nq) in enumerate(tiles):
            bcur = tile_stream(ti, q0, nq, bcur)
        while dq:
            dq.pop(0)()

    for q in jobs:
        run_job(q["TK"], q["NQ"], q["KT"], q["V"], q["QT"], q["LT"], q["x1"], q["x1_off"], q["x2"], q["mask_other"])
    P.barrier()
    A.pop()


def attn_stage2(B, jobs, Cn, sname, window=(None,) * 4):
    P, A, nc = B.P, B.A, B.nc
    TKmax = max(q["TK"] for q in jobs)
    NBmax = cdiv(TKmax, 128)
    A.push()
    Wout = A.alloc([KC, D], BF16)
    if Cn.get("wout_ready"):
        assert A.off == Cn["wout_off"], (A.off, Cn["wout_off"])
    KT = A.alloc([4, NBmax * 128], BF16)
    V1 = A.alloc([NBmax, 516], BF16)
    VCH = 16
    kv_done = {}

    def issue_kv(TK, KT_scr, V_scr):
        kv_done[id(KT_scr)] = True
        NB = cdiv(TK, 128)
        for h in range(4):
            P.dma(sname + "K%d" % h, lambda e, h=h: e.dma_start(out=KT[:, h, 0:TK], in_=KT_scr[h, :, 0:TK]), writes=[("KT", h)])
        for ci, j0 in enumerate(range(0, NB, VCH)):
            j1 = min(NB, j0 + VCH)
            jf = min(j1, TK // 128)
            if jf > j0:
                P.dma(sname + "V%d" % (ci % 4), lambda e, j0=j0, jf=jf: e.dma_start(
                    out=V1[:, j0:jf, :], in_=V_scr[j0 * 128:jf * 128, :].rearrange("(j p) c -> p j c", p=128)),
                      writes=[("V1", ci)])
            if jf < j1:
                nk = min(128, TK - 128 * jf)
                P.dma(sname + "V%d" % (ci % 4), lambda e, jf=jf, nk=nk: e.dma_start(
                    out=V1[:nk, jf, :], in_=V_scr[jf * 128:jf * 128 + nk, :]), writes=[("V1", ci)])

    issue_kv(jobs[0]["TK"], jobs[0]["KT"], jobs[0]["V"])
    gmb = A.alloc([D], F32)
    P.dma("c0", lambda e: e.dma_start(out=gmb, in_=B.inputs["gmb"]), writes=["gmb"])
    Cn["gmb"] = gmb
    attn_consts(B, Cn)
    g8col = A.alloc([1], F32)
    P.dma("c0", lambda e: e.dma_start(out=g8col, in_=B.inputs["subg_col"]), writes=["g8col"])
    P.op("pool", lambda e: e.tensor_scalar(out=g8col, in0=g8col, scalar1=1.0 - LAMBDA_INIT, scalar2=None, op0=ALU.mult),
         reads=["g8col"], writes=["g8col"])
    ones_bf = A.alloc([128], BF16)
    ones_f = A.alloc([128], F32)
    P.op("pool", lambda e: e.memset(ones_bf, 1.0), writes=["ones_bf"])
    P.op("pool", lambda e: e.memset(ones_f, 1.0), writes=["ones_f"])
    if not Cn.get("wout_ready"):
        mark = A.off
        wst = [A.alloc([D], F32) for _ in range(4)]
        B.prep_weight(B.inputs["wout"], D, D, Wout, None, wst, "Wout")
        P.barrier()
        A.off = mark
    QTILE = 512
    qt = [A.alloc([4, QTILE], BF16) for _ in range(2)]
    NPT = 3
    pt = [A.alloc([2, QTILE], BF16) for _ in range(NPT)]
    dtmp = [A.alloc([2, 128], F32) for _ in range(2)]
    attT = A.alloc([4, QTILE], BF16)
    lruT = [A.alloc([4, QTILE], BF16) for _ in range(2)]
    x1r = [A.alloc([D], F32) for _ in range(2)]
    ost = A.alloc([D], F32)
    rec0 = A.alloc([QTILE], F32)
    rec1 = A.alloc([QTILE], F32)
    o_sb = A.alloc([QTILE], F32)
    o_1 = A.alloc([QTILE], F32)
    junk = A.alloc([512], BF16)
    ssm = [A.alloc([1], F32) for _ in range(2)]
    ssm2 = [A.alloc([1], F32) for _ in range(2)]
    rsm = [A.alloc([1], F32) for _ in range(2)]
    ps = B.psum
    psall = B.psum_all
    stpair = [psall[:, 0:1024].rearrange("p (c x) -> p c x", c=2), psall[:, 1024:2048].rearrange("p (c x) -> p c x", c=2)]
    OTb = (ps[4], ps[5])
    Lb = (ps[6], ps[7])
    pb, pbo, db = Cn["pb"], Cn["pbo"], Cn["db"]
    st_c = [0]
    pt_c = [0]
    dt_c = [0]
    x_c = [0]
    qb_c = [0]

    def take_pair():
        pi = st_c[0] % 2
        st_c[0] += 1
        return pi, [("bank", 2 * pi), ("bank", 2 * pi + 1)]

    def run_job(TK, NQ, KT_scr, V_scr, QT_scr, LT_scr, x1_scr, x1_off, x2_dst, mask_other):
        NB = cdiv(TK, 128)
        KOFF = TK - NQ
        assert KOFF % 128 == 0
        nkof = lambda j: min(128, TK - 128 * j)
        if not kv_done.get(id(KT_scr)):
            issue_kv(TK, KT_scr, V_scr)
        vid = lambda j: ("V1", j // VCH)
        tiles = []
        q0 = 0
        while q0 < NQ:
            nq = min(QTILE, NQ - q0)
            tiles.append((q0, nq))
            q0 += nq
        LOOK = int(os.environ.get("K_LOOK2", "2"))
        dq = []

        late = []

        def tick_late(force=False):
            for it in late:
                it[0] -= 1
            while late and (force or late[0][0] <= 0):
                late.pop(0)[1]()

        def push2(fn):
            dq.append(fn)
            while len(dq) > LOOK:
                dq.pop(0)()
                tick_late()

        def load_q(ti):
            q0, nq = tiles[ti]
            b = qb_c[0] % 2
            qb_c[0] += 1
            for h in range(4):
                P.dma(sname + "q%d" % b, lambda e, b=b, h=h, q0=q0, nq=nq: e.dma_start(
                    out=qt[b][:, h, 0:nq], in_=QT_scr[h, :, q0:q0 + nq]), writes=[("qt", b, h)])

            def ld(b=b, q0=q0, nq=nq):
                P.dma(sname + "l%d" % b, lambda e: e.dma_start(
                    out=lruT[b][:, :, 0:nq], in_=LT_scr[:, :, q0:q0 + nq].rearrange("g p t -> p g t")), writes=[("lruT", b)])
            push2(lambda: late.append([1, ld]))
            return b

        def tile_stream(ti, q0, nq, b):
            nsb = cdiv(nq, 128)
            nqs_of = lambda s: min(128, nq - 128 * s)
            jb = (KOFF + q0) // 128
            nb_next = load_q(ti + 1) if ti + 1 < len(tiles) else None
            for h in range(4):
                persub = (h == 0)
                W = window[h]
                jlo = 0 if W is None else max(0, jb - W)
                first = [True]
                jlast = jb + nsb - 1
                for j in range(jlo, jb + nsb):
                    nk = nkof(j)
                    rel = j - jb
                    s_lo = max(0, rel)
                    c0 = s_lo * 128
                    tab = pbo if (mask_other and j * 128 < KOFF) else pb
                    pi, bids = take_pair()
                    stp = stpair[pi]
                    for c in range(2):
                        P.op("pe", lambda e, stp=stp, c=c, h=h, j=j, c0=c0, nq=nq, b=b, nk=nk: e.matmul(
                            stp[:nk, c, c0:nq], lhsT=KT[c * 64:(c + 1) * 64, h, j * 128:j * 128 + nk],
                            rhs=qt[b][c * 64:(c + 1) * 64, h, c0:nq], start=True, stop=True),
                             reads=[("KT", h), ("qt", b, h)], writes=[bids[c]])
                    ri = pt_c[0] % NPT
                    pt_c[0] += 1
                    ptb = pt[ri]
                    pid = ("pt", ri)
                    c1 = c0
                    if rel >= 0:
                        nqs = nqs_of(rel)
                        di = dt_c[0] % 2
                        dt_c[0] += 1
                        for c in range(2):
                            P.op("dve", lambda e, stp=stp, di=di, h=h, c0=c0, nk=nk, nqs=nqs, c=c: e.scalar_tensor_tensor(
                                out=dtmp[di][:nk, c, :nqs], in0=stp[:nk, c, c0:c0 + nqs], scalar=0.125, in1=db[:nk, h, 0:nqs],
                                op0=ALU.mult, op1=ALU.add), reads=[bids[c], "consts"], writes=[("dtmp", di, c)])
                        bconst = 0.0 if persub else SLOPES[h] * 128.0 * rel
                        P.op("act", lambda e, ptb=ptb, di=di, c0=c0, bconst=bconst, nk=nk, nqs=nqs: e.activation(
                            out=ptb[:nk, :, c0:c0 + nqs], in_=dtmp[di][:nk, :, :nqs], func=AF.Exp, bias=bconst),
                             reads=[("dtmp", di, 0), ("dtmp", di, 1)], writes=[pid])
                        c1 = c0 + 128
                    if c1 < nq:
                        if persub:
                            for s in range(c1 // 128, nsb):
                                dj = jb + s - j
                                ce = s * 128 + nqs_of(s)
                                P.op("act", lambda e, ptb=ptb, stp=stp, s=s, ce=ce, dj=dj, tab=tab, h=h, nk=nk: e.activation(
                                    out=ptb[:nk, :, s * 128:ce], in_=stp[:nk, :, s * 128:ce], func=AF.Exp,
                                    bias=tab[:nk, h, dj + 3:dj + 4], scale=0.125),
                                     reads=bids + ["pbo"], writes=[pid])
                        else:
                            dj = jb - j
                            P.op("act", lambda e, ptb=ptb, stp=stp, c1=c1, nq=nq, dj=dj, tab=tab, h=h, nk=nk: e.activation(
                                out=ptb[:nk, :, c1:nq], in_=stp[:nk, :, c1:nq], func=AF.Exp,
                                bias=tab[:nk, h, dj + 3:dj + 4], scale=0.125),
                                 reads=bids + ["pbo"], writes=[pid])

                    def pv(ptb=ptb, pid=pid, j=j, h=h, nk=nk, c0=c0, nq=nq, first=first, last=(j == jlast)):
                        st_flag = first[0]
                        first[0] = False
                        for c in range(2):
                            P.op("pe", lambda e, c=c: e.matmul(
                                OTb[c][:, c0:nq], lhsT=V1[:nk, j, h * 129:h * 129 + 128], rhs=ptb[:nk, c, c0:nq],
                                start=st_flag, stop=last), reads=[pid, vid(j)], writes=[("bank", 4 + c)])
                        for c in range(2):
                            P.op("pe", lambda e, c=c: e.matmul(
                                Lb[c][:, c0:nq], lhsT=ones_bf[:nk, :], rhs=ptb[:nk, c, c0:nq],
                                start=st_flag, stop=last), reads=[pid, "ones_bf"], writes=[("bank", 6 + c)])
                    push2(pv)
                push2(lambda h=h, nq=nq: finalize(h, nq))
            def tail_late(q0=q0, nq=nq, b=b, nsb=nsb, nqs_of=nqs_of):
                late.append([int(os.environ.get("K_TLATE", "8")), lambda: tail(q0, nq, b, nsb, nqs_of)])
            push2(tail_late)
            return nb_next

        def finalize(h, nq):
            tick_late(force=True)
            P.op("dve", lambda e: e.tensor_copy(out=rec0[:, 0:nq], in_=Lb[0][:, 0:nq]), reads=[("bank", 6)], writes=["rec0"])
            P.op("act", lambda e: e.activation(out=rec1[:, 0:nq], in_=Lb[1][:, 0:nq], func=AF.Copy), reads=[("bank", 7)], writes=["rec1"])
            P.op("dve", lambda e: e.tensor_copy(out=o_sb[:, 0:nq], in_=OTb[0][:, 0:nq]), reads=[("bank", 4)], writes=["o_sb"])
            P.op("act", lambda e: e.activation(out=o_1[:, 0:nq], in_=OTb[1][:, 0:nq], func=AF.Copy), reads=[("bank", 5)], writes=["o_1"])
            P.op("dve", lambda e: e.reciprocal(out=rec0[:, 0:nq], in_=rec0[:, 0:nq]), reads=["rec0"], writes=["rec0"])
            P.op("dve", lambda e: e.reciprocal(out=rec1[:, 0:nq], in_=rec1[:, 0:nq]), reads=["rec1"], writes=["rec1"])
            P.op("dve", lambda e: e.tensor_tensor(out=rec0[:, 0:nq], in0=o_sb[:, 0:nq], in1=rec0[:, 0:nq], op=ALU.mult),
                 reads=["o_sb", "rec0"], writes=["rec0"])
            P.op("dve", lambda e: e.tensor_tensor(out=rec1[:, 0:nq], in0=o_1[:, 0:nq], in1=rec1[:, 0:nq], op=ALU.mult),
                 reads=["o_1", "rec1"], writes=["rec1"])
            P.op("dve", lambda e: e.scalar_tensor_tensor(out=o_sb[:, 0:nq], in0=rec1[:, 0:nq], scalar=Cn["negl"][:, 0:1],
                                                         in1=rec0[:, 0:nq], op0=ALU.mult, op1=ALU.add),
                 reads=["rec0", "rec1", "negl"], writes=["o_sb"])
            P.op("pool", lambda e: e.tensor_tensor(out=rec0[:, 0:nq], in0=o_sb[:, 0:nq], in1=o_sb[:, 0:nq], op=ALU.mult),
                 reads=["o_sb"], writes=["rec0"])
            late.append([int(os.environ.get("K_LATE", "5")), lambda: finalize_b(h, nq)])

        def finalize_b(h, nq):
            pi, bids = take_pair()
            ssb = ps[2 * pi]
            P.op("pe", lambda e: e.matmul(ssb[:, 0:nq], lhsT=ones_f[:, :], rhs=rec0[:, 0:nq], start=True, stop=True),
                 reads=["rec0", "ones_f"], writes=bids)
            P.op("act", lambda e: e.activation(out=rec1[:, 0:nq], in_=ssb[:, 0:nq], func=AF.Ln, scale=1.0 / 128, bias=EPS),
                 reads=[bids[0]], writes=["rec1"])
            P.op("act", lambda e: e.activation(out=rec1[:, 0:nq], in_=rec1[:, 0:nq], func=AF.Exp, scale=-0.5),
                 reads=["rec1"], writes=["rec1"])
            P.op("dve", lambda e: e.scalar_tensor_tensor(out=attT[:, h, 0:nq], in0=o_sb[:, 0:nq], scalar=g8col[:, 0:1],
                                                         in1=rec1[:, 0:nq], op0=ALU.mult, op1=ALU.mult),
                 reads=["o_sb", "rec1", "g8col"], writes=[("attT", h)])

        def tail(q0, nq, b, nsb, nqs_of):
            for s in range(nsb):
                n = nqs_of(s)
                pi, bids = take_pair()
                mo = (ps[2 * pi], ps[2 * pi + 1])
                for half in range(2):
                    for kk in range(8):
                        src = attT if kk < 4 else lruT[b]
                        P.op("pe", lambda e, half=half, kk=kk, src=src, s=s, n=n, mo=mo: e.matmul(
                            mo[half][:n, :], lhsT=src[:, kk % 4, s * 128:s * 128 + n],
                            rhs=Wout[:, kk, half * 512:(half + 1) * 512], start=(kk == 0), stop=(kk == 7)),
                             reads=[("attT", kk % 4), ("lruT", b), ("Wout", kk)], writes=[bids[half]])
                sl = s % 2
                P.op("act", lambda e, sl=sl, n=n, mo=mo: e.activation(out=junk[:n, :], in_=mo[0][:n, :], func=AF.Square, accum_out=ssm[sl][:n, :]),
                     reads=[bids[0]], writes=["junkA", ("ssm", sl)])
                P.op("act", lambda e, sl=sl, n=n, mo=mo: e.activation(out=junk[:n, :], in_=mo[1][:n, :], func=AF.Square, accum_out=ssm2[sl][:n, :]),
                     reads=[bids[1]], writes=["junkA", ("ssm2", sl)])
                P.op("pool", lambda e, sl=sl, n=n: e.tensor_tensor(out=ssm[sl][:n, :], in0=ssm[sl][:n, :], in1=ssm2[sl][:n, :], op=ALU.add),
                     reads=[("ssm", sl), ("ssm2", sl)], writes=[("ssm", sl)])
                P.op("pool", lambda e, sl=sl, n=n: e.tensor_scalar(out=ssm[sl][:n, :], in0=ssm[sl][:n, :], scalar1=1.0 / D, scalar2=EPS,
                                                              op0=ALU.mult, op1=ALU.add), reads=[("ssm", sl)], writes=[("ssm", sl)])
                P.op("pool", lambda e, sl=sl, n=n: e.tensor_tensor(out=rsm[sl][:n, :], in0=ssm[sl][:n, :], in1=B.c_mhalf[:n, :], op=ALU.pow),
                     reads=[("ssm", sl)], writes=[("rsm", sl)])
                xi = x_c[0] % 2
                x_c[0] += 1
                a = q0 + s * 128
                P.dma(sname + "x%d" % xi, lambda e, xi=xi, a=a, n=n: e.dma_start(
                    out=x1r[xi][:n, :], in_=x1_scr[x1_off + a:x1_off + a + n, :]), writes=[("x1r", xi)])
                for half in range(2):
                    P.op("dve", lambda e, half=half, sl=sl, n=n, mo=mo: e.scalar_tensor_tensor(
                        out=ost[:n, half * 512:(half + 1) * 512], in0=mo[half][:n, :], scalar=rsm[sl][:n, 0:1],
                        in1=Cn["gmb"][:n, half * 512:(half + 1) * 512], op0=ALU.mult, op1=ALU.mult),
                         reads=[bids[half], ("rsm", sl), "consts"], writes=[("ost", half)])
                    P.op("pool", lambda e, xi=xi, half=half, n=n: e.tensor_tensor(
                        out=ost[:n, half * 512:(half + 1) * 512], in0=ost[:n, half * 512:(half + 1) * 512],
                        in1=x1r[xi][:n, half * 512:(half + 1) * 512], op=ALU.add),
                         reads=[("ost", half), ("x1r", xi)], writes=[("ost", half)])
                P.dma(sname + "o0", lambda e, a=a, n=n: e.dma_start(out=x2_dst[a:a + n, :], in_=ost[:n, :]),
                      reads=[("ost", 0), ("ost", 1)])

        bcur = load_q(0)
        for ti, (q0, nq) in enumerate(tiles):
            bcur = tile_stream(ti, q0, nq, bcur)
        while dq:
            dq.pop(0)()
        tick_late(force=True)

    for q in jobs:
        run_job(q["TK"], q["NQ"], q["KT"], q["V"], q["QT"], q["LT"], q["x1"], q["x1_off"], q["x2"], q["mask_other"])
    P.barrier()
    A.pop()


def cache_prep_ops(B, ck, cv, KT_s, V_s, PAST, sname):
    P, A = B.P, B.A
    NSTEP = PAST // 512
    kin = [A.alloc([4, 512], F32) for _ in range(2)]
    kbf = [A.alloc([4, 512], BF16) for _ in range(2)]
    kT = [A.alloc([4, 512], BF16) for _ in range(2)]
    vin = [A.alloc([4, 512], F32) for _ in range(2)]
    vb = [A.alloc([4, 516], BF16) for _ in range(2)]
    for i in range(2):
        P.op("pool", lambda e, i=i: e.memset(vb[i], 1.0), writes=[("cvb", i)])
    ps = B.psum
    for st in range(NSTEP):
        r = st % 2
        a = st * 512
        P.dma(sname + "k%d" % r, lambda e, r=r, a=a: e.dma_start(
            out=kin[r], in_=ck[a:a + 512, :].rearrange("(j p) c -> p j c", p=128)), writes=[("ckin", r)])
        P.dma(sname + "v%d" % r, lambda e, r=r, a=a: e.dma_start(
            out=vin[r], in_=cv[a:a + 512, :].rearrange("(j p) c -> p j c", p=128)), writes=[("cvin", r)])
        P.op("dve", lambda e, r=r: e.tensor_copy(out=kbf[r], in_=kin[r]), reads=[("ckin", r)], writes=[("ckbf", r)])
        for jj in range(4):
            bank = ps[4 + (st * 4 + jj) % 4]
            bid = ("bank", 4 + (st * 4 + jj) % 4)
            ps_tr = bank[:, :].bitcast(BF16)
            for h in range(4):
                P.op("pe", lambda e, ps_tr=ps_tr, h=h, r=r, jj=jj: e.transpose(
                    out=ps_tr[:, h * 128:(h + 1) * 128], in_=kbf[r][:, jj, h * 128:(h + 1) * 128], identity=B.ident),
                     reads=[("ckbf", r)], writes=[bid])
            P.op("act", lambda e, ps_tr=ps_tr, r=r, jj=jj: e.activation(
                out=kT[r][:, :, jj * 128:(jj + 1) * 128], in_=ps_tr[:, 0:512].rearrange("p (a b) -> p a b", a=4), func=AF.Copy),
                 reads=[bid], writes=[("ckT", r, jj)])
        P.dma(sname + "ko%d" % r, lambda e, r=r, a=a: e.dma_start(
            out=KT_s[:, :, a:a + 512].rearrange("h p t -> p h t"), in_=kT[r]),
              reads=[("ckT", r, x) for x in range(4)])
        for jj in range(4):
            P.op("pool" if jj % 2 else "dve", lambda e, r=r, jj=jj: e.tensor_copy(
                out=vb[r][:, jj, :].rearrange("p (h d) -> p h d", h=4)[:, :, 0:128],
                in_=vin[r][:, jj, :].rearrange("p (h d) -> p h d", h=4)),
                 reads=[("cvin", r), ("cvb", r)], writes=[("cvb", r, jj)])
        P.dma(sname + "vo%d" % r, lambda e, r=r, a=a: e.dma_start(
            out=V_s[a:a + 512, :].rearrange("(j p) c -> p j c", p=128), in_=vb[r]),
              reads=[("cvb", r, x) for x in range(4)] + [("cvb", r)])


SMALL_INPUTS = [
    ("gma_col", [KC]), ("g1a_col", [KC]), ("g2a_col", [KC]),
    ("convw", [4, 4]), ("convb", [4]), ("brg", [4]), ("big", [4]), ("lam", [4]),
    ("flag", [1]), ("maskv", [1]),
]


def build_main(T_OTH, T_OWN, with_sample=True, window=(None,) * 4, stages=("A1", "A2", "C", "D")):
    if os.environ.get("K_WIN", "1") == "1":
        window = (4, 16, None, None)
    B = Builder(T_OTH, T_OWN)
    P = B.P
    T = T_OTH + T_OWN
    x = B.din("x", [T, D])
    for nm, shp in (("f1g", [D, DFF]), ("f1u", [D, DFF]), ("f1d", [DFF, D]),
                    ("f2g", [D, DFF]), ("f2u", [D, DFF]), ("f2d", [DFF, D]),
                    ("win", [D, 2560]), ("wout", [D, D])):
        B.din(nm, shp)
    for nm, shp in (("g1b_bc", [128, D]), ("g2b_bc", [128, D]), ("gmb", [128, D]),
                    ("wr_bd", [128, 4, 128]), ("wi_bd", [128, 4, 128]),
                    ("pb", [128, 4, 71]), ("db", [128, 4, 128]), ("subg", [128, 128]), ("subg_col", [128, 1]),
                    ("lq1", [128, 64]), ("lk1", [128, 64]), ("lq2", [128, 64]), ("lk2", [128, 64])):
        B.din(nm, shp)
    y = B.dout("y", [T_OWN, D])
    ko = B.dout("ko", [T_OWN, 512])
    vo = B.dout("vo", [T_OWN, 512])
    hl = B.dout("hl", [512])
    cb = B.dout("cb", [3, 512])
    x1_scr = B.dscr("x1_scr", [T, D])
    x2_scr = B.dscr("x2_scr", [T_OWN, D])
    KT_scr = B.dscr("KT_scr", [4, 128, T], BF16)
    V_scr = B.dscr("V_scr", [T, 516], BF16)
    QT_scr = B.dscr("QT_scr", [4, 128, T_OWN], BF16)
    LT_scr = B.dscr("LT_scr", [4, 128, T_OWN], BF16)
    S = {}
    NSAMP, PAST = 32, 4096
    if with_sample:
        S["xs"] = B.din("xs", [NSAMP, D])
        S["ck"] = B.din("ck", [PAST, 512])
        S["cv"] = B.din("cv", [PAST, 512])
        S["sh"] = B.din("sh", [512])
        S["sc"] = B.din("sc", [3, 512])
        S["ys"] = B.dout("ys", [NSAMP, D])
        S["kso"] = B.dout("kso", [NSAMP, 512])
        S["vso"] = B.dout("vso", [NSAMP, 512])
        S["hls"] = B.dout("hls", [512])
        S["cbs"] = B.dout("cbs", [3, 512])
        S["xs1"] = B.dscr("xs1_scr", [NSAMP, D])
        S["xs2"] = B.dscr("xs2_scr", [NSAMP, D])
        S["KT"] = B.dscr("KTs_scr", [4, 128, PAST + NSAMP], BF16)
        S["V"] = B.dscr("Vs_scr", [PAST + NSAMP, 516], BF16)
        S["QT"] = B.dscr("QTs_scr", [4, 128, NSAMP], BF16)
        S["LT"] = B.dscr("LTs_scr", [4, 128, NSAMP], BF16)
    B.consts()
    Cn = {}
    for nm, shp in SMALL_INPUTS:
        Cn[nm] = B.load_const(nm, shp)
    P.barrier()
    inp = B.inputs
    if "A1" in stages:
        segs = [(x, x1_scr, T)] + ([(S["xs"], S["xs1"], NSAMP)] if with_sample else [])
        B.ffn_stage(segs, inp["f1g"], inp["f1u"], inp["f1d"], Cn["g1a_col"], inp["g1b_bc"], "a")
    if with_sample and "A2" in stages:
        Cn["cache_prep"] = (S["ck"], S["cv"], S["KT"], S["V"], PAST, "p")
    if "A2" in stages:
        seqs = [dict(x1=x1_scr, T=T, T_OTH=T_OTH, KT=KT_scr, V=V_scr, QT=QT_scr, LT=LT_scr, ko=ko, vo=vo, hl=hl, cb=cb)]
        if with_sample:
            seqs.append(dict(x1=S["xs1"], T=NSAMP, T_OTH=0, KT=S["KT"], V=S["V"], QT=S["QT"], LT=S["LT"],
                             ko=S["kso"], vo=S["vso"], hl=S["hls"], cb=S["cbs"], h0=S["sh"], conv0=S["sc"], koff=PAST))
        mixer_in_stage(B, seqs, Cn, "m")
    if "C" in stages:
        jobs = [dict(TK=T, NQ=T_OWN, KT=KT_scr, V=V_scr, QT=QT_scr, LT=LT_scr, x1=x1_scr, x1_off=T_OTH, x2=x2_scr,
                     mask_other=True)]
        if with_sample:
            jobs.append(dict(TK=PAST + NSAMP, NQ=NSAMP, KT=S["KT"], V=S["V"], QT=S["QT"], LT=S["LT"], x1=S["xs1"],
                             x1_off=0, x2=S["xs2"], mask_other=False))
        (attn_stage2 if os.environ.get("K_ATT", "2") == "2" else attn_stage)(B, jobs, Cn, "c", window=window)
    if "D" in stages:
        segs = [(x2_scr, y, T_OWN)] + ([(S["xs2"], S["ys"], NSAMP)] if with_sample else [])
        B.ffn_stage(segs, inp["f2g"], inp["f2u"], inp["f2d"], Cn["g2a_col"], inp["g2b_bc"], "d")
    B.P.emit()
    return B


def _col(v, n):
    return np.ascontiguousarray(np.asarray(v, np.float32).reshape(n, 128).T)


def _bc(v):
    v = np.asarray(v, np.float32).reshape(1, -1)
    return np.ascontiguousarray(np.broadcast_to(v, (128, v.shape[1])))


def _block_diag(w):
    out = np.zeros((128, 4, 128), np.float32)
    for g in range(4):
        for hb in range(2):
            out[hb * 64:(hb + 1) * 64, g, hb * 64:(hb + 1) * 64] = w[2 * g + hb]
    return out


def _tables():
    k = np.arange(128, dtype=np.float64)
    pb = np.zeros((128, 4, 71), np.float32)
    db = np.zeros((128, 4, 128), np.float32)
    kk = k[:, None]
    qq = k[None, :]
    for h in range(4):
        sl = SLOPES[h]
        for dj in range(-3, 68):
            pb[:, h, dj + 3] = sl * (k - 128.0 * dj)
        v = np.where(kk <= qq, sl * kk, sl * (2 * qq - kk))
        v = np.where((kk // 64) > (qq // 64), NEG, v)
        db[:, h, :] = v
    return pb, db


def shared_inputs(inputs):
    import ml_dtypes
    g = lambda n: np.asarray(inputs[n], np.float32)
    pb, db = _tables()
    d = {
        "f1g": g("ffn1_w_gate")[0], "f1u": g("ffn1_w_up")[0], "f1d": g("ffn1_w_down")[0],
        "f2g": g("ffn2_w_gate")[0], "f2u": g("ffn2_w_up")[0], "f2d": g("ffn2_w_down")[0],
        "win": g("w_in")[0], "wout": g("w_out")[0],
        "g1b_bc": _bc(g("g_ffn1_post")[0]), "g2b_bc": _bc(g("g_ffn2_post")[0]), "gmb": _bc(g("g_mix_post")[0]),
        "wr_bd": _block_diag(g("w_rgate")[0]), "wi_bd": _block_diag(g("w_igate")[0]),
        "pb": pb, "db": db, "subg": _bc(g("subln_g")[0]), "subg_col": _col(g("subln_g")[0], 1),
        "lq1": _bc(g("lambda_q1")[0]), "lk1": _bc(g("lambda_k1")[0]),
        "lq2": _bc(g("lambda_q2")[0]), "lk2": _bc(g("lambda_k2")[0]),
        "gma_col": _col(g("g_mix_pre")[0], 8), "g1a_col": _col(g("g_ffn1_pre")[0], 8), "g2a_col": _col(g("g_ffn2_pre")[0], 8),
        "convw": np.ascontiguousarray(g("conv_w")[0].reshape(4, 4, 128).transpose(2, 1, 0)),
        "convb": _col(g("conv_b")[0], 4), "brg": _col(g("b_rgate")[0], 4), "big": _col(g("b_igate")[0], 4),
        "lam": _col(g("lru_lambda")[0], 4),
        "ident": np.eye(128).astype(ml_dtypes.bfloat16),
    }
    return d


_CACHE = {}


def kernel(**inputs):
    TH = 4096
    WITH_SAMPLE = bool(int(os.environ.get("K_SAMPLE", "1")))
    key = ("main", TH, WITH_SAMPLE)
    if key not in _CACHE:
        _CACHE[key] = build_main(TH, TH, with_sample=WITH_SAMPLE)
    B = _CACHE[key]
    sh = shared_inputs(inputs)
    xp = np.asarray(inputs["x_prompt"], np.float32)
    maps = []
    for c in range(8):
        b, r = c // 2, c % 2
        own = xp[b, r * TH:(r + 1) * TH]
        oth = xp[b, (1 - r) * TH:(2 - r) * TH]
        m = dict(sh)
        m["x"] = np.ascontiguousarray(np.concatenate([oth, own], 0))
        m["flag"] = np.full((128, 1), float(r), np.float32)
        m["maskv"] = np.full((128, 1), 0.0 if r == 1 else NEG, np.float32)
        if WITH_SAMPLE:
            m["xs"] = np.ascontiguousarray(np.asarray(inputs["x_sample"], np.float32)[c])
            m["ck"] = np.ascontiguousarray(np.asarray(inputs["cache_k"], np.float32)[0, c].reshape(4096, 512))
            m["cv"] = np.ascontiguousarray(np.asarray(inputs["cache_v"], np.float32)[0, c].reshape(4096, 512))
            m["sh"] = np.ascontiguousarray(np.asarray(inputs["state_lru_h"], np.float32)[0, c])
            m["sc"] = np.ascontiguousarray(np.asarray(inputs["state_conv"], np.float32)[0, c])
        m = {k: v for k, v in m.items() if k in B.inputs}
        maps.append(m)
    res = run_bass_kernel_spmd(B.nc, maps, core_ids=list(range(8))).results
    y = np.zeros((4, 8192, 1024), np.float32)
    kp = np.zeros((1, 4, 8192, 4, 2, 64), np.float32)
    vp = np.zeros((1, 4, 8192, 4, 128), np.float32)
    hp = np.zeros((1, 4, 512), np.float32)
    cp = np.zeros((1, 4, 3, 512), np.float32)
    ys = np.zeros((8, 32, 1024), np.float32)
    ks = np.zeros((1, 8, 32, 4, 2, 64), np.float32)
    vs = np.zeros((1, 8, 32, 4, 128), np.float32)
    hs = np.zeros((1, 8, 512), np.float32)
    cs = np.zeros((1, 8, 3, 512), np.float32)
    for c in range(8):
        b, r = c // 2, c % 2
        o = res[c]
        sl = slice(r * TH, (r + 1) * TH)
        y[b, sl] = o["y"]
        kp[0, b, sl] = o["ko"].reshape(TH, 4, 2, 64)
        vp[0, b, sl] = o["vo"].reshape(TH, 4, 128)
        if r == 1:
            hp[0, b] = o["hl"]
            cp[0, b] = o["cb"]
        if WITH_SAMPLE:
            ys[c] = o["ys"]
            ks[0, c] = o["kso"].reshape(32, 4, 2, 64)
            vs[0, c] = o["vso"].reshape(32, 4, 128)
            hs[0, c] = o["hls"]
            cs[0, c] = o["cbs"]
    return (y, ys, kp, vp, hp, cp, ks, vs, hs, cs)
```

```python
import numpy as np
import concourse.bass as bass
import concourse.mybir as mybir
from concourse.bass_utils import run_bass_kernel_spmd
from contextlib import ExitStack

F32 = mybir.dt.float32
BF16 = mybir.dt.bfloat16
AF = mybir.ActivationFunctionType
ALU = mybir.AluOpType
AX = mybir.AxisListType

D = 1024
DFF = 2816
NFF = DFF // 128
KC = D // 128
EPS = 1e-6
NEG = -1e30


class Op:
    __slots__ = ("eng", "fn", "deps", "sem", "inc", "val", "is_dma", "needs_inc")


class Prog:
    ENGS = ("pe", "act", "dve", "pool", "sp")

    def __init__(self, nc, stack):
        self.nc = nc
        self.stack = stack
        self.ops = {e: [] for e in self.ENGS}
        self.last_w = {}
        self.readers = {}
        self.barrier_ops = []
        self.esem = {e: stack.enter_context(nc.semaphore("s_" + e)) for e in ("pe", "act", "dve", "pool")}
        self.dma_sems = {}
        self.dma_last = {}
        self.n_sem = 4

    def dsem(self, name):
        if name not in self.dma_sems:
            self.dma_sems[name] = self.stack.enter_context(self.nc.semaphore("d_" + name))
            self.n_sem += 1
        return name

    PSUM_IDS = ("gu", "dn", "fm", "tm", "rg", "bank", "cv")

    def _is_psum(self, t):
        return t == "ps_tr" or (isinstance(t, tuple) and t[0] in self.PSUM_IDS)

    def _add(self, o, reads, writes):
        xr = [t for t in reads if self._is_psum(t)]
        if xr:
            reads = [t for t in reads if not self._is_psum(t)]
            writes = list(writes) + xr
        deps = set(self.barrier_ops)
        for t in reads:
            w = self.last_w.get(t)
            if w is not None:
                deps.add(w)
        for t in writes:
            w = self.last_w.get(t)
            if w is not None:
                deps.add(w)
            for r in self.readers.get(t, ()):
                deps.add(r)
        if o.eng == "pe" and not o.is_dma:
            deps = {d for d in deps if not (d.eng == "pe" and not d.is_dma)}
        deps = {(self.dma_last[d.sem] if d.is_dma else d) for d in deps}
        o.deps = deps
        for d in deps:
            d.needs_inc = True
        for t in reads:
            self.readers.setdefault(t, []).append(o)
        for t in writes:
            self.last_w[t] = o
            self.readers[t] = []
        self.ops[o.eng].append(o)

    def op(self, eng, fn, reads=(), writes=()):
        o = Op()
        o.eng = eng
        o.fn = fn
        o.is_dma = False
        o.needs_inc = False
        o.sem = None
        o.val = 0
        self._add(o, reads, writes)
        return o

    def dma(self, sem_name, fn, reads=(), writes=(), q="sp"):
        o = Op()
        o.eng = q
        o.fn = fn
        o.is_dma = True
        o.needs_inc = True
        o.sem = self.dsem(sem_name)
        o.val = 0
        self._add(o, reads, writes)
        self.dma_last[sem_name] = o
        return o

    def barrier(self):
        b = []
        for e in self.ENGS:
            for o in reversed(self.ops[e]):
                if not o.is_dma:
                    b.append(o)
                    break
        for o in self.dma_last.values():
            b.append(o)
        for o in b:
            o.needs_inc = True
        self.barrier_ops = b
        self.last_w = {}
        self.readers = {}

    def emit(self):
        nc = self.nc
        dcount = {}
        for e in self.ENGS:
            cnt = 0
            for o in self.ops[e]:
                if o.is_dma:
                    dcount[o.sem] = dcount.get(o.sem, 0) + 16
                    o.val = dcount[o.sem]
                elif o.needs_inc:
                    cnt += 1
                    o.val = cnt
        ops = self.ops
        esem = self.esem
        dsems = self.dma_sems

        def run(ename, eng):
            waited = {}
            for o in ops[ename]:
                need = {}
                for d in o.deps:
                    key = d.sem if d.is_dma else d.eng
                    if d.val > need.get(key, 0):
                        need[key] = d.val
                for key, v in need.items():
                    if waited.get(key, 0) < v:
                        sem = esem[key] if key in esem else dsems[key]
                        eng.wait_ge(sem, v)
                        waited[key] = v
                ins = o.fn(eng)
                if o.is_dma:
                    ins.then_inc(dsems[o.sem], 16)
                elif o.needs_inc:
                    ins.then_inc(esem[ename], 1)
            fin = {}
            for o in ops[ename]:
                if o.is_dma:
                    fin[o.sem] = max(fin.get(o.sem, 0), o.val)
            for s, v in fin.items():
                if waited.get(s, 0) < v:
                    eng.wait_ge(dsems[s], v)

        with nc.Block() as block:
            @block.tensor
            def _(e):
                run("pe", e)

            @block.scalar
            def _(e):
                run("act", e)

            @block.vector
            def _(e):
                run("dve", e)

            @block.gpsimd
            def _(e):
                run("pool", e)

            @block.sync
            def _(e):
                run("sp", e)


class Arena:
    def __init__(self, nc, stack, nbytes):
        self.t = stack.enter_context(nc.sbuf_tensor("arena", [128, nbytes // 4], F32))
        self.cap = nbytes
        self.off = 0
        self.marks = []

    def push(self):
        self.marks.append(self.off)

    def pop(self):
        self.off = self.marks.pop()

    def alloc(self, shape, dt):
        n = 1
        for s in shape:
            n *= s
        esz = 4 if dt == F32 else 2
        nb = (n * esz + 31) // 32 * 32
        assert self.off + nb <= self.cap, ("arena overflow", self.off, nb, self.cap)
        ap = self.t[:, self.off // 4:(self.off + nb) // 4]
        self.off += nb
        self.peak = max(getattr(self, 'peak', 0), self.off)
        if dt != F32:
            ap = ap.bitcast(dt)
        ap = ap[:, 0:n]
        if len(shape) == 2:
            ap = ap.rearrange("p (a b) -> p a b", a=shape[0])
        elif len(shape) == 3:
            ap = ap.rearrange("p (a b c) -> p a b c", a=shape[0], b=shape[1])
        return ap


import os
DBG = set(os.environ.get("KDBG", "").split(","))


def cdiv(a, b):
    return (a + b - 1) // b


class Builder:
    def __init__(self, T_OTH, T_OWN, n_samp=32, past=4096, stages="all"):
        self.T_OTH, self.T_OWN, self.NS, self.PAST = T_OTH, T_OWN, n_samp, past
        self.T = T_OTH + T_OWN
        self.stages = stages
        self.nc = bass.Bass("TRN2", target_bir_lowering=False)
        self.stack = ExitStack()
        self.P = Prog(self.nc, self.stack)
        self.A = Arena(self.nc, self.stack, 211456)
        self.inputs = {}
        self.outputs = {}
        nc = self.nc
        self.psum_all = self.stack.enter_context(nc.psum_tensor("psall", [128, 4096], F32))
        self.psum = [self.psum_all[:, i * 512:(i + 1) * 512] for i in range(8)]

    def din(self, name, shape, dt=F32):
        t = self.nc.dram_tensor(name, list(shape), dt, kind="ExternalInput").ap()
        self.inputs[name] = t
        return t

    def dout(self, name, shape, dt=F32):
        t = self.nc.dram_tensor(name, list(shape), dt, kind="ExternalOutput").ap()
        self.outputs[name] = t
        return t

    def dscr(self, name, shape, dt=F32):
        return self.nc.dram_tensor(name, list(shape), dt, kind="Internal").ap()

    def prep_weight(self, w_dram, K, N, dst, gcol, stage_bufs, tag, eng_cycle=("dve", "pool")):
        P = self.P
        nk = K // 128
        for kc in range(nk):
            sb = stage_bufs[kc % len(stage_bufs)]
            sid = ("wst", kc % len(stage_bufs))
            src = w_dram[kc * 128:(kc + 1) * 128, :]
            P.dma("wst%d" % (kc % len(stage_bufs)),
                  lambda e, sb=sb, src=src, N=N: e.dma_start(out=sb[:, 0:N], in_=src),
                  writes=[sid])
            ec = tuple(os.environ.get("K_PREP", "dve,act").split(","))
            en = ec[kc % len(ec)]
            if en == "act":
                if gcol is None:
                    P.op("act", lambda e, sb=sb, kc=kc, N=N, dst=dst: e.activation(out=dst[:, kc, :], in_=sb[:, 0:N], func=AF.Copy),
                         reads=[sid], writes=[(tag, kc)])
                else:
                    P.op("act", lambda e, sb=sb, kc=kc, N=N, dst=dst, gcol=gcol: e.activation(
                        out=dst[:, kc, :], in_=sb[:, 0:N], func=AF.Copy, scale=gcol[:, kc:kc + 1]),
                         reads=[sid, "consts"], writes=[(tag, kc)])
                continue
            if gcol is None:
                P.op(en, lambda e, sb=sb, kc=kc, N=N, dst=dst: e.tensor_copy(out=dst[:, kc, :], in_=sb[:, 0:N]),
                     reads=[sid], writes=[(tag, kc)])
            else:
                P.op(en, lambda e, sb=sb, kc=kc, N=N, dst=dst, gcol=gcol: e.tensor_scalar(
                    out=dst[:, kc, :], in0=sb[:, 0:N], scalar1=gcol[:, kc:kc + 1], scalar2=None, op0=ALU.mult),
                     reads=[sid, "consts"], writes=[(tag, kc)])

    def rstd_of(self, src_ap, np_, junk, ss, rstd, src_ids, tagid):
        P = self.P
        P.op("act", lambda e: e.activation(out=junk[:np_, :], in_=src_ap, func=AF.Square, accum_out=ss[:np_, :]),
             reads=src_ids, writes=[("junk", tagid), ("ss", tagid)])
        P.op("pool", lambda e: e.tensor_scalar(out=ss[:np_, :], in0=ss[:np_, :], scalar1=1.0 / D, scalar2=EPS,
                                               op0=ALU.mult, op1=ALU.add),
             reads=[("ss", tagid)], writes=[("ss", tagid)])
        P.op("pool", lambda e: e.tensor_tensor(out=rstd[:np_, :], in0=ss[:np_, :], in1=self.c_mhalf[:np_, :], op=ALU.pow),
             reads=[("ss", tagid), "consts"], writes=[("rstd", tagid)])

    def ffn_stage(self, segs, wg_d, wu_d, wd_d, gpre_col, gpost_bc, sname):
        P, A, nc = self.P, self.A, self.nc
        A.push()
        Wg = A.alloc([KC, DFF], BF16)
        Wu = A.alloc([KC, DFF], BF16)
        Wd = A.alloc([NFF, D], BF16)
        gph = A.alloc([D], F32)
        mark_act = A.off
        wst = [A.alloc([DFF], F32) for _ in range(5)]
        P.dma("c0", lambda e: e.dma_start(out=gph[:, :], in_=gpost_bc), writes=["gph"])
        P.op("pool", lambda e: e.tensor_scalar(out=gph[:, :], in0=gph[:, :], scalar1=0.5, scalar2=None,
                                               op0=ALU.mult), reads=["gph"], writes=["gph"])
        self.prep_weight(wg_d, D, DFF, Wg, gpre_col, wst, "Wg")
        self.prep_weight(wu_d, D, DFF, Wu, gpre_col, wst, "Wu")
        self.prep_weight(wd_d, DFF, D, Wd, None, wst, "Wd")
        P.barrier()
        A.off = mark_act
        TT = 256
        NXR = 5
        xr = [A.alloc([D], F32) for _ in range(NXR)]
        xs = [A.alloc([D], BF16) for _ in range(2)]
        xnT = [A.alloc([KC, TT], BF16) for _ in range(2)]
        actT = A.alloc([NFF, TT], BF16)
        stmp = [A.alloc([TT], F32) for _ in range(3)]
        ost = [A.alloc([D], F32) for _ in range(2)]
        junk = A.alloc([D], BF16)
        ssb = [A.alloc([1], F32) for _ in range(4)]
        rsb = [A.alloc([1], F32) for _ in range(4)]
        ssp = [A.alloc([1], F32) for _ in range(4)]
        ssq = [A.alloc([1], F32) for _ in range(4)]
        ps = self.psum
        ps_tr = ps[0][:, :].bitcast(BF16)
        gu = [ps[1], ps[2], ps[3]]
        dn = [(ps[4], ps[5]), (ps[6], ps[7])]

        tiles = []
        for (src, dst, n) in segs:
            t0 = 0
            while t0 < n:
                nt = min(TT, n - t0)
                tiles.append((src, dst, t0, nt))
                t0 += nt
        sub_ctr = [0]

        def load_tile(ti):
            src, dst, t0, nt = tiles[ti]
            subs = []
            for s0 in range(0, nt, 128):
                ns = min(128, nt - s0)
                k = sub_ctr[0] % NXR
                sub_ctr[0] += 1
                P.dma(sname + "x%d" % k, lambda e, k=k, src=src, a=t0 + s0, ns=ns: e.dma_start(
                    out=xr[k][:ns, :], in_=src[a:a + ns, :]), writes=[("xr", k)])
                subs.append((k, s0, ns))
            return subs

        loaded = {0: load_tile(0)}
        gu_ctr = 0
        st_ctr = 0
        o_ctr = 0
        subs_of = {}
        gu_state = {'gu': 0, 'st': 0, 'o': 0}

        def prenorm(ti):
            src, dst, t0, nt = tiles[ti]
            subs = loaded.pop(ti)
            subs_of[ti] = subs
            xb = xnT[ti % 2]
            for si, (k, s0, ns) in enumerate(subs):
                sl = (ti * 2 + si) % 4
                self.rstd_of(xr[k][:ns, :], ns, junk, ssb[sl], rsb[sl], [("xr", k)], sl)
                xsb = xs[(ti * 2 + si) % 2]
                xsid = ("xs", (ti * 2 + si) % 2)
                P.op("dve", lambda e, xsb=xsb, k=k, ns=ns, sl=sl: e.tensor_scalar(
                    out=xsb[:ns, :], in0=xr[k][:ns, :], scalar1=rsb[sl][:ns, :], scalar2=None, op0=ALU.mult),
                     reads=[("xr", k), ("rstd", sl)], writes=[xsid])
                for kc in range(KC):
                    P.op("pe", lambda e, xsb=xsb, kc=kc, ns=ns: e.transpose(
                        out=ps_tr[:, kc * 128:kc * 128 + ns], in_=xsb[:ns, kc * 128:(kc + 1) * 128],
                        identity=self.ident[:ns, :ns]),
                         reads=[xsid, "consts"], writes=["ps_tr"])
                P.op("act", lambda e, xb=xb, s0=s0, ns=ns: e.activation(
                    out=xb[:, :, s0:s0 + ns], in_=ps_tr.rearrange("p (a b) -> p a b", a=KC)[:, :, 0:ns],
                    func=AF.Copy),
                     reads=["ps_tr"], writes=[("xnT", ti % 2, si)])
            xn_ids = [("xnT", ti % 2, si) for si in range(len(subs))]

        def gateup(ti):
            nonlocal gu_ctr, st_ctr
            src, dst, t0, nt = tiles[ti]
            subs = subs_of[ti]
            xb = xnT[ti % 2]
            xn_ids = [("xnT", ti % 2, si) for si in range(len(subs))]
            for f in range(NFF):
                g = gu[gu_ctr % 3]
                gid = ("gu", gu_ctr % 3)
                gu_ctr += 1
                for kc in range(KC):
                    P.op("pe", lambda e, g=g, kc=kc, f=f, xb=xb, nt=nt: e.matmul(
                        g[:, 0:nt], lhsT=Wg[:, kc, f * 128:(f + 1) * 128], rhs=xb[:, kc, 0:nt],
                        start=(kc == 0), stop=(kc == KC - 1)),
                         reads=xn_ids + [("Wg", kc)], writes=[gid])
                for kc in range(KC):
                    P.op("pe", lambda e, g=g, kc=kc, f=f, xb=xb, nt=nt: e.matmul(
                        g[:, 256:256 + nt], lhsT=Wu[:, kc, f * 128:(f + 1) * 128], rhs=xb[:, kc, 0:nt],
                        start=(kc == 0), stop=(kc == KC - 1)),
                         reads=xn_ids + [("Wu", kc)], writes=[gid])
                stb = stmp[st_ctr % 3]
                sid = ("stmp", st_ctr % 3)
                st_ctr += 1
                P.op("act", lambda e, g=g, stb=stb, nt=nt: e.activation(out=stb[:, 0:nt], in_=g[:, 0:nt], func=AF.Silu),
                     reads=[gid], writes=[sid])
                P.op("dve", lambda e, g=g, stb=stb, nt=nt, f=f: e.tensor_tensor(
                    out=actT[:, f, 0:nt], in0=stb[:, 0:nt], in1=g[:, 256:256 + nt], op=ALU.mult),
                     reads=[gid, sid], writes=[("actT", f)])

        def down_post(ti):
            nonlocal o_ctr
            src, dst, t0, nt = tiles[ti]
            subs = subs_of.pop(ti)
            for si, (k, s0, ns) in enumerate(subs):
                d0, d1 = dn[si % 2]
                did = ("dn", si % 2)
                for half, dps in enumerate((d0, d1)):
                    for f in range(NFF):
                        P.op("pe", lambda e, dps=dps, f=f, s0=s0, ns=ns, half=half: e.matmul(
                            dps[:ns, :], lhsT=actT[:, f, s0:s0 + ns], rhs=Wd[:, f, half * 512:(half + 1) * 512],
                            start=(f == 0), stop=(f == NFF - 1)),
                             reads=[("actT", f), ("Wd", f)], writes=[did])
                sl = (ti * 2 + si) % 4
                ss2 = ssp[sl]
                P.op("act", lambda e, d0=d0, ns=ns, ss2=ss2: e.activation(
                    out=junk[:ns, 0:512], in_=d0[:ns, :], func=AF.Square, accum_out=ss2[:ns, :]),
                     reads=[did], writes=[("junk", 9), ("ssA", sl)])
                ss3 = ssq[sl]
                P.op("act", lambda e, d1=d1, ns=ns, ss3=ss3: e.activation(
                    out=junk[:ns, 512:1024], in_=d1[:ns, :], func=AF.Square, accum_out=ss3[:ns, :]),
                     reads=[did], writes=[("junk", 10), ("ssB", sl)])
                P.op("pool", lambda e, ss2=ss2, ss3=ss3, ns=ns: e.tensor_tensor(
                    out=ss2[:ns, :], in0=ss2[:ns, :], in1=ss3[:ns, :], op=ALU.add),
                     reads=[("ssA", sl), ("ssB", sl)], writes=[("ssA", sl)])
                P.op("pool", lambda e, ss2=ss2, ns=ns: e.tensor_scalar(
                    out=ss2[:ns, :], in0=ss2[:ns, :], scalar1=1.0 / D, scalar2=EPS, op0=ALU.mult, op1=ALU.add),
                     reads=[("ssA", sl)], writes=[("ssA", sl)])
                P.op("pool", lambda e, ss2=ss2, ss3=ss3, ns=ns: e.tensor_tensor(
                    out=ss3[:ns, :], in0=ss2[:ns, :], in1=self.c_mhalf[:ns, :], op=ALU.pow),
                     reads=[("ssA", sl), "consts"], writes=[("ssB", sl)])
                ob = ost[o_ctr % 2]
                oid = ("ost", o_ctr % 2)
                osem = sname + "o%d" % (o_ctr % int(os.environ.get("NOSEM", "2")))
                o_ctr += 1
                for half, dps in enumerate((d0, d1)):
                    P.op("dve", lambda e, dps=dps, ob=ob, ns=ns, half=half, ss3=ss3: e.scalar_tensor_tensor(
                        out=ob[:ns, half * 512:(half + 1) * 512], in0=dps[:ns, :], scalar=ss3[:ns, :],
                        in1=gph[:ns, half * 512:(half + 1) * 512], op0=ALU.mult, op1=ALU.mult),
                         reads=[did, ("ssB", sl), "consts"], writes=[(oid, half)])
                    P.op("pool", lambda e, ob=ob, ns=ns, half=half, k=k: e.tensor_tensor(
                        out=ob[:ns, half * 512:(half + 1) * 512], in0=ob[:ns, half * 512:(half + 1) * 512],
                        in1=xr[k][:ns, half * 512:(half + 1) * 512], op=ALU.add),
                         reads=[(oid, half), ("xr", k)], writes=[(oid, half)])
                P.dma(osem, lambda e, ob=ob, dst=dst, a=t0 + s0, ns=ns: e.dma_start(
                    out=dst[a:a + ns, :], in_=ob[:ns, :]), reads=[(oid, 0), (oid, 1)])

        if len(tiles) > 1:
            loaded[1] = load_tile(1)
        prenorm(0)
        for ti in range(len(tiles)):
            gateup(ti)
            if ti + 1 < len(tiles):
                prenorm(ti + 1)
            down_post(ti)
            if ti + 2 < len(tiles):
                loaded[ti + 2] = load_tile(ti + 2)
        P.barrier()
        A.pop()

    def consts(self):
        P, A = self.P, self.A
        ident_d = self.din("ident", [128, 128], BF16)
        self.ident = A.alloc([128], BF16)
        self.c_mhalf = A.alloc([1], F32)
        P.dma("c0", lambda e: e.dma_start(out=self.ident[:, :], in_=ident_d[:, :]), writes=["consts"])
        P.op("pool", lambda e: e.memset(self.c_mhalf[:, :], -0.5), writes=["consts_b"])

    def load_const(self, name, shape, dt=F32):
        d = self.din(name, [128] + list(shape), dt)
        t = self.A.alloc(list(shape), dt)
        self.P.dma("c0", lambda e: e.dma_start(out=t, in_=d), writes=["consts_c"])
        return t


def build_ffn_test(ntok):
    B = Builder(0, ntok)
    P = B.P
    x = B.din("x", [ntok, D])
    wg = B.din("wg", [D, DFF])
    wu = B.din("wu", [D, DFF])
    wd = B.din("wd", [DFF, D])
    y = B.dout("y", [ntok, D])
    B.consts()
    gpre = B.load_const("gpre", [KC])
    gpost = B.load_const("gpost", [D])
    P.barrier()
    B.ffn_stage([(x, y, ntok)], wg, wu, wd, gpre, gpost, "f1")
    B.P.emit()
    return B


def mixer_in_stage(B, seqs, Cn, sname):
    P, A, nc = B.P, B.A, B.nc
    A.push()
    TT = 512
    Wout_p = A.alloc([KC, D], BF16)
    Cn["wout_off"] = A.off
    Win = A.alloc([KC, 2560], BF16)
    Wrb = A.alloc([4, 128], BF16)
    Wib = A.alloc([4, 128], BF16)
    mark = A.off
    wst = [A.alloc([2560], F32) for _ in range(4)]
    wrf = A.alloc([4, 128], F32)
    wif = A.alloc([4, 128], F32)
    P.dma("c0", lambda e: e.dma_start(out=wrf, in_=B.inputs["wr_bd"]), writes=["wrf"])
    P.dma("c0", lambda e: e.dma_start(out=wif, in_=B.inputs["wi_bd"]), writes=["wif"])
    if Cn.get("cache_prep") is not None:
        cache_prep_ops(B, *Cn["cache_prep"])
    B.prep_weight(B.inputs["win"], D, 2560, Win, Cn["gma_col"], wst, "Win")
    B.prep_weight(B.inputs["wout"], D, D, Wout_p, None, wst, "Wout")
    Cn["wout_ready"] = True
    P.op("dve", lambda e: e.tensor_copy(out=Wrb, in_=wrf), reads=["wrf"], writes=["Wrb"])
    P.op("dve", lambda e: e.tensor_copy(out=Wib, in_=wif), reads=["wif"], writes=["Wib"])
    P.barrier()
    A.off = mark
    NXR = 6
    xr = [A.alloc([D], F32) for _ in range(NXR)]
    xs = [A.alloc([D], BF16) for _ in range(2)]
    xnT = [A.alloc([KC, TT], BF16) for _ in range(2)]
    junk = A.alloc([D], BF16)
    ssb = [A.alloc([1], F32) for _ in range(4)]
    rsb = [A.alloc([1], F32) for _ in range(4)]
    kst = [A.alloc([TT], BF16) for _ in range(3)]
    vst = [A.alloc([512], F32) for _ in range(2)]
    vbf = [A.alloc([4, 129], BF16) for _ in range(2)]
    for i in range(2):
        P.op("pool", lambda e, i=i: e.memset(vbf[i], 1.0), writes=[("vbf", i)])
    lxb = [[A.alloc([TT + 4], BF16) for _ in range(2)] for _ in range(4)]
    lxl = A.alloc([4, 3], F32)
    lx0 = A.alloc([4, 3], F32)
    dgw = A.alloc([16, 128], BF16)
    xcb = [A.alloc([TT], BF16) for _ in range(4)]
    rb = [A.alloc([TT], F32) for _ in range(4)]
    ib = [A.alloc([TT], F32) for _ in range(4)]
    ab = [A.alloc([TT], F32) for _ in range(4)]
    a2b = [A.alloc([TT], F32) for _ in range(4)]
    hb = [A.alloc([TT], F32) for _ in range(4)]
    gt = [[A.alloc([TT], F32) for _ in range(4)] for _ in range(2)]
    tb = [A.alloc([TT], F32) for _ in range(4)]
    lob = [A.alloc([TT], BF16) for _ in range(4)]
    hstate = A.alloc([4], F32)
    cL = A.alloc([4], F32)
    cL2 = A.alloc([4], F32)
    ps = B.psum
    ps_tr = ps[0][:, :].bitcast(BF16)
    fm = [ps[1], ps[2]]
    cvb = ps[3]
    tm = [ps[4], ps[5]]
    rg = [ps[6], ps[7]]

    if "nocl" not in DBG:
        P.op("act", lambda e: e.activation(out=cL, in_=Cn["lam"], func=AF.Exp, scale=-1.0), reads=["consts"], writes=["cL"])
        P.op("act", lambda e: e.activation(out=cL, in_=cL, func=AF.Ln, bias=1.0), reads=["cL"], writes=["cL"])
    P.op("pool", lambda e: e.tensor_scalar(out=cL2, in0=cL, scalar1=-16.0, scalar2=None, op0=ALU.mult),
         reads=["cL"], writes=["cL2"])
    P.op("pool", lambda e: e.tensor_scalar(out=cL, in0=cL, scalar1=-8.0, scalar2=None, op0=ALU.mult),
         reads=["cL", "cL2"], writes=["cL"])
    for g in range(4):
        for j in range(4):
            P.op("dve", lambda e, g=g, j=j: e.tensor_scalar(out=dgw[:, g * 4 + j, :], in0=B.ident, scalar1=Cn["convw"][:, g, j:j + 1],
                                                            scalar2=None, op0=ALU.mult), reads=["consts"], writes=["dgw"])

    def run_seq(sq, x1_src, T, T_OTH, KT_scr, V_scr, QT_scr, LT_scr, ko, vo, hl_out, cb_out, h0_d, conv0_d, koff):
        sn = sname + str(sq)
        if h0_d is None:
            P.op("pool", lambda e: e.memset(hstate, 0.0), writes=["hstate"])
            for g in range(4):
                P.op("pool", lambda e, g=g: e.memset(lxb[g][0][:, 0:3], 0.0), writes=[("lxh", g, 0)])
        else:
            P.dma(sn + "st", lambda e: e.dma_start(out=hstate, in_=h0_d.rearrange("(g p) -> p g", p=128),
                                                      allow_slow_non_contiguous=True), writes=["hstate"])
            for g in range(4):
                P.dma(sn + "st", lambda e, g=g: e.dma_start(
                    out=lx0[:, g, :], in_=conv0_d[:, g * 128:(g + 1) * 128].rearrange("j p -> p j"),
                    allow_slow_non_contiguous=True), writes=[("lx0", g)])
            for g in range(4):
                P.op("dve", lambda e, g=g: e.tensor_copy(out=lxb[g][0][:, 0:3], in_=lx0[:, g, :]),
                     reads=[("lx0", g)], writes=[("lxh", g, 0)])
        P.barrier()

        tiles = []
        t0 = 0
        while t0 < T:
            lim = T_OTH if t0 < T_OTH else T
            nt = min(TT, lim - t0)
            tiles.append((t0, nt))
            t0 += nt
        sub_ctr = [0]

        def load_tile(ti):
            t0, nt = tiles[ti]
            subs = []
            for s0 in range(0, nt, 128):
                ns = min(128, nt - s0)
                k = sub_ctr[0] % NXR
                sub_ctr[0] += 1
                P.dma(sname + "x%d" % k, lambda e, k=k, a=t0 + s0, ns=ns: e.dma_start(
                    out=xr[k][:ns, :], in_=x1_src[a:a + ns, :]), writes=[("xr", k)])
                subs.append((k, s0, ns))
            return subs

        loaded = {0: load_tile(0)}
        fm_c = [0]
        tm_c = [0]
        ks_c = [0]
        vs_c = [0]
        vb_c = [0]
        def tile_body(ti, t0, nt):
            own = t0 >= T_OTH
            to = t0 - T_OTH
            subs = loaded.pop(ti)
            xb = xnT[ti % 2]
            cur, nxt = ti % 2, (ti + 1) % 2
            for si, (k, s0, ns) in enumerate(subs):
                sl = (ti * 4 + si) % 4
                B.rstd_of(xr[k][:ns, :], ns, junk, ssb[sl], rsb[sl], [("xr", k)], sl)
                xsb = xs[si % 2]
                xsid = ("xs", si % 2)
                P.op("dve", lambda e, xsb=xsb, k=k, ns=ns, sl=sl: e.tensor_scalar(
                    out=xsb[:ns, :], in0=xr[k][:ns, :], scalar1=rsb[sl][:ns, :], scalar2=None, op0=ALU.mult),
                     reads=[("xr", k), ("rstd", sl)], writes=[xsid])
                for kc in range(KC):
                    P.op("pe", lambda e, xsb=xsb, kc=kc, ns=ns: e.transpose(
                        out=ps_tr[:, kc * 128:kc * 128 + ns], in_=xsb[:ns, kc * 128:(kc + 1) * 128],
                        identity=B.ident[:ns, :ns]),
                         reads=[xsid, "consts"], writes=["ps_tr"])
                P.op("act", lambda e, xb=xb, s0=s0, ns=ns: e.activation(
                    out=xb[:, :, s0:s0 + ns], in_=ps_tr.rearrange("p (a b) -> p a b", a=KC)[:, :, 0:ns],
                    func=AF.Copy),
                     reads=["ps_tr"], writes=[("xnT", ti % 2, si)])
                if si % 2 == 1:
                    yield
            xn_ids = [("xnT", ti % 2, si) for si in range(len(subs))]
            if ti + 1 < len(tiles):
                loaded[ti + 1] = load_tile(ti + 1)

            def fm_proj(col0):
                bank = fm[fm_c[0] % 2]
                bid = ("fm", fm_c[0] % 2)
                fm_c[0] += 1
                for kc in range(KC):
                    P.op("pe", lambda e, bank=bank, kc=kc, col0=col0: e.matmul(
                        bank[:, 0:nt], lhsT=Win[:, kc, col0:col0 + 128], rhs=xb[:, kc, 0:nt],
                        start=(kc == 0), stop=(kc == KC - 1)),
                         reads=xn_ids + [("Win", kc)], writes=[bid])
                return bank, bid

            def fm_to_scr(col0, dst_ap):
                bank, bid = fm_proj(col0)
                kb = kst[ks_c[0] % 3]
                kid = ("kst", ks_c[0] % 3)
                ksem = sname + "k%d" % (ks_c[0] % 3)
                ks_c[0] += 1
                P.op("act", lambda e, bank=bank, kb=kb: e.activation(out=kb[:, 0:nt], in_=bank[:, 0:nt], func=AF.Copy),
                     reads=[bid], writes=[kid])
                P.dma(ksem, lambda e, kb=kb, dst_ap=dst_ap: e.dma_start(out=dst_ap, in_=kb[:, 0:nt]), reads=[kid])

            for g in range(4):
                bank, bid = fm_proj(1536 + g * 128)
                P.op("act", lambda e, bank=bank, g=g: e.activation(
                    out=lxb[g][cur][:, 3:3 + nt], in_=bank[:, 0:nt], func=AF.Copy),
                     reads=[bid], writes=[("lx", g, cur)])
                if ti == len(tiles) - 1:
                    P.op("act", lambda e, bank=bank, g=g: e.activation(
                        out=lxl[:, g, :], in_=bank[:, nt - 3:nt], func=AF.Copy), reads=[bid], writes=[("lxl", g)])
                yield
            if own:
                for g in range(4):
                    bank, bid = fm_proj(2048 + g * 128)
                    P.op("act", lambda e, bank=bank, g=g: e.activation(out=gt[cur][g][:, 0:nt], in_=bank[:, 0:nt], func=AF.Copy),
                         reads=[bid], writes=[("gt", cur, g)])
            for h in range(4 if "nofm" not in DBG else 0):
                fm_to_scr(512 + h * 128, KT_scr[h, :, koff + t0:koff + t0 + nt])
                yield
            if own and "nofm" not in DBG:
                for h in range(4):
                    fm_to_scr(h * 128, QT_scr[h, :, to:to + nt])
                yield
            for si, (k, s0, ns) in enumerate(subs if "notm" not in DBG else []):
                for which in (("v", 1024), ("k", 512)):
                    if which[0] == "k" and not own:
                        continue
                    bank = tm[tm_c[0] % 2]
                    bid = ("tm", tm_c[0] % 2)
                    tm_c[0] += 1
                    for kc in range(KC):
                        P.op("pe", lambda e, bank=bank, kc=kc, s0=s0, ns=ns, c0=which[1]: e.matmul(
                            bank[:ns, :], lhsT=xb[:, kc, s0:s0 + ns], rhs=Win[:, kc, c0:c0 + 512],
                            start=(kc == 0), stop=(kc == KC - 1)),
                             reads=xn_ids + [("Win", kc)], writes=[bid])
                    if which[0] == "v" and "nov" not in DBG:
                        vb = vbf[vb_c[0] % 2]
                        vbid = ("vbf", vb_c[0] % 2)
                        vsem = sname + "vb%d" % (vb_c[0] % 2)
                        vb_c[0] += 1
                        P.op("dve", lambda e, bank=bank, vb=vb, ns=ns: e.tensor_copy(
                            out=vb[:ns, :, 0:128], in_=bank[:ns, :].rearrange("p (h d) -> p h d", h=4)),
                             reads=[bid], writes=[vbid])
                        P.dma(vsem, lambda e, vb=vb, a=koff + t0 + s0, ns=ns: e.dma_start(
                            out=V_scr[a:a + ns, :], in_=vb[:ns, :, :].rearrange("p h d -> p (h d)")),
                              reads=[vbid])
                    if own and "noko" not in DBG:
                        vs = vst[vs_c[0] % 2]
                        vsid = ("vst", vs_c[0] % 2)
                        vsem = sname + "vs%d" % (vs_c[0] % 2)
                        vs_c[0] += 1
                        dst = vo if which[0] == "v" else ko
                        P.op("act", lambda e, bank=bank, vs=vs, ns=ns: e.activation(out=vs[:ns, :], in_=bank[:ns, :], func=AF.Copy),
                             reads=[bid], writes=[vsid])
                        P.dma(vsem, lambda e, vs=vs, dst=dst, a=to + s0, ns=ns: e.dma_start(out=dst[a:a + ns, :], in_=vs[:ns, :]),
                              reads=[vsid])
                if si % 2 == 1:
                    yield

        def lru_part(ti, t0, nt, own, to, cur, nxt):
            for g in range(4):
                lx = lxb[g][cur]
                lid = [("lx", g, cur), ("lxh", g, cur)]
                for j in range(4):
                    P.op("pe", lambda e, g=g, lx=lx, j=j: e.matmul(
                        cvb[:, 0:nt], lhsT=dgw[:, g * 4 + j, :], rhs=lx[:, j:j + nt], start=(j == 0), stop=(j == 3)),
                         reads=lid + ["dgw"], writes=[("cv", 0)])
                P.op("dve", lambda e, g=g: e.tensor_scalar(
                    out=xcb[g][:, 0:nt], in0=cvb[:, 0:nt], scalar1=Cn["convb"][:, g:g + 1], scalar2=None, op0=ALU.add),
                     reads=[("cv", 0), "consts"], writes=[("xcb", g)])
                if ti + 1 < len(tiles):
                    boundary = (tiles[ti + 1][0] == T_OTH) and T_OTH > 0
                    if boundary:
                        P.op("pool", lambda e, g=g, lx=lx: e.tensor_scalar(
                            out=lxb[g][nxt][:, 0:3], in0=lx[:, nt:nt + 3], scalar1=Cn["flag"][:, 0:1], scalar2=None,
                            op0=ALU.mult), reads=lid + ["consts"], writes=[("lxh", g, nxt)])
                    else:
                        P.op("pool", lambda e, g=g, lx=lx: e.tensor_copy(out=lxb[g][nxt][:, 0:3], in_=lx[:, nt:nt + 3]),
                             reads=lid, writes=[("lxh", g, nxt)])
                yield
            for g in range(4):
                P.op("pe", lambda e, g=g: e.matmul(rg[0][:, 0:nt], lhsT=Wrb[:, g, :], rhs=xcb[g][:, 0:nt], start=True, stop=True),
                     reads=[("xcb", g), "Wrb"], writes=[("rg", 0)])
                P.op("act", lambda e, g=g: e.activation(out=rb[g][:, 0:nt], in_=rg[0][:, 0:nt], func=AF.Sigmoid,
                                                        bias=Cn["brg"][:, g:g + 1]),
                     reads=[("rg", 0), "consts"], writes=[("rb", g)])
                P.op("pe", lambda e, g=g: e.matmul(rg[1][:, 0:nt], lhsT=Wib[:, g, :], rhs=xcb[g][:, 0:nt], start=True, stop=True),
                     reads=[("xcb", g), "Wib"], writes=[("rg", 1)])
                P.op("act", lambda e, g=g: e.activation(out=ib[g][:, 0:nt], in_=rg[1][:, 0:nt], func=AF.Sigmoid,
                                                        bias=Cn["big"][:, g:g + 1]),
                     reads=[("rg", 1), "consts"], writes=[("ib", g)])
                yield
            if own:
                for g in range(4):
                    P.op("pool", lambda e, g=g: e.tensor_tensor(out=tb[g][:, 0:nt], in0=gt[cur][g][:, 0:nt], in1=gt[cur][g][:, 0:nt], op=ALU.mult),
                         reads=[("gt", cur, g)], writes=[("tb", g)])
                    P.op("pool", lambda e, g=g: e.tensor_scalar(out=tb[g][:, 0:nt], in0=tb[g][:, 0:nt], scalar1=0.044715, scalar2=1.0,
                                                                op0=ALU.mult, op1=ALU.add),
                         reads=[("tb", g)], writes=[("tb", g)])
                    P.op("pool", lambda e, g=g: e.tensor_tensor(out=tb[g][:, 0:nt], in0=tb[g][:, 0:nt], in1=gt[cur][g][:, 0:nt], op=ALU.mult),
                         reads=[("tb", g), ("gt", cur, g)], writes=[("tb", g)])
                    P.op("act", lambda e, g=g: e.activation(out=tb[g][:, 0:nt], in_=tb[g][:, 0:nt], func=AF.Sigmoid, scale=1.5957691216),
                         reads=[("tb", g)], writes=[("tb", g)])
                    P.op("pool", lambda e, g=g: e.tensor_tensor(out=gt[cur][g][:, 0:nt], in0=tb[g][:, 0:nt], in1=gt[cur][g][:, 0:nt], op=ALU.mult),
                         reads=[("tb", g), ("gt", cur, g)], writes=[("gt", cur, g)])
            yield
            for g in range(4):
                P.op("act", lambda e, g=g: e.activation(out=ab[g][:, 0:nt], in_=rb[g][:, 0:nt], func=AF.Exp, scale=cL[:, g:g + 1]),
                     reads=[("rb", g), "cL"], writes=[("ab", g)])
                P.op("act", lambda e, g=g: e.activation(out=a2b[g][:, 0:nt], in_=rb[g][:, 0:nt], func=AF.Exp, scale=cL2[:, g:g + 1]),
                     reads=[("rb", g), "cL2"], writes=[("a2b", g)])
                yield
            for g in range(4):
                P.op("act", lambda e, g=g: e.activation(out=a2b[g][:, 0:nt], in_=a2b[g][:, 0:nt], func=AF.Sqrt, scale=-1.0, bias=1.0),
                     reads=[("a2b", g)], writes=[("a2b", g)])
            yield
            for g in range(4):
                P.op("dve", lambda e, g=g: e.tensor_tensor(out=ib[g][:, 0:nt], in0=ib[g][:, 0:nt], in1=xcb[g][:, 0:nt], op=ALU.mult),
                     reads=[("ib", g), ("xcb", g)], writes=[("ib", g)])
                P.op("dve", lambda e, g=g: e.tensor_tensor(out=ib[g][:, 0:nt], in0=ib[g][:, 0:nt], in1=a2b[g][:, 0:nt], op=ALU.mult),
                     reads=[("ib", g), ("a2b", g)], writes=[("ib", g)])
                P.op("dve", lambda e, g=g: e.tensor_tensor_scan(
                    out=hb[g][:, 0:nt], data0=ab[g][:, 0:nt], data1=ib[g][:, 0:nt], initial=hstate[:, g:g + 1],
                    op0=ALU.mult, op1=ALU.add),
                     reads=[("ab", g), ("ib", g), "hstate"], writes=[("hb", g)])
                yield
            boundary = (ti + 1 < len(tiles)) and (tiles[ti + 1][0] == T_OTH) and T_OTH > 0
            for g in range(4):
                if boundary:
                    P.op("pool", lambda e, g=g: e.tensor_scalar(out=hstate[:, g:g + 1], in0=hb[g][:, nt - 1:nt],
                                                                scalar1=Cn["flag"][:, 0:1], scalar2=None, op0=ALU.mult),
                         reads=[("hb", g), "consts"], writes=["hstate"])
                else:
                    P.op("pool", lambda e, g=g: e.tensor_copy(out=hstate[:, g:g + 1], in_=hb[g][:, nt - 1:nt]),
                         reads=[("hb", g)], writes=["hstate"])
            if own:
                for g in range(4):
                    P.op("dve", lambda e, g=g: e.tensor_tensor(out=lob[g][:, 0:nt], in0=hb[g][:, 0:nt], in1=gt[cur][g][:, 0:nt], op=ALU.mult),
                         reads=[("hb", g), ("gt", cur, g)], writes=[("lob", g)])
                    P.dma(sname + "lo%d" % g, lambda e, g=g: e.dma_start(out=LT_scr[g, :, to:to + nt], in_=lob[g][:, 0:nt]),
                          reads=[("lob", g)])
        def interleave(ga, gb):
            gens = [g for g in (ga, gb) if g is not None]
            while gens:
                for g in list(gens):
                    try:
                        next(g)
                    except StopIteration:
                        gens.remove(g)

        pending = None
        for ti, (t0, nt) in enumerate(tiles):
            interleave(tile_body(ti, t0, nt), pending)
            pending = lru_part(ti, t0, nt, t0 >= T_OTH, t0 - T_OTH, ti % 2, (ti + 1) % 2)
        interleave(None, pending)
        lt0, lnt = tiles[-1]
        lcur = (len(tiles) - 1) % 2
        if "nofin" not in DBG:
            P.dma(sn + "fin", lambda e: e.dma_start(out=hl_out.rearrange("(g p) -> p g", p=128), in_=hstate,
                                                       allow_slow_non_contiguous=True), reads=["hstate"])
        for g in range(4 if "nofin" not in DBG else 0):
            P.dma(sn + "fin", lambda e, g=g: e.dma_start(
                out=cb_out[:, g * 128:(g + 1) * 128].rearrange("j p -> p j"), in_=lxl[:, g, :],
                allow_slow_non_contiguous=True), reads=[("lxl", g)])

    for sq, q in enumerate(seqs):
        run_seq(sq, q['x1'], q['T'], q['T_OTH'], q['KT'], q['V'], q['QT'], q['LT'], q['ko'], q['vo'], q['hl'], q['cb'],
                q.get('h0'), q.get('conv0'), q.get('koff', 0))
    P.barrier()
    A.pop()


SLOPES = [2.0 ** (-8.0 * (i + 1) / 4) for i in range(4)]
LAMBDA_INIT = 0.8 - 0.6 * 1.0


def attn_consts(B, Cn):
    P, A = B.P, B.A
    for nm, shp in (("pb", [4, 71]), ("db", [4, 128]), ("subg", [128]), ("lq1", [64]), ("lk1", [64]), ("lq2", [64]), ("lk2", [64])):
        Cn[nm] = A.alloc(shp, F32)
        P.dma("c0", lambda e, nm=nm: e.dma_start(out=Cn[nm], in_=B.inputs[nm]), writes=["consts"])
    negl = A.alloc([1], F32)
    t1 = A.alloc([1], F32)
    t2 = A.alloc([1], F32)
    j64 = A.alloc([64], F32)
    P.op("dve", lambda e: e.tensor_tensor(out=j64, in0=Cn["lq1"], in1=Cn["lk1"], op=ALU.mult), reads=["consts"], writes=["j64"])
    P.op("dve", lambda e: e.reduce_sum(out=t1, in_=j64, axis=AX.X), reads=["j64"], writes=["t1"])
    P.op("dve", lambda e: e.tensor_tensor(out=j64, in0=Cn["lq2"], in1=Cn["lk2"], op=ALU.mult), reads=["consts", "t1"], writes=["j64"])
    P.op("dve", lambda e: e.reduce_sum(out=t2, in_=j64, axis=AX.X), reads=["j64"], writes=["t2"])
    P.op("act", lambda e: e.activation(out=t1, in_=t1, func=AF.Exp), reads=["t1"], writes=["t1"])
    P.op("act", lambda e: e.activation(out=t2, in_=t2, func=AF.Exp), reads=["t2"], writes=["t2"])
    P.op("pool", lambda e: e.tensor_tensor(out=negl, in0=t2, in1=t1, op=ALU.subtract), reads=["t1", "t2"], writes=["negl"])
    P.op("pool", lambda e: e.tensor_scalar(out=negl, in0=negl, scalar1=-LAMBDA_INIT, scalar2=None, op0=ALU.add),
         reads=["negl"], writes=["negl"])
    subg8 = A.alloc([128], F32)
    P.op("pool", lambda e: e.tensor_scalar(out=subg8, in0=Cn["subg"], scalar1=1.0 - LAMBDA_INIT, scalar2=None, op0=ALU.mult),
         reads=["consts"], writes=["subg8"])
    pbo = A.alloc([4, 71], F32)
    P.op("pool", lambda e: e.tensor_scalar(out=pbo, in0=Cn["pb"], scalar1=Cn["maskv"][:, 0:1], scalar2=None, op0=ALU.add),
         reads=["consts"], writes=["pbo"])
    Cn["negl"], Cn["subg8"], Cn["pbo"] = negl, subg8, pbo


def attn_stage(B, jobs, Cn, sname, window=(None,) * 4):
    P, A, nc = B.P, B.A, B.nc
    TKmax = max(q["TK"] for q in jobs)
    NBmax = cdiv(TKmax, 128)
    A.push()
    KT = A.alloc([4, NBmax * 128], BF16)
    V1 = A.alloc([NBmax, 516], BF16)
    Wout = A.alloc([KC, D], BF16)
    gmb = A.alloc([D], F32)
    P.dma("c0", lambda e: e.dma_start(out=gmb, in_=B.inputs["gmb"]), writes=["gmb"])
    Cn["gmb"] = gmb
    attn_consts(B, Cn)
    mark = A.off
    wst = [A.alloc([D], F32) for _ in range(4)]
    B.prep_weight(B.inputs["wout"], D, D, Wout, None, wst, "Wout")
    P.barrier()
    A.off = mark
    QTILE = 512
    qt = [A.alloc([4, QTILE], BF16) for _ in range(2)]
    NPT = 4
    pt = [A.alloc([QTILE], BF16) for _ in range(NPT)]
    dtmp = [A.alloc([128], F32) for _ in range(2)]
    atok = [A.alloc([512], BF16) for _ in range(4)]
    attT = A.alloc([4, QTILE], BF16)
    lruT = [A.alloc([4, QTILE], BF16) for _ in range(2)]
    x1r = [A.alloc([D], F32) for _ in range(2)]
    ost = [A.alloc([D], F32)] * 2
    otmp = [A.alloc([128], F32) for _ in range(2)]
    ofin = [A.alloc([2, 129], F32) for _ in range(4)]
    junk = A.alloc([512], BF16)
    rl = [A.alloc([2], F32) for _ in range(4)]
    ssn = [A.alloc([1], F32) for _ in range(4)]
    rsn = [A.alloc([1], F32) for _ in range(4)]
    ssm = [A.alloc([1], F32) for _ in range(2)]
    ssm2 = [A.alloc([1], F32) for _ in range(2)]
    rsm = [A.alloc([1], F32) for _ in range(2)]
    ps = B.psum
    pb, pbo, db = Cn["pb"], Cn["pbo"], Cn["db"]
    st_c = [0]
    pt_c = [0]
    dt_c = [0]
    x_c = [0]
    o_c = [0]
    qb_c = [0]
    VCH = 16

    def run_job(TK, NQ, KT_scr, V_scr, QT_scr, LT_scr, x1_scr, x1_off, x2_dst, mask_other):
        NB = cdiv(TK, 128)
        KOFF = TK - NQ
        assert KOFF % 128 == 0
        nkof = lambda j: min(128, TK - 128 * j)
        for h in range(4):
            P.dma(sname + "K%d" % h, lambda e, h=h: e.dma_start(out=KT[:, h, 0:TK], in_=KT_scr[h, :, 0:TK]), writes=[("KT", h)])
        for ci, j0 in enumerate(range(0, NB, VCH)):
            j1 = min(NB, j0 + VCH)
            jf = min(j1, TK // 128)
            if jf > j0:
                P.dma(sname + "V%d" % (ci % 4), lambda e, j0=j0, jf=jf: e.dma_start(
                    out=V1[:, j0:jf, :], in_=V_scr[j0 * 128:jf * 128, :].rearrange("(j p) c -> p j c", p=128)),
                      writes=[("V1", ci)])
            if jf < j1:
                nk = nkof(jf)
                P.dma(sname + "V%d" % (ci % 4), lambda e, jf=jf, nk=nk: e.dma_start(
                    out=V1[:nk, jf, :], in_=V_scr[jf * 128:jf * 128 + nk, :]), writes=[("V1", ci)])
        vid = lambda j: ("V1", j // VCH)

        tiles = []
        q0 = 0
        while q0 < NQ:
            nq = min(QTILE, NQ - q0)
            tiles.append((q0, nq))
            q0 += nq

        LOOK = int(os.environ.get("K_LOOK", "3"))
        dq = []

        def push2(fn):
            dq.append(fn)
            while len(dq) > LOOK:
                dq.pop(0)()

        def load_q(ti):
            q0, nq = tiles[ti]
            b = qb_c[0] % 2
            qb_c[0] += 1
            for h in range(4):
                P.dma(sname + "q%d" % b, lambda e, b=b, h=h, q0=q0, nq=nq: e.dma_start(
                    out=qt[b][:, h, 0:nq], in_=QT_scr[h, :, q0:q0 + nq]), writes=[("qt", b, h)])
            def ld(b=b, q0=q0, nq=nq):
                P.dma(sname + "l%d" % b, lambda e: e.dma_start(
                    out=lruT[b][:, :, 0:nq], in_=LT_scr[:, :, q0:q0 + nq].rearrange("g p t -> p g t")), writes=[("lruT", b)])
            push2(ld)
            return b

        def tile_stream(ti, q0, nq, b):
            nsb = cdiv(nq, 128)
            nqs_of = lambda s: min(128, nq - 128 * s)
            jb = (KOFF + q0) // 128
            nb_next = load_q(ti + 1) if ti + 1 < len(tiles) else None
            for h in range(4):
                persub = (h == 0)
                W = window[h]
                jlo = 0 if W is None else max(0, jb - W)
                first_in_bank = [True] * 4
                for j in range(jlo, jb + nsb):
                    nk = nkof(j)
                    rel = j - jb
                    s_lo = max(0, rel)
                    c0 = s_lo * 128
                    tab = pbo if (mask_other and j * 128 < KOFF) else pb
                    for c in range(2):
                        bi = st_c[0] % 4
                        st_c[0] += 1
                        stb = ps[bi]
                        bid = ("bank", bi)
                        P.op("pe", lambda e, stb=stb, c=c, h=h, j=j, c0=c0, nq=nq, b=b, nk=nk: e.matmul(
                            stb[:nk, c0:nq], lhsT=KT[c * 64:(c + 1) * 64, h, j * 128:j * 128 + nk],
                            rhs=qt[b][c * 64:(c + 1) * 64, h, c0:nq], start=True, stop=True),
                             reads=[("KT", h), ("qt", b, h)], writes=[bid])
                        pi = pt_c[0] % NPT
                        pt_c[0] += 1
                        ptb = pt[pi]
                        pid = ("pt", pi)
                        c1 = c0
                        if rel >= 0:
                            nqs = nqs_of(rel)
                            di = dt_c[0] % 2
                            dt_c[0] += 1
                            P.op("dve", lambda e, stb=stb, di=di, h=h, c0=c0, nk=nk, nqs=nqs: e.scalar_tensor_tensor(
                                out=dtmp[di][:nk, :nqs], in0=stb[:nk, c0:c0 + nqs], scalar=0.125, in1=db[:nk, h, 0:nqs],
                                op0=ALU.mult, op1=ALU.add), reads=[bid, "consts"], writes=[("dtmp", di)])
                            bconst = 0.0 if persub else SLOPES[h] * 128.0 * rel
                            P.op("act", lambda e, ptb=ptb, di=di, c0=c0, bconst=bconst, nk=nk, nqs=nqs: e.activation(
                                out=ptb[:nk, c0:c0 + nqs], in_=dtmp[di][:nk, :nqs], func=AF.Exp, bias=bconst),
                                 reads=[("dtmp", di)], writes=[pid])
                            c1 = c0 + 128
                        if c1 < nq:
                            if persub:
                                for s in range(c1 // 128, nsb):
                                    dj = jb + s - j
                                    ce = s * 128 + nqs_of(s)
                                    P.op("act", lambda e, ptb=ptb, stb=stb, s=s, ce=ce, dj=dj, tab=tab, h=h, nk=nk: e.activation(
                                        out=ptb[:nk, s * 128:ce], in_=stb[:nk, s * 128:ce], func=AF.Exp,
                                        bias=tab[:nk, h, dj + 3:dj + 4], scale=0.125),
                                         reads=[bid, "pbo"], writes=[pid])
                            else:
                                dj = jb - j
                                P.op("act", lambda e, ptb=ptb, stb=stb, c1=c1, nq=nq, dj=dj, tab=tab, h=h, nk=nk: e.activation(
                                    out=ptb[:nk, c1:nq], in_=stb[:nk, c1:nq], func=AF.Exp,
                                    bias=tab[:nk, h, dj + 3:dj + 4], scale=0.125),
                                     reads=[bid, "pbo"], writes=[pid])

                        def pv(ptb=ptb, pid=pid, c=c, j=j, h=h, nk=nk, s_lo=s_lo, fib=first_in_bank, jb=jb, nsb=nsb, nqs_of=nqs_of):
                            for s in range(s_lo, nsb):
                                ob = ps[4 + s]
                                st_flag = fib[s]
                                fib[s] = False
                                nqs = nqs_of(s)
                                last = (j == jb + s)
                                P.op("pe", lambda e, ob=ob, ptb=ptb, s=s, c=c, j=j, h=h, st_flag=st_flag, nk=nk, nqs=nqs, last=last: e.matmul(
                                    ob[:nqs, c * 256:c * 256 + 129], lhsT=ptb[:nk, s * 128:s * 128 + nqs],
                                    rhs=V1[:nk, j, h * 129:(h + 1) * 129], start=st_flag, stop=last,
                                    skip_group_check=True),
                                     reads=[pid, vid(j)], writes=[("bank", 4 + s)])
                        push2(pv)
                push2(lambda h=h, nsb=nsb, nqs_of=nqs_of: finalize(h, nsb, nqs_of))
            push2(lambda ti=ti, q0=q0, nq=nq, b=b, nsb=nsb, nqs_of=nqs_of: tail(q0, nq, b, nsb, nqs_of))
            return nb_next

        def finalize(h, nsb, nqs_of):
            for s in range(nsb):
                n = nqs_of(s)
                ob = ps[4 + s]
                oid = ("bank", 4 + s)
                of = ofin[s]
                fid = ("ofin", s)
                P.op("dve", lambda e, ob=ob, of=of, n=n: e.tensor_copy(
                    out=of[:n, :, :], in_=ob[:n, :].rearrange("p (c x) -> p c x", c=2)[:, :, 0:129]),
                     reads=[oid], writes=[fid])
                r2 = rl[s]
                P.op("dve", lambda e, of=of, r2=r2, n=n: e.reciprocal(out=r2[:n, :], in_=of[:n, :, 128]),
                     reads=[fid], writes=[("rl", s)])
                P.op("pool", lambda e, r2=r2, n=n: e.tensor_tensor(out=r2[:n, 1:2], in0=r2[:n, 1:2], in1=Cn["negl"][:n, :], op=ALU.mult),
                     reads=[("rl", s), "negl"], writes=[("rl", s)])
                oi = o_c[0] % 2
                o_c[0] += 1
                ot = otmp[oi]
                otid = ("otmp", oi)
                P.op("dve", lambda e, of=of, ot=ot, r2=r2, n=n: e.tensor_scalar(
                    out=ot[:n, :], in0=of[:n, 0, 0:128], scalar1=r2[:n, 0:1], scalar2=None, op0=ALU.mult),
                     reads=[fid, ("rl", s)], writes=[otid])
                P.op("dve", lambda e, of=of, ot=ot, r2=r2, n=n: e.scalar_tensor_tensor(
                    out=ot[:n, :], in0=of[:n, 1, 0:128], scalar=r2[:n, 1:2], in1=ot[:n, :], op0=ALU.mult, op1=ALU.add),
                     reads=[fid, ("rl", s), otid], writes=[otid])
                P.op("act", lambda e, ot=ot, s=s, n=n: e.activation(out=junk[:n, 0:128], in_=ot[:n, :], func=AF.Square,
                                                               accum_out=ssn[s][:n, :]),
                     reads=[otid], writes=[("ssn", s), "junkA"])
                P.op("pool", lambda e, s=s, n=n: e.tensor_scalar(out=ssn[s][:n, :], in0=ssn[s][:n, :], scalar1=1.0 / 128, scalar2=EPS,
                                                            op0=ALU.mult, op1=ALU.add), reads=[("ssn", s)], writes=[("ssn", s)])
                P.op("pool", lambda e, s=s, n=n: e.tensor_tensor(out=rsn[s][:n, :], in0=ssn[s][:n, :], in1=B.c_mhalf[:n, :], op=ALU.pow),
                     reads=[("ssn", s)], writes=[("rsn", s)])
                P.op("dve", lambda e, ot=ot, s=s, h=h, n=n: e.scalar_tensor_tensor(
                    out=atok[s][:n, h * 128:(h + 1) * 128], in0=ot[:n, :], scalar=rsn[s][:n, 0:1], in1=Cn["subg8"][:n, :],
                    op0=ALU.mult, op1=ALU.mult), reads=[otid, ("rsn", s), "subg8"], writes=[("atok", s, h)])

        def tail(q0, nq, b, nsb, nqs_of):
            ps_tr = ps[0][:, :].bitcast(BF16)
            for s in range(nsb):
                n = nqs_of(s)
                for h in range(4):
                    P.op("pe", lambda e, s=s, h=h, n=n: e.transpose(
                        out=ps_tr[:, h * 128:h * 128 + n], in_=atok[s][:n, h * 128:(h + 1) * 128], identity=B.ident[:n, :n]),
                         reads=[("atok", s, h)], writes=[("bank", 0)])
                P.op("act", lambda e, s=s, n=n: e.activation(
                    out=attT[:, :, s * 128:s * 128 + n], in_=ps_tr[:, 0:512].rearrange("p (a b) -> p a b", a=4)[:, :, 0:n],
                    func=AF.Copy), reads=[("bank", 0)], writes=[("attT", s)])
            for s in range(nsb):
                n = nqs_of(s)
                mo = (ps[1], ps[2])
                for half in range(2):
                    for kk in range(8):
                        src = attT if kk < 4 else lruT[b]
                        P.op("pe", lambda e, half=half, kk=kk, src=src, s=s, n=n, mo=mo: e.matmul(
                            mo[half][:n, :], lhsT=src[:, kk % 4, s * 128:s * 128 + n],
                            rhs=Wout[:, kk, half * 512:(half + 1) * 512], start=(kk == 0), stop=(kk == 7)),
                             reads=[("attT", s), ("lruT", b), ("Wout", kk)], writes=[("bank", 1 + half)])
                sl = s % 2
                P.op("act", lambda e, sl=sl, n=n: e.activation(out=junk[:n, :], in_=ps[1][:n, :], func=AF.Square, accum_out=ssm[sl][:n, :]),
                     reads=[("bank", 1)], writes=["junkA", ("ssm", sl)])
                P.op("act", lambda e, sl=sl, n=n: e.activation(out=junk[:n, :], in_=ps[2][:n, :], func=AF.Square, accum_out=ssm2[sl][:n, :]),
                     reads=[("bank", 2)], writes=["junkA", ("ssm2", sl)])
                P.op("pool", lambda e, sl=sl, n=n: e.tensor_tensor(out=ssm[sl][:n, :], in0=ssm[sl][:n, :], in1=ssm2[sl][:n, :], op=ALU.add),
                     reads=[("ssm", sl), ("ssm2", sl)], writes=[("ssm", sl)])
                P.op("pool", lambda e, sl=sl, n=n: e.tensor_scalar(out=ssm[sl][:n, :], in0=ssm[sl][:n, :], scalar1=1.0 / D, scalar2=EPS,
                                                              op0=ALU.mult, op1=ALU.add), reads=[("ssm", sl)], writes=[("ssm", sl)])
                P.op("pool", lambda e, sl=sl, n=n: e.tensor_tensor(out=rsm[sl][:n, :], in0=ssm[sl][:n, :], in1=B.c_mhalf[:n, :], op=ALU.pow),
                     reads=[("ssm", sl)], writes=[("rsm", sl)])
                xi = x_c[0] % 2
                x_c[0] += 1
                a = q0 + s * 128
                P.dma(sname + "x%d" % xi, lambda e, xi=xi, a=a, n=n: e.dma_start(
                    out=x1r[xi][:n, :], in_=x1_scr[x1_off + a:x1_off + a + n, :]), writes=[("x1r", xi)])
                for half in range(2):
                    P.op("dve", lambda e, xi=xi, half=half, sl=sl, n=n: e.scalar_tensor_tensor(
                        out=ost[xi][:n, half * 512:(half + 1) * 512], in0=ps[1 + half][:n, :], scalar=rsm[sl][:n, 0:1],
                        in1=Cn["gmb"][:n, half * 512:(half + 1) * 512], op0=ALU.mult, op1=ALU.mult),
                         reads=[("bank", 1 + half), ("rsm", sl), "consts"], writes=[("ost", 0, half)])
                    P.op("pool", lambda e, xi=xi, half=half, n=n: e.tensor_tensor(
                        out=ost[xi][:n, half * 512:(half + 1) * 512], in0=ost[xi][:n, half * 512:(half + 1) * 512],
                        in1=x1r[xi][:n, half * 512:(half + 1) * 512], op=ALU.add),
                         reads=[("ost", 0, half), ("x1r", xi)], writes=[("ost", 0, half)])
                P.dma(sname + "o0", lambda e, xi=xi, a=a, n=n: e.dma_start(out=x2_dst[a:a + n, :], in_=ost[xi][:n, :]),
                      reads=[("ost", 0, 0), ("ost", 0, 1)])

        bcur = load_q(0)
        for ti, (q0, nq) in enumerate(tiles):
            bcur = tile_stream(ti, q0, nq, bcur)
        while dq:
            dq.pop(0)()

    for q in jobs:
        run_job(q["TK"], q["NQ"], q["KT"], q["V"], q["QT"], q["LT"], q["x1"], q["x1_off"], q["x2"], q["mask_other"])
    P.barrier()
    A.pop()


def attn_stage2(B, jobs, Cn, sname, window=(None,) * 4):
    P, A, nc = B.P, B.A, B.nc
    TKmax = max(q["TK"] for q in jobs)
    NBmax = cdiv(TKmax, 128)
    A.push()
    Wout = A.alloc([KC, D], BF16)
    if Cn.get("wout_ready"):
        assert A.off == Cn["wout_off"], (A.off, Cn["wout_off"])
    KT = A.alloc([4, NBmax * 128], BF16)
    V1 = A.alloc([NBmax, 516], BF16)
    VCH = 16
    kv_done = {}

    def issue_kv(TK, KT_scr, V_scr):
        kv_done[id(KT_scr)] = True
        NB = cdiv(TK, 128)
        for h in range(4):
            P.dma(sname + "K%d" % h, lambda e, h=h: e.dma_start(out=KT[:, h, 0:TK], in_=KT_scr[h, :, 0:TK]), writes=[("KT", h)])
        for ci, j0 in enumerate(range(0, NB, VCH)):
            j1 = min(NB, j0 + VCH)
            jf = min(j1, TK // 128)
            if jf > j0:
                P.dma(sname + "V%d" % (ci % 4), lambda e, j0=j0, jf=jf: e.dma_start(
                    out=V1[:, j0:jf, :], in_=V_scr[j0 * 128:jf * 128, :].rearrange("(j p) c -> p j c", p=128)),
                      writes=[("V1", ci)])
            if jf < j1:
                nk = min(128, TK - 128 * jf)
                P.dma(sname + "V%d" % (ci % 4), lambda e, jf=jf, nk=nk: e.dma_start(
                    out=V1[:nk, jf, :], in_=V_scr[jf * 128:jf * 128 + nk, :]), writes=[("V1", ci)])

    issue_kv(jobs[0]["TK"], jobs[0]["KT"], jobs[0]["V"])
    gmb = A.alloc([D], F32)
    P.dma("c0", lambda e: e.dma_start(out=gmb, in_=B.inputs["gmb"]), writes=["gmb"])
    Cn["gmb"] = gmb
    attn_consts(B, Cn)
    g8col = A.alloc([1], F32)
    P.dma("c0", lambda e: e.dma_start(out=g8col, in_=B.inputs["subg_col"]), writes=["g8col"])
    P.op("pool", lambda e: e.tensor_scalar(out=g8col, in0=g8col, scalar1=1.0 - LAMBDA_INIT, scalar2=None, op0=ALU.mult),
         reads=["g8col"], writes=["g8col"])
    ones_bf = A.alloc([128], BF16)
    ones_f = A.alloc([128], F32)
    P.op("pool", lambda e: e.memset(ones_bf, 1.0), writes=["ones_bf"])
    P.op("pool", lambda e: e.memset(ones_f, 1.0), writes=["ones_f"])
    if not Cn.get("wout_ready"):
        mark = A.off
        wst = [A.alloc([D], F32) for _ in range(4)]
        B.prep_weight(B.inputs["wout"], D, D, Wout, None, wst, "Wout")
        P.barrier()
        A.off = mark
    QTILE = 512
    qt = [A.alloc([4, QTILE], BF16) for _ in range(2)]
    NPT = 3
    pt = [A.alloc([2, QTILE], BF16) for _ in range(NPT)]
    dtmp = [A.alloc([2, 128], F32) for _ in range(2)]
    attT = A.alloc([4, QTILE], BF16)
    lruT = [A.alloc([4, QTILE], BF16) for _ in range(2)]
    x1r = [A.alloc([D], F32) for _ in range(2)]
    ost = A.alloc([D], F32)
    rec0 = A.alloc([QTILE], F32)
    rec1 = A.alloc([QTILE], F32)
    o_sb = A.alloc([QTILE], F32)
    o_1 = A.alloc([QTILE], F32)
    junk = A.alloc([512], BF16)
    ssm = [A.alloc([1], F32) for _ in range(2)]
    ssm2 = [A.alloc([1], F32) for _ in range(2)]
    rsm = [A.alloc([1], F32) for _ in range(2)]
    ps = B.psum
    psall = B.psum_all
    stpair = [psall[:, 0:1024].rearrange("p (c x) -> p c x", c=2), psall[:, 1024:2048].rearrange("p (c x) -> p c x", c=2)]
    OTb = (ps[4], ps[5])
    Lb = (ps[6], ps[7])
    pb, pbo, db = Cn["pb"], Cn["pbo"], Cn["db"]
    st_c = [0]
    pt_c = [0]
    dt_c = [0]
    x_c = [0]
    qb_c = [0]

    def take_pair():
        pi = st_c[0] % 2
        st_c[0] += 1
        return pi, [("bank", 2 * pi), ("bank", 2 * pi + 1)]

    def run_job(TK, NQ, KT_scr, V_scr, QT_scr, LT_scr, x1_scr, x1_off, x2_dst, mask_other):
        NB = cdiv(TK, 128)
        KOFF = TK - NQ
        assert KOFF % 128 == 0
        nkof = lambda j: min(128, TK - 128 * j)
        if not kv_done.get(id(KT_scr)):
            issue_kv(TK, KT_scr, V_scr)
        vid = lambda j: ("V1", j // VCH)
        tiles = []
        q0 = 0
        while q0 < NQ:
            nq = min(QTILE, NQ - q0)
            tiles.append((q0, nq))
            q0 += nq
        LOOK = int(os.environ.get("K_LOOK2", "2"))
        dq = []

        late = []

        def tick_late(force=False):
            for it in late:
                it[0] -= 1
            while late and (force or late[0][0] <= 0):
                late.pop(0)[1]()

        def push2(fn):
            dq.append(fn)
            while len(dq) > LOOK:
                dq.pop(0)()
                tick_late()

        def load_q(ti):
            q0, nq = tiles[ti]
            b = qb_c[0] % 2
            qb_c[0] += 1
            for h in range(4):
                P.dma(sname + "q%d" % b, lambda e, b=b, h=h, q0=q0, nq=nq: e.dma_start(
                    out=qt[b][:, h, 0:nq], in_=QT_scr[h, :, q0:q0 + nq]), writes=[("qt", b, h)])

            def ld(b=b, q0=q0, nq=nq):
                P.dma(sname + "l%d" % b, lambda e: e.dma_start(
                    out=lruT[b][:, :, 0:nq], in_=LT_scr[:, :, q0:q0 + nq].rearrange("g p t -> p g t")), writes=[("lruT", b)])
            push2(lambda: late.append([1, ld]))
            return b

        def tile_stream(ti, q0, nq, b):
            nsb = cdiv(nq, 128)
            nqs_of = lambda s: min(128, nq - 128 * s)
            jb = (KOFF + q0) // 128
            nb_next = load_q(ti + 1) if ti + 1 < len(tiles) else None
            for h in range(4):
                persub = (h == 0)
                W = window[h]
                jlo = 0 if W is None else max(0, jb - W)
                first = [True]
                jlast = jb + nsb - 1
                for j in range(jlo, jb + nsb):
                    nk = nkof(j)
                    rel = j - jb
                    s_lo = max(0, rel)
                    c0 = s_lo * 128
                    tab = pbo if (mask_other and j * 128 < KOFF) else pb
                    pi, bids = take_pair()
                    stp = stpair[pi]
                    for c in range(2):
                        P.op("pe", lambda e, stp=stp, c=c, h=h, j=j, c0=c0, nq=nq, b=b, nk=nk: e.matmul(
                            stp[:nk, c, c0:nq], lhsT=KT[c * 64:(c + 1) * 64, h, j * 128:j * 128 + nk],
                            rhs=qt[b][c * 64:(c + 1) * 64, h, c0:nq], start=True, stop=True),
                             reads=[("KT", h), ("qt", b, h)], writes=[bids[c]])
                    ri = pt_c[0] % NPT
                    pt_c[0] += 1
                    ptb = pt[ri]
                    pid = ("pt", ri)
                    c1 = c0
                    if rel >= 0:
                        nqs = nqs_of(rel)
                        di = dt_c[0] % 2
                        dt_c[0] += 1
                        for c in range(2):
                            P.op("dve", lambda e, stp=stp, di=di, h=h, c0=c0, nk=nk, nqs=nqs, c=c: e.scalar_tensor_tensor(
                                out=dtmp[di][:nk, c, :nqs], in0=stp[:nk, c, c0:c0 + nqs], scalar=0.125, in1=db[:nk, h, 0:nqs],
                                op0=ALU.mult, op1=ALU.add), reads=[bids[c], "consts"], writes=[("dtmp", di, c)])
                        bconst = 0.0 if persub else SLOPES[h] * 128.0 * rel
                        P.op("act", lambda e, ptb=ptb, di=di, c0=c0, bconst=bconst, nk=nk, nqs=nqs: e.activation(
                            out=ptb[:nk, :, c0:c0 + nqs], in_=dtmp[di][:nk, :, :nqs], func=AF.Exp, bias=bconst),
                             reads=[("dtmp", di, 0), ("dtmp", di, 1)], writes=[pid])
                        c1 = c0 + 128
                    if c1 < nq:
                        if persub:
                            for s in range(c1 // 128, nsb):
                                dj = jb + s - j
                                ce = s * 128 + nqs_of(s)
                                P.op("act", lambda e, ptb=ptb, stp=stp, s=s, ce=ce, dj=dj, tab=tab, h=h, nk=nk: e.activation(
                                    out=ptb[:nk, :, s * 128:ce], in_=stp[:nk, :, s * 128:ce], func=AF.Exp,
                                    bias=tab[:nk, h, dj + 3:dj + 4], scale=0.125),
                                     reads=bids + ["pbo"], writes=[pid])
                        else:
                            dj = jb - j
                            P.op("act", lambda e, ptb=ptb, stp=stp, c1=c1, nq=nq, dj=dj, tab=tab, h=h, nk=nk: e.activation(
                                out=ptb[:nk, :, c1:nq], in_=stp[:nk, :, c1:nq], func=AF.Exp,
                                bias=tab[:nk, h, dj + 3:dj + 4], scale=0.125),
                                 reads=bids + ["pbo"], writes=[pid])

                    def pv(ptb=ptb, pid=pid, j=j, h=h, nk=nk, c0=c0, nq=nq, first=first, last=(j == jlast)):
                        st_flag = first[0]
                        first[0] = False
                        for c in range(2):
                            P.op("pe", lambda e, c=c: e.matmul(
                                OTb[c][:, c0:nq], lhsT=V1[:nk, j, h * 129:h * 129 + 128], rhs=ptb[:nk, c, c0:nq],
                                start=st_flag, stop=last), reads=[pid, vid(j)], writes=[("bank", 4 + c)])
                        for c in range(2):
                            P.op("pe", lambda e, c=c: e.matmul(
                                Lb[c][:, c0:nq], lhsT=ones_bf[:nk, :], rhs=ptb[:nk, c, c0:nq],
                                start=st_flag, stop=last), reads=[pid, "ones_bf"], writes=[("bank", 6 + c)])
                    push2(pv)
                push2(lambda h=h, nq=nq: finalize(h, nq))
            def tail_late(q0=q0, nq=nq, b=b, nsb=nsb, nqs_of=nqs_of):
                late.append([int(os.environ.get("K_TLATE", "8")), lambda: tail(q0, nq, b, nsb, nqs_of)])
            push2(tail_late)
            return nb_next

        def finalize(h, nq):
            tick_late(force=True)
            P.op("dve", lambda e: e.tensor_copy(out=rec0[:, 0:nq], in_=Lb[0][:, 0:nq]), reads=[("bank", 6)], writes=["rec0"])
            P.op("act", lambda e: e.activation(out=rec1[:, 0:nq], in_=Lb[1][:, 0:nq], func=AF.Copy), reads=[("bank", 7)], writes=["rec1"])
            P.op("dve", lambda e: e.tensor_copy(out=o_sb[:, 0:nq], in_=OTb[0][:, 0:nq]), reads=[("bank", 4)], writes=["o_sb"])
            P.op("act", lambda e: e.activation(out=o_1[:, 0:nq], in_=OTb[1][:, 0:nq], func=AF.Copy), reads=[("bank", 5)], writes=["o_1"])
            P.op("dve", lambda e: e.reciprocal(out=rec0[:, 0:nq], in_=rec0[:, 0:nq]), reads=["rec0"], writes=["rec0"])
            P.op("dve", lambda e: e.reciprocal(out=rec1[:, 0:nq], in_=rec1[:, 0:nq]), reads=["rec1"], writes=["rec1"])
            P.op("dve", lambda e: e.tensor_tensor(out=rec0[:, 0:nq], in0=o_sb[:, 0:nq], in1=rec0[:, 0:nq], op=ALU.mult),
                 reads=["o_sb", "rec0"], writes=["rec0"])
            P.op("dve", lambda e: e.tensor_tensor(out=rec1[:, 0:nq], in0=o_1[:, 0:nq], in1=rec1[:, 0:nq], op=ALU.mult),
                 reads=["o_1", "rec1"], writes=["rec1"])
            P.op("dve", lambda e: e.scalar_tensor_tensor(out=o_sb[:, 0:nq], in0=rec1[:, 0:nq], scalar=Cn["negl"][:, 0:1],
                                                         in1=rec0[:, 0:nq], op0=ALU.mult, op1=ALU.add),
                 reads=["rec0", "rec1", "negl"], writes=["o_sb"])
            P.op("pool", lambda e: e.tensor_tensor(out=rec0[:, 0:nq], in0=o_sb[:, 0:nq], in1=o_sb[:, 0:nq], op=ALU.mult),
                 reads=["o_sb"], writes=["rec0"])
            late.append([int(os.environ.get("K_LATE", "8")), lambda: finalize_b(h, nq)])

        def finalize_b(h, nq):
            pi, bids = take_pair()
            ssb = ps[2 * pi]
            P.op("pe", lambda e: e.matmul(ssb[:, 0:nq], lhsT=ones_f[:, :], rhs=rec0[:, 0:nq], start=True, stop=True),
                 reads=["rec0", "ones_f"], writes=bids)
            P.op("act", lambda e: e.activation(out=rec1[:, 0:nq], in_=ssb[:, 0:nq], func=AF.Ln, scale=1.0 / 128, bias=EPS),
                 reads=[bids[0]], writes=["rec1"])
            P.op("act", lambda e: e.activation(out=rec1[:, 0:nq], in_=rec1[:, 0:nq], func=AF.Exp, scale=-0.5),
                 reads=["rec1"], writes=["rec1"])
            P.op("dve", lambda e: e.scalar_tensor_tensor(out=attT[:, h, 0:nq], in0=o_sb[:, 0:nq], scalar=g8col[:, 0:1],
                                                         in1=rec1[:, 0:nq], op0=ALU.mult, op1=ALU.mult),
                 reads=["o_sb", "rec1", "g8col"], writes=[("attT", h)])

        def tail(q0, nq, b, nsb, nqs_of):
            for s in range(nsb):
                n = nqs_of(s)
                pi, bids = take_pair()
                mo = (ps[2 * pi], ps[2 * pi + 1])
                for half in range(2):
                    for kk in range(8):
                        src = attT if kk < 4 else lruT[b]
                        P.op("pe", lambda e, half=half, kk=kk, src=src, s=s, n=n, mo=mo: e.matmul(
                            mo[half][:n, :], lhsT=src[:, kk % 4, s * 128:s * 128 + n],
                            rhs=Wout[:, kk, half * 512:(half + 1) * 512], start=(kk == 0), stop=(kk == 7)),
                             reads=[("attT", kk % 4), ("lruT", b), ("Wout", kk)], writes=[bids[half]])
                sl = s % 2
                P.op("act", lambda e, sl=sl, n=n, mo=mo: e.activation(out=junk[:n, :], in_=mo[0][:n, :], func=AF.Square, accum_out=ssm[sl][:n, :]),
                     reads=[bids[0]], writes=["junkA", ("ssm", sl)])
                P.op("act", lambda e, sl=sl, n=n, mo=mo: e.activation(out=junk[:n, :], in_=mo[1][:n, :], func=AF.Square, accum_out=ssm2[sl][:n, :]),
                     reads=[bids[1]], writes=["junkA", ("ssm2", sl)])
                P.op("pool", lambda e, sl=sl, n=n: e.tensor_tensor(out=ssm[sl][:n, :], in0=ssm[sl][:n, :], in1=ssm2[sl][:n, :], op=ALU.add),
                     reads=[("ssm", sl), ("ssm2", sl)], writes=[("ssm", sl)])
                P.op("pool", lambda e, sl=sl, n=n: e.tensor_scalar(out=ssm[sl][:n, :], in0=ssm[sl][:n, :], scalar1=1.0 / D, scalar2=EPS,
                                                              op0=ALU.mult, op1=ALU.add), reads=[("ssm", sl)], writes=[("ssm", sl)])
                P.op("pool", lambda e, sl=sl, n=n: e.tensor_tensor(out=rsm[sl][:n, :], in0=ssm[sl][:n, :], in1=B.c_mhalf[:n, :], op=ALU.pow),
                     reads=[("ssm", sl)], writes=[("rsm", sl)])
                xi = x_c[0] % 2
                x_c[0] += 1
                a = q0 + s * 128
                P.dma(sname + "x%d" % xi, lambda e, xi=xi, a=a, n=n: e.dma_start(
                    out=x1r[xi][:n, :], in_=x1_scr[x1_off + a:x1_off + a + n, :]), writes=[("x1r", xi)])
                for half in range(2):
                    P.op("dve", lambda e, half=half, sl=sl, n=n, mo=mo: e.scalar_tensor_tensor(
                        out=ost[:n, half * 512:(half + 1) * 512], in0=mo[half][:n, :], scalar=rsm[sl][:n, 0:1],
                        in1=Cn["gmb"][:n, half * 512:(half + 1) * 512], op0=ALU.mult, op1=ALU.mult),
                         reads=[bids[half], ("rsm", sl), "consts"], writes=[("ost", half)])
                    P.op("pool", lambda e, xi=xi, half=half, n=n: e.tensor_tensor(
                        out=ost[:n, half * 512:(half + 1) * 512], in0=ost[:n, half * 512:(half + 1) * 512],
                        in1=x1r[xi][:n, half * 512:(half + 1) * 512], op=ALU.add),
                         reads=[("ost", half), ("x1r", xi)], writes=[("ost", half)])
                P.dma(sname + "o0", lambda e, a=a, n=n: e.dma_start(out=x2_dst[a:a + n, :], in_=ost[:n, :]),
                      reads=[("ost", 0), ("ost", 1)])

        bcur = load_q(0)
        for ti, (q0, nq) in enumerate(tiles):
            bcur = tile_stream(ti, q0, nq, bcur)
        while dq:
            dq.pop(0)()
        tick_late(force=True)

    for q in jobs:
        run_job(q["TK"], q["NQ"], q["KT"], q["V"], q["QT"], q["LT"], q["x1"], q["x1_off"], q["x2"], q["mask_other"])
    P.barrier()
    A.pop()


def cache_prep_ops(B, ck, cv, KT_s, V_s, PAST, sname):
    P, A = B.P, B.A
    NSTEP = PAST // 512
    kin = [A.alloc([4, 512], F32) for _ in range(2)]
    kbf = [A.alloc([4, 512], BF16) for _ in range(2)]
    kT = [A.alloc([4, 512], BF16) for _ in range(2)]
    vin = [A.alloc([4, 512], F32) for _ in range(2)]
    vb = [A.alloc([4, 516], BF16) for _ in range(2)]
    for i in range(2):
        P.op("pool", lambda e, i=i: e.memset(vb[i], 1.0), writes=[("cvb", i)])
    ps = B.psum
    for st in range(NSTEP):
        r = st % 2
        a = st * 512
        P.dma(sname + "k%d" % r, lambda e, r=r, a=a: e.dma_start(
            out=kin[r], in_=ck[a:a + 512, :].rearrange("(j p) c -> p j c", p=128)), writes=[("ckin", r)])
        P.dma(sname + "v%d" % r, lambda e, r=r, a=a: e.dma_start(
            out=vin[r], in_=cv[a:a + 512, :].rearrange("(j p) c -> p j c", p=128)), writes=[("cvin", r)])
        P.op("dve", lambda e, r=r: e.tensor_copy(out=kbf[r], in_=kin[r]), reads=[("ckin", r)], writes=[("ckbf", r)])
        for jj in range(4):
            bank = ps[4 + (st * 4 + jj) % 4]
            bid = ("bank", 4 + (st * 4 + jj) % 4)
            ps_tr = bank[:, :].bitcast(BF16)
            for h in range(4):
                P.op("pe", lambda e, ps_tr=ps_tr, h=h, r=r, jj=jj: e.transpose(
                    out=ps_tr[:, h * 128:(h + 1) * 128], in_=kbf[r][:, jj, h * 128:(h + 1) * 128], identity=B.ident),
                     reads=[("ckbf", r)], writes=[bid])
            P.op("act", lambda e, ps_tr=ps_tr, r=r, jj=jj: e.activation(
                out=kT[r][:, :, jj * 128:(jj + 1) * 128], in_=ps_tr[:, 0:512].rearrange("p (a b) -> p a b", a=4), func=AF.Copy),
                 reads=[bid], writes=[("ckT", r, jj)])
        P.dma(sname + "ko%d" % r, lambda e, r=r, a=a: e.dma_start(
            out=KT_s[:, :, a:a + 512].rearrange("h p t -> p h t"), in_=kT[r]),
              reads=[("ckT", r, x) for x in range(4)])
        for jj in range(4):
            P.op("pool" if jj % 2 else "dve", lambda e, r=r, jj=jj: e.tensor_copy(
                out=vb[r][:, jj, :].rearrange("p (h d) -> p h d", h=4)[:, :, 0:128],
                in_=vin[r][:, jj, :].rearrange("p (h d) -> p h d", h=4)),
                 reads=[("cvin", r), ("cvb", r)], writes=[("cvb", r, jj)])
        P.dma(sname + "vo%d" % r, lambda e, r=r, a=a: e.dma_start(
            out=V_s[a:a + 512, :].rearrange("(j p) c -> p j c", p=128), in_=vb[r]),
              reads=[("cvb", r, x) for x in range(4)] + [("cvb", r)])


SMALL_INPUTS = [
    ("gma_col", [KC]), ("g1a_col", [KC]), ("g2a_col", [KC]),
    ("convw", [4, 4]), ("convb", [4]), ("brg", [4]), ("big", [4]), ("lam", [4]),
    ("flag", [1]), ("maskv", [1]),
]


def build_main(T_OTH, T_OWN, with_sample=True, window=(None,) * 4, stages=("A1", "A2", "C", "D")):
    if os.environ.get("K_WIN", "1") == "1":
        window = (4, 16, None, None)
    B = Builder(T_OTH, T_OWN)
    P = B.P
    T = T_OTH + T_OWN
    x = B.din("x", [T, D])
    for nm, shp in (("f1g", [D, DFF]), ("f1u", [D, DFF]), ("f1d", [DFF, D]),
                    ("f2g", [D, DFF]), ("f2u", [D, DFF]), ("f2d", [DFF, D]),
                    ("win", [D, 2560]), ("wout", [D, D])):
        B.din(nm, shp)
    for nm, shp in (("g1b_bc", [128, D]), ("g2b_bc", [128, D]), ("gmb", [128, D]),
                    ("wr_bd", [128, 4, 128]), ("wi_bd", [128, 4, 128]),
                    ("pb", [128, 4, 71]), ("db", [128, 4, 128]), ("subg", [128, 128]), ("subg_col", [128, 1]),
                    ("lq1", [128, 64]), ("lk1", [128, 64]), ("lq2", [128, 64]), ("lk2", [128, 64])):
        B.din(nm, shp)
    y = B.dout("y", [T_OWN, D])
    ko = B.dout("ko", [T_OWN, 512])
    vo = B.dout("vo", [T_OWN, 512])
    hl = B.dout("hl", [512])
    cb = B.dout("cb", [3, 512])
    x1_scr = B.dscr("x1_scr", [T, D])
    x2_scr = B.dscr("x2_scr", [T_OWN, D])
    KT_scr = B.dscr("KT_scr", [4, 128, T], BF16)
    V_scr = B.dscr("V_scr", [T, 516], BF16)
    QT_scr = B.dscr("QT_scr", [4, 128, T_OWN], BF16)
    LT_scr = B.dscr("LT_scr", [4, 128, T_OWN], BF16)
    S = {}
    NSAMP, PAST = 32, 4096
    if with_sample:
        S["xs"] = B.din("xs", [NSAMP, D])
        S["ck"] = B.din("ck", [PAST, 512])
        S["cv"] = B.din("cv", [PAST, 512])
        S["sh"] = B.din("sh", [512])
        S["sc"] = B.din("sc", [3, 512])
        S["ys"] = B.dout("ys", [NSAMP, D])
        S["kso"] = B.dout("kso", [NSAMP, 512])
        S["vso"] = B.dout("vso", [NSAMP, 512])
        S["hls"] = B.dout("hls", [512])
        S["cbs"] = B.dout("cbs", [3, 512])
        S["xs1"] = B.dscr("xs1_scr", [NSAMP, D])
        S["xs2"] = B.dscr("xs2_scr", [NSAMP, D])
        S["KT"] = B.dscr("KTs_scr", [4, 128, PAST + NSAMP], BF16)
        S["V"] = B.dscr("Vs_scr", [PAST + NSAMP, 516], BF16)
        S["QT"] = B.dscr("QTs_scr", [4, 128, NSAMP], BF16)
        S["LT"] = B.dscr("LTs_scr", [4, 128, NSAMP], BF16)
    B.consts()
    Cn = {}
    for nm, shp in SMALL_INPUTS:
        Cn[nm] = B.load_const(nm, shp)
    P.barrier()
    inp = B.inputs
    if "A1" in stages:
        segs = [(x, x1_scr, T)] + ([(S["xs"], S["xs1"], NSAMP)] if with_sample else [])
        B.ffn_stage(segs, inp["f1g"], inp["f1u"], inp["f1d"], Cn["g1a_col"], inp["g1b_bc"], "a")
    if with_sample and "A2" in stages:
        Cn["cache_prep"] = (S["ck"], S["cv"], S["KT"], S["V"], PAST, "p")
    if "A2" in stages:
        seqs = [dict(x1=x1_scr, T=T, T_OTH=T_OTH, KT=KT_scr, V=V_scr, QT=QT_scr, LT=LT_scr, ko=ko, vo=vo, hl=hl, cb=cb)]
        if with_sample:
            seqs.append(dict(x1=S["xs1"], T=NSAMP, T_OTH=0, KT=S["KT"], V=S["V"], QT=S["QT"], LT=S["LT"],
                             ko=S["kso"], vo=S["vso"], hl=S["hls"], cb=S["cbs"], h0=S["sh"], conv0=S["sc"], koff=PAST))
        mixer_in_stage(B, seqs, Cn, "m")
    if "C" in stages:
        jobs = [dict(TK=T, NQ=T_OWN, KT=KT_scr, V=V_scr, QT=QT_scr, LT=LT_scr, x1=x1_scr, x1_off=T_OTH, x2=x2_scr,
                     mask_other=True)]
        if with_sample:
            jobs.append(dict(TK=PAST + NSAMP, NQ=NSAMP, KT=S["KT"], V=S["V"], QT=S["QT"], LT=S["LT"], x1=S["xs1"],
                             x1_off=0, x2=S["xs2"], mask_other=False))
        (attn_stage2 if os.environ.get("K_ATT", "2") == "2" else attn_stage)(B, jobs, Cn, "c", window=window)
    if "D" in stages:
        segs = [(x2_scr, y, T_OWN)] + ([(S["xs2"], S["ys"], NSAMP)] if with_sample else [])
        B.ffn_stage(segs, inp["f2g"], inp["f2u"], inp["f2d"], Cn["g2a_col"], inp["g2b_bc"], "d")
    B.P.emit()
    return B


def _col(v, n):
    return np.ascontiguousarray(np.asarray(v, np.float32).reshape(n, 128).T)


def _bc(v):
    v = np.asarray(v, np.float32).reshape(1, -1)
    return np.ascontiguousarray(np.broadcast_to(v, (128, v.shape[1])))


def _block_diag(w):
    out = np.zeros((128, 4, 128), np.float32)
    for g in range(4):
        for hb in range(2):
            out[hb * 64:(hb + 1) * 64, g, hb * 64:(hb + 1) * 64] = w[2 * g + hb]
    return out


def _tables():
    k = np.arange(128, dtype=np.float64)
    pb = np.zeros((128, 4, 71), np.float32)
    db = np.zeros((128, 4, 128), np.float32)
    kk = k[:, None]
    qq = k[None, :]
    for h in range(4):
        sl = SLOPES[h]
        for dj in range(-3, 68):
            pb[:, h, dj + 3] = sl * (k - 128.0 * dj)
        v = np.where(kk <= qq, sl * kk, sl * (2 * qq - kk))
        v = np.where((kk // 64) > (qq // 64), NEG, v)
        db[:, h, :] = v
    return pb, db


def shared_inputs(inputs):
    import ml_dtypes
    g = lambda n: np.asarray(inputs[n], np.float32)
    pb, db = _tables()
    d = {
        "f1g": g("ffn1_w_gate")[0], "f1u": g("ffn1_w_up")[0], "f1d": g("ffn1_w_down")[0],
        "f2g": g("ffn2_w_gate")[0], "f2u": g("ffn2_w_up")[0], "f2d": g("ffn2_w_down")[0],
        "win": g("w_in")[0], "wout": g("w_out")[0],
        "g1b_bc": _bc(g("g_ffn1_post")[0]), "g2b_bc": _bc(g("g_ffn2_post")[0]), "gmb": _bc(g("g_mix_post")[0]),
        "wr_bd": _block_diag(g("w_rgate")[0]), "wi_bd": _block_diag(g("w_igate")[0]),
        "pb": pb, "db": db, "subg": _bc(g("subln_g")[0]), "subg_col": _col(g("subln_g")[0], 1),
        "lq1": _bc(g("lambda_q1")[0]), "lk1": _bc(g("lambda_k1")[0]),
        "lq2": _bc(g("lambda_q2")[0]), "lk2": _bc(g("lambda_k2")[0]),
        "gma_col": _col(g("g_mix_pre")[0], 8), "g1a_col": _col(g("g_ffn1_pre")[0], 8), "g2a_col": _col(g("g_ffn2_pre")[0], 8),
        "convw": np.ascontiguousarray(g("conv_w")[0].reshape(4, 4, 128).transpose(2, 1, 0)),
        "convb": _col(g("conv_b")[0], 4), "brg": _col(g("b_rgate")[0], 4), "big": _col(g("b_igate")[0], 4),
        "lam": _col(g("lru_lambda")[0], 4),
        "ident": np.eye(128).astype(ml_dtypes.bfloat16),
    }
    return d


_CACHE = {}


def kernel(**inputs):
    TH = 4096
    WITH_SAMPLE = bool(int(os.environ.get("K_SAMPLE", "1")))
    key = ("main", TH, WITH_SAMPLE)
    if key not in _CACHE:
        _CACHE[key] = build_main(TH, TH, with_sample=WITH_SAMPLE)
    B = _CACHE[key]
    sh = shared_inputs(inputs)
    xp = np.asarray(inputs["x_prompt"], np.float32)
    maps = []
    for c in range(8):
        b, r = c // 2, c % 2
        own = xp[b, r * TH:(r + 1) * TH]
        oth = xp[b, (1 - r) * TH:(2 - r) * TH]
        m = dict(sh)
        m["x"] = np.ascontiguousarray(np.concatenate([oth, own], 0))
        m["flag"] = np.full((128, 1), float(r), np.float32)
        m["maskv"] = np.full((128, 1), 0.0 if r == 1 else NEG, np.float32)
        if WITH_SAMPLE:
            m["xs"] = np.ascontiguousarray(np.asarray(inputs["x_sample"], np.float32)[c])
            m["ck"] = np.ascontiguousarray(np.asarray(inputs["cache_k"], np.float32)[0, c].reshape(4096, 512))
            m["cv"] = np.ascontiguousarray(np.asarray(inputs["cache_v"], np.float32)[0, c].reshape(4096, 512))
            m["sh"] = np.ascontiguousarray(np.asarray(inputs["state_lru_h"], np.float32)[0, c])
            m["sc"] = np.ascontiguousarray(np.asarray(inputs["state_conv"], np.float32)[0, c])
        m = {k: v for k, v in m.items() if k in B.inputs}
        maps.append(m)
    res = run_bass_kernel_spmd(B.nc, maps, core_ids=list(range(8))).results
    y = np.zeros((4, 8192, 1024), np.float32)
    kp = np.zeros((1, 4, 8192, 4, 2, 64), np.float32)
    vp = np.zeros((1, 4, 8192, 4, 128), np.float32)
    hp = np.zeros((1, 4, 512), np.float32)
    cp = np.zeros((1, 4, 3, 512), np.float32)
    ys = np.zeros((8, 32, 1024), np.float32)
    ks = np.zeros((1, 8, 32, 4, 2, 64), np.float32)
    vs = np.zeros((1, 8, 32, 4, 128), np.float32)
    hs = np.zeros((1, 8, 512), np.float32)
    cs = np.zeros((1, 8, 3, 512), np.float32)
    for c in range(8):
        b, r = c // 2, c % 2
        o = res[c]
        sl = slice(r * TH, (r + 1) * TH)
        y[b, sl] = o["y"]
        kp[0, b, sl] = o["ko"].reshape(TH, 4, 2, 64)
        vp[0, b, sl] = o["vo"].reshape(TH, 4, 128)
        if r == 1:
            hp[0, b] = o["hl"]
            cp[0, b] = o["cb"]
        if WITH_SAMPLE:
            ys[c] = o["ys"]
            ks[0, c] = o["kso"].reshape(32, 4, 2, 64)
            vs[0, c] = o["vso"].reshape(32, 4, 128)
            hs[0, c] = o["hls"]
            cs[0, c] = o["cbs"]
    return (y, ys, kp, vp, hp, cp, ks, vs, hs, cs)
```
